# Optimizing a Trainium2 kernel written in Bass

```python
import math
import jax
import jax.numpy as jnp
from jax import lax
import numpy as np

D_MODEL = 1024
BATCH = 16
SEQ = 4096
DEPTH = 4

GRID_W = 64
CTX_LEN = 256
N_MIXERS = 3
N_RET = len(range(0, DEPTH, N_MIXERS))
N_HG = len(range(1, DEPTH, N_MIXERS))
N_M2 = len(range(2, DEPTH, N_MIXERS))
NORM_EPS = 1e-6

RET_HEADS = 4
RET_DK = D_MODEL // RET_HEADS
RET_DV = 2 * RET_DK
RET_PROJ = 2 * RET_HEADS * RET_DK + 2 * RET_HEADS * RET_DV
RET_CHUNK = 64
ROPE_BASE = 10000.0

HG_DK = 128
HG_HEADS = D_MODEL // HG_DK
HG_PROJ = 5 * D_MODEL
HG_CHUNK = 32

M2_DINNER = 2 * D_MODEL
M2_HEADDIM = 64
M2_HEADS = M2_DINNER // M2_HEADDIM
M2_GROUPS = 4
M2_HPG = M2_HEADS // M2_GROUPS
M2_DSTATE = 128
M2_CONV_DIM = M2_DINNER + 2 * M2_GROUPS * M2_DSTATE
M2_PROJ = M2_DINNER + M2_CONV_DIM + 2 * M2_HEADS
M2_CONV_W = 3
M2_CHUNK = 64

FFN_HIDDEN = 2816
FFN_CONV_W = 3

kernel_name = 'hybrid_ret_hgrn2_ssd_prefix_dit'


def rms_norm(x, g):
    xf = x.astype(jnp.float32)
    y = xf * lax.rsqrt(jnp.mean(xf * xf, -1, keepdims=True) + NORM_EPS)
    return (y * g.astype(jnp.float32)).astype(x.dtype)


def dwconv(x, w, b):
    width, ch = w.shape
    y = lax.conv_general_dilated(x, w[:, None, :].astype(x.dtype), window_strides=(1,),
                                 padding=[(width // 2, width // 2)],
                                 dimension_numbers=('NWC', 'WIO', 'NWC'), feature_group_count=ch)
    return y + b.astype(x.dtype)


def row_conv(x, w, b):
    bt, length, ch = x.shape
    rows = length // GRID_W
    return dwconv(x.reshape(bt * rows, GRID_W, ch), w, b).reshape(bt, length, ch)


def axial_rope(t):
    length, dk = t.shape[2], t.shape[-1]
    nf = dk // 4
    pos = jnp.arange(length)
    row = (pos // GRID_W).astype(jnp.float32)
    col = (pos % GRID_W).astype(jnp.float32)
    inv = ROPE_BASE ** (-jnp.arange(nf, dtype=jnp.float32) / nf)

    def rot(u, p):
        ang = p[:, None] * inv
        cos, sin = jnp.cos(ang), jnp.sin(ang)
        u1, u2 = u[..., :nf], u[..., nf:]
        return jnp.concatenate([u1 * cos - u2 * sin, u1 * sin + u2 * cos], -1)

    tf = t.astype(jnp.float32)
    return jnp.concatenate([rot(tf[..., :2 * nf], row), rot(tf[..., 2 * nf:], col)], -1).astype(t.dtype)


def chunk_scan(q, k, v, log_a, s0, chunk, return_out):
    f32 = jnp.float32
    bt, g, length, kd = q.shape
    r, vd = v.shape[2], v.shape[-1]
    n = length // chunk
    per_channel = log_a.shape[-1] != 1

    def blocks(t, axis):
        t = t.astype(f32).reshape(t.shape[:axis] + (n, chunk) + t.shape[axis + 1:])
        return jnp.moveaxis(t, axis, 0)

    tril = jnp.tril(jnp.ones((chunk, chunk), dtype=bool))[:, :, None]

    def step(state, inp):
        qc, kc, vc, ac = inp
        b = jnp.cumsum(ac, axis=3)
        b_last = b[:, :, :, -1:, :]
        kw = kc[:, :, None] * jnp.exp(b_last - b)
        new_state = state * jnp.exp(b_last)[:, :, :, 0, :, None] + jnp.einsum('bgrsk,bgrsv->bgrkv', kw, vc)
        if not return_out:
            return new_state, None
        o = jnp.einsum('bgrtk,bgrkv->bgrtv', qc[:, :, None] * jnp.exp(b), state)
        diff = b[:, :, :, :, None, :] - b[:, :, :, None, :, :]
        dec = jnp.exp(jnp.where(tril, diff, -jnp.inf))
        if per_channel:
            scores = jnp.einsum('bgtk,bgrtsk,bgsk->bgrts', qc, dec, kc)
        else:
            scores = jnp.einsum('bgtk,bgsk->bgts', qc, kc)[:, :, None] * dec[..., 0]
        o = o + jnp.einsum('bgrts,bgrsv->bgrtv', scores, vc)
        return new_state, o

    state, o = lax.scan(step, s0, (blocks(q, 2), blocks(k, 2), blocks(v, 3), blocks(log_a, 3)))
    if return_out:
        o = jnp.moveaxis(o, 0, 3).reshape(bt, g, r, length, vd)
    return o, state


def bidir_scan(q_c, k_c, v_c, a_c, q_l, k_l, v_l, a_l, chunk, need_ctx):
    bt, g, _, kd = q_l.shape
    r, vd = v_l[0].shape[2], v_l[0].shape[-1]
    outs_c, outs_l = [], []
    for d in range(2):
        rev = (lambda t, ax: jnp.flip(t, ax)) if d == 1 else (lambda t, ax: t)
        s0 = jnp.zeros((bt, g, r, kd, vd), jnp.float32)
        o_c, s_c = chunk_scan(rev(q_c, 2), rev(k_c[d], 2), rev(v_c[d], 3), rev(a_c[d], 3), s0, chunk, need_ctx)
        o_l, _ = chunk_scan(rev(q_l, 2), rev(k_l[d], 2), rev(v_l[d], 3), rev(a_l[d], 3), s_c, chunk, True)
        outs_l.append(rev(o_l, 3))
        if need_ctx:
            outs_c.append(rev(o_c, 3))
    y_c = outs_c[0] + outs_c[1] if need_ctx else None
    return y_c, outs_l[0] + outs_l[1]


def retention_mixer(u_c, u_l, w_in, w_out, decay, gn, need_ctx):
    hk, hv = RET_HEADS * RET_DK, RET_HEADS * RET_DV
    log_gamma = jax.nn.log_sigmoid(decay.astype(jnp.float32))

    def project(u, grid):
        bt, length, _ = u.shape
        q, k, v, g = jnp.split(u @ w_in, [hk, 2 * hk, 2 * hk + hv], axis=-1)
        q = q.reshape(bt, length, RET_HEADS, RET_DK).transpose(0, 2, 1, 3)
        k = k.reshape(bt, length, RET_HEADS, RET_DK).transpose(0, 2, 1, 3) * (RET_DK ** -0.5)
        if grid:
            q, k = axial_rope(q), axial_rope(k)
        v = v.reshape(bt, length, RET_HEADS, RET_DV).transpose(0, 2, 1, 3)[:, :, None]
        a = tuple(jnp.broadcast_to(log_gamma[d][None, :, None, None, None], (bt, RET_HEADS, 1, length, 1))
                  for d in range(2))
        return q, (k, k), (v, v), a, g

    def readout(o, g):
        bt, _, _, length, _ = o.shape
        o = o[:, :, 0].transpose(0, 2, 1, 3)
        o = o - o.mean(-1, keepdims=True)
        o = o * lax.rsqrt(jnp.mean(o * o, -1, keepdims=True) + NORM_EPS)
        o = o.reshape(bt, length, hv) * gn.astype(jnp.float32)
        return (jax.nn.silu(g.astype(jnp.float32)) * o).astype(w_out.dtype) @ w_out

    q_c, k_c, v_c, a_c, g_c = project(u_c, False)
    q_l, k_l, v_l, a_l, g_l = project(u_l, True)
    o_c, o_l = bidir_scan(q_c, k_c, v_c, a_c, q_l, k_l, v_l, a_l, RET_CHUNK, need_ctx)
    y_c = readout(o_c, g_c) if need_ctx else None
    return y_c, readout(o_l, g_l)


def hgrn2_mixer(u_c, u_l, w_in, w_out, lb, gn, need_ctx):
    lbh = lb.astype(jnp.float32).reshape(1, HG_HEADS, 1, HG_DK)

    def project(u):
        bt, length, _ = u.shape
        heads = lambda t: t.astype(jnp.float32).reshape(bt, length, HG_HEADS, HG_DK).transpose(0, 2, 1, 3)
        q, f_f, f_b, i, g = jnp.split(u @ w_in, 5, axis=-1)
        ks, las = [], []
        for f in (f_f, f_b):
            f = heads(f)
            ks.append((1.0 - lbh) * jax.nn.sigmoid(-f))
            las.append(jnp.logaddexp(jnp.log(lbh), jnp.log1p(-lbh) + jax.nn.log_sigmoid(f))[:, :, None])
        v = heads(i)[:, :, None]
        return jax.nn.silu(heads(q)), tuple(ks), (v, v), tuple(las), g

    def readout(o, g):
        bt, _, _, length, _ = o.shape
        o = o[:, :, 0].transpose(0, 2, 1, 3)
        o = o * lax.rsqrt(jnp.mean(o * o, -1, keepdims=True) + NORM_EPS)
        o = o.reshape(bt, length, D_MODEL) * gn.astype(jnp.float32)
        return (jax.nn.silu(g.astype(jnp.float32)) * o).astype(w_out.dtype) @ w_out

    q_c, k_c, v_c, a_c, g_c = project(u_c)
    q_l, k_l, v_l, a_l, g_l = project(u_l)
    o_c, o_l = bidir_scan(q_c, k_c, v_c, a_c, q_l, k_l, v_l, a_l, HG_CHUNK, need_ctx)
    y_c = readout(o_c, g_c) if need_ctx else None
    return y_c, readout(o_l, g_l)


def mamba2_mixer(u_c, u_l, w_in, w_out, conv_w, conv_b, dt_bias, a_log, d_skip, gn, need_ctx):
    f32 = jnp.float32
    a_neg = -jnp.exp(a_log.astype(f32))
    gsz = M2_DINNER // M2_GROUPS

    def project(u):
        bt, length, _ = u.shape
        z, xbc, dt = jnp.split(u @ w_in, [M2_DINNER, M2_DINNER + M2_CONV_DIM], axis=-1)
        xbc = jax.nn.silu(dwconv(xbc, conv_w, conv_b))
        xs, bm, cm = jnp.split(xbc, [M2_DINNER, M2_DINNER + M2_GROUPS * M2_DSTATE], axis=-1)
        xh = xs.astype(f32).reshape(bt, length, M2_GROUPS, M2_HPG, M2_HEADDIM).transpose(0, 2, 3, 1, 4)
        kb = bm.reshape(bt, length, M2_GROUPS, M2_DSTATE).transpose(0, 2, 1, 3)
        qc = cm.reshape(bt, length, M2_GROUPS, M2_DSTATE).transpose(0, 2, 1, 3)
        dt = jax.nn.softplus(dt.astype(f32).reshape(bt, length, 2, M2_HEADS) + dt_bias.astype(f32))
        dt = dt.reshape(bt, length, 2, M2_GROUPS, M2_HPG).transpose(2, 0, 3, 4, 1)[..., None]
        la = dt * a_neg.reshape(2, 1, M2_GROUPS, M2_HPG, 1, 1)
        return qc, (kb, kb), (xh * dt[0], xh * dt[1]), (la[0], la[1]), xh, z

    def readout(o, xh, z):
        bt, _, _, length, _ = o.shape
        y = o + d_skip.astype(f32).reshape(1, M2_GROUPS, M2_HPG, 1, 1) * xh
        y = y.transpose(0, 3, 1, 2, 4).reshape(bt, length, M2_GROUPS, gsz)
        y = y * jax.nn.silu(z.astype(f32).reshape(bt, length, M2_GROUPS, gsz))
        y = y * lax.rsqrt(jnp.mean(y * y, -1, keepdims=True) + NORM_EPS)
        y = y.reshape(bt, length, M2_DINNER) * gn.astype(f32)
        return y.astype(w_out.dtype) @ w_out

    q_c, k_c, v_c, a_c, x_c, z_c = project(u_c)
    q_l, k_l, v_l, a_l, x_l, z_l = project(u_l)
    o_c, o_l = bidir_scan(q_c, k_c, v_c, a_c, q_l, k_l, v_l, a_l, M2_CHUNK, need_ctx)
    y_c = readout(o_c, x_c, z_c) if need_ctx else None
    return y_c, readout(o_l, x_l, z_l)


def conv_ffn(u, w_up, conv_w, conv_b, w_down, grid):
    a, v = jnp.split(u @ w_up, 2, axis=-1)
    a = row_conv(a, conv_w, conv_b) if grid else dwconv(a, conv_w, conv_b)
    return (jax.nn.gelu(a) * v) @ w_down


def setup_inputs(seed: int = 0) -> dict:
    key = jax.random.key(seed)
    keys = iter(jax.random.split(key, 32))
    f32 = jnp.float32

    def nrm(shape, scale):
        return jax.random.normal(next(keys), shape, f32) * scale

    D = D_MODEL
    x = nrm((BATCH, SEQ, D), 1.0)
    c = nrm((BATCH, D), 1.0)
    ctx = nrm((BATCH, CTX_LEN, D), 1.0)
    c_ctx = nrm((D,), 1.0)
    ada_w = nrm((DEPTH, D, 6 * D), 0.5 * D ** -0.5)
    ada_b = nrm((DEPTH, 6 * D), 0.02)
    norm_g = 1.0 + nrm((DEPTH, 4, D), 0.05)
    ret_w_in = nrm((N_RET, D, RET_PROJ), D ** -0.5)
    ret_w_out = nrm((N_RET, RET_HEADS * RET_DV, D), (RET_HEADS * RET_DV) ** -0.5)
    eps = 2.0 ** (-5.0 - jnp.arange(RET_HEADS, dtype=f32))
    ret_decay = jnp.log((1.0 - eps) / eps) + nrm((N_RET, 2, RET_HEADS), 0.1)
    ret_gn = 1.0 + nrm((N_RET, RET_HEADS * RET_DV), 0.05)
    hg_w_in = nrm((N_HG, D, HG_PROJ), D ** -0.5)
    hg_w_out = nrm((N_HG, D, D), D ** -0.5)
    hg_lb = nrm((DEPTH, HG_HEADS * HG_DK), 0.1)
    hg_gn = 1.0 + nrm((N_HG, D), 0.05)
    m2_w_in = nrm((N_M2, D, M2_PROJ), D ** -0.5)
    m2_w_out = nrm((N_M2, M2_DINNER, D), M2_DINNER ** -0.5)
    m2_conv_w = nrm((N_M2, M2_CONV_W, M2_CONV_DIM), M2_CONV_W ** -0.5)
    m2_conv_b = nrm((N_M2, M2_CONV_DIM), 0.02)
    dt = jnp.exp(jax.random.uniform(next(keys), (N_M2, 2, M2_HEADS), f32, math.log(1e-3), math.log(1e-1)))
    m2_dt_bias = dt + jnp.log(-jnp.expm1(-dt))
    m2_a_log = jnp.log(jax.random.uniform(next(keys), (N_M2, 2, M2_HEADS), f32, 1.0, 16.0))
    m2_d = 1.0 + nrm((N_M2, M2_HEADS), 0.1)
    m2_gn = 1.0 + nrm((N_M2, M2_DINNER), 0.05)
    ffn_w_up = nrm((DEPTH, D, 2 * FFN_HIDDEN), D ** -0.5)
    ffn_conv_w = nrm((DEPTH, FFN_CONV_W, FFN_HIDDEN), FFN_CONV_W ** -0.5)
    ffn_conv_b = nrm((DEPTH, FFN_HIDDEN), 0.02)
    ffn_w_down = nrm((DEPTH, FFN_HIDDEN, D), FFN_HIDDEN ** -0.5)
    return {'x': x, 'c': c, 'ctx': ctx, 'c_ctx': c_ctx, 'ada_w': ada_w, 'ada_b': ada_b, 'norm_g': norm_g,
            'ret_w_in': ret_w_in, 'ret_w_out': ret_w_out, 'ret_decay': ret_decay, 'ret_gn': ret_gn,
            'hg_w_in': hg_w_in, 'hg_w_out': hg_w_out, 'hg_lb': hg_lb, 'hg_gn': hg_gn,
            'm2_w_in': m2_w_in, 'm2_w_out': m2_w_out, 'm2_conv_w': m2_conv_w, 'm2_conv_b': m2_conv_b,
            'm2_dt_bias': m2_dt_bias, 'm2_a_log': m2_a_log, 'm2_d': m2_d, 'm2_gn': m2_gn,
            'ffn_w_up': ffn_w_up, 'ffn_conv_w': ffn_conv_w, 'ffn_conv_b': ffn_conv_b, 'ffn_w_down': ffn_w_down}


def reference(x, c, ctx, c_ctx, ada_w, ada_b, norm_g, ret_w_in, ret_w_out, ret_decay, ret_gn,
              hg_w_in, hg_w_out, hg_lb, hg_gn, m2_w_in, m2_w_out, m2_conv_w, m2_conv_b,
              m2_dt_bias, m2_a_log, m2_d, m2_gn, ffn_w_up, ffn_conv_w, ffn_conv_b, ffn_w_down):
    h_l, h_c = x, ctx
    lb_cum = jnp.cumsum(jax.nn.softmax(hg_lb.astype(jnp.float32), axis=0), axis=0)
    lb_all = lb_cum - lb_cum[0]
    for i in range(DEPTH):
        kind, j = i % N_MIXERS, i // N_MIXERS
        need_ctx = i < DEPTH - 1
        mod_l = (jax.nn.silu(c) @ ada_w[i] + ada_b[i])[:, None, :]
        mod_c = jax.nn.silu(c_ctx) @ ada_w[i] + ada_b[i]
        sh1_l, sc1_l, g1_l, sh2_l, sc2_l, g2_l = jnp.split(mod_l, 6, axis=-1)
        sh1_c, sc1_c, g1_c, sh2_c, sc2_c, g2_c = jnp.split(mod_c, 6, axis=-1)
        u_l = rms_norm(h_l, norm_g[i, 0]) * (1.0 + sc1_l) + sh1_l
        u_c = rms_norm(h_c, norm_g[i, 0]) * (1.0 + sc1_c) + sh1_c
        if kind == 0:
            y_c, y_l = retention_mixer(u_c, u_l, ret_w_in[j], ret_w_out[j], ret_decay[j], ret_gn[j], need_ctx)
        elif kind == 1:
            y_c, y_l = hgrn2_mixer(u_c, u_l, hg_w_in[j], hg_w_out[j], lb_all[i], hg_gn[j], need_ctx)
        else:
            y_c, y_l = mamba2_mixer(u_c, u_l, m2_w_in[j], m2_w_out[j], m2_conv_w[j], m2_conv_b[j],
                                    m2_dt_bias[j], m2_a_log[j], m2_d[j], m2_gn[j], need_ctx)
        h_l = h_l + g1_l * rms_norm(y_l, norm_g[i, 1])
        u_l = rms_norm(h_l, norm_g[i, 2]) * (1.0 + sc2_l) + sh2_l
        h_l = h_l + g2_l * rms_norm(conv_ffn(u_l, ffn_w_up[i], ffn_conv_w[i], ffn_conv_b[i], ffn_w_down[i], True),
                                    norm_g[i, 3])
        if need_ctx:
            h_c = h_c + g1_c * rms_norm(y_c, norm_g[i, 1])
            u_c = rms_norm(h_c, norm_g[i, 2]) * (1.0 + sc2_c) + sh2_c
            h_c = h_c + g2_c * rms_norm(conv_ffn(u_c, ffn_w_up[i], ffn_conv_w[i], ffn_conv_b[i], ffn_w_down[i], False),
                                        norm_g[i, 3])
    return h_l
```

```python
import contextlib
import numpy as np
import concourse.bass as bass
import concourse.mybir as mybir
from concourse.bass_utils import run_bass_kernel_spmd

F32 = mybir.dt.float32
BF16 = mybir.dt.bfloat16
AF = mybir.ActivationFunctionType
ALU = mybir.AluOpType

PE, ACT, DVE, POOL, SP = "pe", "act", "dve", "pool", "sp"
CENG = (PE, ACT, DVE, POOL)

D = 1024
EPS = 1e-6
GRID_W = 64
N_CORES = 8


class Buf:
    __slots__ = ("name", "last_w", "readers", "dram")

    def __init__(self, name, dram=False):
        self.name = name
        self.last_w = None
        self.readers = []
        self.dram = dram


class Op:
    __slots__ = ("eng", "fn", "deps", "is_dma", "signal", "val", "dsem_buf")

    def __init__(self, eng, fn, is_dma, dsem_buf):
        self.eng = eng
        self.fn = fn
        self.deps = set()
        self.is_dma = is_dma
        self.signal = False
        self.val = None
        self.dsem_buf = dsem_buf


class Prog:
    def __init__(self, same_eng_sync=True):
        self.nc = bass.Bass("TRN2", target_bir_lowering=False)
        self.ops = []
        self.same_eng_sync = same_eng_sync
        self.keep = []
        self.bufs = []
        nc = self.nc
        self.engs = {PE: nc.tensor, ACT: nc.scalar, DVE: nc.vector, POOL: nc.gpsimd, SP: nc.sync}
        self.sems = {}
        for e in CENG:
            cm = nc.semaphore("s_" + e)
            self.sems[e] = cm.__enter__()
            self.keep.append(cm)
        self.cnt = {e: 0 for e in CENG}
        self.dsem_free = {True: [], False: []}
        self.dsem_all = []
        self.waited = {}
        self.n_inst = 0
        self.n_wait = 0
        self.phase_stack = None
        self.uid = 0

    def begin_phase(self):
        self.phase_stack = contextlib.ExitStack()

    def sb(self, shape, dt, name=None):
        self.uid += 1
        return self.phase_stack.enter_context(self.nc.sbuf_tensor(f"{name or 't'}_{self.uid}", list(shape), dt))

    def ps(self, shape, dt=F32, name=None):
        self.uid += 1
        return self.phase_stack.enter_context(self.nc.psum_tensor(f"{name or 'p'}_{self.uid}", list(shape), dt))

    def buf(self, name=None, dram=False):
        b = Buf(name or "b", dram)
        self.bufs.append(b)
        return b

    def op(self, eng, fn, reads=(), writes=(), dma=False):
        idx = len(self.ops)
        dsem_buf = None
        if dma:
            for b in list(writes) + list(reads):
                if not b.dram:
                    dsem_buf = b
                    break
            if dsem_buf is None:
                dsem_buf = (list(writes) + list(reads))[0]
        o = Op(eng, fn, dma, dsem_buf)
        for b in reads:
            if b.last_w is not None:
                o.deps.add(b.last_w)
        for b in writes:
            if b.last_w is not None:
                o.deps.add(b.last_w)
            for r in b.readers:
                o.deps.add(r)
        for b in reads:
            b.readers.append(idx)
        for b in writes:
            b.last_w = idx
            b.readers = []
        o.deps.discard(idx)
        self.ops.append(o)
        return idx

    def end_phase(self, final=False):
        nc = self.nc
        ops = self.ops
        engs = self.engs
        sems = self.sems
        cnt = self.cnt
        waited = self.waited
        last_on = {}
        for i, o in enumerate(ops):
            last_on[o.eng] = i
            for d in o.deps:
                s = ops[d]
                if s.is_dma:
                    continue
                if s.eng == o.eng and (s.eng == PE or not self.same_eng_sync):
                    continue
                s.signal = True
        for e in CENG:
            for i in range(len(ops) - 1, -1, -1):
                if ops[i].eng == e and not ops[i].is_dma:
                    ops[i].signal = True
                    break
        dsems = {}
        for i, o in enumerate(ops):
            eng = engs[o.eng]
            need = {}
            for d in o.deps:
                s = ops[d]
                if s.is_dma:
                    st = dsems[(id(s.dsem_buf), s.eng == POOL)]
                    key = ("d", id(st))
                    v = st[1]
                    sem = st[0]
                else:
                    if s.eng == o.eng and (s.eng == PE or not self.same_eng_sync):
                        continue
                    key = ("e", s.eng)
                    v = s.val
                    sem = sems[s.eng]
                if need.get(key, (None, -1))[1] < v:
                    need[key] = (sem, v)
            for key, (sem, v) in need.items():
                if waited.get((o.eng, key), -1) >= v:
                    continue
                waited[(o.eng, key)] = v
                eng.wait_ge(sem, v)
                self.n_wait += 1
            inst = o.fn(eng)
            self.n_inst += 1
            if o.is_dma:
                k = (id(o.dsem_buf), o.eng == POOL)
                if k not in dsems:
                    fl = self.dsem_free[k[1]]
                    if fl:
                        dsems[k] = fl.pop()
                    else:
                        cm = nc.semaphore(f"d{len(self.dsem_all)}")
                        st = [cm.__enter__(), 0]
                        self.keep.append(cm)
                        self.dsem_all.append(st)
                        dsems[k] = st
                st = dsems[k]
                st[1] += 16
                inst.then_inc(st[0], 16)
            elif o.signal:
                cnt[o.eng] += 1
                o.val = cnt[o.eng]
                inst.then_inc(sems[o.eng], 1)
        targets = [SP] if final else list(engs.keys())
        for e in targets:
            eng = engs[e]
            for s in CENG:
                if cnt[s] == 0 or (s == e and (e == PE or not self.same_eng_sync)):
                    continue
                key = ("e", s)
                if waited.get((e, key), -1) >= cnt[s]:
                    continue
                waited[(e, key)] = cnt[s]
                eng.wait_ge(sems[s], cnt[s])
                self.n_wait += 1
            for st in dsems.values():
                key = ("d", id(st))
                if waited.get((e, key), -1) >= st[1]:
                    continue
                waited[(e, key)] = st[1]
                eng.wait_ge(st[0], st[1])
                self.n_wait += 1
        for k, st in dsems.items():
            self.dsem_free[k[1]].append(st)
        self.ops = []
        for b in self.bufs:
            b.last_w = None
            b.readers = []
        self.bufs = [b for b in self.bufs if b.dram]
        if self.phase_stack is not None:
            self.phase_stack.close()
            self.phase_stack = None


class Tl:
    def __init__(self, P, shape, dt, name=None, psum=False):
        self.t = P.ps(shape, dt, name) if psum else P.sb(shape, dt, name)
        self.b = P.buf(name)


class Ring:
    def __init__(self, P, n, shape, dt, name=None, psum=False):
        self.tiles = [Tl(P, shape, dt, name, psum) for _ in range(n)]
        self.i = 0

    def next(self):
        t = self.tiles[self.i % len(self.tiles)]
        self.i += 1
        return t


class Cfg:
    def __init__(self, NB=2, LAT=4096, CTX=256, kinds=(0, 1, 2, 0), debug=False, phases=None):
        self.NB, self.LAT, self.CTX = NB, LAT, CTX
        self.kinds = tuple(kinds)
        self.debug = debug
        self.phases = phases
        self.TOK = LAT + CTX
        self.CT = CTX // 128
        self.LT = LAT // 128
        self.NT = self.CT + self.LT


class Builder:
    def __init__(self, cfg):
        self.cfg = cfg
        self.P = Prog()
        self.nc = self.P.nc
        self.dram_bufs = {}

    def dma(self, q, out, in_, R, W, **kw):
        self.P.op(q, lambda e: e.dma_start(out=out, in_=in_, **kw), R, W, dma=True)

    def mm(self, out, lhsT, rhs, start, stop, R, W, skip=False):
        self.P.op(PE, lambda e: e.matmul(out, lhsT=lhsT, rhs=rhs, start=start, stop=stop, skip_group_check=skip), R, W)

    def tr(self, out, in_, R, W):
        ident = self.ident.t[:]
        self.P.op(PE, lambda e: e.transpose(out=out, in_=in_, identity=ident), list(R) + [self.ident.b], W)

    def act(self, out, in_, func, R, W, eng=ACT, **kw):
        self.P.op(eng, lambda e: e.activation(out=out, in_=in_, func=func, **kw), R, W)

    def tt(self, eng, out, in0, in1, op, R, W):
        self.P.op(eng, lambda e: e.tensor_tensor(out=out, in0=in0, in1=in1, op=op), R, W)

    def ts(self, eng, out, in0, s1, s2, op0, op1, R, W):
        if op1 is None:
            self.P.op(eng, lambda e: e.tensor_scalar(out=out, in0=in0, scalar1=s1, scalar2=None, op0=op0), R, W)
        else:
            self.P.op(eng, lambda e: e.tensor_scalar(out=out, in0=in0, scalar1=s1, scalar2=s2, op0=op0, op1=op1), R, W)

    def stt(self, out, in0, scalar, in1, op0, op1, R, W):
        self.P.op(DVE, lambda e: e.scalar_tensor_tensor(out=out, in0=in0, scalar=scalar, in1=in1, op0=op0, op1=op1), R, W)

    def cp(self, eng, out, in_, R, W):
        if eng == ACT:
            self.act(out, in_, AF.Copy, R, W)
        else:
            self.P.op(eng, lambda e: e.tensor_copy(out=out, in_=in_), R, W)

    def memset(self, eng, ap, val, W):
        self.P.op(eng, lambda e: e.memset(ap, val), (), W)

    def dbuf(self, key):
        if key not in self.dram_bufs:
            self.dram_bufs[key] = self.P.buf(str(key), dram=True)
        return self.dram_bufs[key]

    def rstd(self, out, in_, scale, eps, R, W, n=1):
        nh = self.neghalf.t[:, 0:n]
        self.ts(POOL, out, in_, scale, eps, ALU.mult, ALU.add, R, W)
        self.tt(POOL, out, out, nh, ALU.pow, list(W) + [self.neghalf.b], W)

    def declare(self):
        nc, c = self.nc, self.cfg
        L = len(c.kinds)
        self.L = L
        di = lambda n, s, dt=F32: nc.dram_tensor(n, list(s), dt, kind="ExternalInput").ap()
        dx = lambda n, s, dt=F32: nc.dram_tensor(n, list(s), dt, kind="Internal").ap()
        self.x_in = di("x", [c.NB, c.LAT, D])
        self.ctx_in = di("ctx", [c.NB, c.CTX, D])
        self.crow = di("crow", [c.NB + 1, D])
        self.ada_w = di("ada_w", [L, D, 6 * D])
        self.ada_b = di("ada_b", [L, 6 * D])
        self.norm_g = di("norm_g", [L, 4 * D])
        n_ret = sum(1 for k in c.kinds if k == 0)
        n_hg = sum(1 for k in c.kinds if k == 1)
        n_m2 = sum(1 for k in c.kinds if k == 2)
        self.ret_w_in = di("ret_w_in", [max(n_ret, 1), D, 6144])
        self.ret_w_out = di("ret_w_out", [max(n_ret, 1), 2048, D])
        self.ret_decay = di("ret_decay", [max(n_ret, 1), 8])
        self.ret_gn = di("ret_gn", [max(n_ret, 1), 2048])
        self.hg_w_in = di("hg_w_in", [max(n_hg, 1), D, 5120])
        self.hg_w_out = di("hg_w_out", [max(n_hg, 1), D, D])
        self.hg_lb = di("hg_lb", [L, D])
        self.hg_gn = di("hg_gn", [max(n_hg, 1), D])
        self.m2_w_in = di("m2_w_in", [max(n_m2, 1), D, 5184])
        self.m2_w_out = di("m2_w_out", [max(n_m2, 1), 2048, D])
        self.m2_conv_w = di("m2_conv_w", [max(n_m2, 1), 3, 3072])
        self.m2_conv_b = di("m2_conv_b", [max(n_m2, 1), 3072])
        self.m2_dt_bias = di("m2_dt_bias", [max(n_m2, 1), 64])
        self.m2_a_log = di("m2_a_log", [max(n_m2, 1), 64])
        self.m2_d = di("m2_d", [max(n_m2, 1), 32])
        self.m2_gn = di("m2_gn", [max(n_m2, 1), 2048])
        self.ffn_w_up = di("ffn_w_up", [L, D, 5632])
        self.ffn_cw = di("ffn_cw", [L, 128, 22 * 3])
        self.ffn_cb = di("ffn_cb", [L, 128, 22])
        self.ffn_w_down = di("ffn_w_down", [L, 2816, D])
        self.c_ident = di("c_ident", [128, 128])
        self.c_rope = di("c_rope", [max(c.LT, 1), 128, 512])
        self.c_tab = di("c_tab", [128, 24 * 128])
        self.out = nc.dram_tensor("out", [c.NB, c.LAT, D], F32, kind="ExternalOutput").ap()
        if c.debug:
            self.dbg = nc.dram_tensor("dbg", [L, c.NB, c.TOK, D], F32, kind="ExternalOutput").ap()
        self.H = dx("H", [c.NB, c.TOK, D])
        self.MOD = dx("MOD", [L, c.NB + 1, 6, D])
        self.sQT = dx("sQT", [2, c.NB, c.NT, 128, 1024], BF16)
        self.sKT = dx("sKT", [2, c.NB, c.NT, 128, 1024], BF16)
        self.sKt = dx("sKt", [2, c.NB, c.TOK, 1024], BF16)
        self.sV = dx("sV", [c.NB, c.TOK, 2048], BF16)
        self.sG = dx("sG", [c.NB, c.TOK, 2048], BF16)
        self.sO = dx("sO", [2, c.NB, c.TOK, 2048])
        self.sCS = dx("sCS", [2, c.NB, c.NT, 128, 48])
        self.sX = dx("sX", [c.NB, c.TOK + 2 * c.NT + 8, 3072])
        self.sDT = dx("sDT", [c.NB, c.TOK, 128])

    def setup_consts(self):
        P, nc = self.P, self.nc
        self.const_stack = contextlib.ExitStack()
        P.phase_stack = self.const_stack
        self.ident = Tl(P, [128, 128], BF16, "ident")
        self.neghalf = Tl(P, [128, 8], F32, "neghalf")
        P.phase_stack = None
        P.begin_phase()
        self.dma(POOL, self.ident.t[:], self.c_ident[:, :], [], [self.ident.b])
        self.memset(POOL, self.neghalf.t[:], -0.5, [self.neghalf.b])
        P.end_phase()

    def load_ctab(self):
        ct = Tl(self.P, [128, 24, 128], F32, "ctab")
        self.dma(SP, ct.t[:], self.c_tab.rearrange("p (a b) -> p a b", a=24), [], [ct.b])
        return ct

    def h_src(self, li, b, ti):
        c = self.cfg
        if li == 0:
            if ti < c.CT:
                return self.ctx_in[b, ti * 128:(ti + 1) * 128, :], None
            return self.x_in[b, (ti - c.CT) * 128:(ti - c.CT + 1) * 128, :], None
        return self.H[b, ti * 128:(ti + 1) * 128, :], self.dbuf(("H", b, ti))

    def h_mid(self, b, ti):
        return self.H[b, ti * 128:(ti + 1) * 128, :], self.dbuf(("H", b, ti))

    def h_dst(self, li, b, ti):
        c = self.cfg
        if li == self.L - 1 and ti >= c.CT:
            return self.out[b, (ti - c.CT) * 128:(ti - c.CT + 1) * 128, :], self.dbuf(("out", b, ti))
        return self.H[b, ti * 128:(ti + 1) * 128, :], self.dbuf(("H", b, ti))

    def load_w(self, wt, src, kchunks, ncols):
        v = src.rearrange("(k p) n -> p k n", p=128)
        for k in range(kchunks):
            self.dma(POOL, wt.t[:, k, :], v[:, k, :], [], [wt.b])

    def load_tab(self, tl, li, row, vec):
        self.dma(SP, tl.t[:], self.MOD[li, row, vec, :].partition_broadcast(128), [self.dbuf("MOD")], [tl.b])

    def phase_mod(self):
        P, c = self.P, self.cfg
        R3 = c.NB + 1
        P.begin_phase()
        cT = Tl(P, [128, R3, 8], F32, "cT")
        cTb = Tl(P, [128, 8, R3], BF16, "cTb")
        self.dma(SP, cT.t[:], self.crow.rearrange("r (p k) -> p r k", k=8), [], [cT.b])
        self.act(cTb.t[:].rearrange("p k r -> p r k"), cT.t[:], AF.Silu, [cT.b], [cTb.b])
        wr = Ring(P, 2, [128, 8, 1536], BF16, "adaw")
        pr = Ring(P, 2, [128, 512], F32, "psm", psum=True)
        raw = Tl(P, [R3, 6 * D], F32, "raw")
        adab = Tl(P, [R3, 6 * D], F32, "adab")
        ng = Tl(P, [R3, 4 * D], F32, "ng")
        mv = Ring(P, 2, [R3, 6, D], F32, "mv")
        modb = self.dbuf("MOD")
        for li in range(self.L):
            self.dma(SP, adab.t[:], self.ada_b[li, :].partition_broadcast(R3), [], [adab.b])
            self.dma(SP, ng.t[:], self.norm_g[li, :].partition_broadcast(R3), [], [ng.b])
            wv = self.ada_w[li].rearrange("(p k) n -> p k n", k=8)
            for j in range(4):
                w = wr.next()
                self.dma(POOL, w.t[:], wv[:, :, j * 1536:(j + 1) * 1536], [], [w.b])
                for n in range(3):
                    ps = pr.next()
                    for k in range(8):
                        self.mm(ps.t[0:R3, :], cTb.t[:, k, :], w.t[:, k, n * 512:(n + 1) * 512], k == 0, k == 7,
                                [cTb.b, w.b], [ps.b])
                    c0 = j * 1536 + n * 512
                    self.tt(DVE, raw.t[:, c0:c0 + 512], ps.t[0:R3, :], adab.t[:, c0:c0 + 512], ALU.add,
                            [ps.b, adab.b], [raw.b])
            m = mv.next()
            r = raw.t
            self.cp(DVE, m.t[:, 0, :], r[:, 0:D], [raw.b], [m.b])
            self.stt(m.t[:, 1, :], r[:, D:2 * D], 1.0, ng.t[:, 0:D], ALU.add, ALU.mult, [raw.b, ng.b], [m.b])
            self.tt(DVE, m.t[:, 2, :], r[:, 2 * D:3 * D], ng.t[:, D:2 * D], ALU.mult, [raw.b, ng.b], [m.b])
            self.cp(DVE, m.t[:, 3, :], r[:, 3 * D:4 * D], [raw.b], [m.b])
            self.stt(m.t[:, 4, :], r[:, 4 * D:5 * D], 1.0, ng.t[:, 2 * D:3 * D], ALU.add, ALU.mult, [raw.b, ng.b], [m.b])
            self.tt(DVE, m.t[:, 5, :], r[:, 5 * D:6 * D], ng.t[:, 3 * D:4 * D], ALU.mult, [raw.b, ng.b], [m.b])
            self.dma(SP, self.MOD[li], m.t[:], [m.b], [modb])
        P.end_phase()

    def pre_norm(self, h, sc, sh, u, ss, junk):
        self.act(junk.t[:], h.t[:], AF.Square, [h.b], [junk.b, ss.b], accum_out=ss.t[:, 0:1])
        self.rstd(ss.t[:, 1:2], ss.t[:, 0:1], 1.0 / D, EPS, [ss.b], [ss.b])
        self.tt(POOL, h.t[:], h.t[:], sc.t[:], ALU.mult, [h.b, sc.b], [h.b])
        self.stt(u.t[:], h.t[:], ss.t[:, 1:2], sh.t[:], ALU.mult, ALU.add, [h.b, ss.b, sh.b], [u.b])

    def transpose_to(self, src, nblk, psT, dst_ap, dst_b, src_b, eng=ACT, src_off=0):
        for k in range(nblk):
            self.tr(psT.t[:, k * 128:(k + 1) * 128], src[:, src_off + k * 128: src_off + (k + 1) * 128], [src_b], [psT.b])
        self.cp(eng, dst_ap, psT.t[:, 0:nblk * 128], [psT.b], [dst_b])

    def post_residual(self, psY, hres, gtab, hn, ss, junk, dst, dst_b):
        self.act(junk.t[:], psY.t[:], AF.Square, [psY.b], [junk.b, ss.b], accum_out=ss.t[:, 0:1])
        self.rstd(ss.t[:, 1:2], ss.t[:, 0:1], 1.0 / D, EPS, [ss.b], [ss.b])
        self.stt(hn.t[:], psY.t[:], ss.t[:, 1:2], gtab.t[:], ALU.mult, ALU.mult, [psY.b, ss.b, gtab.b], [hn.b])
        self.tt(POOL, hn.t[:], hn.t[:], hres.t[:], ALU.add, [hn.b, hres.b], [hn.b])
        self.dma(SP, dst, hn.t[:], [hn.b], [dst_b] if dst_b is not None else [])

    def tile_list(self, li, with_ctx=True):
        c = self.cfg
        last = (li == self.L - 1)
        out = []
        for b in range(c.NB):
            for ti in range(c.NT):
                if ti < c.CT and (last and not with_ctx):
                    continue
                out.append((b, ti))
        return out

    def phase_ffn(self, li):
        P, c = self.P, self.cfg
        last = (li == self.L - 1)
        P.begin_phase()
        wup = Tl(P, [128, 8, 5632], BF16, "wup")
        wdn = Tl(P, [128, 22, 1024], BF16, "wdn")
        self.load_w(wup, self.ffn_w_up[li], 8, 5632)
        self.load_w(wdn, self.ffn_w_down[li], 22, 1024)
        cw = Tl(P, [128, 22, 3], F32, "cw")
        cb = Tl(P, [128, 22], F32, "cb")
        self.dma(SP, cw.t[:], self.ffn_cw[li].rearrange("p (a b) -> p a b", b=3), [], [cw.b])
        self.dma(SP, cb.t[:], self.ffn_cb[li], [], [cb.b])
        tabs = [Tl(P, [128, D], F32, f"tab{i}") for i in range(3)]
        hr = Ring(P, 4, [128, D], F32, "h")
        ur = Ring(P, 2, [128, D], BF16, "u")
        junk = Tl(P, [128, D], BF16, "junk")
        ssr = Ring(P, 6, [128, 2], F32, "ss")
        uTr = Ring(P, 2, [128, 8, 256], BF16, "uT")
        mT = Tl(P, [128, 22, 256], BF16, "mT")
        cbr = Ring(P, 3, [128, 256], F32, "cbuf")
        hrel = Ring(P, 2, [128, D], F32, "hrel")
        hnr = Ring(P, 2, [128, D], F32, "hn")
        psT = Tl(P, [128, 1024], BF16, "psT", psum=True)
        psA = Ring(P, 3, [128, 512], F32, "psA", psum=True)
        psV = Ring(P, 2, [128, 512], F32, "psV", psum=True)
        psY = Tl(P, [128, 1024], F32, "psY", psum=True)
        sts = []
        for b in range(c.NB):
            if not last:
                for s_ in range(c.CT // 2):
                    sts.append((c.NB, b, [2 * s_, 2 * s_ + 1], False))
        for b in range(c.NB):
            for s_ in range(c.LT // 2):
                sts.append((b, b, [c.CT + 2 * s_, c.CT + 2 * s_ + 1], True))
        st_a = {"row": None}
        st_m = {"row": None}

        def ld(st):
            row, b, tis, grid = st
            hs = []
            for ti in tis:
                h = hr.next()
                src, sb_ = self.h_mid(b, ti)
                self.dma(SP, h.t[:], src, [sb_], [h.b])
                hs.append(h)
            return {"h": hs}

        def pre_a(st, cx):
            row, b, tis, grid = st
            if st_a["row"] != row:
                st_a["row"] = row
                self.load_tab(tabs[0], li, row, 3)
                self.load_tab(tabs[1], li, row, 4)
            cx["u"] = []
            for h in cx["h"]:
                u = ur.next()
                self.pre_norm(h, tabs[1], tabs[0], u, ssr.next(), junk)
                cx["u"].append(u)

        def pre_b(st, cx):
            uT = uTr.next()
            for j, u in enumerate(cx["u"]):
                for k in range(8):
                    self.tr(psT.t[:, k * 128:(k + 1) * 128], u.t[:, k * 128:(k + 1) * 128], [u.b], [psT.b])
                self.cp(ACT, uT.t[:, :, j * 128:(j + 1) * 128], psT.t[:].rearrange("p (k t) -> p k t", k=8),
                        [psT.b], [uT.b])
            cx["uT"] = uT

        def main(st, cx, hook):
            row, b, tis, grid = st
            uT = cx["uT"]
            if st_m["row"] != row:
                st_m["row"] = row
                self.load_tab(tabs[2], li, row, 5)
            hres = []
            for ti in tis:
                hh = hrel.next()
                src, sb_ = self.h_mid(b, ti)
                self.dma(SP, hh.t[:], src, [sb_], [hh.b])
                hres.append(hh)
            nr = 4 if grid else 1
            w = 256 // nr

            def A(fc):
                pa = psA.next()
                for k in range(8):
                    self.mm(pa.t[:, 0:256], wup.t[:, k, fc * 128:(fc + 1) * 128], uT.t[:, k, :], k == 0, k == 7,
                            [wup.b, uT.b], [pa.b])
                cbuf = cbr.next()
                self.act(cbuf.t[:], pa.t[:, 0:256], AF.Identity, [pa.b, cw.b, cb.b], [cbuf.b],
                         scale=cw.t[:, fc, 1:2], bias=cb.t[:, fc:fc + 1])
                pv = pa.t[:, 0:256].rearrange("p (r w) -> p r w", r=nr)
                cv = cbuf.t[:].rearrange("p (r w) -> p r w", r=nr)
                self.stt(cv[:, :, 1:w], pv[:, :, 0:w - 1], cw.t[:, fc, 0:1], cv[:, :, 1:w], ALU.mult, ALU.add,
                         [pa.b, cw.b, cbuf.b], [cbuf.b])
                self.stt(cv[:, :, 0:w - 1], pv[:, :, 1:w], cw.t[:, fc, 2:3], cv[:, :, 0:w - 1], ALU.mult, ALU.add,
                         [pa.b, cw.b, cbuf.b], [cbuf.b])
                self.act(mT.t[:, fc, :], cbuf.t[:], AF.Gelu_apprx_tanh, [cbuf.b], [mT.b])

            def V(fc):
                pvv = psV.next()
                for k in range(8):
                    self.mm(pvv.t[:, 0:256], wup.t[:, k, 2816 + fc * 128:2816 + (fc + 1) * 128], uT.t[:, k, :],
                            k == 0, k == 7, [wup.b, uT.b], [pvv.b])
                self.tt(DVE, mT.t[:, fc, :], pvv.t[:, 0:256], mT.t[:, fc, :], ALU.mult, [pvv.b, mT.b], [mT.b])

            A(0)
            A(1)
            for fc in range(22):
                if fc + 2 < 22:
                    A(fc + 2)
                V(fc)
                if fc == 12:
                    hook()
            for j, ti in enumerate(tis):
                for n in range(2):
                    for fc in range(22):
                        self.mm(psY.t[:, n * 512:(n + 1) * 512], mT.t[:, fc, j * 128:(j + 1) * 128],
                                wdn.t[:, fc, n * 512:(n + 1) * 512], fc == 0, fc == 21, [mT.b, wdn.b], [psY.b])
                hn = hnr.next()
                dst, db_ = self.h_dst(li, b, ti)
                self.post_residual(psY, hres[j], tabs[2], hn, ssr.next(), junk, dst, db_)
                if c.debug:
                    self.dma(SP, self.dbg[li, b, ti * 128:(ti + 1) * 128, :], hn.t[:], [hn.b], [])

        self.run_pipe(sts, ld, pre_a, pre_b, main)
        P.end_phase()

    def phase_mixer(self, li):
        kind = self.cfg.kinds[li]
        j = sum(1 for k in self.cfg.kinds[:li] if k == kind)
        if kind == 0:
            self.ret_proj(li, j)
            self.ret_scan(li, j, 0)
            self.ret_scan(li, j, 1)
            self.ret_readout(li, j)
        elif kind == 1:
            self.hg_proj(li, j)
            self.hg_scan(li, j, 0)
            self.hg_scan(li, j, 1)
            self.hg_readout(li, j)
        else:
            self.m2_proj(li, j)
            self.m2_conv(li, j)
            self.m2_scan(li, j, 0)
            self.m2_scan(li, j, 1)
            self.m2_readout(li, j)

    def proj_tiles(self):
        c = self.cfg
        out = [(c.NB, b, ti) for b in range(c.NB) for ti in range(c.CT)]
        out += [(b, b, ti) for b in range(c.NB) for ti in range(c.CT, c.NT)]
        return out

    def run_pipelined(self, items, pre, main, rowof=lambda it: it[0]):
        cur = pre(items[0])
        for i, it in enumerate(items):
            nxt = None
            if i + 1 < len(items) and rowof(items[i + 1]) == rowof(it):
                nxt = pre(items[i + 1])
            main(it, cur)
            if nxt is None and i + 1 < len(items):
                nxt = pre(items[i + 1])
            cur = nxt

    def run_pipe(self, items, ld, pre_a, pre_b, main):
        n = len(items)
        ctxs = {}

        def do_ld(k):
            if k < n:
                ctxs[k] = ld(items[k])

        def do_a(k):
            if k < n:
                pre_a(items[k], ctxs[k])

        def do_b(k):
            if k < n:
                pre_b(items[k], ctxs[k])

        do_ld(0)
        do_ld(1)
        do_a(0)
        do_b(0)
        for i in range(n):
            do_ld(i + 2)
            do_a(i + 1)
            main(items[i], ctxs[i], lambda k=i + 1: do_b(k))
            ctxs.pop(i)

    def make_stages(self, li, tabs, hr, ur, ssr, uTr, junk, psTr, vecs=(0, 1)):
        state = {"row": None}

        def ld(it):
            row, b, ti = it
            h = hr.next()
            src, sb_ = self.h_src(li, b, ti)
            self.dma(SP, h.t[:], src, [sb_] if sb_ is not None else [], [h.b])
            return {"h": h}

        def pre_a(it, cx):
            row, b, ti = it
            if state["row"] != row:
                state["row"] = row
                for i, v in enumerate(vecs):
                    self.load_tab(tabs[i], li, row, v)
            u = ur.next()
            self.pre_norm(cx["h"], tabs[1], tabs[0], u, ssr.next(), junk)
            cx["u"] = u

        def pre_b(it, cx):
            uT = uTr.next()
            psT = psTr.next()
            self.transpose_to(cx["u"].t, 8, psT, uT.t[:].rearrange("p k t -> p (k t)"), uT.b, cx["u"].b)
            cx["uT"] = uT

        return ld, pre_a, pre_b

    def load_wout_scaled(self, wo, src, kchunks, gn_src):
        P = self.P
        self.load_w(wo, src, kchunks, 1024)
        gnT = Tl(P, [128, kchunks], F32, "gnT")
        self.dma(SP, gnT.t[:], gn_src.rearrange("(k p) -> p k", p=128), [], [gnT.b], allow_slow_non_contiguous=True)
        for k in range(kchunks):
            self.act(wo.t[:, k, :], wo.t[:, k, :], AF.Copy, [wo.b, gnT.b], [wo.b], scale=gnT.t[:, k:k + 1])

    def ret_proj(self, li, j):
        P, c = self.P, self.cfg
        P.begin_phase()
        w = Tl(P, [128, 8, 6144], BF16, "w_in")
        self.load_w(w, self.ret_w_in[j], 8, 6144)
        tabs = [Tl(P, [128, D], F32, f"tab{i}") for i in range(2)]
        hr = Ring(P, 3, [128, D], F32, "h")
        ur = Ring(P, 2, [128, D], BF16, "u")
        junk = Tl(P, [128, D], BF16, "junk")
        ssr = Ring(P, 4, [128, 2], F32, "ss")
        uTr = Ring(P, 2, [128, 8, 128], BF16, "uT")
        qk32r = Ring(P, 2, [128, 2048], F32, "qk32")
        ropeB = Tl(P, [128, 2048], F32, "ropeB")
        qkrr = Ring(P, 2, [128, 2048], BF16, "qkr")
        qkTr = Ring(P, 2, [128, 2048], BF16, "qkT")
        vr = Ring(P, 2, [128, 2048], BF16, "vbf")
        gr = Ring(P, 2, [128, 2048], BF16, "gbf")
        rr = Ring(P, 2, [128, 512], F32, "rope")
        psTr = Ring(P, 2, [128, 1024], BF16, "psT", psum=True)
        psM = Ring(P, 6, [128, 512], F32, "psM", psum=True)
        ld, pre_a, pre_b = self.make_stages(li, tabs, hr, ur, ssr, uTr, junk, psTr)

        def main(it, cx, hook):
            uT = cx["uT"]
            row, b, ti = it
            tok = slice(ti * 128, (ti + 1) * 128)
            qk32, qkr, qkT, vb, gb = qk32r.next(), qkrr.next(), qkTr.next(), vr.next(), gr.next()
            lat = ti >= c.CT
            if lat:
                rt = rr.next()
                self.dma(SP, rt.t[:], self.c_rope[ti - c.CT], [], [rt.b])
            for n in range(12):
                ps = psM.next()
                for k in range(8):
                    self.mm(ps.t[:], uT.t[:, k, :], w.t[:, k, n * 512:(n + 1) * 512], k == 0, k == 7, [uT.b, w.b], [ps.b])
                if n < 4:
                    self.act(qk32.t[:, n * 512:(n + 1) * 512], ps.t[:], AF.Copy, [ps.b], [qk32.b],
                             scale=(0.0625 if n >= 2 else 1.0))
                elif n < 8:
                    self.act(vb.t[:, (n - 4) * 512:(n - 3) * 512], ps.t[:], AF.Copy, [ps.b], [vb.b])
                else:
                    self.act(gb.t[:, (n - 8) * 512:(n - 7) * 512], ps.t[:], AF.Silu, [ps.b], [gb.b])
                if n == 3:
                    if lat:
                        v5 = qk32.t[:].rearrange("p (s j h e) -> p s j h e", s=8, j=2, h=2, e=64)
                        b5 = ropeB.t[:].rearrange("p (s j h e) -> p s j h e", s=8, j=2, h=2, e=64)
                        sv = rt.t[:, 256:512].rearrange("p (j h e) -> p j h e", j=2, h=2, e=64)
                        for hh in range(2):
                            self.tt(POOL, b5[:, :, :, hh, :], v5[:, :, :, 1 - hh, :],
                                    sv[:, :, hh, :].unsqueeze(1).broadcast_to([128, 8, 2, 64]), ALU.mult,
                                    [qk32.b, rt.b], [ropeB.b])
                        q3 = qk32.t[:].rearrange("p (s f) -> p s f", s=8)
                        self.tt(DVE, q3, q3, rt.t[:, 0:256].unsqueeze(1).broadcast_to([128, 8, 256]), ALU.mult,
                                [qk32.b, rt.b, ropeB.b], [qk32.b])
                        self.tt(DVE, qkr.t[:], qk32.t[:], ropeB.t[:], ALU.add, [qk32.b, ropeB.b], [qkr.b])
                    else:
                        self.cp(DVE, qkr.t[:], qk32.t[:], [qk32.b], [qkr.b])
                    self.dma(SP, self.sKt[0, b, tok, :], qkr.t[:, 1024:2048], [qkr.b], [self.dbuf(("Kt", b, ti))])
                    for half in range(2):
                        psT = psTr.next()
                        self.transpose_to(qkr.t, 8, psT, qkT.t[:, half * 1024:(half + 1) * 1024], qkT.b, qkr.b,
                                          eng=DVE, src_off=half * 1024)
                    self.dma(SP, self.sQT[0, b, ti], qkT.t[:, 0:1024], [qkT.b], [self.dbuf(("QT", b, ti))])
                    self.dma(SP, self.sKT[0, b, ti], qkT.t[:, 1024:2048], [qkT.b], [self.dbuf(("KT", b, ti))])
                if n == 7:
                    self.dma(SP, self.sV[b, tok, :], vb.t[:], [vb.b], [self.dbuf(("V", b, ti))])
                    hook()
                if n == 11:
                    self.dma(SP, self.sG[b, tok, :], gb.t[:], [gb.b], [self.dbuf(("G", b, ti))])

        self.run_pipe(self.proj_tiles(), ld, pre_a, pre_b, main)
        P.end_phase()

    def scan_order(self, d):
        c = self.cfg
        if d == 0:
            return list(range(c.NT))
        return list(range(c.CT - 1, -1, -1)) + list(range(c.NT - 1, c.CT - 1, -1))

    def ret_scan(self, li, j, d):
        P, c = self.P, self.cfg
        last = (li == self.L - 1)
        P.begin_phase()
        ct = self.load_ctab()
        dec = Tl(P, [128, 8], F32, "dec")
        lg = Tl(P, [128, 8], F32, "lg")
        nlg = Tl(P, [128, 8], F32, "nlg")
        self.dma(SP, dec.t[:], self.ret_decay[j, :].partition_broadcast(128), [], [dec.b])
        self.act(nlg.t[:], dec.t[:], AF.Exp, [dec.b], [nlg.b], scale=-1.0)
        self.act(nlg.t[:], nlg.t[:], AF.Ln, [nlg.b], [nlg.b], bias=1.0)
        self.ts(DVE, lg.t[:], nlg.t[:], -1.0, None, ALU.mult, None, [nlg.b], [lg.b])
        Dm = Tl(P, [128, 4, 128], F32, "Dm")
        E = Tl(P, [128, 4, 128], F32, "E")
        wc = Tl(P, [128, 4], F32, "wc")
        gC = Tl(P, [128, 4], F32, "gC")
        for h in range(4):
            col = slice(d * 4 + h, d * 4 + h + 1)
            sc = lg.t[:, col] if d == 0 else nlg.t[:, col]
            self.act(Dm.t[:, h, :], ct.t[:, 0, :], AF.Exp, [ct.b, lg.b, nlg.b], [Dm.b], scale=sc)
            self.tt(DVE, Dm.t[:, h, :], Dm.t[:, h, :], ct.t[:, 1 + d, :], ALU.mult, [Dm.b, ct.b], [Dm.b])
            self.act(E.t[:, h, :], ct.t[:, 3 + d, :], AF.Exp, [ct.b, lg.b], [E.b], scale=lg.t[:, col])
            self.act(wc.t[:, h:h + 1], ct.t[:, 5, d:d + 1], AF.Exp, [ct.b, lg.b], [wc.b], scale=lg.t[:, col])
            self.act(gC.t[:, h:h + 1], ct.t[:, 5, 2:3], AF.Exp, [ct.b, lg.b], [gC.b], scale=lg.t[:, col])
        S32 = {}
        Sbf = {}
        for b in range(c.NB):
            for h in range(4):
                for cc in range(2):
                    S32[b, h, cc] = Tl(P, [128, 512], F32, "S32")
                    Sbf[b, h, cc] = Tl(P, [128, 512], BF16, "Sbf")
                    self.memset(POOL, S32[b, h, cc].t[:], 0.0, [S32[b, h, cc].b])
                    self.memset(DVE, Sbf[b, h, cc].t[:], 0.0, [Sbf[b, h, cc].b])
        qTr = Ring(P, 3, [128, 8, 128], BF16, "qT")
        kTr = Ring(P, 3, [128, 8, 128], BF16, "kT")
        ktr = Ring(P, 3, [128, 4, 256], BF16, "kt")
        vr = Ring(P, 3, [128, 2048], BF16, "v")
        otr = Ring(P, 2, [128, 2048], F32, "ot")
        oflr = Ring(P, 3, [128, 2048], F32, "ofl") if d == 1 else None
        PTr = Ring(P, 2, [128, 4, 128], BF16, "PT")
        qsr = Ring(P, 2, [128, 8, 128], BF16, "qs")
        kwr = Ring(P, 2, [128, 4, 256], BF16, "kw")
        psS = Tl(P, [128, 512], F32, "psS", psum=True)
        psO = Tl(P, [128, 2048], F32, "psO", psum=True)
        psK = Ring(P, 3, [128, 512], F32, "psK", psum=True)
        steps = [(ti, b) for ti in self.scan_order(d) for b in range(c.NB)]

        def ld(st):
            ti, b = st
            tok = slice(ti * 128, (ti + 1) * 128)
            qT, kT, kt, v = qTr.next(), kTr.next(), ktr.next(), vr.next()
            need_o = not (last and ti < c.CT)
            if need_o:
                self.dma(SP, qT.t[:].rearrange("p k t -> p (k t)"), self.sQT[0, b, ti], [self.dbuf(("QT", b, ti))], [qT.b])
                self.dma(SP, kT.t[:].rearrange("p k t -> p (k t)"), self.sKT[0, b, ti], [self.dbuf(("KT", b, ti))], [kT.b])
            self.dma(SP, kt.t[:].rearrange("p h f -> p (h f)"), self.sKt[0, b, tok, :], [self.dbuf(("Kt", b, ti))], [kt.b])
            self.dma(SP, v.t[:], self.sV[b, tok, :], [self.dbuf(("V", b, ti))], [v.b])
            ofl = None
            if d == 1 and need_o:
                ofl = oflr.next()
                self.dma(SP, ofl.t[:], self.sO[0, b, tok, :], [self.dbuf(("O", 0, b, ti))], [ofl.b])
            return qT, kT, kt, v, need_o, ofl

        def comp(st, tl):
            ti, b = st
            tok = slice(ti * 128, (ti + 1) * 128)
            qT, kT, kt, v, need_o, ofl = tl
            if need_o:
                for h in range(4):
                    for cc in range(2):
                        self.mm(psS.t[:, h * 128:(h + 1) * 128], kT.t[:, h * 2 + cc, :], qT.t[:, h * 2 + cc, :], cc == 0, cc == 1,
                                [kT.b, qT.b], [psS.b])
                PT = PTr.next()
                self.tt(DVE, PT.t[:], psS.t[:].rearrange("p (h t) -> p h t", h=4), Dm.t[:], ALU.mult, [psS.b, Dm.b], [PT.b])
                qs = qsr.next()
                self.tt(POOL, qs.t[:].rearrange("p (h c) t -> p h c t", h=4), qT.t[:].rearrange("p (h c) t -> p h c t", h=4),
                        E.t[:].unsqueeze(2).broadcast_to([128, 4, 2, 128]), ALU.mult, [qT.b, E.b], [qs.b])
            kw = kwr.next()
            self.tt(POOL, kw.t[:], kt.t[:], wc.t[:, 0:4].unsqueeze(2).broadcast_to([128, 4, 256]), ALU.mult, [kt.b, wc.b], [kw.b])
            if need_o:
                for h in range(4):
                    vh = v.t[:, h * 512:(h + 1) * 512]
                    po = psO.t[:, h * 512:(h + 1) * 512]
                    self.mm(po, PT.t[:, h, :], vh, True, False, [PT.b, v.b], [psO.b])
                    for cc in range(2):
                        self.mm(po, qs.t[:, h * 2 + cc, :], Sbf[b, h, cc].t[:], False, cc == 1,
                                [qs.b, Sbf[b, h, cc].b], [psO.b])
                ot = otr.next()
                for h in range(4):
                    self.act(ot.t[:, h * 512:(h + 1) * 512], psO.t[:, h * 512:(h + 1) * 512], AF.Copy, [psO.b], [ot.b])
                if d == 1:
                    self.tt(POOL, ot.t[:], ot.t[:], ofl.t[:], ALU.add, [ot.b, ofl.b], [ot.b])
                self.dma(SP, self.sO[d, b, tok, :], ot.t[:], [ot.b], [self.dbuf(("O", d, b, ti))])
            for h in range(4):
                vh = v.t[:, h * 512:(h + 1) * 512]
                for cc in range(2):
                    pk = psK.next()
                    self.mm(pk.t[:], kw.t[:, h, cc * 128:(cc + 1) * 128], vh, True, True, [kw.b, v.b], [pk.b])
                    s32 = S32[b, h, cc]
                    self.stt(s32.t[:], s32.t[:], gC.t[:, h:h + 1], pk.t[:], ALU.mult, ALU.add, [s32.b, gC.b, pk.b], [s32.b])
                    self.cp(ACT, Sbf[b, h, cc].t[:], s32.t[:], [s32.b], [Sbf[b, h, cc].b])

        self.run_pipelined(steps, ld, comp, rowof=lambda it: 0)
        P.end_phase()

    def readout_main(self, li, wo, nk, g1, psYr, hnr, ssr, junk):
        state = {"row": None}

        def main(it, cx, hook):
            row, b, ti = it
            onT = cx["onT"]
            if state["row"] != row:
                state["row"] = row
                self.load_tab(g1, li, row, 2)
            psY = psYr.next()
            for n in range(2):
                for k in range(nk):
                    self.mm(psY.t[:, n * 512:(n + 1) * 512], onT.t[:, k, :], wo.t[:, k, n * 512:(n + 1) * 512],
                            k == 0, k == nk - 1, [onT.b, wo.b], [psY.b])
                if n == 0:
                    hook()
            hn = hnr.next()
            dst, db_ = self.h_mid(b, ti)
            self.post_residual(psY, cx["h"], g1, hn, ssr.next(), junk, dst, db_)
        return main

    def readout_items(self, li):
        keep = set(self.tile_list(li, with_ctx=False))
        return [it for it in self.proj_tiles() if (it[1], it[2]) in keep]

    def ret_readout(self, li, j):
        P, c = self.P, self.cfg
        P.begin_phase()
        wo = Tl(P, [128, 16, 1024], BF16, "w_out")
        self.load_wout_scaled(wo, self.ret_w_out[j], 16, self.ret_gn[j])
        g1 = Tl(P, [128, D], F32, "g1")
        ofr = Ring(P, 3, [128, 2048], F32, "of")
        gr = Ring(P, 3, [128, 2048], BF16, "g")
        hr = Ring(P, 3, [128, D], F32, "h")
        onr = Ring(P, 2, [128, 2048], BF16, "on")
        onTr = Ring(P, 2, [128, 16, 128], BF16, "onT")
        str_ = Ring(P, 2, [128, 4, 6], F32, "bst")
        mvr = Ring(P, 2, [128, 4, 2], F32, "mv")
        rsr = Ring(P, 2, [128, 4], F32, "rs")
        hnr = Ring(P, 2, [128, D], F32, "hn")
        ssr = Ring(P, 4, [128, 2], F32, "ss")
        junk = Tl(P, [128, D], BF16, "junk")
        psTr = Ring(P, 2, [128, 1024], BF16, "psT", psum=True)
        psYr = Ring(P, 2, [128, 1024], F32, "psY", psum=True)

        def ld(it):
            row, b, ti = it
            tok = slice(ti * 128, (ti + 1) * 128)
            of, g, h = ofr.next(), gr.next(), hr.next()
            self.dma(SP, of.t[:], self.sO[1, b, tok, :], [self.dbuf(("O", 1, b, ti))], [of.b])
            self.dma(SP, g.t[:], self.sG[b, tok, :], [self.dbuf(("G", b, ti))], [g.b])
            src, sb_ = self.h_src(li, b, ti)
            self.dma(SP, h.t[:], src, [sb_] if sb_ is not None else [], [h.b])
            return {"of": of, "g": g, "h": h}

        def pre_a(it, cx):
            of, g = cx["of"], cx["g"]
            on, bst, mv, rs = onr.next(), str_.next(), mvr.next(), rsr.next()
            for h in range(4):
                oh = of.t[:, h * 512:(h + 1) * 512]
                self.P.op(DVE, lambda e, o=bst.t[:, h, :], i=oh: e.bn_stats(out=o, in_=i), [of.b], [bst.b])
            for h in range(4):
                self.P.op(DVE, lambda e, o=mv.t[:, h, :], i=bst.t[:, h, :]: e.bn_aggr(out=o, in_=i), [bst.b], [mv.b])
            self.rstd(rs.t[:], mv.t[:, :, 1], 1.0, EPS, [mv.b], [rs.b], n=4)
            for h in range(4):
                oh = of.t[:, h * 512:(h + 1) * 512]
                self.stt(oh, oh, mv.t[:, h, 0:1], g.t[:, h * 512:(h + 1) * 512], ALU.subtract, ALU.mult,
                         [of.b, mv.b, g.b], [of.b])
            for h in range(4):
                oh = of.t[:, h * 512:(h + 1) * 512]
                self.act(on.t[:, h * 512:(h + 1) * 512], oh, AF.Copy, [of.b, rs.b], [on.b], scale=rs.t[:, h:h + 1])
            cx["on"] = on

        def pre_b(it, cx):
            on, onT = cx["on"], onTr.next()
            for half in range(2):
                psT = psTr.next()
                self.transpose_to(on.t, 8, psT, onT.t[:, half * 8:(half + 1) * 8, :].rearrange("p k t -> p (k t)"),
                                  onT.b, on.b, eng=(ACT if half == 0 else DVE), src_off=half * 1024)
            cx["onT"] = onT

        self.run_pipe(self.readout_items(li), ld, pre_a, pre_b, self.readout_main(li, wo, 16, g1, psYr, hnr, ssr, junk))
        P.end_phase()

    def hg_proj(self, li, j):
        P, c = self.P, self.cfg
        L = self.L
        P.begin_phase()
        ct = self.load_ctab()
        w = Tl(P, [128, 8, 5120], BF16, "w_in")
        self.load_w(w, self.hg_w_in[j], 8, 5120)
        lbx = Tl(P, [128, L, D], F32, "lbx")
        lb = Tl(P, [128, D], F32, "lb")
        omlb = Tl(P, [128, D], F32, "omlb")
        den = Tl(P, [128, D], F32, "den")
        self.dma(SP, lbx.t[:].rearrange("p l d -> p (l d)"), self.hg_lb.rearrange("l d -> (l d)").partition_broadcast(128), [], [lbx.b])
        self.act(lbx.t[:], lbx.t[:], AF.Exp, [lbx.b], [lbx.b])
        self.cp(DVE, den.t[:], lbx.t[:, 0, :], [lbx.b], [den.b])
        self.memset(DVE, lb.t[:], 0.0, [lb.b])
        for r in range(1, L):
            self.tt(DVE, den.t[:], den.t[:], lbx.t[:, r, :], ALU.add, [den.b, lbx.b], [den.b])
            if r <= li:
                self.tt(DVE, lb.t[:], lb.t[:], lbx.t[:, r, :], ALU.add, [lb.b, lbx.b], [lb.b])
        self.P.op(DVE, lambda e: e.reciprocal(out=den.t[:], in_=den.t[:]), [den.b], [den.b])
        self.tt(DVE, lb.t[:], lb.t[:], den.t[:], ALU.mult, [lb.b, den.b], [lb.b])
        self.ts(DVE, omlb.t[:], lb.t[:], -1.0, 1.0, ALU.mult, ALU.add, [lb.b], [omlb.b])
        tabs = [Tl(P, [128, D], F32, f"tab{i}") for i in range(2)]
        hr = Ring(P, 3, [128, D], F32, "h")
        ur = Ring(P, 2, [128, D], BF16, "u")
        junk = Tl(P, [128, D], BF16, "junk")
        ssr = Ring(P, 4, [128, 2], F32, "ss")
        uTr = Ring(P, 2, [128, 8, 128], BF16, "uT")
        qs = Tl(P, [128, D], F32, "qs")
        a32 = [Tl(P, [128, D], F32, f"a32{d}") for d in range(2)]
        la32 = [Tl(P, [128, D], F32, f"la{d}") for d in range(2)]
        k32 = a32
        e32r = Ring(P, 2, [128, D], F32, "e32")
        qtr = Ring(P, 2, [128, D], BF16, "qt")
        ktr = Ring(P, 2, [128, D], BF16, "kt")
        stg = Ring(P, 4, [128, D], BF16, "stg")
        vr = Ring(P, 2, [128, D], BF16, "vbf")
        gr = Ring(P, 2, [128, D], BF16, "gbf")
        csr = Ring(P, 2, [128, 48], F32, "cs")
        psTr = Ring(P, 2, [128, 1024], BF16, "psT", psum=True)
        psM = Ring(P, 5, [128, 512], F32, "psM", psum=True)
        psC = Tl(P, [128, 512], F32, "psC", psum=True)
        ld, pre_a, pre_b = self.make_stages(li, tabs, hr, ur, ssr, uTr, junk, psTr)

        def main(it, cx, hook):
            uT = cx["uT"]
            row, b, ti = it
            tok = slice(ti * 128, (ti + 1) * 128)
            vb, gb = vr.next(), gr.next()
            for n in range(10):
                if n == 7:
                    hook()
                ps = psM.next()
                for k in range(8):
                    self.mm(ps.t[:], uT.t[:, k, :], w.t[:, k, n * 512:(n + 1) * 512], k == 0, k == 7, [uT.b, w.b], [ps.b])
                cs_ = slice((n % 2) * 512, (n % 2) * 512 + 512)
                if n < 2:
                    self.act(qs.t[:, cs_], ps.t[:], AF.Silu, [ps.b], [qs.b])
                elif n < 6:
                    d = (n - 2) // 2
                    self.act(a32[d].t[:, cs_], ps.t[:], AF.Sigmoid, [ps.b], [a32[d].b])
                    self.tt(DVE, a32[d].t[:, cs_], a32[d].t[:, cs_], omlb.t[:, cs_], ALU.mult, [a32[d].b, omlb.b], [a32[d].b])
                    self.tt(POOL, a32[d].t[:, cs_], a32[d].t[:, cs_], lb.t[:, cs_], ALU.add, [a32[d].b, lb.b], [a32[d].b])
                    self.act(la32[d].t[:, cs_], a32[d].t[:, cs_], AF.Ln, [a32[d].b], [la32[d].b])
                    self.ts(POOL, a32[d].t[:, cs_], a32[d].t[:, cs_], -1.0, 1.0, ALU.mult, ALU.add, [a32[d].b], [a32[d].b])
                elif n < 8:
                    self.act(vb.t[:, cs_], ps.t[:], AF.Copy, [ps.b], [vb.b])
                else:
                    self.act(gb.t[:, cs_], ps.t[:], AF.Silu, [ps.b], [gb.b])
            self.dma(SP, self.sV[b, tok, 0:D], vb.t[:], [vb.b], [self.dbuf(("V", b, ti))])
            self.dma(SP, self.sG[b, tok, 0:D], gb.t[:], [gb.b], [self.dbuf(("G", b, ti))])
            for d in range(2):
                qt, kt = qtr.next(), ktr.next()
                for n in range(2):
                    cs_ = slice(n * 512, n * 512 + 512)
                    ps = psM.next()
                    self.mm(ps.t[:], ct.t[:, 8 + d, :], la32[d].t[:, cs_], True, True, [ct.b, la32[d].b], [ps.b])
                    e1, e2 = e32r.next(), e32r.next()
                    self.act(e1.t[:, 0:512], ps.t[:], AF.Exp, [ps.b], [e1.b])
                    self.act(e2.t[:, 0:512], ps.t[:], AF.Exp, [ps.b], [e2.b], scale=-1.0)
                    self.tt(DVE, qt.t[:, cs_], qs.t[:, cs_], e1.t[:, 0:512], ALU.mult, [qs.b, e1.b], [qt.b])
                    self.tt(POOL, kt.t[:, cs_], k32[d].t[:, cs_], e2.t[:, 0:512], ALU.mult, [k32[d].b, e2.b], [kt.b])
                for h in range(8):
                    self.mm(psC.t[:, h * 6:(h + 1) * 6], la32[d].t[:, h * 128:(h + 1) * 128], ct.t[:, 10 + d, 0:6],
                            True, True, [la32[d].b, ct.b], [psC.b])
                cs = csr.next()
                self.act(cs.t[:], psC.t[:, 0:48], AF.Exp, [psC.b], [cs.b])
                self.dma(SP, self.sCS[d, b, ti], cs.t[:], [cs.b], [self.dbuf(("CS", d, b, ti))])
                self.dma(SP, self.sKt[d, b, tok, :], kt.t[:], [kt.b], [self.dbuf(("Kt", d, b, ti))])
                for which, src in ((0, qt), (1, kt)):
                    psT = psTr.next()
                    st = stg.next()
                    self.transpose_to(src.t, 8, psT, st.t[:], st.b, src.b, eng=(DVE if which == 0 else ACT))
                    dst = (self.sQT if which == 0 else self.sKT)[d, b, ti]
                    self.dma(SP, dst, st.t[:], [st.b], [self.dbuf(("QT" if which == 0 else "KT", d, b, ti))])

        self.run_pipe(self.proj_tiles(), ld, pre_a, pre_b, main)
        P.end_phase()

    def hg_scan(self, li, j, d):
        P, c = self.P, self.cfg
        last = (li == self.L - 1)
        P.begin_phase()
        ct = self.load_ctab()
        S = {}
        for b in range(c.NB):
            for hh in range(2):
                S[b, hh] = Tl(P, [128, 4, 128], F32, "S")
                self.memset(POOL, S[b, hh].t[:], 0.0, [S[b, hh].b])
        qTr = Ring(P, 3, [128, 8, 128], BF16, "qT")
        kTr = Ring(P, 3, [128, 8, 128], BF16, "kT")
        ktr = Ring(P, 3, [128, D], BF16, "kt")
        vr = Ring(P, 3, [128, D], BF16, "v")
        csr = Ring(P, 3, [128, 8, 2, 3], F32, "cs")
        otr = Ring(P, 2, [128, D], F32, "ot")
        oflr = Ring(P, 3, [128, D], F32, "ofl") if d == 1 else None
        PTr = Ring(P, 2, [128, 4, 128], BF16, "PT")
        Sxr = Ring(P, 4, [128, 4, 128], BF16, "Sx")
        tmpr = Ring(P, 2, [128, 4, 128], F32, "tmp")
        psS = Ring(P, 2, [128, 512], F32, "psS", psum=True)
        psO = Ring(P, 2, [128, 512], F32, "psO", psum=True)
        psK = Ring(P, 4, [128, 512], F32, "psK", psum=True)
        steps = [(ti, b) for ti in self.scan_order(d) for b in range(c.NB)]
        corder = (0, 1) if d == 0 else (1, 0)

        def ld(st):
            ti, b = st
            tok = slice(ti * 128, (ti + 1) * 128)
            qT, kT, kt, v, cs = qTr.next(), kTr.next(), ktr.next(), vr.next(), csr.next()
            need_o = not (last and ti < c.CT)
            self.dma(SP, qT.t[:].rearrange("p k t -> p (k t)"), self.sQT[d, b, ti], [self.dbuf(("QT", d, b, ti))], [qT.b])
            self.dma(SP, kT.t[:].rearrange("p k t -> p (k t)"), self.sKT[d, b, ti], [self.dbuf(("KT", d, b, ti))], [kT.b])
            self.dma(SP, kt.t[:], self.sKt[d, b, tok, :], [self.dbuf(("Kt", d, b, ti))], [kt.b])
            self.dma(SP, v.t[:], self.sV[b, tok, 0:D], [self.dbuf(("V", b, ti))], [v.b])
            self.dma(SP, cs.t[:].rearrange("p h c k -> p (h c k)"), self.sCS[d, b, ti], [self.dbuf(("CS", d, b, ti))], [cs.b])
            ofl = None
            if d == 1 and need_o:
                ofl = oflr.next()
                self.dma(SP, ofl.t[:], self.sO[0, b, tok, 0:D], [self.dbuf(("O", 0, b, ti))], [ofl.b])
            return qT, kT, kt, v, cs, need_o, ofl

        def comp(st, tl):
            ti, b = st
            tok = slice(ti * 128, (ti + 1) * 128)
            qT, kT, kt, v, cs, need_o, ofl = tl
            ot = otr.next() if need_o else None
            for hh in range(2):
                Sb = S[b, hh]
                hs = slice(hh * 4, hh * 4 + 4)

                def bc(kind, cc):
                    return cs.t[:, hs, cc, kind:kind + 1].broadcast_to([128, 4, 128])

                if need_o:
                    ps = psS.next()
                    for hl in range(4):
                        h = hh * 4 + hl
                        self.mm(ps.t[:, hl * 128:(hl + 1) * 128], kT.t[:, h, :], qT.t[:, h, :], True, True, [kT.b, qT.b], [ps.b])
                    PT = PTr.next()
                    self.tt(DVE, PT.t[:], ps.t[:].rearrange("p (h t) -> p h t", h=4),
                            ct.t[:, 6 + d, :].unsqueeze(1).broadcast_to([128, 4, 128]), ALU.mult, [ps.b, ct.b], [PT.b])
                    po = psO.next()
                    for hl in range(4):
                        h = hh * 4 + hl
                        self.mm(po.t[:, hl * 128:(hl + 1) * 128], PT.t[:, hl, :], v.t[:, h * 128:(h + 1) * 128], hl == 0, False,
                                [PT.b, v.b], [po.b], skip=True)
                for ci, cc in enumerate(corder):
                    rows = slice(cc * 64, cc * 64 + 64)
                    if need_o:
                        Sx = Sxr.next()
                        self.tt(POOL, Sx.t[:], Sb.t[:], bc(0, cc), ALU.mult, [Sb.b, cs.b], [Sx.b])
                        for hl in range(4):
                            h = hh * 4 + hl
                            self.mm(po.t[rows, hl * 128:(hl + 1) * 128], qT.t[:, h, rows], Sx.t[:, hl, :], False,
                                    True, [qT.b, Sx.b], [po.b], skip=True)
                    pk = psK.next()
                    for hl in range(4):
                        h = hh * 4 + hl
                        self.mm(pk.t[:, hl * 128:(hl + 1) * 128], kt.t[rows, h * 128:(h + 1) * 128],
                                v.t[rows, h * 128:(h + 1) * 128], True, True, [kt.b, v.b], [pk.b])
                    tmp = tmpr.next()
                    self.tt(DVE, tmp.t[:], pk.t[:].rearrange("p (h t) -> p h t", h=4), bc(2, cc), ALU.mult, [pk.b, cs.b], [tmp.b])
                    self.tt(DVE, Sb.t[:], Sb.t[:], bc(1, cc), ALU.mult, [Sb.b, cs.b], [Sb.b])
                    self.tt(DVE, Sb.t[:], Sb.t[:], tmp.t[:], ALU.add, [Sb.b, tmp.b], [Sb.b])
                if need_o and d == 0:
                    self.act(ot.t[:, hh * 512:(hh + 1) * 512], po.t[:], AF.Copy, [po.b], [ot.b])
                elif need_o:
                    self.tt(DVE, ot.t[:, hh * 512:(hh + 1) * 512], po.t[:], ofl.t[:, hh * 512:(hh + 1) * 512], ALU.add,
                            [po.b, ofl.b], [ot.b])
            if need_o:
                self.dma(SP, self.sO[d, b, tok, 0:D], ot.t[:], [ot.b], [self.dbuf(("O", d, b, ti))])

        self.run_pipelined(steps, ld, comp, rowof=lambda it: 0)
        P.end_phase()

    def hg_readout(self, li, j):
        P, c = self.P, self.cfg
        P.begin_phase()
        wo = Tl(P, [128, 8, 1024], BF16, "w_out")
        self.load_wout_scaled(wo, self.hg_w_out[j], 8, self.hg_gn[j])
        g1 = Tl(P, [128, D], F32, "g1")
        ofr = Ring(P, 3, [128, D], F32, "of")
        gr = Ring(P, 3, [128, D], BF16, "g")
        hr = Ring(P, 3, [128, D], F32, "h")
        sqr = Ring(P, 2, [128, D], F32, "sq")
        onr = Ring(P, 2, [128, D], BF16, "on")
        onTr = Ring(P, 2, [128, 8, 128], BF16, "onT")
        rsr = Ring(P, 2, [128, 8], F32, "rs")
        hnr = Ring(P, 2, [128, D], F32, "hn")
        ssr = Ring(P, 4, [128, 2], F32, "ss")
        junk = Tl(P, [128, D], BF16, "junk")
        psTr = Ring(P, 2, [128, 1024], BF16, "psT", psum=True)
        psYr = Ring(P, 2, [128, 1024], F32, "psY", psum=True)

        def ld(it):
            row, b, ti = it
            tok = slice(ti * 128, (ti + 1) * 128)
            of, g, h = ofr.next(), gr.next(), hr.next()
            self.dma(SP, of.t[:], self.sO[1, b, tok, 0:D], [self.dbuf(("O", 1, b, ti))], [of.b])
            self.dma(SP, g.t[:], self.sG[b, tok, 0:D], [self.dbuf(("G", b, ti))], [g.b])
            src, sb_ = self.h_src(li, b, ti)
            self.dma(SP, h.t[:], src, [sb_] if sb_ is not None else [], [h.b])
            return {"of": of, "g": g, "h": h}

        def pre_a(it, cx):
            of, g = cx["of"], cx["g"]
            on, rs, sq = onr.next(), rsr.next(), sqr.next()
            self.tt(POOL, sq.t[:], of.t[:], of.t[:], ALU.mult, [of.b], [sq.b])
            self.P.op(DVE, lambda e, o=rs.t[:], i=sq.t[:].rearrange("p (h v) -> p h v", h=8):
                      e.tensor_reduce(out=o, in_=i, axis=mybir.AxisListType.X, op=ALU.add), [sq.b], [rs.b])
            self.rstd(rs.t[:], rs.t[:], 1.0 / 128, EPS, [rs.b], [rs.b], n=8)
            self.tt(DVE, of.t[:], of.t[:], g.t[:], ALU.mult, [of.b, g.b], [of.b])
            self.tt(DVE, on.t[:].rearrange("p (h v) -> p h v", h=8), of.t[:].rearrange("p (h v) -> p h v", h=8),
                    rs.t[:].unsqueeze(2).broadcast_to([128, 8, 128]), ALU.mult, [of.b, rs.b], [on.b])
            cx["on"] = on

        def pre_b(it, cx):
            on, onT = cx["on"], onTr.next()
            psT = psTr.next()
            self.transpose_to(on.t, 8, psT, onT.t[:].rearrange("p k t -> p (k t)"), onT.b, on.b)
            cx["onT"] = onT

        self.run_pipe(self.readout_items(li), ld, pre_a, pre_b, self.readout_main(li, wo, 8, g1, psYr, hnr, ssr, junk))
        P.end_phase()

    def x_row(self, ti):
        c = self.cfg
        if ti < c.CT:
            return 1 + ti * 128
        return c.CTX + 3 + (ti - c.CT) * 128

    def m2_proj(self, li, j):
        P, c = self.P, self.cfg
        P.begin_phase()
        w = Tl(P, [128, 8, 5184], BF16, "w_in")
        self.load_w(w, self.m2_w_in[j], 8, 5184)
        tabs = [Tl(P, [128, D], F32, f"tab{i}") for i in range(2)]
        dtb = Tl(P, [128, 64], F32, "dtb")
        aneg = Tl(P, [128, 64], F32, "aneg")
        self.dma(SP, dtb.t[:], self.m2_dt_bias[j, :].partition_broadcast(128), [], [dtb.b])
        self.dma(SP, aneg.t[:], self.m2_a_log[j, :].partition_broadcast(128), [], [aneg.b])
        self.act(aneg.t[:], aneg.t[:], AF.Exp, [aneg.b], [aneg.b])
        self.ts(DVE, aneg.t[:], aneg.t[:], -1.0, None, ALU.mult, None, [aneg.b], [aneg.b])
        zero = Tl(P, [1, 3072], F32, "zero")
        self.memset(DVE, zero.t[:], 0.0, [zero.b])
        for b in range(c.NB):
            for r in (0, c.CTX + 1, c.CTX + 2, c.CTX + c.LAT + 3):
                self.dma(SP, self.sX[b, r:r + 1, :], zero.t[:], [zero.b], [self.dbuf(("Xpad", b, r))])
        hr = Ring(P, 3, [128, D], F32, "h")
        ur = Ring(P, 2, [128, D], BF16, "u")
        junk = Tl(P, [128, D], BF16, "junk")
        ssr = Ring(P, 4, [128, 2], F32, "ss")
        uTr = Ring(P, 2, [128, 8, 128], BF16, "uT")
        zr = Ring(P, 2, [128, 2048], BF16, "zb")
        xr = Ring(P, 2, [128, 3072], F32, "xbc")
        dr = Ring(P, 2, [128, 128], F32, "dtla")
        psTr = Ring(P, 2, [128, 1024], BF16, "psT", psum=True)
        psM = Ring(P, 6, [128, 512], F32, "psM", psum=True)
        ld, pre_a, pre_b = self.make_stages(li, tabs, hr, ur, ssr, uTr, junk, psTr)

        def main(it, cx, hook):
            uT = cx["uT"]
            row, b, ti = it
            tok = slice(ti * 128, (ti + 1) * 128)
            zb, xb, dl = zr.next(), xr.next(), dr.next()
            for n in range(11):
                if n == 6:
                    hook()
                ps = psM.next()
                wd = 512 if n < 10 else 64
                for k in range(8):
                    self.mm(ps.t[:, 0:wd], uT.t[:, k, :], w.t[:, k, n * 512:n * 512 + wd], k == 0, k == 7, [uT.b, w.b], [ps.b])
                if n < 4:
                    self.act(zb.t[:, n * 512:(n + 1) * 512], ps.t[:], AF.Silu, [ps.b], [zb.b])
                elif n < 10:
                    eng = ACT if n % 2 == 0 else DVE
                    self.cp(eng, xb.t[:, (n - 4) * 512:(n - 3) * 512], ps.t[:], [ps.b], [xb.b])
                else:
                    self.tt(DVE, dl.t[:, 0:64], ps.t[:, 0:64], dtb.t[:], ALU.add, [ps.b, dtb.b], [dl.b])
                    self.act(dl.t[:, 0:64], dl.t[:, 0:64], AF.Exp, [dl.b], [dl.b])
                    self.act(dl.t[:, 0:64], dl.t[:, 0:64], AF.Ln, [dl.b], [dl.b], bias=1.0)
                    self.tt(DVE, dl.t[:, 64:128], dl.t[:, 0:64], aneg.t[:], ALU.mult, [dl.b, aneg.b], [dl.b])
            self.dma(SP, self.sG[b, tok, :], zb.t[:], [zb.b], [self.dbuf(("G", b, ti))])
            r0 = self.x_row(ti)
            self.dma(SP, self.sX[b, r0:r0 + 128, :], xb.t[:], [xb.b], [self.dbuf(("X", b, ti))])
            self.dma(SP, self.sDT[b, tok, :], dl.t[:], [dl.b], [self.dbuf(("DT", b, ti))])

        self.run_pipe(self.proj_tiles(), ld, pre_a, pre_b, main)
        P.end_phase()

    def m2_conv(self, li, j):
        P, c = self.P, self.cfg
        P.begin_phase()
        cw = Tl(P, [128, 3, 3072], F32, "cw")
        cb = Tl(P, [128, 3072], F32, "cb")
        self.dma(SP, cw.t[:].rearrange("p a b -> p (a b)"), self.m2_conv_w[j].rearrange("a b -> (a b)").partition_broadcast(128), [], [cw.b])
        self.dma(SP, cb.t[:], self.m2_conv_b[j, :].partition_broadcast(128), [], [cb.b])
        xr = [Ring(P, 2, [128, 3072], F32, f"x{i}") for i in range(3)]
        actr = Ring(P, 2, [128, 3072], BF16, "act")
        stg = Ring(P, 2, [128, 1024], BF16, "stg")
        psTr = Ring(P, 2, [128, 1024], BF16, "psT", psum=True)
        items = [(b, ti) for b in range(c.NB) for ti in range(c.NT)]

        def ld(it):
            b, ti = it
            r0 = self.x_row(ti)
            xs = [xr[i].next() for i in range(3)]
            deps = [self.dbuf(("X", b, t2)) for t2 in range(c.NT)] + [self.dbuf(("Xpad", b, r)) for r in (0, c.CTX + 1, c.CTX + 2, c.CTX + c.LAT + 3)]
            for i in range(3):
                self.dma(SP, xs[i].t[:], self.sX[b, r0 - 1 + i:r0 - 1 + i + 128, :], deps, [xs[i].b])
            return xs

        def comp(it, xs):
            b, ti = it
            tok = slice(ti * 128, (ti + 1) * 128)
            x0, x1, x2 = xs
            self.tt(POOL, x0.t[:], x0.t[:], cw.t[:, 0, :], ALU.mult, [x0.b, cw.b], [x0.b])
            self.tt(DVE, x1.t[:], x1.t[:], cw.t[:, 1, :], ALU.mult, [x1.b, cw.b], [x1.b])
            self.tt(DVE, x2.t[:], x2.t[:], cw.t[:, 2, :], ALU.mult, [x2.b, cw.b], [x2.b])
            self.tt(DVE, x1.t[:], x1.t[:], cb.t[:], ALU.add, [x1.b, cb.b], [x1.b])
            self.tt(DVE, x1.t[:], x1.t[:], x2.t[:], ALU.add, [x1.b, x2.b], [x1.b])
            self.tt(DVE, x1.t[:], x1.t[:], x0.t[:], ALU.add, [x1.b, x0.b], [x1.b])
            a = actr.next()
            self.act(a.t[:], x1.t[:], AF.Silu, [x1.b], [a.b])
            self.dma(SP, self.sV[b, tok, :], a.t[:, 0:2048], [a.b], [self.dbuf(("V", b, ti))])
            self.dma(SP, self.sKt[0, b, tok, 0:512], a.t[:, 2048:2560], [a.b], [self.dbuf(("Kt", b, ti))])
            psT = psTr.next()
            st = stg.next()
            self.transpose_to(a.t, 8, psT, st.t[:], st.b, a.b, eng=ACT, src_off=2048)
            self.dma(SP, self.sKT[0, b, ti][:, 0:512], st.t[:, 0:512], [st.b], [self.dbuf(("KT", b, ti))])
            self.dma(SP, self.sQT[0, b, ti][:, 0:512], st.t[:, 512:1024], [st.b], [self.dbuf(("QT", b, ti))])

        self.run_pipelined(items, ld, comp, rowof=lambda it: 0)
        P.end_phase()

    def m2_scan(self, li, j, d):
        P, c = self.P, self.cfg
        last = (li == self.L - 1)
        P.begin_phase()
        ct = self.load_ctab()
        S32, Sbf = {}, {}
        for b in range(c.NB):
            for g in range(4):
                S32[b, g] = Tl(P, [128, 8, 64], F32, "S32")
                Sbf[b, g] = Tl(P, [128, 512], BF16, "Sbf")
                self.memset(POOL, S32[b, g].t[:], 0.0, [S32[b, g].b])
                self.memset(DVE, Sbf[b, g].t[:], 0.0, [Sbf[b, g].b])
        CTr = Ring(P, 3, [128, 4, 128], BF16, "CT")
        BTr = Ring(P, 3, [128, 4, 128], BF16, "BT")
        Btr = Ring(P, 3, [128, 512], BF16, "Bt")
        Xr = Ring(P, 3, [128, 32, 64], BF16, "X")
        dlr = Ring(P, 3, [128, 128], F32, "dtla")
        ear = Ring(P, 2, [128, 96], F32, "eall")
        xdr = Ring(P, 2, [128, 32, 64], BF16, "xdt")
        xwr = Ring(P, 2, [128, 32, 64], BF16, "xw")
        Rr = Ring(P, 2, [128, 8, 128], F32, "R")
        Lr = Ring(P, 2, [128, 8, 128], BF16, "L")
        CBr = Ring(P, 2, [128, 128], F32, "CBm")
        Pmr = Ring(P, 2, [128, 8, 128], BF16, "Pm")
        tmpr = Ring(P, 2, [128, 8, 64], F32, "tmp")
        otr = Ring(P, 2, [128, 2048], F32, "ot")
        psE = Tl(P, [128, 512], F32, "psE", psum=True)
        psD = Tl(P, [128, 1024], F32, "psD", psum=True)
        psCB = Tl(P, [128, 512], F32, "psCB", psum=True)
        psO = Ring(P, 2, [128, 512], F32, "psO", psum=True)
        psI = Tl(P, [128, 512], F32, "psI", psum=True)
        psK = Tl(P, [128, 512], F32, "psK", psum=True)
        steps = [(ti, b) for ti in self.scan_order(d) for b in range(c.NB)]
        tri = ct.t[:, 12 + d, :]
        G = ct.t[:, 14 + d, :]
        ones = ct.t[:, 16, :]

        def ld(st):
            ti, b = st
            tok = slice(ti * 128, (ti + 1) * 128)
            CT, BT, Bt, X, dl = CTr.next(), BTr.next(), Btr.next(), Xr.next(), dlr.next()
            need_o = not (last and ti < c.CT)
            if need_o:
                self.dma(SP, CT.t[:].rearrange("p g t -> p (g t)"), self.sQT[0, b, ti][:, 0:512], [self.dbuf(("QT", b, ti))], [CT.b])
                self.dma(SP, BT.t[:].rearrange("p g t -> p (g t)"), self.sKT[0, b, ti][:, 0:512], [self.dbuf(("KT", b, ti))], [BT.b])
            self.dma(SP, Bt.t[:], self.sKt[0, b, tok, 0:512], [self.dbuf(("Kt", b, ti))], [Bt.b])
            self.dma(SP, X.t[:].rearrange("p h e -> p (h e)"), self.sV[b, tok, :], [self.dbuf(("V", b, ti))], [X.b])
            self.dma(SP, dl.t[:], self.sDT[b, tok, :], [self.dbuf(("DT", b, ti))], [dl.b])
            ofl = None
            return CT, BT, Bt, X, dl, need_o, ofl

        def comp(st, tl):
            ti, b = st
            tok = slice(ti * 128, (ti + 1) * 128)
            CT, BT, Bt, X, dl, need_o, ofl = tl
            la = dl.t[:, 64 + d * 32:64 + (d + 1) * 32]
            dt = dl.t[:, d * 32:(d + 1) * 32]
            self.mm(psE.t[:, 0:32], tri, la, True, True, [ct.b, dl.b], [psE.b])
            self.mm(psE.t[:, 32:64], G, la, True, True, [ct.b, dl.b], [psE.b])
            self.mm(psE.t[:, 64:96], ones, la, True, True, [ct.b, dl.b], [psE.b])
            ea = ear.next()
            self.act(ea.t[:], psE.t[:, 0:96], AF.Exp, [psE.b], [ea.b])
            xd, xw = xdr.next(), xwr.next()
            self.tt(DVE, xd.t[:], X.t[:], dt.unsqueeze(2).broadcast_to([128, 32, 64]), ALU.mult, [X.b, dl.b], [xd.b])
            self.tt(POOL, xw.t[:], xd.t[:], ea.t[:, 32:64].unsqueeze(2).broadcast_to([128, 32, 64]), ALU.mult, [xd.b, ea.b], [xw.b])
            ot = otr.next() if need_o else None
            for g in range(4):
                hs = slice(g * 8, g * 8 + 8)
                if need_o:
                    R = Rr.next()
                    for r in range(8):
                        self.act(R.t[:, r, :], tri, AF.Copy, [dl.b, ct.b], [R.b], scale=la[:, g * 8 + r:g * 8 + r + 1])
                    for half in range(2):
                        self.mm(psD.t[:, half * 512:(half + 1) * 512], G,
                                R.t[:, half * 4:(half + 1) * 4, :].rearrange("p h t -> p (h t)"), True, True, [ct.b, R.b], [psD.b])
                    Lg = Lr.next()
                    self.act(Lg.t[:].rearrange("p h t -> p (h t)"), psD.t[:], AF.Exp, [psD.b], [Lg.b])
                    self.mm(psCB.t[:, 0:128], BT.t[:, g, :], CT.t[:, g, :], True, True, [BT.b, CT.b], [psCB.b])
                    CBm = CBr.next()
                    self.tt(DVE, CBm.t[:], psCB.t[:, 0:128], ct.t[:, 1 + d, :], ALU.mult, [psCB.b, ct.b], [CBm.b])
                    Pm = Pmr.next()
                    self.tt(DVE, Pm.t[:], Lg.t[:], CBm.t[:].unsqueeze(1).broadcast_to([128, 8, 128]), ALU.mult, [Lg.b, CBm.b], [Pm.b])
                    po = psO.next()
                    for r in range(8):
                        self.mm(po.t[:, r * 64:(r + 1) * 64], Pm.t[:, r, :], xd.t[:, g * 8 + r, :], r == 0, r == 7,
                                [Pm.b, xd.b], [po.b], skip=True)
                    self.mm(psI.t[:], CT.t[:, g, :], Sbf[b, g].t[:], True, True, [CT.b, Sbf[b, g].b], [psI.b])
                    tmp = tmpr.next()
                    self.tt(DVE, tmp.t[:], psI.t[:].rearrange("p (h e) -> p h e", h=8),
                            ea.t[:, hs].unsqueeze(2).broadcast_to([128, 8, 64]), ALU.mult, [psI.b, ea.b], [tmp.b])
                    self.tt(DVE, ot.t[:, g * 512:(g + 1) * 512], tmp.t[:].rearrange("p h e -> p (h e)"), po.t[:], ALU.add,
                            [tmp.b, po.b], [ot.b])
                self.mm(psK.t[:], Bt.t[:, g * 128:(g + 1) * 128], xw.t[:, hs, :].rearrange("p h e -> p (h e)"), True, True,
                        [Bt.b, xw.b], [psK.b])
                s32 = S32[b, g]
                self.tt(POOL, s32.t[:], s32.t[:], ea.t[:, 64 + g * 8:64 + g * 8 + 8].unsqueeze(2).broadcast_to([128, 8, 64]),
                        ALU.mult, [s32.b, ea.b], [s32.b])
                self.tt(DVE, s32.t[:], s32.t[:], psK.t[:].rearrange("p (h e) -> p h e", h=8), ALU.add, [s32.b, psK.b], [s32.b])
                self.cp(ACT, Sbf[b, g].t[:], s32.t[:].rearrange("p h e -> p (h e)"), [s32.b], [Sbf[b, g].b])
            if need_o:
                self.dma(SP, self.sO[d, b, tok, :], ot.t[:], [ot.b], [self.dbuf(("O", d, b, ti))])

        self.run_pipelined(steps, ld, comp, rowof=lambda it: 0)
        P.end_phase()

    def m2_readout(self, li, j):
        P, c = self.P, self.cfg
        P.begin_phase()
        wo = Tl(P, [128, 16, 1024], BF16, "w_out")
        self.load_wout_scaled(wo, self.m2_w_out[j], 16, self.m2_gn[j])
        dsk = Tl(P, [128, 32], F32, "dsk")
        self.dma(SP, dsk.t[:], self.m2_d[j, :].partition_broadcast(128), [], [dsk.b])
        g1 = Tl(P, [128, D], F32, "g1")
        ofr = Ring(P, 3, [128, 2048], F32, "of")
        obr = Ring(P, 3, [128, 2048], F32, "ob")
        xr = Ring(P, 3, [128, 32, 64], BF16, "x")
        zr = Ring(P, 3, [128, 2048], BF16, "z")
        hr = Ring(P, 3, [128, D], F32, "h")
        tmpr = Ring(P, 2, [128, 32, 64], F32, "tmp")
        junk2 = Tl(P, [128, 512], BF16, "junk2")
        onr = Ring(P, 2, [128, 2048], BF16, "on")
        onTr = Ring(P, 2, [128, 16, 128], BF16, "onT")
        rsr = Ring(P, 2, [128, 4], F32, "rs")
        hnr = Ring(P, 2, [128, D], F32, "hn")
        ssr = Ring(P, 4, [128, 2], F32, "ss")
        junk = Tl(P, [128, D], BF16, "junk")
        psTr = Ring(P, 2, [128, 1024], BF16, "psT", psum=True)
        psYr = Ring(P, 2, [128, 1024], F32, "psY", psum=True)

        def ld(it):
            row, b, ti = it
            tok = slice(ti * 128, (ti + 1) * 128)
            of, ob, x, z, h = ofr.next(), obr.next(), xr.next(), zr.next(), hr.next()
            self.dma(SP, of.t[:], self.sO[1, b, tok, :], [self.dbuf(("O", 1, b, ti))], [of.b])
            self.dma(SP, ob.t[:], self.sO[0, b, tok, :], [self.dbuf(("O", 0, b, ti))], [ob.b])
            self.dma(SP, x.t[:].rearrange("p h e -> p (h e)"), self.sV[b, tok, :], [self.dbuf(("V", b, ti))], [x.b])
            self.dma(SP, z.t[:], self.sG[b, tok, :], [self.dbuf(("G", b, ti))], [z.b])
            src, sb_ = self.h_src(li, b, ti)
            self.dma(SP, h.t[:], src, [sb_] if sb_ is not None else [], [h.b])
            return {"of": of, "ob": ob, "x": x, "z": z, "h": h}

        def pre_a(it, cx):
            of, ob, x, z = cx["of"], cx["ob"], cx["x"], cx["z"]
            on, rs, tmp = onr.next(), rsr.next(), tmpr.next()
            self.tt(DVE, tmp.t[:], x.t[:], dsk.t[:].unsqueeze(2).broadcast_to([128, 32, 64]), ALU.mult, [x.b, dsk.b], [tmp.b])
            self.tt(POOL, of.t[:], of.t[:], ob.t[:], ALU.add, [of.b, ob.b], [of.b])
            self.tt(DVE, of.t[:], of.t[:], tmp.t[:].rearrange("p h e -> p (h e)"), ALU.add, [of.b, tmp.b], [of.b])
            self.tt(DVE, of.t[:], of.t[:], z.t[:], ALU.mult, [of.b, z.b], [of.b])
            for g in range(4):
                self.act(junk2.t[:], of.t[:, g * 512:(g + 1) * 512], AF.Square, [of.b], [junk2.b, rs.b], accum_out=rs.t[:, g:g + 1])
            self.rstd(rs.t[:], rs.t[:], 1.0 / 512, EPS, [rs.b], [rs.b], n=4)
            self.tt(DVE, on.t[:].rearrange("p (g v) -> p g v", g=4), of.t[:].rearrange("p (g v) -> p g v", g=4),
                    rs.t[:].unsqueeze(2).broadcast_to([128, 4, 512]), ALU.mult, [of.b, rs.b], [on.b])
            cx["on"] = on

        def pre_b(it, cx):
            on, onT = cx["on"], onTr.next()
            for half in range(2):
                psT = psTr.next()
                self.transpose_to(on.t, 8, psT, onT.t[:, half * 8:(half + 1) * 8, :].rearrange("p k t -> p (k t)"),
                                  onT.b, on.b, eng=(ACT if half == 0 else DVE), src_off=half * 1024)
            cx["onT"] = onT

        self.run_pipe(self.readout_items(li), ld, pre_a, pre_b, self.readout_main(li, wo, 16, g1, psYr, hnr, ssr, junk))
        P.end_phase()

    def copy_in(self):
        P, c = self.P, self.cfg
        P.begin_phase()
        hr = Ring(P, 3, [128, D], F32, "h")
        for b in range(c.NB):
            for ti in range(c.NT):
                h = hr.next()
                src, _ = self.h_src(0, b, ti)
                self.dma(SP, h.t[:], src, [], [h.b])
                dst, db_ = self.h_mid(b, ti)
                self.dma(SP, dst, h.t[:], [h.b], [db_])
        P.end_phase()

    def build(self):
        c = self.cfg
        self.declare()
        self.setup_consts()
        phases = c.phases
        if phases is None:
            phases = ["mod"]
            for li, k in enumerate(c.kinds):
                phases += [("mix", li), ("ffn", li)]
        for ph in phases:
            if ph == "mod":
                self.phase_mod()
            elif ph == "copy_in":
                self.copy_in()
            elif ph[0] == "ffn":
                self.phase_ffn(ph[1])
            elif ph[0] == "mix":
                self.phase_mixer(ph[1])
            elif ph[0] == "call":
                getattr(self, ph[1])(*ph[2:])
        self.P.begin_phase()
        self.P.end_phase(final=True)
        return self.nc


def const_tables(cfg):
    ident = np.eye(128, dtype=np.float32)
    LT = max(cfg.LT, 1)
    nf = 64
    inv = (10000.0 ** (-np.arange(nf, dtype=np.float32) / nf)).astype(np.float32)
    rope = np.zeros((LT, 128, 512), np.float32)
    p = np.arange(128)
    for lt in range(LT):
        row = (2 * lt + p // GRID_W).astype(np.float32)
        col = (p % GRID_W).astype(np.float32)
        ar = row[:, None] * inv[None, :]
        ac = col[:, None] * inv[None, :]
        cr, sr, cc, sc = np.cos(ar), np.sin(ar), np.cos(ac), np.sin(ac)
        rope[lt, :, 0:256] = np.concatenate([cr, cr, cc, cc], 1)
        rope[lt, :, 256:512] = np.concatenate([-sr, sr, -sc, sc], 1)
    tab = np.zeros((128, 24, 128), np.float32)
    s = np.arange(128)[:, None].astype(np.float32)
    t = np.arange(128)[None, :].astype(np.float32)
    tab[:, 0] = t - s
    tab[:, 1] = (t >= s)
    tab[:, 2] = (s >= t)
    tab[:, 3] = t + 1.0
    tab[:, 4] = 128.0 - t
    tab[:, 5, 0] = 127.0 - s[:, 0]
    tab[:, 5, 1] = s[:, 0]
    tab[:, 5, 2] = 128.0
    same = ((s // 64) == (t // 64)).astype(np.float32)
    sl = s % 64
    tab[:, 6] = same * (t >= s)
    tab[:, 7] = same * (s >= t)
    tab[:, 8] = same * ((s <= t).astype(np.float32) - (sl <= 31))
    tab[:, 9] = same * ((s >= t).astype(np.float32) - (sl >= 32))
    for cc in range(2):
        inc = ((s[:, 0] // 64) == cc).astype(np.float32)
        slc = s[:, 0] % 64
        tab[:, 10, 3 * cc + 0] = inc * (slc <= 31)
        tab[:, 10, 3 * cc + 1] = inc
        tab[:, 10, 3 * cc + 2] = inc * (slc > 31)
        tab[:, 11, 3 * cc + 0] = inc * (slc >= 32)
        tab[:, 11, 3 * cc + 1] = inc
        tab[:, 11, 3 * cc + 2] = inc * (slc < 32)
    tab[:, 12] = (s <= t)
    tab[:, 13] = (s >= t)
    tab[:, 14] = (s > t)
    tab[:, 15] = (s < t)
    tab[:, 16] = 1.0
    return ident, rope, tab.reshape(128, 24 * 128)


def make_in_maps(cfg, inputs, n_cores):
    ident, rope, tab = const_tables(cfg)
    f = lambda a: np.ascontiguousarray(np.asarray(a, dtype=np.float32))
    L = len(cfg.kinds)
    nr, nh, nm = (max(1, sum(1 for k in cfg.kinds if k == q)) for q in (0, 1, 2))
    inputs = dict(inputs)
    for k in inputs:
        if k.startswith("ret_"):
            inputs[k] = np.asarray(inputs[k])[:nr]
        elif k.startswith("hg_") and k != "hg_lb":
            inputs[k] = np.asarray(inputs[k])[:nh]
        elif k.startswith("m2_"):
            inputs[k] = np.asarray(inputs[k])[:nm]
    shared = {
        "ada_w": f(inputs["ada_w"][:L]), "ada_b": f(inputs["ada_b"][:L]),
        "norm_g": f(inputs["norm_g"][:L]).reshape(L, 4 * D),
        "ret_w_in": f(inputs["ret_w_in"]), "ret_w_out": f(inputs["ret_w_out"]),
        "ret_decay": f(inputs["ret_decay"]).reshape(-1, 8), "ret_gn": f(inputs["ret_gn"]),
        "hg_w_in": f(inputs["hg_w_in"]), "hg_w_out": f(inputs["hg_w_out"]), "hg_lb": f(inputs["hg_lb"][:L]),
        "hg_gn": f(inputs["hg_gn"]),
        "m2_w_in": f(inputs["m2_w_in"]), "m2_w_out": f(inputs["m2_w_out"]), "m2_conv_w": f(inputs["m2_conv_w"]),
        "m2_conv_b": f(inputs["m2_conv_b"]), "m2_dt_bias": f(inputs["m2_dt_bias"]).reshape(-1, 64),
        "m2_a_log": f(inputs["m2_a_log"]).reshape(-1, 64), "m2_d": f(inputs["m2_d"]), "m2_gn": f(inputs["m2_gn"]),
        "ffn_w_up": f(inputs["ffn_w_up"][:L]), "ffn_w_down": f(inputs["ffn_w_down"][:L]),
        "ffn_cw": f(np.asarray(inputs["ffn_conv_w"][:L]).reshape(L, 3, 22, 128).transpose(0, 3, 2, 1)).reshape(L, 128, 66),
        "ffn_cb": f(np.asarray(inputs["ffn_conv_b"][:L]).reshape(L, 22, 128).transpose(0, 2, 1)),
        "c_ident": ident, "c_rope": rope, "c_tab": tab,
    }
    x, cc, ctx, c_ctx = (np.asarray(inputs[k], dtype=np.float32) for k in ("x", "c", "ctx", "c_ctx"))
    maps = []
    for i in range(n_cores):
        sl = slice(i * cfg.NB, (i + 1) * cfg.NB)
        m = dict(shared)
        m["x"] = f(x[sl])
        m["ctx"] = f(ctx[sl])
        m["crow"] = f(np.concatenate([cc[sl], c_ctx[None, :]], 0))
        maps.append(m)
    return maps


_CACHE = {}


def kernel(**inputs):
    cfg = Cfg()
    if "nc" not in _CACHE:
        _CACHE["nc"] = Builder(cfg).build()
    nc = _CACHE["nc"]
    maps = make_in_maps(cfg, inputs, N_CORES)
    res = run_bass_kernel_spmd(nc, maps, core_ids=list(range(N_CORES)))
    out = np.concatenate([np.asarray(r["out"]) for r in res.results], axis=0)
    return out.astype(np.float32)
```

```python
import contextlib
import numpy as np
import concourse.bass as bass
import concourse.mybir as mybir
from concourse.bass_utils import run_bass_kernel_spmd

F32 = mybir.dt.float32
BF16 = mybir.dt.bfloat16
AF = mybir.ActivationFunctionType
ALU = mybir.AluOpType

PE, ACT, DVE, POOL, SP = "pe", "act", "dve", "pool", "sp"
CENG = (PE, ACT, DVE, POOL)

D = 1024
EPS = 1e-6
GRID_W = 64
N_CORES = 8


class Buf:
    __slots__ = ("name", "last_w", "readers", "dram")

    def __init__(self, name, dram=False):
        self.name = name
        self.last_w = None
        self.readers = []
        self.dram = dram


class Op:
    __slots__ = ("eng", "fn", "deps", "weak", "is_dma", "signal", "val", "dsem_buf")

    def __init__(self, eng, fn, is_dma, dsem_buf):
        self.eng = eng
        self.fn = fn
        self.deps = set()
        self.weak = set()
        self.is_dma = is_dma
        self.signal = False
        self.val = None
        self.dsem_buf = dsem_buf


class Prog:
    def __init__(self, same_eng_sync=True):
        self.nc = bass.Bass("TRN2", target_bir_lowering=False)
        self.ops = []
        self.same_eng_sync = same_eng_sync
        self.keep = []
        self.bufs = []
        nc = self.nc
        self.engs = {PE: nc.tensor, ACT: nc.scalar, DVE: nc.vector, POOL: nc.gpsimd, SP: nc.sync}
        self.sems = {}
        for e in CENG:
            cm = nc.semaphore("s_" + e)
            self.sems[e] = cm.__enter__()
            self.keep.append(cm)
        self.cnt = {e: 0 for e in CENG}
        self.dsem_free = {True: [], False: []}
        self.dsem_all = []
        self.waited = {}
        self.n_inst = 0
        self.n_wait = 0
        self.phase_stack = None
        self.uid = 0

    def begin_phase(self):
        self.phase_stack = contextlib.ExitStack()

    def sb(self, shape, dt, name=None):
        self.uid += 1
        return self.phase_stack.enter_context(self.nc.sbuf_tensor(f"{name or 't'}_{self.uid}", list(shape), dt))

    def ps(self, shape, dt=F32, name=None):
        self.uid += 1
        return self.phase_stack.enter_context(self.nc.psum_tensor(f"{name or 'p'}_{self.uid}", list(shape), dt))

    def buf(self, name=None, dram=False):
        b = Buf(name or "b", dram)
        self.bufs.append(b)
        return b

    def op(self, eng, fn, reads=(), writes=(), dma=False):
        idx = len(self.ops)
        dsem_buf = None
        if dma:
            for b in list(writes) + list(reads):
                if not b.dram:
                    dsem_buf = b
                    break
            if dsem_buf is None:
                dsem_buf = (list(writes) + list(reads))[0]
        o = Op(eng, fn, dma, dsem_buf)
        for b in reads:
            if b.last_w is not None:
                o.deps.add(b.last_w)
        for b in writes:
            if b.last_w is not None:
                o.weak.add(b.last_w)
            for r in b.readers:
                o.weak.add(r)
        for b in reads:
            b.readers.append(idx)
        for b in writes:
            b.last_w = idx
            b.readers = []
        o.deps.discard(idx)
        for d in o.weak:
            if d != idx and d not in o.deps:
                s_ = self.ops[d]
                if s_.is_dma or dma or s_.eng != eng:
                    o.deps.add(d)
        self.ops.append(o)
        return idx

    def end_phase(self, final=False):
        nc = self.nc
        ops = self.ops
        engs = self.engs
        sems = self.sems
        cnt = self.cnt
        waited = self.waited
        last_on = {}
        for i, o in enumerate(ops):
            last_on[o.eng] = i
            for d in o.deps:
                s = ops[d]
                if s.is_dma:
                    continue
                if s.eng == o.eng and (s.eng == PE or not self.same_eng_sync):
                    continue
                s.signal = True
        for e in CENG:
            for i in range(len(ops) - 1, -1, -1):
                if ops[i].eng == e and not ops[i].is_dma:
                    ops[i].signal = True
                    break
        dsems = {}
        for i, o in enumerate(ops):
            eng = engs[o.eng]
            need = {}
            for d in o.deps:
                s = ops[d]
                if s.is_dma:
                    st = dsems[(id(s.dsem_buf), s.eng == POOL)]
                    key = ("d", id(st))
                    v = st[1]
                    sem = st[0]
                else:
                    if s.eng == o.eng and (s.eng == PE or not self.same_eng_sync):
                        continue
                    key = ("e", s.eng)
                    v = s.val
                    sem = sems[s.eng]
                if need.get(key, (None, -1))[1] < v:
                    need[key] = (sem, v)
            for key, (sem, v) in need.items():
                if waited.get((o.eng, key), -1) >= v:
                    continue
                waited[(o.eng, key)] = v
                eng.wait_ge(sem, v)
                self.n_wait += 1
            inst = o.fn(eng)
            self.n_inst += 1
            if o.is_dma:
                k = (id(o.dsem_buf), o.eng == POOL)
                if k not in dsems:
                    fl = self.dsem_free[k[1]]
                    if fl:
                        dsems[k] = fl.pop()
                    else:
                        cm = nc.semaphore(f"d{len(self.dsem_all)}")
                        st = [cm.__enter__(), 0]
                        self.keep.append(cm)
                        self.dsem_all.append(st)
                        dsems[k] = st
                st = dsems[k]
                st[1] += 16
                inst.then_inc(st[0], 16)
            elif o.signal:
                cnt[o.eng] += 1
                o.val = cnt[o.eng]
                inst.then_inc(sems[o.eng], 1)
        targets = [SP] if final else list(engs.keys())
        for e in targets:
            eng = engs[e]
            for s in CENG:
                if cnt[s] == 0 or (s == e and (e == PE or not self.same_eng_sync)):
                    continue
                key = ("e", s)
                if waited.get((e, key), -1) >= cnt[s]:
                    continue
                waited[(e, key)] = cnt[s]
                eng.wait_ge(sems[s], cnt[s])
                self.n_wait += 1
            for st in dsems.values():
                key = ("d", id(st))
                if waited.get((e, key), -1) >= st[1]:
                    continue
                waited[(e, key)] = st[1]
                eng.wait_ge(st[0], st[1])
                self.n_wait += 1
        for k, st in dsems.items():
            self.dsem_free[k[1]].append(st)
        self.ops = []
        for b in self.bufs:
            b.last_w = None
            b.readers = []
        self.bufs = [b for b in self.bufs if b.dram]
        if self.phase_stack is not None:
            self.phase_stack.close()
            self.phase_stack = None


class Tl:
    def __init__(self, P, shape, dt, name=None, psum=False):
        self.t = P.ps(shape, dt, name) if psum else P.sb(shape, dt, name)
        self.b = P.buf(name)


class Ring:
    def __init__(self, P, n, shape, dt, name=None, psum=False):
        self.tiles = [Tl(P, shape, dt, name, psum) for _ in range(n)]
        self.i = 0

    def next(self):
        t = self.tiles[self.i % len(self.tiles)]
        self.i += 1
        return t


class Cfg:
    def __init__(self, NB=2, LAT=4096, CTX=256, kinds=(0, 1, 2, 0), debug=False, phases=None):
        self.NB, self.LAT, self.CTX = NB, LAT, CTX
        self.kinds = tuple(kinds)
        self.debug = debug
        self.phases = phases
        self.TOK = LAT + CTX
        self.CT = CTX // 128
        self.LT = LAT // 128
        self.NT = self.CT + self.LT


class Builder:
    def __init__(self, cfg):
        self.cfg = cfg
        self.P = Prog()
        self.nc = self.P.nc
        self.dram_bufs = {}

    def dma(self, q, out, in_, R, W, **kw):
        self.P.op(q, lambda e: e.dma_start(out=out, in_=in_, **kw), R, W, dma=True)

    def mm(self, out, lhsT, rhs, start, stop, R, W, skip=False):
        self.P.op(PE, lambda e: e.matmul(out, lhsT=lhsT, rhs=rhs, start=start, stop=stop, skip_group_check=skip), R, W)

    def tr(self, out, in_, R, W):
        ident = self.ident.t[:]
        self.P.op(PE, lambda e: e.transpose(out=out, in_=in_, identity=ident), list(R) + [self.ident.b], W)

    def act(self, out, in_, func, R, W, eng=ACT, **kw):
        self.P.op(eng, lambda e: e.activation(out=out, in_=in_, func=func, **kw), R, W)

    def tt(self, eng, out, in0, in1, op, R, W):
        self.P.op(eng, lambda e: e.tensor_tensor(out=out, in0=in0, in1=in1, op=op), R, W)

    def ts(self, eng, out, in0, s1, s2, op0, op1, R, W):
        if op1 is None:
            self.P.op(eng, lambda e: e.tensor_scalar(out=out, in0=in0, scalar1=s1, scalar2=None, op0=op0), R, W)
        else:
            self.P.op(eng, lambda e: e.tensor_scalar(out=out, in0=in0, scalar1=s1, scalar2=s2, op0=op0, op1=op1), R, W)

    def stt(self, out, in0, scalar, in1, op0, op1, R, W):
        self.P.op(DVE, lambda e: e.scalar_tensor_tensor(out=out, in0=in0, scalar=scalar, in1=in1, op0=op0, op1=op1), R, W)

    def cp(self, eng, out, in_, R, W):
        if eng == ACT:
            self.act(out, in_, AF.Copy, R, W)
        else:
            self.P.op(eng, lambda e: e.tensor_copy(out=out, in_=in_), R, W)

    def memset(self, eng, ap, val, W):
        self.P.op(eng, lambda e: e.memset(ap, val), (), W)

    def dbuf(self, key):
        if key not in self.dram_bufs:
            self.dram_bufs[key] = self.P.buf(str(key), dram=True)
        return self.dram_bufs[key]

    def rstd(self, out, in_, scale, eps, R, W, n=1):
        nh = self.neghalf.t[:, 0:n]
        self.ts(POOL, out, in_, scale, eps, ALU.mult, ALU.add, R, W)
        self.tt(POOL, out, out, nh, ALU.pow, list(W) + [self.neghalf.b], W)

    def declare(self):
        nc, c = self.nc, self.cfg
        L = len(c.kinds)
        self.L = L
        di = lambda n, s, dt=F32: nc.dram_tensor(n, list(s), dt, kind="ExternalInput").ap()
        dx = lambda n, s, dt=F32: nc.dram_tensor(n, list(s), dt, kind="Internal").ap()
        self.x_in = di("x", [c.NB, c.LAT, D])
        self.ctx_in = di("ctx", [c.NB, c.CTX, D])
        self.crow = di("crow", [c.NB + 1, D])
        self.ada_w = di("ada_w", [L, D, 6 * D])
        self.ada_b = di("ada_b", [L, 6 * D])
        self.norm_g = di("norm_g", [L, 4 * D])
        n_ret = sum(1 for k in c.kinds if k == 0)
        n_hg = sum(1 for k in c.kinds if k == 1)
        n_m2 = sum(1 for k in c.kinds if k == 2)
        self.ret_w_in = di("ret_w_in", [max(n_ret, 1), D, 6144])
        self.ret_w_out = di("ret_w_out", [max(n_ret, 1), 2048, D])
        self.ret_decay = di("ret_decay", [max(n_ret, 1), 8])
        self.ret_gn = di("ret_gn", [max(n_ret, 1), 2048])
        self.hg_w_in = di("hg_w_in", [max(n_hg, 1), D, 5120])
        self.hg_w_out = di("hg_w_out", [max(n_hg, 1), D, D])
        self.hg_lb = di("hg_lb", [L, D])
        self.hg_gn = di("hg_gn", [max(n_hg, 1), D])
        self.m2_w_in = di("m2_w_in", [max(n_m2, 1), D, 5184])
        self.m2_w_out = di("m2_w_out", [max(n_m2, 1), 2048, D])
        self.m2_conv_w = di("m2_conv_w", [max(n_m2, 1), 3, 3072])
        self.m2_conv_b = di("m2_conv_b", [max(n_m2, 1), 3072])
        self.m2_dt_bias = di("m2_dt_bias", [max(n_m2, 1), 64])
        self.m2_a_log = di("m2_a_log", [max(n_m2, 1), 64])
        self.m2_d = di("m2_d", [max(n_m2, 1), 32])
        self.m2_gn = di("m2_gn", [max(n_m2, 1), 2048])
        self.ffn_w_up = di("ffn_w_up", [L, D, 5632])
        self.ffn_cw = di("ffn_cw", [L, 128, 22 * 3])
        self.ffn_cb = di("ffn_cb", [L, 128, 22])
        self.ffn_w_down = di("ffn_w_down", [L, 2816, D])
        self.c_ident = di("c_ident", [128, 128])
        self.c_rope = di("c_rope", [max(c.LT, 1), 128, 512])
        self.c_tab = di("c_tab", [128, 24 * 128])
        self.out = nc.dram_tensor("out", [c.NB, c.LAT, D], F32, kind="ExternalOutput").ap()
        if c.debug:
            self.dbg = nc.dram_tensor("dbg", [L, c.NB, c.TOK, D], F32, kind="ExternalOutput").ap()
        self.H = dx("H", [c.NB, c.TOK, D])
        self.MOD = dx("MOD", [L, c.NB + 1, 6, D])
        self.sQT = dx("sQT", [2, c.NB, c.NT, 128, 1024], BF16)
        self.sKT = dx("sKT", [2, c.NB, c.NT, 128, 1024], BF16)
        self.sKt = dx("sKt", [2, c.NB, c.TOK, 1024], BF16)
        self.sV = dx("sV", [c.NB, c.TOK, 2048], BF16)
        self.sG = dx("sG", [c.NB, c.TOK, 2048], BF16)
        self.sO = dx("sO", [2, c.NB, c.TOK, 2048])
        self.sCS = dx("sCS", [2, c.NB, c.NT, 128, 48])
        self.sX = dx("sX", [c.NB, c.TOK + 2 * c.NT + 8, 3072])
        self.sDT = dx("sDT", [c.NB, c.TOK, 128])

    def setup_consts(self):
        P, nc = self.P, self.nc
        self.const_stack = contextlib.ExitStack()
        P.phase_stack = self.const_stack
        self.ident = Tl(P, [128, 128], BF16, "ident")
        self.neghalf = Tl(P, [128, 8], F32, "neghalf")
        P.phase_stack = None
        P.begin_phase()
        self.dma(POOL, self.ident.t[:], self.c_ident[:, :], [], [self.ident.b])
        self.memset(POOL, self.neghalf.t[:], -0.5, [self.neghalf.b])
        P.end_phase()

    def load_ctab(self):
        ct = Tl(self.P, [128, 24, 128], F32, "ctab")
        self.dma(SP, ct.t[:], self.c_tab.rearrange("p (a b) -> p a b", a=24), [], [ct.b])
        return ct

    def h_src(self, li, b, ti):
        c = self.cfg
        if li == 0:
            if ti < c.CT:
                return self.ctx_in[b, ti * 128:(ti + 1) * 128, :], None
            return self.x_in[b, (ti - c.CT) * 128:(ti - c.CT + 1) * 128, :], None
        return self.H[b, ti * 128:(ti + 1) * 128, :], self.dbuf(("H", b, ti))

    def h_mid(self, b, ti):
        return self.H[b, ti * 128:(ti + 1) * 128, :], self.dbuf(("H", b, ti))

    def h_dst(self, li, b, ti):
        c = self.cfg
        if li == self.L - 1 and ti >= c.CT:
            return self.out[b, (ti - c.CT) * 128:(ti - c.CT + 1) * 128, :], self.dbuf(("out", b, ti))
        return self.H[b, ti * 128:(ti + 1) * 128, :], self.dbuf(("H", b, ti))

    def load_w(self, wt, src, kchunks, ncols):
        v = src.rearrange("(k p) n -> p k n", p=128)
        for k in range(kchunks):
            self.dma(POOL, wt.t[:, k, :], v[:, k, :], [], [wt.b])

    def load_tab(self, tl, li, row, vec):
        self.dma(SP, tl.t[:], self.MOD[li, row, vec, :].partition_broadcast(128), [self.dbuf("MOD")], [tl.b])

    def phase_mod(self):
        P, c = self.P, self.cfg
        R3 = c.NB + 1
        P.begin_phase()
        cT = Tl(P, [128, R3, 8], F32, "cT")
        cTb = Tl(P, [128, 8, R3], BF16, "cTb")
        self.dma(SP, cT.t[:], self.crow.rearrange("r (p k) -> p r k", k=8), [], [cT.b])
        self.act(cTb.t[:].rearrange("p k r -> p r k"), cT.t[:], AF.Silu, [cT.b], [cTb.b])
        wr = Ring(P, 2, [128, 8, 1536], BF16, "adaw")
        pr = Ring(P, 2, [128, 512], F32, "psm", psum=True)
        raw = Tl(P, [R3, 6 * D], F32, "raw")
        adab = Tl(P, [R3, 6 * D], F32, "adab")
        ng = Tl(P, [R3, 4 * D], F32, "ng")
        mv = Ring(P, 2, [R3, 6, D], F32, "mv")
        modb = self.dbuf("MOD")
        for li in range(self.L):
            self.dma(SP, adab.t[:], self.ada_b[li, :].partition_broadcast(R3), [], [adab.b])
            self.dma(SP, ng.t[:], self.norm_g[li, :].partition_broadcast(R3), [], [ng.b])
            wv = self.ada_w[li].rearrange("(p k) n -> p k n", k=8)
            for j in range(4):
                w = wr.next()
                self.dma(POOL, w.t[:], wv[:, :, j * 1536:(j + 1) * 1536], [], [w.b])
                for n in range(3):
                    ps = pr.next()
                    for k in range(8):
                        self.mm(ps.t[0:R3, :], cTb.t[:, k, :], w.t[:, k, n * 512:(n + 1) * 512], k == 0, k == 7,
                                [cTb.b, w.b], [ps.b])
                    c0 = j * 1536 + n * 512
                    self.tt(DVE, raw.t[:, c0:c0 + 512], ps.t[0:R3, :], adab.t[:, c0:c0 + 512], ALU.add,
                            [ps.b, adab.b], [raw.b])
            m = mv.next()
            r = raw.t
            self.cp(DVE, m.t[:, 0, :], r[:, 0:D], [raw.b], [m.b])
            self.stt(m.t[:, 1, :], r[:, D:2 * D], 1.0, ng.t[:, 0:D], ALU.add, ALU.mult, [raw.b, ng.b], [m.b])
            self.tt(DVE, m.t[:, 2, :], r[:, 2 * D:3 * D], ng.t[:, D:2 * D], ALU.mult, [raw.b, ng.b], [m.b])
            self.cp(DVE, m.t[:, 3, :], r[:, 3 * D:4 * D], [raw.b], [m.b])
            self.stt(m.t[:, 4, :], r[:, 4 * D:5 * D], 1.0, ng.t[:, 2 * D:3 * D], ALU.add, ALU.mult, [raw.b, ng.b], [m.b])
            self.tt(DVE, m.t[:, 5, :], r[:, 5 * D:6 * D], ng.t[:, 3 * D:4 * D], ALU.mult, [raw.b, ng.b], [m.b])
            self.dma(SP, self.MOD[li], m.t[:], [m.b], [modb])
        P.end_phase()

    def pre_norm(self, h, sc, sh, u, ss, junk):
        self.act(junk.t[:], h.t[:], AF.Square, [h.b], [junk.b, ss.b], accum_out=ss.t[:, 0:1])
        self.rstd(ss.t[:, 1:2], ss.t[:, 0:1], 1.0 / D, EPS, [ss.b], [ss.b])
        self.tt(POOL, h.t[:], h.t[:], sc.t[:], ALU.mult, [h.b, sc.b], [h.b])
        self.stt(u.t[:], h.t[:], ss.t[:, 1:2], sh.t[:], ALU.mult, ALU.add, [h.b, ss.b, sh.b], [u.b])

    def transpose_to(self, src, nblk, psT, dst_ap, dst_b, src_b, eng=ACT, src_off=0):
        for k in range(nblk):
            self.tr(psT.t[:, k * 128:(k + 1) * 128], src[:, src_off + k * 128: src_off + (k + 1) * 128], [src_b], [psT.b])
        self.cp(eng, dst_ap, psT.t[:, 0:nblk * 128], [psT.b], [dst_b])

    def post_residual(self, psY, hres, gtab, hn, ss, junk, dst, dst_b):
        self.act(junk.t[:], psY.t[:], AF.Square, [psY.b], [junk.b, ss.b], accum_out=ss.t[:, 0:1])
        self.rstd(ss.t[:, 1:2], ss.t[:, 0:1], 1.0 / D, EPS, [ss.b], [ss.b])
        self.stt(hn.t[:], psY.t[:], ss.t[:, 1:2], gtab.t[:], ALU.mult, ALU.mult, [psY.b, ss.b, gtab.b], [hn.b])
        self.tt(POOL, hn.t[:], hn.t[:], hres.t[:], ALU.add, [hn.b, hres.b], [hn.b])
        self.dma(SP, dst, hn.t[:], [hn.b], [dst_b] if dst_b is not None else [])

    def tile_list(self, li, with_ctx=True):
        c = self.cfg
        last = (li == self.L - 1)
        out = []
        for b in range(c.NB):
            for ti in range(c.NT):
                if ti < c.CT and (last and not with_ctx):
                    continue
                out.append((b, ti))
        return out

    def phase_ffn(self, li):
        P, c = self.P, self.cfg
        last = (li == self.L - 1)
        P.begin_phase()
        wup = Tl(P, [128, 8, 5632], BF16, "wup")
        wdn = Tl(P, [128, 22, 1024], BF16, "wdn")
        self.load_w(wup, self.ffn_w_up[li], 8, 5632)
        self.load_w(wdn, self.ffn_w_down[li], 22, 1024)
        cw = Tl(P, [128, 22, 3], F32, "cw")
        cb = Tl(P, [128, 22], F32, "cb")
        self.dma(SP, cw.t[:], self.ffn_cw[li].rearrange("p (a b) -> p a b", b=3), [], [cw.b])
        self.dma(SP, cb.t[:], self.ffn_cb[li], [], [cb.b])
        tabs = [Tl(P, [128, D], F32, f"tab{i}") for i in range(3)]
        hr = Ring(P, 4, [128, D], F32, "h")
        ur = Ring(P, 2, [128, D], BF16, "u")
        junk = Tl(P, [128, D], BF16, "junk")
        ssr = Ring(P, 6, [128, 2], F32, "ss")
        uTr = Ring(P, 2, [128, 8, 256], BF16, "uT")
        mT = Tl(P, [128, 22, 256], BF16, "mT")
        cbr = Ring(P, 3, [128, 256], F32, "cbuf")
        hrel = Ring(P, 2, [128, D], F32, "hrel")
        hnr = Ring(P, 2, [128, D], F32, "hn")
        psT = Tl(P, [128, 1024], BF16, "psT", psum=True)
        psA = Ring(P, 3, [128, 512], F32, "psA", psum=True)
        psV = Ring(P, 2, [128, 512], F32, "psV", psum=True)
        psY = Tl(P, [128, 1024], F32, "psY", psum=True)
        sts = []
        for b in range(c.NB):
            if not last:
                for s_ in range(c.CT // 2):
                    sts.append((c.NB, b, [2 * s_, 2 * s_ + 1], False))
        for b in range(c.NB):
            for s_ in range(c.LT // 2):
                sts.append((b, b, [c.CT + 2 * s_, c.CT + 2 * s_ + 1], True))
        st_a = {"row": None}
        st_m = {"row": None}

        def ld(st):
            row, b, tis, grid = st
            hs = []
            for ti in tis:
                h = hr.next()
                src, sb_ = self.h_mid(b, ti)
                self.dma(SP, h.t[:], src, [sb_], [h.b])
                hs.append(h)
            return {"h": hs}

        def pre_a(st, cx):
            row, b, tis, grid = st
            if st_a["row"] != row:
                st_a["row"] = row
                self.load_tab(tabs[0], li, row, 3)
                self.load_tab(tabs[1], li, row, 4)
            cx["u"] = []
            for h in cx["h"]:
                u = ur.next()
                self.pre_norm(h, tabs[1], tabs[0], u, ssr.next(), junk)
                cx["u"].append(u)

        def pre_b(st, cx):
            uT = uTr.next()
            for j, u in enumerate(cx["u"]):
                for k in range(8):
                    self.tr(psT.t[:, k * 128:(k + 1) * 128], u.t[:, k * 128:(k + 1) * 128], [u.b], [psT.b])
                self.cp(ACT, uT.t[:, :, j * 128:(j + 1) * 128], psT.t[:].rearrange("p (k t) -> p k t", k=8),
                        [psT.b], [uT.b])
            cx["uT"] = uT

        def main(st, cx, hook):
            row, b, tis, grid = st
            uT = cx["uT"]
            if st_m["row"] != row:
                st_m["row"] = row
                self.load_tab(tabs[2], li, row, 5)
            hres = []
            for ti in tis:
                hh = hrel.next()
                src, sb_ = self.h_mid(b, ti)
                self.dma(SP, hh.t[:], src, [sb_], [hh.b])
                hres.append(hh)
            nr = 4 if grid else 1
            w = 256 // nr

            def A(fc):
                pa = psA.next()
                for k in range(8):
                    self.mm(pa.t[:, 0:256], wup.t[:, k, fc * 128:(fc + 1) * 128], uT.t[:, k, :], k == 0, k == 7,
                            [wup.b, uT.b], [pa.b])
                cbuf = cbr.next()
                self.act(cbuf.t[:], pa.t[:, 0:256], AF.Identity, [pa.b, cw.b, cb.b], [cbuf.b],
                         scale=cw.t[:, fc, 1:2], bias=cb.t[:, fc:fc + 1])
                pv = pa.t[:, 0:256].rearrange("p (r w) -> p r w", r=nr)
                cv = cbuf.t[:].rearrange("p (r w) -> p r w", r=nr)
                self.stt(cv[:, :, 1:w], pv[:, :, 0:w - 1], cw.t[:, fc, 0:1], cv[:, :, 1:w], ALU.mult, ALU.add,
                         [pa.b, cw.b, cbuf.b], [cbuf.b])
                self.stt(cv[:, :, 0:w - 1], pv[:, :, 1:w], cw.t[:, fc, 2:3], cv[:, :, 0:w - 1], ALU.mult, ALU.add,
                         [pa.b, cw.b, cbuf.b], [cbuf.b])
                self.act(mT.t[:, fc, :], cbuf.t[:], AF.Gelu_apprx_tanh, [cbuf.b], [mT.b])

            def V(fc):
                pvv = psV.next()
                for k in range(8):
                    self.mm(pvv.t[:, 0:256], wup.t[:, k, 2816 + fc * 128:2816 + (fc + 1) * 128], uT.t[:, k, :],
                            k == 0, k == 7, [wup.b, uT.b], [pvv.b])
                self.tt(DVE, mT.t[:, fc, :], pvv.t[:, 0:256], mT.t[:, fc, :], ALU.mult, [pvv.b, mT.b], [mT.b])

            A(0)
            A(1)
            for fc in range(22):
                if fc + 2 < 22:
                    A(fc + 2)
                V(fc)
                if fc == 12:
                    hook()
            for j, ti in enumerate(tis):
                for n in range(2):
                    for fc in range(22):
                        self.mm(psY.t[:, n * 512:(n + 1) * 512], mT.t[:, fc, j * 128:(j + 1) * 128],
                                wdn.t[:, fc, n * 512:(n + 1) * 512], fc == 0, fc == 21, [mT.b, wdn.b], [psY.b])
                hn = hnr.next()
                dst, db_ = self.h_dst(li, b, ti)
                self.post_residual(psY, hres[j], tabs[2], hn, ssr.next(), junk, dst, db_)
                if c.debug:
                    self.dma(SP, self.dbg[li, b, ti * 128:(ti + 1) * 128, :], hn.t[:], [hn.b], [])

        self.run_pipe(sts, ld, pre_a, pre_b, main)
        P.end_phase()

    def phase_mixer(self, li):
        kind = self.cfg.kinds[li]
        j = sum(1 for k in self.cfg.kinds[:li] if k == kind)
        if kind == 0:
            self.ret_proj(li, j)
            self.ret_scan(li, j, 0)
            self.ret_scan(li, j, 1)
            self.ret_readout(li, j)
        elif kind == 1:
            self.hg_proj(li, j)
            self.hg_scan(li, j, 0)
            self.hg_scan(li, j, 1)
            self.hg_readout(li, j)
        else:
            self.m2_proj(li, j)
            self.m2_conv(li, j)
            self.m2_scan(li, j, 0)
            self.m2_scan(li, j, 1)
            self.m2_readout(li, j)

    def proj_tiles(self):
        c = self.cfg
        out = [(c.NB, b, ti) for b in range(c.NB) for ti in range(c.CT)]
        out += [(b, b, ti) for b in range(c.NB) for ti in range(c.CT, c.NT)]
        return out

    def run_pipelined(self, items, pre, main, rowof=lambda it: it[0]):
        cur = pre(items[0])
        for i, it in enumerate(items):
            nxt = None
            if i + 1 < len(items) and rowof(items[i + 1]) == rowof(it):
                nxt = pre(items[i + 1])
            main(it, cur)
            if nxt is None and i + 1 < len(items):
                nxt = pre(items[i + 1])
            cur = nxt

    def run_pipe(self, items, ld, pre_a, pre_b, main):
        n = len(items)
        ctxs = {}

        def do_ld(k):
            if k < n:
                ctxs[k] = ld(items[k])

        def do_a(k):
            if k < n:
                pre_a(items[k], ctxs[k])

        def do_b(k):
            if k < n:
                pre_b(items[k], ctxs[k])

        do_ld(0)
        do_ld(1)
        do_a(0)
        do_b(0)
        for i in range(n):
            do_ld(i + 2)
            do_a(i + 1)
            main(items[i], ctxs[i], lambda k=i + 1: do_b(k))
            ctxs.pop(i)

    def make_stages(self, li, tabs, hr, ur, ssr, uTr, junk, psTr, vecs=(0, 1)):
        state = {"row": None}

        def ld(it):
            row, b, ti = it
            h = hr.next()
            src, sb_ = self.h_src(li, b, ti)
            self.dma(SP, h.t[:], src, [sb_] if sb_ is not None else [], [h.b])
            return {"h": h}

        def pre_a(it, cx):
            row, b, ti = it
            if state["row"] != row:
                state["row"] = row
                for i, v in enumerate(vecs):
                    self.load_tab(tabs[i], li, row, v)
            u = ur.next()
            self.pre_norm(cx["h"], tabs[1], tabs[0], u, ssr.next(), junk)
            cx["u"] = u

        def pre_b(it, cx):
            uT = uTr.next()
            psT = psTr.next()
            self.transpose_to(cx["u"].t, 8, psT, uT.t[:].rearrange("p k t -> p (k t)"), uT.b, cx["u"].b)
            cx["uT"] = uT

        return ld, pre_a, pre_b

    def load_wout_scaled(self, wo, src, kchunks, gn_src):
        P = self.P
        self.load_w(wo, src, kchunks, 1024)
        gnT = Tl(P, [128, kchunks], F32, "gnT")
        self.dma(SP, gnT.t[:], gn_src.rearrange("(k p) -> p k", p=128), [], [gnT.b], allow_slow_non_contiguous=True)
        for k in range(kchunks):
            self.act(wo.t[:, k, :], wo.t[:, k, :], AF.Copy, [wo.b, gnT.b], [wo.b], scale=gnT.t[:, k:k + 1])

    def ret_proj(self, li, j):
        P, c = self.P, self.cfg
        P.begin_phase()
        w = Tl(P, [128, 8, 6144], BF16, "w_in")
        self.load_w(w, self.ret_w_in[j], 8, 6144)
        tabs = [Tl(P, [128, D], F32, f"tab{i}") for i in range(2)]
        hr = Ring(P, 3, [128, D], F32, "h")
        ur = Ring(P, 2, [128, D], BF16, "u")
        junk = Tl(P, [128, D], BF16, "junk")
        ssr = Ring(P, 4, [128, 2], F32, "ss")
        uTr = Ring(P, 2, [128, 8, 128], BF16, "uT")
        qk32r = Ring(P, 2, [128, 2048], F32, "qk32")
        ropeB = Tl(P, [128, 2048], F32, "ropeB")
        qkrr = Ring(P, 2, [128, 2048], BF16, "qkr")
        qkTr = Ring(P, 2, [128, 2048], BF16, "qkT")
        vr = Ring(P, 2, [128, 2048], BF16, "vbf")
        gr = Ring(P, 2, [128, 2048], BF16, "gbf")
        rr = Ring(P, 2, [128, 512], F32, "rope")
        psTr = Ring(P, 2, [128, 1024], BF16, "psT", psum=True)
        psM = Ring(P, 6, [128, 512], F32, "psM", psum=True)
        ld, pre_a, pre_b = self.make_stages(li, tabs, hr, ur, ssr, uTr, junk, psTr)

        def main(it, cx, hook):
            uT = cx["uT"]
            row, b, ti = it
            tok = slice(ti * 128, (ti + 1) * 128)
            qk32, qkr, qkT, vb, gb = qk32r.next(), qkrr.next(), qkTr.next(), vr.next(), gr.next()
            lat = ti >= c.CT
            if lat:
                rt = rr.next()
                self.dma(SP, rt.t[:], self.c_rope[ti - c.CT], [], [rt.b])
            for n in range(12):
                ps = psM.next()
                for k in range(8):
                    self.mm(ps.t[:], uT.t[:, k, :], w.t[:, k, n * 512:(n + 1) * 512], k == 0, k == 7, [uT.b, w.b], [ps.b])
                if n < 4:
                    self.act(qk32.t[:, n * 512:(n + 1) * 512], ps.t[:], AF.Copy, [ps.b], [qk32.b],
                             scale=(0.0625 if n >= 2 else 1.0))
                elif n < 8:
                    self.act(vb.t[:, (n - 4) * 512:(n - 3) * 512], ps.t[:], AF.Copy, [ps.b], [vb.b])
                else:
                    self.act(gb.t[:, (n - 8) * 512:(n - 7) * 512], ps.t[:], AF.Silu, [ps.b], [gb.b])
                if n == 3:
                    if lat:
                        v5 = qk32.t[:].rearrange("p (s j h e) -> p s j h e", s=8, j=2, h=2, e=64)
                        b5 = ropeB.t[:].rearrange("p (s j h e) -> p s j h e", s=8, j=2, h=2, e=64)
                        sv = rt.t[:, 256:512].rearrange("p (j h e) -> p j h e", j=2, h=2, e=64)
                        for hh in range(2):
                            self.tt(POOL, b5[:, :, :, hh, :], v5[:, :, :, 1 - hh, :],
                                    sv[:, :, hh, :].unsqueeze(1).broadcast_to([128, 8, 2, 64]), ALU.mult,
                                    [qk32.b, rt.b], [ropeB.b])
                        q3 = qk32.t[:].rearrange("p (s f) -> p s f", s=8)
                        self.tt(DVE, q3, q3, rt.t[:, 0:256].unsqueeze(1).broadcast_to([128, 8, 256]), ALU.mult,
                                [qk32.b, rt.b, ropeB.b], [qk32.b])
                        self.tt(DVE, qkr.t[:], qk32.t[:], ropeB.t[:], ALU.add, [qk32.b, ropeB.b], [qkr.b])
                    else:
                        self.cp(DVE, qkr.t[:], qk32.t[:], [qk32.b], [qkr.b])
                    self.dma(SP, self.sKt[0, b, tok, :], qkr.t[:, 1024:2048], [qkr.b], [self.dbuf(("Kt", b, ti))])
                    for half in range(2):
                        psT = psTr.next()
                        self.transpose_to(qkr.t, 8, psT, qkT.t[:, half * 1024:(half + 1) * 1024], qkT.b, qkr.b,
                                          eng=DVE, src_off=half * 1024)
                    self.dma(SP, self.sQT[0, b, ti], qkT.t[:, 0:1024], [qkT.b], [self.dbuf(("QT", b, ti))])
                    self.dma(SP, self.sKT[0, b, ti], qkT.t[:, 1024:2048], [qkT.b], [self.dbuf(("KT", b, ti))])
                if n == 7:
                    self.dma(SP, self.sV[b, tok, :], vb.t[:], [vb.b], [self.dbuf(("V", b, ti))])
                    hook()
                if n == 11:
                    self.dma(SP, self.sG[b, tok, :], gb.t[:], [gb.b], [self.dbuf(("G", b, ti))])

        self.run_pipe(self.proj_tiles(), ld, pre_a, pre_b, main)
        P.end_phase()

    def scan_order(self, d):
        c = self.cfg
        if d == 0:
            return list(range(c.NT))
        return list(range(c.CT - 1, -1, -1)) + list(range(c.NT - 1, c.CT - 1, -1))

    def ret_scan(self, li, j, d):
        P, c = self.P, self.cfg
        last = (li == self.L - 1)
        P.begin_phase()
        ct = self.load_ctab()
        dec = Tl(P, [128, 8], F32, "dec")
        lg = Tl(P, [128, 8], F32, "lg")
        nlg = Tl(P, [128, 8], F32, "nlg")
        self.dma(SP, dec.t[:], self.ret_decay[j, :].partition_broadcast(128), [], [dec.b])
        self.act(nlg.t[:], dec.t[:], AF.Exp, [dec.b], [nlg.b], scale=-1.0)
        self.act(nlg.t[:], nlg.t[:], AF.Ln, [nlg.b], [nlg.b], bias=1.0)
        self.ts(DVE, lg.t[:], nlg.t[:], -1.0, None, ALU.mult, None, [nlg.b], [lg.b])
        Dm = Tl(P, [128, 4, 128], F32, "Dm")
        E = Tl(P, [128, 4, 128], F32, "E")
        wc = Tl(P, [128, 4], F32, "wc")
        gC = Tl(P, [128, 4], F32, "gC")
        for h in range(4):
            col = slice(d * 4 + h, d * 4 + h + 1)
            sc = lg.t[:, col] if d == 0 else nlg.t[:, col]
            self.act(Dm.t[:, h, :], ct.t[:, 0, :], AF.Exp, [ct.b, lg.b, nlg.b], [Dm.b], scale=sc)
            self.tt(DVE, Dm.t[:, h, :], Dm.t[:, h, :], ct.t[:, 1 + d, :], ALU.mult, [Dm.b, ct.b], [Dm.b])
            self.act(E.t[:, h, :], ct.t[:, 3 + d, :], AF.Exp, [ct.b, lg.b], [E.b], scale=lg.t[:, col])
            self.act(wc.t[:, h:h + 1], ct.t[:, 5, d:d + 1], AF.Exp, [ct.b, lg.b], [wc.b], scale=lg.t[:, col])
            self.act(gC.t[:, h:h + 1], ct.t[:, 5, 2:3], AF.Exp, [ct.b, lg.b], [gC.b], scale=lg.t[:, col])
        S32 = {}
        Sbf = {}
        for b in range(c.NB):
            for h in range(4):
                for cc in range(2):
                    S32[b, h, cc] = Tl(P, [128, 512], F32, "S32")
                    Sbf[b, h, cc] = Tl(P, [128, 512], BF16, "Sbf")
                    self.memset(POOL, S32[b, h, cc].t[:], 0.0, [S32[b, h, cc].b])
                    self.memset(DVE, Sbf[b, h, cc].t[:], 0.0, [Sbf[b, h, cc].b])
        qTr = Ring(P, 3, [128, 8, 128], BF16, "qT")
        kTr = Ring(P, 3, [128, 8, 128], BF16, "kT")
        ktr = Ring(P, 3, [128, 4, 256], BF16, "kt")
        vr = Ring(P, 3, [128, 2048], BF16, "v")
        otr = Ring(P, 2, [128, 2048], F32, "ot")
        oflr = Ring(P, 3, [128, 2048], F32, "ofl") if d == 1 else None
        PTr = Ring(P, 2, [128, 4, 128], BF16, "PT")
        qsr = Ring(P, 2, [128, 8, 128], BF16, "qs")
        kwr = Ring(P, 2, [128, 4, 256], BF16, "kw")
        psS = Tl(P, [128, 512], F32, "psS", psum=True)
        psO = Tl(P, [128, 2048], F32, "psO", psum=True)
        psK = Ring(P, 3, [128, 512], F32, "psK", psum=True)
        steps = [(ti, b) for ti in self.scan_order(d) for b in range(c.NB)]

        def ld(st):
            ti, b = st
            tok = slice(ti * 128, (ti + 1) * 128)
            qT, kT, kt, v = qTr.next(), kTr.next(), ktr.next(), vr.next()
            need_o = not (last and ti < c.CT)
            if need_o:
                self.dma(SP, qT.t[:].rearrange("p k t -> p (k t)"), self.sQT[0, b, ti], [self.dbuf(("QT", b, ti))], [qT.b])
                self.dma(SP, kT.t[:].rearrange("p k t -> p (k t)"), self.sKT[0, b, ti], [self.dbuf(("KT", b, ti))], [kT.b])
            self.dma(SP, kt.t[:].rearrange("p h f -> p (h f)"), self.sKt[0, b, tok, :], [self.dbuf(("Kt", b, ti))], [kt.b])
            self.dma(SP, v.t[:], self.sV[b, tok, :], [self.dbuf(("V", b, ti))], [v.b])
            ofl = None
            if d == 1 and need_o:
                ofl = oflr.next()
                self.dma(SP, ofl.t[:], self.sO[0, b, tok, :], [self.dbuf(("O", 0, b, ti))], [ofl.b])
            return qT, kT, kt, v, need_o, ofl

        def comp(st, tl):
            ti, b = st
            tok = slice(ti * 128, (ti + 1) * 128)
            qT, kT, kt, v, need_o, ofl = tl
            if need_o:
                for h in range(4):
                    for cc in range(2):
                        self.mm(psS.t[:, h * 128:(h + 1) * 128], kT.t[:, h * 2 + cc, :], qT.t[:, h * 2 + cc, :], cc == 0, cc == 1,
                                [kT.b, qT.b], [psS.b])
                PT = PTr.next()
                self.tt(DVE, PT.t[:], psS.t[:].rearrange("p (h t) -> p h t", h=4), Dm.t[:], ALU.mult, [psS.b, Dm.b], [PT.b])
                qs = qsr.next()
                self.tt(POOL, qs.t[:].rearrange("p (h c) t -> p h c t", h=4), qT.t[:].rearrange("p (h c) t -> p h c t", h=4),
                        E.t[:].unsqueeze(2).broadcast_to([128, 4, 2, 128]), ALU.mult, [qT.b, E.b], [qs.b])
            kw = kwr.next()
            self.tt(POOL, kw.t[:], kt.t[:], wc.t[:, 0:4].unsqueeze(2).broadcast_to([128, 4, 256]), ALU.mult, [kt.b, wc.b], [kw.b])
            if need_o:
                for h in range(4):
                    vh = v.t[:, h * 512:(h + 1) * 512]
                    po = psO.t[:, h * 512:(h + 1) * 512]
                    self.mm(po, PT.t[:, h, :], vh, True, False, [PT.b, v.b], [psO.b])
                    for cc in range(2):
                        self.mm(po, qs.t[:, h * 2 + cc, :], Sbf[b, h, cc].t[:], False, cc == 1,
                                [qs.b, Sbf[b, h, cc].b], [psO.b])
                ot = otr.next()
                for h in range(4):
                    self.act(ot.t[:, h * 512:(h + 1) * 512], psO.t[:, h * 512:(h + 1) * 512], AF.Copy, [psO.b], [ot.b])
                if d == 1:
                    self.tt(POOL, ot.t[:], ot.t[:], ofl.t[:], ALU.add, [ot.b, ofl.b], [ot.b])
                self.dma(SP, self.sO[d, b, tok, :], ot.t[:], [ot.b], [self.dbuf(("O", d, b, ti))])
            for h in range(4):
                vh = v.t[:, h * 512:(h + 1) * 512]
                for cc in range(2):
                    pk = psK.next()
                    self.mm(pk.t[:], kw.t[:, h, cc * 128:(cc + 1) * 128], vh, True, True, [kw.b, v.b], [pk.b])
                    s32 = S32[b, h, cc]
                    self.stt(s32.t[:], s32.t[:], gC.t[:, h:h + 1], pk.t[:], ALU.mult, ALU.add, [s32.b, gC.b, pk.b], [s32.b])
                    self.cp(ACT, Sbf[b, h, cc].t[:], s32.t[:], [s32.b], [Sbf[b, h, cc].b])

        self.run_pipelined(steps, ld, comp, rowof=lambda it: 0)
        P.end_phase()

    def readout_main(self, li, wo, nk, g1, psYr, hnr, ssr, junk):
        state = {"row": None}

        def main(it, cx, hook):
            row, b, ti = it
            onT = cx["onT"]
            if state["row"] != row:
                state["row"] = row
                self.load_tab(g1, li, row, 2)
            psY = psYr.next()
            for n in range(2):
                for k in range(nk):
                    self.mm(psY.t[:, n * 512:(n + 1) * 512], onT.t[:, k, :], wo.t[:, k, n * 512:(n + 1) * 512],
                            k == 0, k == nk - 1, [onT.b, wo.b], [psY.b])
                if n == 0:
                    hook()
            hn = hnr.next()
            dst, db_ = self.h_mid(b, ti)
            self.post_residual(psY, cx["h"], g1, hn, ssr.next(), junk, dst, db_)
        return main

    def readout_items(self, li):
        keep = set(self.tile_list(li, with_ctx=False))
        return [it for it in self.proj_tiles() if (it[1], it[2]) in keep]

    def ret_readout(self, li, j):
        P, c = self.P, self.cfg
        P.begin_phase()
        wo = Tl(P, [128, 16, 1024], BF16, "w_out")
        self.load_wout_scaled(wo, self.ret_w_out[j], 16, self.ret_gn[j])
        g1 = Tl(P, [128, D], F32, "g1")
        ofr = Ring(P, 3, [128, 2048], F32, "of")
        gr = Ring(P, 3, [128, 2048], BF16, "g")
        hr = Ring(P, 3, [128, D], F32, "h")
        onr = Ring(P, 2, [128, 2048], BF16, "on")
        onTr = Ring(P, 2, [128, 16, 128], BF16, "onT")
        str_ = Ring(P, 2, [128, 4, 6], F32, "bst")
        mvr = Ring(P, 2, [128, 4, 2], F32, "mv")
        rsr = Ring(P, 2, [128, 4], F32, "rs")
        hnr = Ring(P, 2, [128, D], F32, "hn")
        ssr = Ring(P, 4, [128, 2], F32, "ss")
        junk = Tl(P, [128, D], BF16, "junk")
        psTr = Ring(P, 2, [128, 1024], BF16, "psT", psum=True)
        psYr = Ring(P, 2, [128, 1024], F32, "psY", psum=True)

        def ld(it):
            row, b, ti = it
            tok = slice(ti * 128, (ti + 1) * 128)
            of, g, h = ofr.next(), gr.next(), hr.next()
            self.dma(SP, of.t[:], self.sO[1, b, tok, :], [self.dbuf(("O", 1, b, ti))], [of.b])
            self.dma(SP, g.t[:], self.sG[b, tok, :], [self.dbuf(("G", b, ti))], [g.b])
            src, sb_ = self.h_src(li, b, ti)
            self.dma(SP, h.t[:], src, [sb_] if sb_ is not None else [], [h.b])
            return {"of": of, "g": g, "h": h}

        def pre_a(it, cx):
            of, g = cx["of"], cx["g"]
            on, bst, mv, rs = onr.next(), str_.next(), mvr.next(), rsr.next()
            for h in range(4):
                oh = of.t[:, h * 512:(h + 1) * 512]
                self.P.op(DVE, lambda e, o=bst.t[:, h, :], i=oh: e.bn_stats(out=o, in_=i), [of.b], [bst.b])
            for h in range(4):
                self.P.op(DVE, lambda e, o=mv.t[:, h, :], i=bst.t[:, h, :]: e.bn_aggr(out=o, in_=i), [bst.b], [mv.b])
            self.rstd(rs.t[:], mv.t[:, :, 1], 1.0, EPS, [mv.b], [rs.b], n=4)
            for h in range(4):
                oh = of.t[:, h * 512:(h + 1) * 512]
                self.stt(oh, oh, mv.t[:, h, 0:1], g.t[:, h * 512:(h + 1) * 512], ALU.subtract, ALU.mult,
                         [of.b, mv.b, g.b], [of.b])
            for h in range(4):
                oh = of.t[:, h * 512:(h + 1) * 512]
                self.act(on.t[:, h * 512:(h + 1) * 512], oh, AF.Copy, [of.b, rs.b], [on.b], scale=rs.t[:, h:h + 1])
            cx["on"] = on

        def pre_b(it, cx):
            on, onT = cx["on"], onTr.next()
            for half in range(2):
                psT = psTr.next()
                self.transpose_to(on.t, 8, psT, onT.t[:, half * 8:(half + 1) * 8, :].rearrange("p k t -> p (k t)"),
                                  onT.b, on.b, eng=(ACT if half == 0 else DVE), src_off=half * 1024)
            cx["onT"] = onT

        self.run_pipe(self.readout_items(li), ld, pre_a, pre_b, self.readout_main(li, wo, 16, g1, psYr, hnr, ssr, junk))
        P.end_phase()

    def hg_proj(self, li, j):
        P, c = self.P, self.cfg
        L = self.L
        P.begin_phase()
        ct = self.load_ctab()
        w = Tl(P, [128, 8, 5120], BF16, "w_in")
        self.load_w(w, self.hg_w_in[j], 8, 5120)
        lbx = Tl(P, [128, L, D], F32, "lbx")
        lb = Tl(P, [128, D], F32, "lb")
        omlb = Tl(P, [128, D], F32, "omlb")
        den = Tl(P, [128, D], F32, "den")
        self.dma(SP, lbx.t[:].rearrange("p l d -> p (l d)"), self.hg_lb.rearrange("l d -> (l d)").partition_broadcast(128), [], [lbx.b])
        self.act(lbx.t[:], lbx.t[:], AF.Exp, [lbx.b], [lbx.b])
        self.cp(DVE, den.t[:], lbx.t[:, 0, :], [lbx.b], [den.b])
        self.memset(DVE, lb.t[:], 0.0, [lb.b])
        for r in range(1, L):
            self.tt(DVE, den.t[:], den.t[:], lbx.t[:, r, :], ALU.add, [den.b, lbx.b], [den.b])
            if r <= li:
                self.tt(DVE, lb.t[:], lb.t[:], lbx.t[:, r, :], ALU.add, [lb.b, lbx.b], [lb.b])
        self.P.op(DVE, lambda e: e.reciprocal(out=den.t[:], in_=den.t[:]), [den.b], [den.b])
        self.tt(DVE, lb.t[:], lb.t[:], den.t[:], ALU.mult, [lb.b, den.b], [lb.b])
        self.ts(DVE, omlb.t[:], lb.t[:], -1.0, 1.0, ALU.mult, ALU.add, [lb.b], [omlb.b])
        tabs = [Tl(P, [128, D], F32, f"tab{i}") for i in range(2)]
        hr = Ring(P, 3, [128, D], F32, "h")
        ur = Ring(P, 2, [128, D], BF16, "u")
        junk = Tl(P, [128, D], BF16, "junk")
        ssr = Ring(P, 4, [128, 2], F32, "ss")
        uTr = Ring(P, 2, [128, 8, 128], BF16, "uT")
        qs = Tl(P, [128, D], F32, "qs")
        a32 = [Tl(P, [128, D], F32, f"a32{d}") for d in range(2)]
        la32 = [Tl(P, [128, D], F32, f"la{d}") for d in range(2)]
        k32 = a32
        e32r = Ring(P, 2, [128, D], F32, "e32")
        qtr = Ring(P, 2, [128, D], BF16, "qt")
        ktr = Ring(P, 2, [128, D], BF16, "kt")
        stg = Ring(P, 4, [128, D], BF16, "stg")
        vr = Ring(P, 2, [128, D], BF16, "vbf")
        gr = Ring(P, 2, [128, D], BF16, "gbf")
        csr = Ring(P, 2, [128, 48], F32, "cs")
        psTr = Ring(P, 2, [128, 1024], BF16, "psT", psum=True)
        psM = Ring(P, 5, [128, 512], F32, "psM", psum=True)
        psC = Tl(P, [128, 512], F32, "psC", psum=True)
        ld, pre_a, pre_b = self.make_stages(li, tabs, hr, ur, ssr, uTr, junk, psTr)

        def main(it, cx, hook):
            uT = cx["uT"]
            row, b, ti = it
            tok = slice(ti * 128, (ti + 1) * 128)
            vb, gb = vr.next(), gr.next()
            for n in range(10):
                if n == 7:
                    hook()
                ps = psM.next()
                for k in range(8):
                    self.mm(ps.t[:], uT.t[:, k, :], w.t[:, k, n * 512:(n + 1) * 512], k == 0, k == 7, [uT.b, w.b], [ps.b])
                cs_ = slice((n % 2) * 512, (n % 2) * 512 + 512)
                if n < 2:
                    self.act(qs.t[:, cs_], ps.t[:], AF.Silu, [ps.b], [qs.b])
                elif n < 6:
                    d = (n - 2) // 2
                    self.act(a32[d].t[:, cs_], ps.t[:], AF.Sigmoid, [ps.b], [a32[d].b])
                    self.tt(DVE, a32[d].t[:, cs_], a32[d].t[:, cs_], omlb.t[:, cs_], ALU.mult, [a32[d].b, omlb.b], [a32[d].b])
                    self.tt(POOL, a32[d].t[:, cs_], a32[d].t[:, cs_], lb.t[:, cs_], ALU.add, [a32[d].b, lb.b], [a32[d].b])
                    self.act(la32[d].t[:, cs_], a32[d].t[:, cs_], AF.Ln, [a32[d].b], [la32[d].b])
                    self.ts(POOL, a32[d].t[:, cs_], a32[d].t[:, cs_], -1.0, 1.0, ALU.mult, ALU.add, [a32[d].b], [a32[d].b])
                elif n < 8:
                    self.act(vb.t[:, cs_], ps.t[:], AF.Copy, [ps.b], [vb.b])
                else:
                    self.act(gb.t[:, cs_], ps.t[:], AF.Silu, [ps.b], [gb.b])
            self.dma(SP, self.sV[b, tok, 0:D], vb.t[:], [vb.b], [self.dbuf(("V", b, ti))])
            self.dma(SP, self.sG[b, tok, 0:D], gb.t[:], [gb.b], [self.dbuf(("G", b, ti))])
            for d in range(2):
                qt, kt = qtr.next(), ktr.next()
                for n in range(2):
                    cs_ = slice(n * 512, n * 512 + 512)
                    ps = psM.next()
                    self.mm(ps.t[:], ct.t[:, 8 + d, :], la32[d].t[:, cs_], True, True, [ct.b, la32[d].b], [ps.b])
                    e1, e2 = e32r.next(), e32r.next()
                    self.act(e1.t[:, 0:512], ps.t[:], AF.Exp, [ps.b], [e1.b])
                    self.act(e2.t[:, 0:512], ps.t[:], AF.Exp, [ps.b], [e2.b], scale=-1.0)
                    self.tt(DVE, qt.t[:, cs_], qs.t[:, cs_], e1.t[:, 0:512], ALU.mult, [qs.b, e1.b], [qt.b])
                    self.tt(POOL, kt.t[:, cs_], k32[d].t[:, cs_], e2.t[:, 0:512], ALU.mult, [k32[d].b, e2.b], [kt.b])
                for h in range(8):
                    self.mm(psC.t[:, h * 6:(h + 1) * 6], la32[d].t[:, h * 128:(h + 1) * 128], ct.t[:, 10 + d, 0:6],
                            True, True, [la32[d].b, ct.b], [psC.b])
                cs = csr.next()
                self.act(cs.t[:], psC.t[:, 0:48], AF.Exp, [psC.b], [cs.b])
                self.dma(SP, self.sCS[d, b, ti], cs.t[:], [cs.b], [self.dbuf(("CS", d, b, ti))])
                self.dma(SP, self.sKt[d, b, tok, :], kt.t[:], [kt.b], [self.dbuf(("Kt", d, b, ti))])
                for which, src in ((0, qt), (1, kt)):
                    psT = psTr.next()
                    st = stg.next()
                    self.transpose_to(src.t, 8, psT, st.t[:], st.b, src.b, eng=(DVE if which == 0 else ACT))
                    dst = (self.sQT if which == 0 else self.sKT)[d, b, ti]
                    self.dma(SP, dst, st.t[:], [st.b], [self.dbuf(("QT" if which == 0 else "KT", d, b, ti))])

        self.run_pipe(self.proj_tiles(), ld, pre_a, pre_b, main)
        P.end_phase()

    def hg_scan(self, li, j, d):
        P, c = self.P, self.cfg
        last = (li == self.L - 1)
        P.begin_phase()
        ct = self.load_ctab()
        S = {}
        for b in range(c.NB):
            for hh in range(2):
                S[b, hh] = Tl(P, [128, 4, 128], F32, "S")
                self.memset(POOL, S[b, hh].t[:], 0.0, [S[b, hh].b])
        qTr = Ring(P, 3, [128, 8, 128], BF16, "qT")
        kTr = Ring(P, 3, [128, 8, 128], BF16, "kT")
        ktr = Ring(P, 3, [128, D], BF16, "kt")
        vr = Ring(P, 3, [128, D], BF16, "v")
        csr = Ring(P, 3, [128, 8, 2, 3], F32, "cs")
        otr = Ring(P, 2, [128, D], F32, "ot")
        oflr = Ring(P, 3, [128, D], F32, "ofl") if d == 1 else None
        PTr = Ring(P, 2, [128, 4, 128], BF16, "PT")
        Sxr = Ring(P, 4, [128, 4, 128], BF16, "Sx")
        tmpr = Ring(P, 2, [128, 4, 128], F32, "tmp")
        psS = Ring(P, 2, [128, 512], F32, "psS", psum=True)
        psO = Ring(P, 2, [128, 512], F32, "psO", psum=True)
        psK = Ring(P, 4, [128, 512], F32, "psK", psum=True)
        steps = [(ti, b) for ti in self.scan_order(d) for b in range(c.NB)]
        corder = (0, 1) if d == 0 else (1, 0)

        def ld(st):
            ti, b = st
            tok = slice(ti * 128, (ti + 1) * 128)
            qT, kT, kt, v, cs = qTr.next(), kTr.next(), ktr.next(), vr.next(), csr.next()
            need_o = not (last and ti < c.CT)
            self.dma(SP, qT.t[:].rearrange("p k t -> p (k t)"), self.sQT[d, b, ti], [self.dbuf(("QT", d, b, ti))], [qT.b])
            self.dma(SP, kT.t[:].rearrange("p k t -> p (k t)"), self.sKT[d, b, ti], [self.dbuf(("KT", d, b, ti))], [kT.b])
            self.dma(SP, kt.t[:], self.sKt[d, b, tok, :], [self.dbuf(("Kt", d, b, ti))], [kt.b])
            self.dma(SP, v.t[:], self.sV[b, tok, 0:D], [self.dbuf(("V", b, ti))], [v.b])
            self.dma(SP, cs.t[:].rearrange("p h c k -> p (h c k)"), self.sCS[d, b, ti], [self.dbuf(("CS", d, b, ti))], [cs.b])
            ofl = None
            if d == 1 and need_o:
                ofl = oflr.next()
                self.dma(SP, ofl.t[:], self.sO[0, b, tok, 0:D], [self.dbuf(("O", 0, b, ti))], [ofl.b])
            return qT, kT, kt, v, cs, need_o, ofl

        def comp(st, tl):
            ti, b = st
            tok = slice(ti * 128, (ti + 1) * 128)
            qT, kT, kt, v, cs, need_o, ofl = tl
            ot = otr.next() if need_o else None
            for hh in range(2):
                Sb = S[b, hh]
                hs = slice(hh * 4, hh * 4 + 4)

                def bc(kind, cc):
                    return cs.t[:, hs, cc, kind:kind + 1].broadcast_to([128, 4, 128])

                if need_o:
                    ps = psS.next()
                    for hl in range(4):
                        h = hh * 4 + hl
                        self.mm(ps.t[:, hl * 128:(hl + 1) * 128], kT.t[:, h, :], qT.t[:, h, :], True, True, [kT.b, qT.b], [ps.b])
                    PT = PTr.next()
                    self.tt(DVE, PT.t[:], ps.t[:].rearrange("p (h t) -> p h t", h=4),
                            ct.t[:, 6 + d, :].unsqueeze(1).broadcast_to([128, 4, 128]), ALU.mult, [ps.b, ct.b], [PT.b])
                    po = psO.next()
                    for hl in range(4):
                        h = hh * 4 + hl
                        self.mm(po.t[:, hl * 128:(hl + 1) * 128], PT.t[:, hl, :], v.t[:, h * 128:(h + 1) * 128], hl == 0, False,
                                [PT.b, v.b], [po.b], skip=True)
                for ci, cc in enumerate(corder):
                    rows = slice(cc * 64, cc * 64 + 64)
                    if need_o:
                        Sx = Sxr.next()
                        self.tt(POOL, Sx.t[:], Sb.t[:], bc(0, cc), ALU.mult, [Sb.b, cs.b], [Sx.b])
                        for hl in range(4):
                            h = hh * 4 + hl
                            self.mm(po.t[rows, hl * 128:(hl + 1) * 128], qT.t[:, h, rows], Sx.t[:, hl, :], False,
                                    True, [qT.b, Sx.b], [po.b], skip=True)
                    pk = psK.next()
                    for hl in range(4):
                        h = hh * 4 + hl
                        self.mm(pk.t[:, hl * 128:(hl + 1) * 128], kt.t[rows, h * 128:(h + 1) * 128],
                                v.t[rows, h * 128:(h + 1) * 128], True, True, [kt.b, v.b], [pk.b])
                    tmp = tmpr.next()
                    self.tt(DVE, tmp.t[:], pk.t[:].rearrange("p (h t) -> p h t", h=4), bc(2, cc), ALU.mult, [pk.b, cs.b], [tmp.b])
                    self.tt(DVE, Sb.t[:], Sb.t[:], bc(1, cc), ALU.mult, [Sb.b, cs.b], [Sb.b])
                    self.tt(DVE, Sb.t[:], Sb.t[:], tmp.t[:], ALU.add, [Sb.b, tmp.b], [Sb.b])
                if need_o and d == 0:
                    self.act(ot.t[:, hh * 512:(hh + 1) * 512], po.t[:], AF.Copy, [po.b], [ot.b])
                elif need_o:
                    self.tt(DVE, ot.t[:, hh * 512:(hh + 1) * 512], po.t[:], ofl.t[:, hh * 512:(hh + 1) * 512], ALU.add,
                            [po.b, ofl.b], [ot.b])
            if need_o:
                self.dma(SP, self.sO[d, b, tok, 0:D], ot.t[:], [ot.b], [self.dbuf(("O", d, b, ti))])

        self.run_pipelined(steps, ld, comp, rowof=lambda it: 0)
        P.end_phase()

    def hg_readout(self, li, j):
        P, c = self.P, self.cfg
        P.begin_phase()
        wo = Tl(P, [128, 8, 1024], BF16, "w_out")
        self.load_wout_scaled(wo, self.hg_w_out[j], 8, self.hg_gn[j])
        g1 = Tl(P, [128, D], F32, "g1")
        ofr = Ring(P, 3, [128, D], F32, "of")
        gr = Ring(P, 3, [128, D], BF16, "g")
        hr = Ring(P, 3, [128, D], F32, "h")
        sqr = Ring(P, 2, [128, D], F32, "sq")
        onr = Ring(P, 2, [128, D], BF16, "on")
        onTr = Ring(P, 2, [128, 8, 128], BF16, "onT")
        rsr = Ring(P, 2, [128, 8], F32, "rs")
        hnr = Ring(P, 2, [128, D], F32, "hn")
        ssr = Ring(P, 4, [128, 2], F32, "ss")
        junk = Tl(P, [128, D], BF16, "junk")
        psTr = Ring(P, 2, [128, 1024], BF16, "psT", psum=True)
        psYr = Ring(P, 2, [128, 1024], F32, "psY", psum=True)

        def ld(it):
            row, b, ti = it
            tok = slice(ti * 128, (ti + 1) * 128)
            of, g, h = ofr.next(), gr.next(), hr.next()
            self.dma(SP, of.t[:], self.sO[1, b, tok, 0:D], [self.dbuf(("O", 1, b, ti))], [of.b])
            self.dma(SP, g.t[:], self.sG[b, tok, 0:D], [self.dbuf(("G", b, ti))], [g.b])
            src, sb_ = self.h_src(li, b, ti)
            self.dma(SP, h.t[:], src, [sb_] if sb_ is not None else [], [h.b])
            return {"of": of, "g": g, "h": h}

        def pre_a(it, cx):
            of, g = cx["of"], cx["g"]
            on, rs, sq = onr.next(), rsr.next(), sqr.next()
            self.tt(POOL, sq.t[:], of.t[:], of.t[:], ALU.mult, [of.b], [sq.b])
            self.P.op(DVE, lambda e, o=rs.t[:], i=sq.t[:].rearrange("p (h v) -> p h v", h=8):
                      e.tensor_reduce(out=o, in_=i, axis=mybir.AxisListType.X, op=ALU.add), [sq.b], [rs.b])
            self.rstd(rs.t[:], rs.t[:], 1.0 / 128, EPS, [rs.b], [rs.b], n=8)
            self.tt(DVE, of.t[:], of.t[:], g.t[:], ALU.mult, [of.b, g.b], [of.b])
            self.tt(DVE, on.t[:].rearrange("p (h v) -> p h v", h=8), of.t[:].rearrange("p (h v) -> p h v", h=8),
                    rs.t[:].unsqueeze(2).broadcast_to([128, 8, 128]), ALU.mult, [of.b, rs.b], [on.b])
            cx["on"] = on

        def pre_b(it, cx):
            on, onT = cx["on"], onTr.next()
            psT = psTr.next()
            self.transpose_to(on.t, 8, psT, onT.t[:].rearrange("p k t -> p (k t)"), onT.b, on.b)
            cx["onT"] = onT

        self.run_pipe(self.readout_items(li), ld, pre_a, pre_b, self.readout_main(li, wo, 8, g1, psYr, hnr, ssr, junk))
        P.end_phase()

    def x_row(self, ti):
        c = self.cfg
        if ti < c.CT:
            return 1 + ti * 128
        return c.CTX + 3 + (ti - c.CT) * 128

    def m2_proj(self, li, j):
        P, c = self.P, self.cfg
        P.begin_phase()
        w = Tl(P, [128, 8, 5184], BF16, "w_in")
        self.load_w(w, self.m2_w_in[j], 8, 5184)
        tabs = [Tl(P, [128, D], F32, f"tab{i}") for i in range(2)]
        dtb = Tl(P, [128, 64], F32, "dtb")
        aneg = Tl(P, [128, 64], F32, "aneg")
        self.dma(SP, dtb.t[:], self.m2_dt_bias[j, :].partition_broadcast(128), [], [dtb.b])
        self.dma(SP, aneg.t[:], self.m2_a_log[j, :].partition_broadcast(128), [], [aneg.b])
        self.act(aneg.t[:], aneg.t[:], AF.Exp, [aneg.b], [aneg.b])
        self.ts(DVE, aneg.t[:], aneg.t[:], -1.0, None, ALU.mult, None, [aneg.b], [aneg.b])
        zero = Tl(P, [1, 3072], F32, "zero")
        self.memset(DVE, zero.t[:], 0.0, [zero.b])
        for b in range(c.NB):
            for r in (0, c.CTX + 1, c.CTX + 2, c.CTX + c.LAT + 3):
                self.dma(SP, self.sX[b, r:r + 1, :], zero.t[:], [zero.b], [self.dbuf(("Xpad", b, r))])
        hr = Ring(P, 3, [128, D], F32, "h")
        ur = Ring(P, 2, [128, D], BF16, "u")
        junk = Tl(P, [128, D], BF16, "junk")
        ssr = Ring(P, 4, [128, 2], F32, "ss")
        uTr = Ring(P, 2, [128, 8, 128], BF16, "uT")
        zr = Ring(P, 2, [128, 2048], BF16, "zb")
        xr = Ring(P, 2, [128, 3072], F32, "xbc")
        dr = Ring(P, 2, [128, 128], F32, "dtla")
        psTr = Ring(P, 2, [128, 1024], BF16, "psT", psum=True)
        psM = Ring(P, 6, [128, 512], F32, "psM", psum=True)
        ld, pre_a, pre_b = self.make_stages(li, tabs, hr, ur, ssr, uTr, junk, psTr)

        def main(it, cx, hook):
            uT = cx["uT"]
            row, b, ti = it
            tok = slice(ti * 128, (ti + 1) * 128)
            zb, xb, dl = zr.next(), xr.next(), dr.next()
            for n in range(11):
                if n == 6:
                    hook()
                ps = psM.next()
                wd = 512 if n < 10 else 64
                for k in range(8):
                    self.mm(ps.t[:, 0:wd], uT.t[:, k, :], w.t[:, k, n * 512:n * 512 + wd], k == 0, k == 7, [uT.b, w.b], [ps.b])
                if n < 4:
                    self.act(zb.t[:, n * 512:(n + 1) * 512], ps.t[:], AF.Silu, [ps.b], [zb.b])
                elif n < 10:
                    eng = ACT if n % 2 == 0 else DVE
                    self.cp(eng, xb.t[:, (n - 4) * 512:(n - 3) * 512], ps.t[:], [ps.b], [xb.b])
                else:
                    self.tt(DVE, dl.t[:, 0:64], ps.t[:, 0:64], dtb.t[:], ALU.add, [ps.b, dtb.b], [dl.b])
                    self.act(dl.t[:, 0:64], dl.t[:, 0:64], AF.Exp, [dl.b], [dl.b])
                    self.act(dl.t[:, 0:64], dl.t[:, 0:64], AF.Ln, [dl.b], [dl.b], bias=1.0)
                    self.tt(DVE, dl.t[:, 64:128], dl.t[:, 0:64], aneg.t[:], ALU.mult, [dl.b, aneg.b], [dl.b])
            self.dma(SP, self.sG[b, tok, :], zb.t[:], [zb.b], [self.dbuf(("G", b, ti))])
            r0 = self.x_row(ti)
            self.dma(SP, self.sX[b, r0:r0 + 128, :], xb.t[:], [xb.b], [self.dbuf(("X", b, ti))])
            self.dma(SP, self.sDT[b, tok, :], dl.t[:], [dl.b], [self.dbuf(("DT", b, ti))])

        self.run_pipe(self.proj_tiles(), ld, pre_a, pre_b, main)
        P.end_phase()

    def m2_conv(self, li, j):
        P, c = self.P, self.cfg
        P.begin_phase()
        cw = Tl(P, [128, 3, 3072], F32, "cw")
        cb = Tl(P, [128, 3072], F32, "cb")
        self.dma(SP, cw.t[:].rearrange("p a b -> p (a b)"), self.m2_conv_w[j].rearrange("a b -> (a b)").partition_broadcast(128), [], [cw.b])
        self.dma(SP, cb.t[:], self.m2_conv_b[j, :].partition_broadcast(128), [], [cb.b])
        xr = [Ring(P, 2, [128, 3072], F32, f"x{i}") for i in range(3)]
        actr = Ring(P, 2, [128, 3072], BF16, "act")
        stg = Ring(P, 2, [128, 1024], BF16, "stg")
        psTr = Ring(P, 2, [128, 1024], BF16, "psT", psum=True)
        items = [(b, ti) for b in range(c.NB) for ti in range(c.NT)]

        def ld(it):
            b, ti = it
            r0 = self.x_row(ti)
            xs = [xr[i].next() for i in range(3)]
            deps = [self.dbuf(("X", b, t2)) for t2 in range(c.NT)] + [self.dbuf(("Xpad", b, r)) for r in (0, c.CTX + 1, c.CTX + 2, c.CTX + c.LAT + 3)]
            for i in range(3):
                self.dma(SP, xs[i].t[:], self.sX[b, r0 - 1 + i:r0 - 1 + i + 128, :], deps, [xs[i].b])
            return xs

        def comp(it, xs):
            b, ti = it
            tok = slice(ti * 128, (ti + 1) * 128)
            x0, x1, x2 = xs
            self.tt(POOL, x0.t[:], x0.t[:], cw.t[:, 0, :], ALU.mult, [x0.b, cw.b], [x0.b])
            self.tt(DVE, x1.t[:], x1.t[:], cw.t[:, 1, :], ALU.mult, [x1.b, cw.b], [x1.b])
            self.tt(DVE, x2.t[:], x2.t[:], cw.t[:, 2, :], ALU.mult, [x2.b, cw.b], [x2.b])
            self.tt(DVE, x1.t[:], x1.t[:], cb.t[:], ALU.add, [x1.b, cb.b], [x1.b])
            self.tt(DVE, x1.t[:], x1.t[:], x2.t[:], ALU.add, [x1.b, x2.b], [x1.b])
            self.tt(DVE, x1.t[:], x1.t[:], x0.t[:], ALU.add, [x1.b, x0.b], [x1.b])
            a = actr.next()
            self.act(a.t[:], x1.t[:], AF.Silu, [x1.b], [a.b])
            self.dma(SP, self.sV[b, tok, :], a.t[:, 0:2048], [a.b], [self.dbuf(("V", b, ti))])
            self.dma(SP, self.sKt[0, b, tok, 0:512], a.t[:, 2048:2560], [a.b], [self.dbuf(("Kt", b, ti))])
            psT = psTr.next()
            st = stg.next()
            self.transpose_to(a.t, 8, psT, st.t[:], st.b, a.b, eng=ACT, src_off=2048)
            self.dma(SP, self.sKT[0, b, ti][:, 0:512], st.t[:, 0:512], [st.b], [self.dbuf(("KT", b, ti))])
            self.dma(SP, self.sQT[0, b, ti][:, 0:512], st.t[:, 512:1024], [st.b], [self.dbuf(("QT", b, ti))])

        self.run_pipelined(items, ld, comp, rowof=lambda it: 0)
        P.end_phase()

    def m2_scan(self, li, j, d):
        P, c = self.P, self.cfg
        last = (li == self.L - 1)
        P.begin_phase()
        ct = self.load_ctab()
        S32, Sbf = {}, {}
        for b in range(c.NB):
            for g in range(4):
                S32[b, g] = Tl(P, [128, 8, 64], F32, "S32")
                Sbf[b, g] = Tl(P, [128, 512], BF16, "Sbf")
                self.memset(POOL, S32[b, g].t[:], 0.0, [S32[b, g].b])
                self.memset(DVE, Sbf[b, g].t[:], 0.0, [Sbf[b, g].b])
        CTr = Ring(P, 3, [128, 4, 128], BF16, "CT")
        BTr = Ring(P, 3, [128, 4, 128], BF16, "BT")
        Btr = Ring(P, 3, [128, 512], BF16, "Bt")
        Xr = Ring(P, 3, [128, 32, 64], BF16, "X")
        dlr = Ring(P, 3, [128, 128], F32, "dtla")
        ear = Ring(P, 2, [128, 96], F32, "eall")
        xdr = Ring(P, 2, [128, 32, 64], BF16, "xdt")
        xwr = Ring(P, 2, [128, 32, 64], BF16, "xw")
        Rr = Ring(P, 2, [128, 8, 128], F32, "R")
        Lr = Ring(P, 2, [128, 8, 128], BF16, "L")
        CBr = Ring(P, 2, [128, 128], F32, "CBm")
        Pmr = Ring(P, 2, [128, 8, 128], BF16, "Pm")
        tmpr = Ring(P, 2, [128, 8, 64], F32, "tmp")
        otr = Ring(P, 2, [128, 2048], F32, "ot")
        psE = Tl(P, [128, 512], F32, "psE", psum=True)
        psD = Tl(P, [128, 1024], F32, "psD", psum=True)
        psCB = Tl(P, [128, 512], F32, "psCB", psum=True)
        psO = Ring(P, 2, [128, 512], F32, "psO", psum=True)
        psI = Tl(P, [128, 512], F32, "psI", psum=True)
        psK = Tl(P, [128, 512], F32, "psK", psum=True)
        steps = [(ti, b) for ti in self.scan_order(d) for b in range(c.NB)]
        tri = ct.t[:, 12 + d, :]
        G = ct.t[:, 14 + d, :]
        ones = ct.t[:, 16, :]

        def ld(st):
            ti, b = st
            tok = slice(ti * 128, (ti + 1) * 128)
            CT, BT, Bt, X, dl = CTr.next(), BTr.next(), Btr.next(), Xr.next(), dlr.next()
            need_o = not (last and ti < c.CT)
            if need_o:
                self.dma(SP, CT.t[:].rearrange("p g t -> p (g t)"), self.sQT[0, b, ti][:, 0:512], [self.dbuf(("QT", b, ti))], [CT.b])
                self.dma(SP, BT.t[:].rearrange("p g t -> p (g t)"), self.sKT[0, b, ti][:, 0:512], [self.dbuf(("KT", b, ti))], [BT.b])
            self.dma(SP, Bt.t[:], self.sKt[0, b, tok, 0:512], [self.dbuf(("Kt", b, ti))], [Bt.b])
            self.dma(SP, X.t[:].rearrange("p h e -> p (h e)"), self.sV[b, tok, :], [self.dbuf(("V", b, ti))], [X.b])
            self.dma(SP, dl.t[:], self.sDT[b, tok, :], [self.dbuf(("DT", b, ti))], [dl.b])
            ofl = None
            return CT, BT, Bt, X, dl, need_o, ofl

        def comp(st, tl):
            ti, b = st
            tok = slice(ti * 128, (ti + 1) * 128)
            CT, BT, Bt, X, dl, need_o, ofl = tl
            la = dl.t[:, 64 + d * 32:64 + (d + 1) * 32]
            dt = dl.t[:, d * 32:(d + 1) * 32]
            self.mm(psE.t[:, 0:32], tri, la, True, True, [ct.b, dl.b], [psE.b])
            self.mm(psE.t[:, 32:64], G, la, True, True, [ct.b, dl.b], [psE.b])
            self.mm(psE.t[:, 64:96], ones, la, True, True, [ct.b, dl.b], [psE.b])
            ea = ear.next()
            self.act(ea.t[:], psE.t[:, 0:96], AF.Exp, [psE.b], [ea.b])
            xd, xw = xdr.next(), xwr.next()
            self.tt(DVE, xd.t[:], X.t[:], dt.unsqueeze(2).broadcast_to([128, 32, 64]), ALU.mult, [X.b, dl.b], [xd.b])
            self.tt(POOL, xw.t[:], xd.t[:], ea.t[:, 32:64].unsqueeze(2).broadcast_to([128, 32, 64]), ALU.mult, [xd.b, ea.b], [xw.b])
            ot = otr.next() if need_o else None
            for g in range(4):
                hs = slice(g * 8, g * 8 + 8)
                if need_o:
                    R = Rr.next()
                    for r in range(8):
                        self.act(R.t[:, r, :], tri, AF.Copy, [dl.b, ct.b], [R.b], scale=la[:, g * 8 + r:g * 8 + r + 1])
                    for half in range(2):
                        self.mm(psD.t[:, half * 512:(half + 1) * 512], G,
                                R.t[:, half * 4:(half + 1) * 4, :].rearrange("p h t -> p (h t)"), True, True, [ct.b, R.b], [psD.b])
                    Lg = Lr.next()
                    self.act(Lg.t[:].rearrange("p h t -> p (h t)"), psD.t[:], AF.Exp, [psD.b], [Lg.b])
                    self.mm(psCB.t[:, 0:128], BT.t[:, g, :], CT.t[:, g, :], True, True, [BT.b, CT.b], [psCB.b])
                    CBm = CBr.next()
                    self.tt(DVE, CBm.t[:], psCB.t[:, 0:128], ct.t[:, 1 + d, :], ALU.mult, [psCB.b, ct.b], [CBm.b])
                    Pm = Pmr.next()
                    self.tt(DVE, Pm.t[:], Lg.t[:], CBm.t[:].unsqueeze(1).broadcast_to([128, 8, 128]), ALU.mult, [Lg.b, CBm.b], [Pm.b])
                    po = psO.next()
                    for r in range(8):
                        self.mm(po.t[:, r * 64:(r + 1) * 64], Pm.t[:, r, :], xd.t[:, g * 8 + r, :], r == 0, r == 7,
                                [Pm.b, xd.b], [po.b], skip=True)
                    self.mm(psI.t[:], CT.t[:, g, :], Sbf[b, g].t[:], True, True, [CT.b, Sbf[b, g].b], [psI.b])
                    tmp = tmpr.next()
                    self.tt(DVE, tmp.t[:], psI.t[:].rearrange("p (h e) -> p h e", h=8),
                            ea.t[:, hs].unsqueeze(2).broadcast_to([128, 8, 64]), ALU.mult, [psI.b, ea.b], [tmp.b])
                    self.tt(DVE, ot.t[:, g * 512:(g + 1) * 512], tmp.t[:].rearrange("p h e -> p (h e)"), po.t[:], ALU.add,
                            [tmp.b, po.b], [ot.b])
                self.mm(psK.t[:], Bt.t[:, g * 128:(g + 1) * 128], xw.t[:, hs, :].rearrange("p h e -> p (h e)"), True, True,
                        [Bt.b, xw.b], [psK.b])
                s32 = S32[b, g]
                self.tt(POOL, s32.t[:], s32.t[:], ea.t[:, 64 + g * 8:64 + g * 8 + 8].unsqueeze(2).broadcast_to([128, 8, 64]),
                        ALU.mult, [s32.b, ea.b], [s32.b])
                self.tt(DVE, s32.t[:], s32.t[:], psK.t[:].rearrange("p (h e) -> p h e", h=8), ALU.add, [s32.b, psK.b], [s32.b])
                self.cp(ACT, Sbf[b, g].t[:], s32.t[:].rearrange("p h e -> p (h e)"), [s32.b], [Sbf[b, g].b])
            if need_o:
                self.dma(SP, self.sO[d, b, tok, :], ot.t[:], [ot.b], [self.dbuf(("O", d, b, ti))])

        self.run_pipelined(steps, ld, comp, rowof=lambda it: 0)
        P.end_phase()

    def m2_readout(self, li, j):
        P, c = self.P, self.cfg
        P.begin_phase()
        wo = Tl(P, [128, 16, 1024], BF16, "w_out")
        self.load_wout_scaled(wo, self.m2_w_out[j], 16, self.m2_gn[j])
        dsk = Tl(P, [128, 32], F32, "dsk")
        self.dma(SP, dsk.t[:], self.m2_d[j, :].partition_broadcast(128), [], [dsk.b])
        g1 = Tl(P, [128, D], F32, "g1")
        ofr = Ring(P, 3, [128, 2048], F32, "of")
        obr = Ring(P, 3, [128, 2048], F32, "ob")
        xr = Ring(P, 3, [128, 32, 64], BF16, "x")
        zr = Ring(P, 3, [128, 2048], BF16, "z")
        hr = Ring(P, 3, [128, D], F32, "h")
        tmpr = Ring(P, 2, [128, 32, 64], F32, "tmp")
        junk2 = Tl(P, [128, 512], BF16, "junk2")
        onr = Ring(P, 2, [128, 2048], BF16, "on")
        onTr = Ring(P, 2, [128, 16, 128], BF16, "onT")
        rsr = Ring(P, 2, [128, 4], F32, "rs")
        hnr = Ring(P, 2, [128, D], F32, "hn")
        ssr = Ring(P, 4, [128, 2], F32, "ss")
        junk = Tl(P, [128, D], BF16, "junk")
        psTr = Ring(P, 2, [128, 1024], BF16, "psT", psum=True)
        psYr = Ring(P, 2, [128, 1024], F32, "psY", psum=True)

        def ld(it):
            row, b, ti = it
            tok = slice(ti * 128, (ti + 1) * 128)
            of, ob, x, z, h = ofr.next(), obr.next(), xr.next(), zr.next(), hr.next()
            self.dma(SP, of.t[:], self.sO[1, b, tok, :], [self.dbuf(("O", 1, b, ti))], [of.b])
            self.dma(SP, ob.t[:], self.sO[0, b, tok, :], [self.dbuf(("O", 0, b, ti))], [ob.b])
            self.dma(SP, x.t[:].rearrange("p h e -> p (h e)"), self.sV[b, tok, :], [self.dbuf(("V", b, ti))], [x.b])
            self.dma(SP, z.t[:], self.sG[b, tok, :], [self.dbuf(("G", b, ti))], [z.b])
            src, sb_ = self.h_src(li, b, ti)
            self.dma(SP, h.t[:], src, [sb_] if sb_ is not None else [], [h.b])
            return {"of": of, "ob": ob, "x": x, "z": z, "h": h}

        def pre_a(it, cx):
            of, ob, x, z = cx["of"], cx["ob"], cx["x"], cx["z"]
            on, rs, tmp = onr.next(), rsr.next(), tmpr.next()
            self.tt(DVE, tmp.t[:], x.t[:], dsk.t[:].unsqueeze(2).broadcast_to([128, 32, 64]), ALU.mult, [x.b, dsk.b], [tmp.b])
            self.tt(POOL, of.t[:], of.t[:], ob.t[:], ALU.add, [of.b, ob.b], [of.b])
            self.tt(DVE, of.t[:], of.t[:], tmp.t[:].rearrange("p h e -> p (h e)"), ALU.add, [of.b, tmp.b], [of.b])
            self.tt(DVE, of.t[:], of.t[:], z.t[:], ALU.mult, [of.b, z.b], [of.b])
            for g in range(4):
                self.act(junk2.t[:], of.t[:, g * 512:(g + 1) * 512], AF.Square, [of.b], [junk2.b, rs.b], accum_out=rs.t[:, g:g + 1])
            self.rstd(rs.t[:], rs.t[:], 1.0 / 512, EPS, [rs.b], [rs.b], n=4)
            self.tt(DVE, on.t[:].rearrange("p (g v) -> p g v", g=4), of.t[:].rearrange("p (g v) -> p g v", g=4),
                    rs.t[:].unsqueeze(2).broadcast_to([128, 4, 512]), ALU.mult, [of.b, rs.b], [on.b])
            cx["on"] = on

        def pre_b(it, cx):
            on, onT = cx["on"], onTr.next()
            for half in range(2):
                psT = psTr.next()
                self.transpose_to(on.t, 8, psT, onT.t[:, half * 8:(half + 1) * 8, :].rearrange("p k t -> p (k t)"),
                                  onT.b, on.b, eng=(ACT if half == 0 else DVE), src_off=half * 1024)
            cx["onT"] = onT

        self.run_pipe(self.readout_items(li), ld, pre_a, pre_b, self.readout_main(li, wo, 16, g1, psYr, hnr, ssr, junk))
        P.end_phase()

    def copy_in(self):
        P, c = self.P, self.cfg
        P.begin_phase()
        hr = Ring(P, 3, [128, D], F32, "h")
        for b in range(c.NB):
            for ti in range(c.NT):
                h = hr.next()
                src, _ = self.h_src(0, b, ti)
                self.dma(SP, h.t[:], src, [], [h.b])
                dst, db_ = self.h_mid(b, ti)
                self.dma(SP, dst, h.t[:], [h.b], [db_])
        P.end_phase()

    def build(self):
        c = self.cfg
        self.declare()
        self.setup_consts()
        phases = c.phases
        if phases is None:
            phases = ["mod"]
            for li, k in enumerate(c.kinds):
                phases += [("mix", li), ("ffn", li)]
        for ph in phases:
            if ph == "mod":
                self.phase_mod()
            elif ph == "copy_in":
                self.copy_in()
            elif ph[0] == "ffn":
                self.phase_ffn(ph[1])
            elif ph[0] == "mix":
                self.phase_mixer(ph[1])
            elif ph[0] == "call":
                getattr(self, ph[1])(*ph[2:])
        self.P.begin_phase()
        self.P.end_phase(final=True)
        return self.nc


def const_tables(cfg):
    ident = np.eye(128, dtype=np.float32)
    LT = max(cfg.LT, 1)
    nf = 64
    inv = (10000.0 ** (-np.arange(nf, dtype=np.float32) / nf)).astype(np.float32)
    rope = np.zeros((LT, 128, 512), np.float32)
    p = np.arange(128)
    for lt in range(LT):
        row = (2 * lt + p // GRID_W).astype(np.float32)
        col = (p % GRID_W).astype(np.float32)
        ar = row[:, None] * inv[None, :]
        ac = col[:, None] * inv[None, :]
        cr, sr, cc, sc = np.cos(ar), np.sin(ar), np.cos(ac), np.sin(ac)
        rope[lt, :, 0:256] = np.concatenate([cr, cr, cc, cc], 1)
        rope[lt, :, 256:512] = np.concatenate([-sr, sr, -sc, sc], 1)
    tab = np.zeros((128, 24, 128), np.float32)
    s = np.arange(128)[:, None].astype(np.float32)
    t = np.arange(128)[None, :].astype(np.float32)
    tab[:, 0] = t - s
    tab[:, 1] = (t >= s)
    tab[:, 2] = (s >= t)
    tab[:, 3] = t + 1.0
    tab[:, 4] = 128.0 - t
    tab[:, 5, 0] = 127.0 - s[:, 0]
    tab[:, 5, 1] = s[:, 0]
    tab[:, 5, 2] = 128.0
    same = ((s // 64) == (t // 64)).astype(np.float32)
    sl = s % 64
    tab[:, 6] = same * (t >= s)
    tab[:, 7] = same * (s >= t)
    tab[:, 8] = same * ((s <= t).astype(np.float32) - (sl <= 31))
    tab[:, 9] = same * ((s >= t).astype(np.float32) - (sl >= 32))
    for cc in range(2):
        inc = ((s[:, 0] // 64) == cc).astype(np.float32)
        slc = s[:, 0] % 64
        tab[:, 10, 3 * cc + 0] = inc * (slc <= 31)
        tab[:, 10, 3 * cc + 1] = inc
        tab[:, 10, 3 * cc + 2] = inc * (slc > 31)
        tab[:, 11, 3 * cc + 0] = inc * (slc >= 32)
        tab[:, 11, 3 * cc + 1] = inc
        tab[:, 11, 3 * cc + 2] = inc * (slc < 32)
    tab[:, 12] = (s <= t)
    tab[:, 13] = (s >= t)
    tab[:, 14] = (s > t)
    tab[:, 15] = (s < t)
    tab[:, 16] = 1.0
    return ident, rope, tab.reshape(128, 24 * 128)


def make_in_maps(cfg, inputs, n_cores):
    ident, rope, tab = const_tables(cfg)
    f = lambda a: np.ascontiguousarray(np.asarray(a, dtype=np.float32))
    L = len(cfg.kinds)
    nr, nh, nm = (max(1, sum(1 for k in cfg.kinds if k == q)) for q in (0, 1, 2))
    inputs = dict(inputs)
    for k in inputs:
        if k.startswith("ret_"):
            inputs[k] = np.asarray(inputs[k])[:nr]
        elif k.startswith("hg_") and k != "hg_lb":
            inputs[k] = np.asarray(inputs[k])[:nh]
        elif k.startswith("m2_"):
            inputs[k] = np.asarray(inputs[k])[:nm]
    shared = {
        "ada_w": f(inputs["ada_w"][:L]), "ada_b": f(inputs["ada_b"][:L]),
        "norm_g": f(inputs["norm_g"][:L]).reshape(L, 4 * D),
        "ret_w_in": f(inputs["ret_w_in"]), "ret_w_out": f(inputs["ret_w_out"]),
        "ret_decay": f(inputs["ret_decay"]).reshape(-1, 8), "ret_gn": f(inputs["ret_gn"]),
        "hg_w_in": f(inputs["hg_w_in"]), "hg_w_out": f(inputs["hg_w_out"]), "hg_lb": f(inputs["hg_lb"][:L]),
        "hg_gn": f(inputs["hg_gn"]),
        "m2_w_in": f(inputs["m2_w_in"]), "m2_w_out": f(inputs["m2_w_out"]), "m2_conv_w": f(inputs["m2_conv_w"]),
        "m2_conv_b": f(inputs["m2_conv_b"]), "m2_dt_bias": f(inputs["m2_dt_bias"]).reshape(-1, 64),
        "m2_a_log": f(inputs["m2_a_log"]).reshape(-1, 64), "m2_d": f(inputs["m2_d"]), "m2_gn": f(inputs["m2_gn"]),
        "ffn_w_up": f(inputs["ffn_w_up"][:L]), "ffn_w_down": f(inputs["ffn_w_down"][:L]),
        "ffn_cw": f(np.asarray(inputs["ffn_conv_w"][:L]).reshape(L, 3, 22, 128).transpose(0, 3, 2, 1)).reshape(L, 128, 66),
        "ffn_cb": f(np.asarray(inputs["ffn_conv_b"][:L]).reshape(L, 22, 128).transpose(0, 2, 1)),
        "c_ident": ident, "c_rope": rope, "c_tab": tab,
    }
    x, cc, ctx, c_ctx = (np.asarray(inputs[k], dtype=np.float32) for k in ("x", "c", "ctx", "c_ctx"))
    maps = []
    for i in range(n_cores):
        sl = slice(i * cfg.NB, (i + 1) * cfg.NB)
        m = dict(shared)
        m["x"] = f(x[sl])
        m["ctx"] = f(ctx[sl])
        m["crow"] = f(np.concatenate([cc[sl], c_ctx[None, :]], 0))
        maps.append(m)
    return maps


_CACHE = {}


def kernel(**inputs):
    cfg = Cfg()
    if "nc" not in _CACHE:
        _CACHE["nc"] = Builder(cfg).build()
    nc = _CACHE["nc"]
    maps = make_in_maps(cfg, inputs, N_CORES)
    res = run_bass_kernel_spmd(nc, maps, core_ids=list(range(N_CORES)))
    out = np.concatenate([np.asarray(r["out"]) for r in res.results], axis=0)
    return out.astype(np.float32)
```

```python
import contextlib
import numpy as np
import concourse.bass as bass
import concourse.mybir as mybir
from concourse.bass_utils import run_bass_kernel_spmd

F32 = mybir.dt.float32
BF16 = mybir.dt.bfloat16
AF = mybir.ActivationFunctionType
ALU = mybir.AluOpType

PE, ACT, DVE, POOL, SP = "pe", "act", "dve", "pool", "sp"
CENG = (PE, ACT, DVE, POOL)

D = 1024
EPS = 1e-6
GRID_W = 64
N_CORES = 8


class Buf:
    __slots__ = ("name", "last_w", "readers", "dram", "strict")

    def __init__(self, name, dram=False):
        self.name = name
        self.last_w = None
        self.readers = []
        self.dram = dram
        self.strict = bool(name) and str(name).startswith("junk")


class Op:
    __slots__ = ("eng", "fn", "deps", "weak", "is_dma", "signal", "val", "dsem_buf")

    def __init__(self, eng, fn, is_dma, dsem_buf):
        self.eng = eng
        self.fn = fn
        self.deps = set()
        self.weak = set()
        self.is_dma = is_dma
        self.signal = False
        self.val = None
        self.dsem_buf = dsem_buf


class Prog:
    def __init__(self, same_eng_sync=True):
        self.nc = bass.Bass("TRN2", target_bir_lowering=False)
        self.ops = []
        self.same_eng_sync = same_eng_sync
        self.keep = []
        self.bufs = []
        nc = self.nc
        self.engs = {PE: nc.tensor, ACT: nc.scalar, DVE: nc.vector, POOL: nc.gpsimd, SP: nc.sync}
        self.sems = {}
        for e in CENG:
            cm = nc.semaphore("s_" + e)
            self.sems[e] = cm.__enter__()
            self.keep.append(cm)
        self.cnt = {e: 0 for e in CENG}
        self.dsem_free = {True: [], False: []}
        self.dsem_all = []
        self.waited = {}
        self.n_inst = 0
        self.n_wait = 0
        self.phase_stack = None
        self.uid = 0

    def begin_phase(self):
        self.phase_stack = contextlib.ExitStack()

    def sb(self, shape, dt, name=None):
        self.uid += 1
        return self.phase_stack.enter_context(self.nc.sbuf_tensor(f"{name or 't'}_{self.uid}", list(shape), dt))

    def ps(self, shape, dt=F32, name=None):
        self.uid += 1
        return self.phase_stack.enter_context(self.nc.psum_tensor(f"{name or 'p'}_{self.uid}", list(shape), dt))

    def buf(self, name=None, dram=False):
        b = Buf(name or "b", dram)
        self.bufs.append(b)
        return b

    def op(self, eng, fn, reads=(), writes=(), dma=False):
        idx = len(self.ops)
        dsem_buf = None
        if dma:
            for b in list(writes) + list(reads):
                if not b.dram:
                    dsem_buf = b
                    break
            if dsem_buf is None:
                dsem_buf = (list(writes) + list(reads))[0]
        o = Op(eng, fn, dma, dsem_buf)
        for b in reads:
            if b.last_w is not None:
                o.deps.add(b.last_w)
        for b in writes:
            if b.last_w is not None:
                (o.deps if b.strict else o.weak).add(b.last_w)
            for r in b.readers:
                o.weak.add(r)
        for b in reads:
            b.readers.append(idx)
        for b in writes:
            b.last_w = idx
            b.readers = []
        o.deps.discard(idx)
        for d in o.weak:
            if d != idx and d not in o.deps:
                s_ = self.ops[d]
                if s_.is_dma or dma or s_.eng != eng:
                    o.deps.add(d)
        self.ops.append(o)
        return idx

    def end_phase(self, final=False):
        nc = self.nc
        ops = self.ops
        engs = self.engs
        sems = self.sems
        cnt = self.cnt
        waited = self.waited
        last_on = {}
        for i, o in enumerate(ops):
            last_on[o.eng] = i
            for d in o.deps:
                s = ops[d]
                if s.is_dma:
                    continue
                if s.eng == o.eng and (s.eng == PE or not self.same_eng_sync):
                    continue
                s.signal = True
        for e in CENG:
            for i in range(len(ops) - 1, -1, -1):
                if ops[i].eng == e and not ops[i].is_dma:
                    ops[i].signal = True
                    break
        dsems = {}
        for i, o in enumerate(ops):
            eng = engs[o.eng]
            need = {}
            for d in o.deps:
                s = ops[d]
                if s.is_dma:
                    st = dsems[(id(s.dsem_buf), s.eng == POOL)]
                    key = ("d", id(st))
                    v = st[1]
                    sem = st[0]
                else:
                    if s.eng == o.eng and (s.eng == PE or not self.same_eng_sync):
                        continue
                    key = ("e", s.eng)
                    v = s.val
                    sem = sems[s.eng]
                if need.get(key, (None, -1))[1] < v:
                    need[key] = (sem, v)
            for key, (sem, v) in need.items():
                if waited.get((o.eng, key), -1) >= v:
                    continue
                waited[(o.eng, key)] = v
                eng.wait_ge(sem, v)
                self.n_wait += 1
            inst = o.fn(eng)
            self.n_inst += 1
            if o.is_dma:
                k = (id(o.dsem_buf), o.eng == POOL)
                if k not in dsems:
                    fl = self.dsem_free[k[1]]
                    if fl:
                        dsems[k] = fl.pop()
                    else:
                        cm = nc.semaphore(f"d{len(self.dsem_all)}")
                        st = [cm.__enter__(), 0]
                        self.keep.append(cm)
                        self.dsem_all.append(st)
                        dsems[k] = st
                st = dsems[k]
                st[1] += 16
                inst.then_inc(st[0], 16)
            elif o.signal:
                cnt[o.eng] += 1
                o.val = cnt[o.eng]
                inst.then_inc(sems[o.eng], 1)
        targets = [SP] if final else list(engs.keys())
        for e in targets:
            eng = engs[e]
            for s in CENG:
                if cnt[s] == 0 or (s == e and (e == PE or not self.same_eng_sync)):
                    continue
                key = ("e", s)
                if waited.get((e, key), -1) >= cnt[s]:
                    continue
                waited[(e, key)] = cnt[s]
                eng.wait_ge(sems[s], cnt[s])
                self.n_wait += 1
            for st in dsems.values():
                key = ("d", id(st))
                if waited.get((e, key), -1) >= st[1]:
                    continue
                waited[(e, key)] = st[1]
                eng.wait_ge(st[0], st[1])
                self.n_wait += 1
        for k, st in dsems.items():
            self.dsem_free[k[1]].append(st)
        self.ops = []
        for b in self.bufs:
            b.last_w = None
            b.readers = []
        self.bufs = [b for b in self.bufs if b.dram]
        if self.phase_stack is not None:
            self.phase_stack.close()
            self.phase_stack = None


class Tl:
    def __init__(self, P, shape, dt, name=None, psum=False):
        self.t = P.ps(shape, dt, name) if psum else P.sb(shape, dt, name)
        self.b = P.buf(name)


class Ring:
    def __init__(self, P, n, shape, dt, name=None, psum=False):
        self.tiles = [Tl(P, shape, dt, name, psum) for _ in range(n)]
        self.i = 0

    def next(self):
        t = self.tiles[self.i % len(self.tiles)]
        self.i += 1
        return t


class Cfg:
    def __init__(self, NB=2, LAT=4096, CTX=256, kinds=(0, 1, 2, 0), debug=False, phases=None):
        self.NB, self.LAT, self.CTX = NB, LAT, CTX
        self.kinds = tuple(kinds)
        self.debug = debug
        self.phases = phases
        self.TOK = LAT + CTX
        self.CT = CTX // 128
        self.LT = LAT // 128
        self.NT = self.CT + self.LT


class Builder:
    def __init__(self, cfg):
        self.cfg = cfg
        self.P = Prog()
        self.nc = self.P.nc
        self.dram_bufs = {}

    def dma(self, q, out, in_, R, W, **kw):
        self.P.op(q, lambda e: e.dma_start(out=out, in_=in_, **kw), R, W, dma=True)

    def mm(self, out, lhsT, rhs, start, stop, R, W, skip=False):
        self.P.op(PE, lambda e: e.matmul(out, lhsT=lhsT, rhs=rhs, start=start, stop=stop, skip_group_check=skip), R, W)

    def tr(self, out, in_, R, W):
        ident = self.ident.t[:]
        self.P.op(PE, lambda e: e.transpose(out=out, in_=in_, identity=ident), list(R) + [self.ident.b], W)

    def act(self, out, in_, func, R, W, eng=ACT, **kw):
        self.P.op(eng, lambda e: e.activation(out=out, in_=in_, func=func, **kw), R, W)

    def tt(self, eng, out, in0, in1, op, R, W):
        self.P.op(eng, lambda e: e.tensor_tensor(out=out, in0=in0, in1=in1, op=op), R, W)

    def ts(self, eng, out, in0, s1, s2, op0, op1, R, W):
        if op1 is None:
            self.P.op(eng, lambda e: e.tensor_scalar(out=out, in0=in0, scalar1=s1, scalar2=None, op0=op0), R, W)
        else:
            self.P.op(eng, lambda e: e.tensor_scalar(out=out, in0=in0, scalar1=s1, scalar2=s2, op0=op0, op1=op1), R, W)

    def stt(self, out, in0, scalar, in1, op0, op1, R, W):
        self.P.op(DVE, lambda e: e.scalar_tensor_tensor(out=out, in0=in0, scalar=scalar, in1=in1, op0=op0, op1=op1), R, W)

    def cp(self, eng, out, in_, R, W):
        if eng == ACT:
            self.act(out, in_, AF.Copy, R, W)
        else:
            self.P.op(eng, lambda e: e.tensor_copy(out=out, in_=in_), R, W)

    def memset(self, eng, ap, val, W):
        self.P.op(eng, lambda e: e.memset(ap, val), (), W)

    def dbuf(self, key):
        if key not in self.dram_bufs:
            self.dram_bufs[key] = self.P.buf(str(key), dram=True)
        return self.dram_bufs[key]

    def rstd(self, out, in_, scale, eps, R, W, n=1):
        nh = self.neghalf.t[:, 0:n]
        self.ts(POOL, out, in_, scale, eps, ALU.mult, ALU.add, R, W)
        self.tt(POOL, out, out, nh, ALU.pow, list(W) + [self.neghalf.b], W)

    def declare(self):
        nc, c = self.nc, self.cfg
        L = len(c.kinds)
        self.L = L
        di = lambda n, s, dt=F32: nc.dram_tensor(n, list(s), dt, kind="ExternalInput").ap()
        dx = lambda n, s, dt=F32: nc.dram_tensor(n, list(s), dt, kind="Internal").ap()
        self.x_in = di("x", [c.NB, c.LAT, D])
        self.ctx_in = di("ctx", [c.NB, c.CTX, D])
        self.crow = di("crow", [c.NB + 1, D])
        self.ada_w = di("ada_w", [L, D, 6 * D])
        self.ada_b = di("ada_b", [L, 6 * D])
        self.norm_g = di("norm_g", [L, 4 * D])
        n_ret = sum(1 for k in c.kinds if k == 0)
        n_hg = sum(1 for k in c.kinds if k == 1)
        n_m2 = sum(1 for k in c.kinds if k == 2)
        self.ret_w_in = di("ret_w_in", [max(n_ret, 1), D, 6144])
        self.ret_w_out = di("ret_w_out", [max(n_ret, 1), 2048, D])
        self.ret_decay = di("ret_decay", [max(n_ret, 1), 8])
        self.ret_gn = di("ret_gn", [max(n_ret, 1), 2048])
        self.hg_w_in = di("hg_w_in", [max(n_hg, 1), D, 5120])
        self.hg_w_out = di("hg_w_out", [max(n_hg, 1), D, D])
        self.hg_lb = di("hg_lb", [L, D])
        self.hg_gn = di("hg_gn", [max(n_hg, 1), D])
        self.m2_w_in = di("m2_w_in", [max(n_m2, 1), D, 5184])
        self.m2_w_out = di("m2_w_out", [max(n_m2, 1), 2048, D])
        self.m2_conv_w = di("m2_conv_w", [max(n_m2, 1), 3, 3072])
        self.m2_conv_b = di("m2_conv_b", [max(n_m2, 1), 3072])
        self.m2_dt_bias = di("m2_dt_bias", [max(n_m2, 1), 64])
        self.m2_a_log = di("m2_a_log", [max(n_m2, 1), 64])
        self.m2_d = di("m2_d", [max(n_m2, 1), 32])
        self.m2_gn = di("m2_gn", [max(n_m2, 1), 2048])
        self.ffn_w_up = di("ffn_w_up", [L, D, 5632])
        self.ffn_cw = di("ffn_cw", [L, 128, 22 * 3])
        self.ffn_cb = di("ffn_cb", [L, 128, 22])
        self.ffn_w_down = di("ffn_w_down", [L, 2816, D])
        self.c_ident = di("c_ident", [128, 128])
        self.c_rope = di("c_rope", [max(c.LT, 1), 128, 512])
        self.c_tab = di("c_tab", [128, 24 * 128])
        self.out = nc.dram_tensor("out", [c.NB, c.LAT, D], F32, kind="ExternalOutput").ap()
        if c.debug:
            self.dbg = nc.dram_tensor("dbg", [L, c.NB, c.TOK, D], F32, kind="ExternalOutput").ap()
        self.H = dx("H", [c.NB, c.TOK, D])
        self.MOD = dx("MOD", [L, c.NB + 1, 6, D])
        self.sQT = dx("sQT", [2, c.NB, c.NT, 128, 1024], BF16)
        self.sKT = dx("sKT", [2, c.NB, c.NT, 128, 1024], BF16)
        self.sKt = dx("sKt", [2, c.NB, c.TOK, 1024], BF16)
        self.sV = dx("sV", [c.NB, c.TOK, 2048], BF16)
        self.sG = dx("sG", [c.NB, c.TOK, 2048], BF16)
        self.sO = dx("sO", [2, c.NB, c.TOK, 2048])
        self.sCS = dx("sCS", [2, c.NB, c.NT, 128, 48])
        self.sX = dx("sX", [c.NB, c.TOK + 2 * c.NT + 8, 3072])
        self.sDT = dx("sDT", [c.NB, c.TOK, 128])

    def setup_consts(self):
        P, nc = self.P, self.nc
        self.const_stack = contextlib.ExitStack()
        P.phase_stack = self.const_stack
        self.ident = Tl(P, [128, 128], BF16, "ident")
        self.neghalf = Tl(P, [128, 8], F32, "neghalf")
        P.phase_stack = None
        P.begin_phase()
        self.dma(POOL, self.ident.t[:], self.c_ident[:, :], [], [self.ident.b])
        self.memset(POOL, self.neghalf.t[:], -0.5, [self.neghalf.b])
        P.end_phase()

    def load_ctab(self):
        ct = Tl(self.P, [128, 24, 128], F32, "ctab")
        self.dma(SP, ct.t[:], self.c_tab.rearrange("p (a b) -> p a b", a=24), [], [ct.b])
        return ct

    def h_src(self, li, b, ti):
        c = self.cfg
        if li == 0:
            if ti < c.CT:
                return self.ctx_in[b, ti * 128:(ti + 1) * 128, :], None
            return self.x_in[b, (ti - c.CT) * 128:(ti - c.CT + 1) * 128, :], None
        return self.H[b, ti * 128:(ti + 1) * 128, :], self.dbuf(("H", b, ti))

    def h_mid(self, b, ti):
        return self.H[b, ti * 128:(ti + 1) * 128, :], self.dbuf(("H", b, ti))

    def h_dst(self, li, b, ti):
        c = self.cfg
        if li == self.L - 1 and ti >= c.CT:
            return self.out[b, (ti - c.CT) * 128:(ti - c.CT + 1) * 128, :], self.dbuf(("out", b, ti))
        return self.H[b, ti * 128:(ti + 1) * 128, :], self.dbuf(("H", b, ti))

    def load_w(self, wt, src, kchunks, ncols):
        v = src.rearrange("(k p) n -> p k n", p=128)
        for k in range(kchunks):
            self.dma(POOL, wt.t[:, k, :], v[:, k, :], [], [wt.b])

    def load_tab(self, tl, li, row, vec):
        self.dma(SP, tl.t[:], self.MOD[li, row, vec, :].partition_broadcast(128), [self.dbuf("MOD")], [tl.b])

    def phase_mod(self):
        P, c = self.P, self.cfg
        R3 = c.NB + 1
        P.begin_phase()
        cT = Tl(P, [128, R3, 8], F32, "cT")
        cTb = Tl(P, [128, 8, R3], BF16, "cTb")
        self.dma(SP, cT.t[:], self.crow.rearrange("r (p k) -> p r k", k=8), [], [cT.b])
        self.act(cTb.t[:].rearrange("p k r -> p r k"), cT.t[:], AF.Silu, [cT.b], [cTb.b])
        wr = Ring(P, 2, [128, 8, 1536], BF16, "adaw")
        pr = Ring(P, 2, [128, 512], F32, "psm", psum=True)
        raw = Tl(P, [R3, 6 * D], F32, "raw")
        adab = Tl(P, [R3, 6 * D], F32, "adab")
        ng = Tl(P, [R3, 4 * D], F32, "ng")
        mv = Ring(P, 2, [R3, 6, D], F32, "mv")
        modb = self.dbuf("MOD")
        for li in range(self.L):
            self.dma(SP, adab.t[:], self.ada_b[li, :].partition_broadcast(R3), [], [adab.b])
            self.dma(SP, ng.t[:], self.norm_g[li, :].partition_broadcast(R3), [], [ng.b])
            wv = self.ada_w[li].rearrange("(p k) n -> p k n", k=8)
            for j in range(4):
                w = wr.next()
                self.dma(POOL, w.t[:], wv[:, :, j * 1536:(j + 1) * 1536], [], [w.b])
                for n in range(3):
                    ps = pr.next()
                    for k in range(8):
                        self.mm(ps.t[0:R3, :], cTb.t[:, k, :], w.t[:, k, n * 512:(n + 1) * 512], k == 0, k == 7,
                                [cTb.b, w.b], [ps.b])
                    c0 = j * 1536 + n * 512
                    self.tt(DVE, raw.t[:, c0:c0 + 512], ps.t[0:R3, :], adab.t[:, c0:c0 + 512], ALU.add,
                            [ps.b, adab.b], [raw.b])
            m = mv.next()
            r = raw.t
            self.cp(DVE, m.t[:, 0, :], r[:, 0:D], [raw.b], [m.b])
            self.stt(m.t[:, 1, :], r[:, D:2 * D], 1.0, ng.t[:, 0:D], ALU.add, ALU.mult, [raw.b, ng.b], [m.b])
            self.tt(DVE, m.t[:, 2, :], r[:, 2 * D:3 * D], ng.t[:, D:2 * D], ALU.mult, [raw.b, ng.b], [m.b])
            self.cp(DVE, m.t[:, 3, :], r[:, 3 * D:4 * D], [raw.b], [m.b])
            self.stt(m.t[:, 4, :], r[:, 4 * D:5 * D], 1.0, ng.t[:, 2 * D:3 * D], ALU.add, ALU.mult, [raw.b, ng.b], [m.b])
            self.tt(DVE, m.t[:, 5, :], r[:, 5 * D:6 * D], ng.t[:, 3 * D:4 * D], ALU.mult, [raw.b, ng.b], [m.b])
            self.dma(SP, self.MOD[li], m.t[:], [m.b], [modb])
        P.end_phase()

    def pre_norm(self, h, sc, sh, u, ss, junk):
        self.act(junk.t[:], h.t[:], AF.Square, [h.b], [junk.b, ss.b], accum_out=ss.t[:, 0:1])
        self.rstd(ss.t[:, 1:2], ss.t[:, 0:1], 1.0 / D, EPS, [ss.b], [ss.b])
        self.tt(POOL, h.t[:], h.t[:], sc.t[:], ALU.mult, [h.b, sc.b], [h.b])
        self.stt(u.t[:], h.t[:], ss.t[:, 1:2], sh.t[:], ALU.mult, ALU.add, [h.b, ss.b, sh.b], [u.b])

    def transpose_to(self, src, nblk, psT, dst_ap, dst_b, src_b, eng=ACT, src_off=0):
        for k in range(nblk):
            self.tr(psT.t[:, k * 128:(k + 1) * 128], src[:, src_off + k * 128: src_off + (k + 1) * 128], [src_b], [psT.b])
        self.cp(eng, dst_ap, psT.t[:, 0:nblk * 128], [psT.b], [dst_b])

    def post_residual(self, psY, hres, gtab, hn, ss, junk, dst, dst_b):
        self.act(junk.t[:], psY.t[:], AF.Square, [psY.b], [junk.b, ss.b], accum_out=ss.t[:, 0:1])
        self.rstd(ss.t[:, 1:2], ss.t[:, 0:1], 1.0 / D, EPS, [ss.b], [ss.b])
        self.stt(hn.t[:], psY.t[:], ss.t[:, 1:2], gtab.t[:], ALU.mult, ALU.mult, [psY.b, ss.b, gtab.b], [hn.b])
        self.tt(POOL, hn.t[:], hn.t[:], hres.t[:], ALU.add, [hn.b, hres.b], [hn.b])
        self.dma(SP, dst, hn.t[:], [hn.b], [dst_b] if dst_b is not None else [])

    def tile_list(self, li, with_ctx=True):
        c = self.cfg
        last = (li == self.L - 1)
        out = []
        for b in range(c.NB):
            for ti in range(c.NT):
                if ti < c.CT and (last and not with_ctx):
                    continue
                out.append((b, ti))
        return out

    def phase_ffn(self, li):
        P, c = self.P, self.cfg
        last = (li == self.L - 1)
        P.begin_phase()
        wup = Tl(P, [128, 8, 5632], BF16, "wup")
        wdn = Tl(P, [128, 22, 1024], BF16, "wdn")
        self.load_w(wup, self.ffn_w_up[li], 8, 5632)
        self.load_w(wdn, self.ffn_w_down[li], 22, 1024)
        cw = Tl(P, [128, 22, 3], F32, "cw")
        cb = Tl(P, [128, 22], F32, "cb")
        self.dma(SP, cw.t[:], self.ffn_cw[li].rearrange("p (a b) -> p a b", b=3), [], [cw.b])
        self.dma(SP, cb.t[:], self.ffn_cb[li], [], [cb.b])
        tabs = [Tl(P, [128, D], F32, f"tab{i}") for i in range(3)]
        hr = Ring(P, 4, [128, D], F32, "h")
        ur = Ring(P, 2, [128, D], BF16, "u")
        junk = Tl(P, [128, D], BF16, "junk")
        ssr = Ring(P, 6, [128, 2], F32, "ss")
        uTr = Ring(P, 2, [128, 8, 256], BF16, "uT")
        mT = Tl(P, [128, 22, 256], BF16, "mT")
        cbr = Ring(P, 3, [128, 256], F32, "cbuf")
        hrel = Ring(P, 2, [128, D], F32, "hrel")
        hnr = Ring(P, 2, [128, D], F32, "hn")
        psT = Tl(P, [128, 1024], BF16, "psT", psum=True)
        psA = Ring(P, 3, [128, 512], F32, "psA", psum=True)
        psV = Ring(P, 2, [128, 512], F32, "psV", psum=True)
        psY = Tl(P, [128, 1024], F32, "psY", psum=True)
        sts = []
        for b in range(c.NB):
            if not last:
                for s_ in range(c.CT // 2):
                    sts.append((c.NB, b, [2 * s_, 2 * s_ + 1], False))
        for b in range(c.NB):
            for s_ in range(c.LT // 2):
                sts.append((b, b, [c.CT + 2 * s_, c.CT + 2 * s_ + 1], True))
        st_a = {"row": None}
        st_m = {"row": None}

        def ld(st):
            row, b, tis, grid = st
            hs = []
            for ti in tis:
                h = hr.next()
                src, sb_ = self.h_mid(b, ti)
                self.dma(SP, h.t[:], src, [sb_], [h.b])
                hs.append(h)
            return {"h": hs}

        def pre_a(st, cx):
            row, b, tis, grid = st
            if st_a["row"] != row:
                st_a["row"] = row
                self.load_tab(tabs[0], li, row, 3)
                self.load_tab(tabs[1], li, row, 4)
            cx["u"] = []
            for h in cx["h"]:
                u = ur.next()
                self.pre_norm(h, tabs[1], tabs[0], u, ssr.next(), junk)
                cx["u"].append(u)

        def pre_b(st, cx):
            uT = uTr.next()
            for j, u in enumerate(cx["u"]):
                for k in range(8):
                    self.tr(psT.t[:, k * 128:(k + 1) * 128], u.t[:, k * 128:(k + 1) * 128], [u.b], [psT.b])
                self.cp(ACT, uT.t[:, :, j * 128:(j + 1) * 128], psT.t[:].rearrange("p (k t) -> p k t", k=8),
                        [psT.b], [uT.b])
            cx["uT"] = uT

        def main(st, cx, hook):
            row, b, tis, grid = st
            uT = cx["uT"]
            if st_m["row"] != row:
                st_m["row"] = row
                self.load_tab(tabs[2], li, row, 5)
            hres = []
            for ti in tis:
                hh = hrel.next()
                src, sb_ = self.h_mid(b, ti)
                self.dma(SP, hh.t[:], src, [sb_], [hh.b])
                hres.append(hh)
            nr = 4 if grid else 1
            w = 256 // nr

            def A(fc):
                pa = psA.next()
                for k in range(8):
                    self.mm(pa.t[:, 0:256], wup.t[:, k, fc * 128:(fc + 1) * 128], uT.t[:, k, :], k == 0, k == 7,
                            [wup.b, uT.b], [pa.b])
                cbuf = cbr.next()
                self.act(cbuf.t[:], pa.t[:, 0:256], AF.Identity, [pa.b, cw.b, cb.b], [cbuf.b],
                         scale=cw.t[:, fc, 1:2], bias=cb.t[:, fc:fc + 1])
                pv = pa.t[:, 0:256].rearrange("p (r w) -> p r w", r=nr)
                cv = cbuf.t[:].rearrange("p (r w) -> p r w", r=nr)
                self.stt(cv[:, :, 1:w], pv[:, :, 0:w - 1], cw.t[:, fc, 0:1], cv[:, :, 1:w], ALU.mult, ALU.add,
                         [pa.b, cw.b, cbuf.b], [cbuf.b])
                self.stt(cv[:, :, 0:w - 1], pv[:, :, 1:w], cw.t[:, fc, 2:3], cv[:, :, 0:w - 1], ALU.mult, ALU.add,
                         [pa.b, cw.b, cbuf.b], [cbuf.b])
                self.act(mT.t[:, fc, :], cbuf.t[:], AF.Gelu_apprx_tanh, [cbuf.b], [mT.b])

            def V(fc):
                pvv = psV.next()
                for k in range(8):
                    self.mm(pvv.t[:, 0:256], wup.t[:, k, 2816 + fc * 128:2816 + (fc + 1) * 128], uT.t[:, k, :],
                            k == 0, k == 7, [wup.b, uT.b], [pvv.b])
                self.tt(DVE, mT.t[:, fc, :], pvv.t[:, 0:256], mT.t[:, fc, :], ALU.mult, [pvv.b, mT.b], [mT.b])

            A(0)
            A(1)
            for fc in range(22):
                if fc + 2 < 22:
                    A(fc + 2)
                V(fc)
                if fc == 12:
                    hook()
            for j, ti in enumerate(tis):
                for n in range(2):
                    for fc in range(22):
                        self.mm(psY.t[:, n * 512:(n + 1) * 512], mT.t[:, fc, j * 128:(j + 1) * 128],
                                wdn.t[:, fc, n * 512:(n + 1) * 512], fc == 0, fc == 21, [mT.b, wdn.b], [psY.b])
                hn = hnr.next()
                dst, db_ = self.h_dst(li, b, ti)
                self.post_residual(psY, hres[j], tabs[2], hn, ssr.next(), junk, dst, db_)
                if c.debug:
                    self.dma(SP, self.dbg[li, b, ti * 128:(ti + 1) * 128, :], hn.t[:], [hn.b], [])

        self.run_pipe(sts, ld, pre_a, pre_b, main)
        P.end_phase()

    def phase_mixer(self, li):
        kind = self.cfg.kinds[li]
        j = sum(1 for k in self.cfg.kinds[:li] if k == kind)
        if kind == 0:
            self.ret_proj(li, j)
            self.ret_scan(li, j, 0)
            self.ret_scan(li, j, 1)
            self.ret_readout(li, j)
        elif kind == 1:
            self.hg_proj(li, j)
            self.hg_scan(li, j, 0)
            self.hg_scan(li, j, 1)
            self.hg_readout(li, j)
        else:
            self.m2_proj(li, j)
            self.m2_conv(li, j)
            self.m2_scan(li, j, 0)
            self.m2_scan(li, j, 1)
            self.m2_readout(li, j)

    def proj_tiles(self):
        c = self.cfg
        out = [(c.NB, b, ti) for b in range(c.NB) for ti in range(c.CT)]
        out += [(b, b, ti) for b in range(c.NB) for ti in range(c.CT, c.NT)]
        return out

    def run_pipelined(self, items, pre, main, rowof=lambda it: it[0]):
        cur = pre(items[0])
        for i, it in enumerate(items):
            nxt = None
            if i + 1 < len(items) and rowof(items[i + 1]) == rowof(it):
                nxt = pre(items[i + 1])
            main(it, cur)
            if nxt is None and i + 1 < len(items):
                nxt = pre(items[i + 1])
            cur = nxt

    def run_pipe(self, items, ld, pre_a, pre_b, main):
        n = len(items)
        ctxs = {}

        def do_ld(k):
            if k < n:
                ctxs[k] = ld(items[k])

        def do_a(k):
            if k < n:
                pre_a(items[k], ctxs[k])

        def do_b(k):
            if k < n:
                pre_b(items[k], ctxs[k])

        do_ld(0)
        do_ld(1)
        do_a(0)
        do_b(0)
        for i in range(n):
            do_ld(i + 2)
            do_a(i + 1)
            main(items[i], ctxs[i], lambda k=i + 1: do_b(k))
            ctxs.pop(i)

    def make_stages(self, li, tabs, hr, ur, ssr, uTr, junk, psTr, vecs=(0, 1)):
        state = {"row": None}

        def ld(it):
            row, b, ti = it
            h = hr.next()
            src, sb_ = self.h_src(li, b, ti)
            self.dma(SP, h.t[:], src, [sb_] if sb_ is not None else [], [h.b])
            return {"h": h}

        def pre_a(it, cx):
            row, b, ti = it
            if state["row"] != row:
                state["row"] = row
                for i, v in enumerate(vecs):
                    self.load_tab(tabs[i], li, row, v)
            u = ur.next()
            self.pre_norm(cx["h"], tabs[1], tabs[0], u, ssr.next(), junk)
            cx["u"] = u

        def pre_b(it, cx):
            uT = uTr.next()
            psT = psTr.next()
            self.transpose_to(cx["u"].t, 8, psT, uT.t[:].rearrange("p k t -> p (k t)"), uT.b, cx["u"].b)
            cx["uT"] = uT

        return ld, pre_a, pre_b

    def load_wout_scaled(self, wo, src, kchunks, gn_src):
        P = self.P
        self.load_w(wo, src, kchunks, 1024)
        gnT = Tl(P, [128, kchunks], F32, "gnT")
        self.dma(SP, gnT.t[:], gn_src.rearrange("(k p) -> p k", p=128), [], [gnT.b], allow_slow_non_contiguous=True)
        for k in range(kchunks):
            self.act(wo.t[:, k, :], wo.t[:, k, :], AF.Copy, [wo.b, gnT.b], [wo.b], scale=gnT.t[:, k:k + 1])

    def ret_proj(self, li, j):
        P, c = self.P, self.cfg
        P.begin_phase()
        w = Tl(P, [128, 8, 6144], BF16, "w_in")
        self.load_w(w, self.ret_w_in[j], 8, 6144)
        tabs = [Tl(P, [128, D], F32, f"tab{i}") for i in range(2)]
        hr = Ring(P, 3, [128, D], F32, "h")
        ur = Ring(P, 2, [128, D], BF16, "u")
        junk = Tl(P, [128, D], BF16, "junk")
        ssr = Ring(P, 4, [128, 2], F32, "ss")
        uTr = Ring(P, 2, [128, 8, 128], BF16, "uT")
        qk32r = Ring(P, 2, [128, 2048], F32, "qk32")
        ropeB = Tl(P, [128, 2048], F32, "ropeB")
        qkrr = Ring(P, 2, [128, 2048], BF16, "qkr")
        qkTr = Ring(P, 2, [128, 2048], BF16, "qkT")
        vr = Ring(P, 2, [128, 2048], BF16, "vbf")
        gr = Ring(P, 2, [128, 2048], BF16, "gbf")
        rr = Ring(P, 2, [128, 512], F32, "rope")
        psTr = Ring(P, 2, [128, 1024], BF16, "psT", psum=True)
        psM = Ring(P, 6, [128, 512], F32, "psM", psum=True)
        ld, pre_a, pre_b = self.make_stages(li, tabs, hr, ur, ssr, uTr, junk, psTr)

        def main(it, cx, hook):
            uT = cx["uT"]
            row, b, ti = it
            tok = slice(ti * 128, (ti + 1) * 128)
            qk32, qkr, qkT, vb, gb = qk32r.next(), qkrr.next(), qkTr.next(), vr.next(), gr.next()
            lat = ti >= c.CT
            if lat:
                rt = rr.next()
                self.dma(SP, rt.t[:], self.c_rope[ti - c.CT], [], [rt.b])
            for n in range(12):
                ps = psM.next()
                for k in range(8):
                    self.mm(ps.t[:], uT.t[:, k, :], w.t[:, k, n * 512:(n + 1) * 512], k == 0, k == 7, [uT.b, w.b], [ps.b])
                if n < 4:
                    self.act(qk32.t[:, n * 512:(n + 1) * 512], ps.t[:], AF.Copy, [ps.b], [qk32.b],
                             scale=(0.0625 if n >= 2 else 1.0))
                elif n < 8:
                    self.act(vb.t[:, (n - 4) * 512:(n - 3) * 512], ps.t[:], AF.Copy, [ps.b], [vb.b])
                else:
                    self.act(gb.t[:, (n - 8) * 512:(n - 7) * 512], ps.t[:], AF.Silu, [ps.b], [gb.b])
                if n == 3:
                    if lat:
                        v5 = qk32.t[:].rearrange("p (s j h e) -> p s j h e", s=8, j=2, h=2, e=64)
                        b5 = ropeB.t[:].rearrange("p (s j h e) -> p s j h e", s=8, j=2, h=2, e=64)
                        sv = rt.t[:, 256:512].rearrange("p (j h e) -> p j h e", j=2, h=2, e=64)
                        for hh in range(2):
                            self.tt(POOL, b5[:, :, :, hh, :], v5[:, :, :, 1 - hh, :],
                                    sv[:, :, hh, :].unsqueeze(1).broadcast_to([128, 8, 2, 64]), ALU.mult,
                                    [qk32.b, rt.b], [ropeB.b])
                        q3 = qk32.t[:].rearrange("p (s f) -> p s f", s=8)
                        self.tt(DVE, q3, q3, rt.t[:, 0:256].unsqueeze(1).broadcast_to([128, 8, 256]), ALU.mult,
                                [qk32.b, rt.b, ropeB.b], [qk32.b])
                        self.tt(DVE, qkr.t[:], qk32.t[:], ropeB.t[:], ALU.add, [qk32.b, ropeB.b], [qkr.b])
                    else:
                        self.cp(DVE, qkr.t[:], qk32.t[:], [qk32.b], [qkr.b])
                    self.dma(SP, self.sKt[0, b, tok, :], qkr.t[:, 1024:2048], [qkr.b], [self.dbuf(("Kt", b, ti))])
                    for half in range(2):
                        psT = psTr.next()
                        self.transpose_to(qkr.t, 8, psT, qkT.t[:, half * 1024:(half + 1) * 1024], qkT.b, qkr.b,
                                          eng=DVE, src_off=half * 1024)
                    self.dma(SP, self.sQT[0, b, ti], qkT.t[:, 0:1024], [qkT.b], [self.dbuf(("QT", b, ti))])
                    self.dma(SP, self.sKT[0, b, ti], qkT.t[:, 1024:2048], [qkT.b], [self.dbuf(("KT", b, ti))])
                if n == 7:
                    self.dma(SP, self.sV[b, tok, :], vb.t[:], [vb.b], [self.dbuf(("V", b, ti))])
                    hook()
                if n == 11:
                    self.dma(SP, self.sG[b, tok, :], gb.t[:], [gb.b], [self.dbuf(("G", b, ti))])

        self.run_pipe(self.proj_tiles(), ld, pre_a, pre_b, main)
        P.end_phase()

    def scan_order(self, d):
        c = self.cfg
        if d == 0:
            return list(range(c.NT))
        return list(range(c.CT - 1, -1, -1)) + list(range(c.NT - 1, c.CT - 1, -1))

    def ret_scan(self, li, j, d):
        P, c = self.P, self.cfg
        last = (li == self.L - 1)
        P.begin_phase()
        ct = self.load_ctab()
        dec = Tl(P, [128, 8], F32, "dec")
        lg = Tl(P, [128, 8], F32, "lg")
        nlg = Tl(P, [128, 8], F32, "nlg")
        self.dma(SP, dec.t[:], self.ret_decay[j, :].partition_broadcast(128), [], [dec.b])
        self.act(nlg.t[:], dec.t[:], AF.Exp, [dec.b], [nlg.b], scale=-1.0)
        self.act(nlg.t[:], nlg.t[:], AF.Ln, [nlg.b], [nlg.b], bias=1.0)
        self.ts(DVE, lg.t[:], nlg.t[:], -1.0, None, ALU.mult, None, [nlg.b], [lg.b])
        Dm = Tl(P, [128, 4, 128], F32, "Dm")
        E = Tl(P, [128, 4, 128], F32, "E")
        wc = Tl(P, [128, 4], F32, "wc")
        gC = Tl(P, [128, 4], F32, "gC")
        for h in range(4):
            col = slice(d * 4 + h, d * 4 + h + 1)
            sc = lg.t[:, col] if d == 0 else nlg.t[:, col]
            self.act(Dm.t[:, h, :], ct.t[:, 0, :], AF.Exp, [ct.b, lg.b, nlg.b], [Dm.b], scale=sc)
            self.tt(DVE, Dm.t[:, h, :], Dm.t[:, h, :], ct.t[:, 1 + d, :], ALU.mult, [Dm.b, ct.b], [Dm.b])
            self.act(E.t[:, h, :], ct.t[:, 3 + d, :], AF.Exp, [ct.b, lg.b], [E.b], scale=lg.t[:, col])
            self.act(wc.t[:, h:h + 1], ct.t[:, 5, d:d + 1], AF.Exp, [ct.b, lg.b], [wc.b], scale=lg.t[:, col])
            self.act(gC.t[:, h:h + 1], ct.t[:, 5, 2:3], AF.Exp, [ct.b, lg.b], [gC.b], scale=lg.t[:, col])
        S32 = {}
        Sbf = {}
        for b in range(c.NB):
            for h in range(4):
                for cc in range(2):
                    S32[b, h, cc] = Tl(P, [128, 512], F32, "S32")
                    Sbf[b, h, cc] = Tl(P, [128, 512], BF16, "Sbf")
                    self.memset(POOL, S32[b, h, cc].t[:], 0.0, [S32[b, h, cc].b])
                    self.memset(DVE, Sbf[b, h, cc].t[:], 0.0, [Sbf[b, h, cc].b])
        qTr = Ring(P, 3, [128, 8, 128], BF16, "qT")
        kTr = Ring(P, 3, [128, 8, 128], BF16, "kT")
        ktr = Ring(P, 3, [128, 4, 256], BF16, "kt")
        vr = Ring(P, 3, [128, 2048], BF16, "v")
        otr = Ring(P, 2, [128, 2048], F32, "ot")
        oflr = Ring(P, 3, [128, 2048], F32, "ofl") if d == 1 else None
        PTr = Ring(P, 2, [128, 4, 128], BF16, "PT")
        qsr = Ring(P, 2, [128, 8, 128], BF16, "qs")
        kwr = Ring(P, 2, [128, 4, 256], BF16, "kw")
        psS = Tl(P, [128, 512], F32, "psS", psum=True)
        psO = Tl(P, [128, 2048], F32, "psO", psum=True)
        psK = Ring(P, 3, [128, 512], F32, "psK", psum=True)
        steps = [(ti, b) for ti in self.scan_order(d) for b in range(c.NB)]

        def ld(st):
            ti, b = st
            tok = slice(ti * 128, (ti + 1) * 128)
            qT, kT, kt, v = qTr.next(), kTr.next(), ktr.next(), vr.next()
            need_o = not (last and ti < c.CT)
            if need_o:
                self.dma(SP, qT.t[:].rearrange("p k t -> p (k t)"), self.sQT[0, b, ti], [self.dbuf(("QT", b, ti))], [qT.b])
                self.dma(SP, kT.t[:].rearrange("p k t -> p (k t)"), self.sKT[0, b, ti], [self.dbuf(("KT", b, ti))], [kT.b])
            self.dma(SP, kt.t[:].rearrange("p h f -> p (h f)"), self.sKt[0, b, tok, :], [self.dbuf(("Kt", b, ti))], [kt.b])
            self.dma(SP, v.t[:], self.sV[b, tok, :], [self.dbuf(("V", b, ti))], [v.b])
            ofl = None
            if d == 1 and need_o:
                ofl = oflr.next()
                self.dma(SP, ofl.t[:], self.sO[0, b, tok, :], [self.dbuf(("O", 0, b, ti))], [ofl.b])
            return qT, kT, kt, v, need_o, ofl

        def comp(st, tl):
            ti, b = st
            tok = slice(ti * 128, (ti + 1) * 128)
            qT, kT, kt, v, need_o, ofl = tl
            if need_o:
                for h in range(4):
                    for cc in range(2):
                        self.mm(psS.t[:, h * 128:(h + 1) * 128], kT.t[:, h * 2 + cc, :], qT.t[:, h * 2 + cc, :], cc == 0, cc == 1,
                                [kT.b, qT.b], [psS.b])
                PT = PTr.next()
                self.tt(DVE, PT.t[:], psS.t[:].rearrange("p (h t) -> p h t", h=4), Dm.t[:], ALU.mult, [psS.b, Dm.b], [PT.b])
                qs = qsr.next()
                self.tt(POOL, qs.t[:].rearrange("p (h c) t -> p h c t", h=4), qT.t[:].rearrange("p (h c) t -> p h c t", h=4),
                        E.t[:].unsqueeze(2).broadcast_to([128, 4, 2, 128]), ALU.mult, [qT.b, E.b], [qs.b])
            kw = kwr.next()
            self.tt(POOL, kw.t[:], kt.t[:], wc.t[:, 0:4].unsqueeze(2).broadcast_to([128, 4, 256]), ALU.mult, [kt.b, wc.b], [kw.b])
            if need_o:
                for h in range(4):
                    vh = v.t[:, h * 512:(h + 1) * 512]
                    po = psO.t[:, h * 512:(h + 1) * 512]
                    self.mm(po, PT.t[:, h, :], vh, True, False, [PT.b, v.b], [psO.b])
                    for cc in range(2):
                        self.mm(po, qs.t[:, h * 2 + cc, :], Sbf[b, h, cc].t[:], False, cc == 1,
                                [qs.b, Sbf[b, h, cc].b], [psO.b])
                ot = otr.next()
                for h in range(4):
                    self.act(ot.t[:, h * 512:(h + 1) * 512], psO.t[:, h * 512:(h + 1) * 512], AF.Copy, [psO.b], [ot.b])
                if d == 1:
                    self.tt(POOL, ot.t[:], ot.t[:], ofl.t[:], ALU.add, [ot.b, ofl.b], [ot.b])
                self.dma(SP, self.sO[d, b, tok, :], ot.t[:], [ot.b], [self.dbuf(("O", d, b, ti))])
            for h in range(4):
                vh = v.t[:, h * 512:(h + 1) * 512]
                for cc in range(2):
                    pk = psK.next()
                    self.mm(pk.t[:], kw.t[:, h, cc * 128:(cc + 1) * 128], vh, True, True, [kw.b, v.b], [pk.b])
                    s32 = S32[b, h, cc]
                    self.stt(s32.t[:], s32.t[:], gC.t[:, h:h + 1], pk.t[:], ALU.mult, ALU.add, [s32.b, gC.b, pk.b], [s32.b])
                    self.cp(ACT, Sbf[b, h, cc].t[:], s32.t[:], [s32.b], [Sbf[b, h, cc].b])

        self.run_pipelined(steps, ld, comp, rowof=lambda it: 0)
        P.end_phase()

    def readout_main(self, li, wo, nk, g1, psYr, hnr, ssr, junk):
        state = {"row": None}

        def main(it, cx, hook):
            row, b, ti = it
            onT = cx["onT"]
            if state["row"] != row:
                state["row"] = row
                self.load_tab(g1, li, row, 2)
            psY = psYr.next()
            for n in range(2):
                for k in range(nk):
                    self.mm(psY.t[:, n * 512:(n + 1) * 512], onT.t[:, k, :], wo.t[:, k, n * 512:(n + 1) * 512],
                            k == 0, k == nk - 1, [onT.b, wo.b], [psY.b])
                if n == 0:
                    hook()
            hn = hnr.next()
            dst, db_ = self.h_mid(b, ti)
            self.post_residual(psY, cx["h"], g1, hn, ssr.next(), junk, dst, db_)
        return main

    def readout_items(self, li):
        keep = set(self.tile_list(li, with_ctx=False))
        return [it for it in self.proj_tiles() if (it[1], it[2]) in keep]

    def ret_readout(self, li, j):
        P, c = self.P, self.cfg
        P.begin_phase()
        wo = Tl(P, [128, 16, 1024], BF16, "w_out")
        self.load_wout_scaled(wo, self.ret_w_out[j], 16, self.ret_gn[j])
        g1 = Tl(P, [128, D], F32, "g1")
        ofr = Ring(P, 3, [128, 2048], F32, "of")
        gr = Ring(P, 3, [128, 2048], BF16, "g")
        hr = Ring(P, 3, [128, D], F32, "h")
        onr = Ring(P, 2, [128, 2048], BF16, "on")
        onTr = Ring(P, 2, [128, 16, 128], BF16, "onT")
        str_ = Ring(P, 2, [128, 4, 6], F32, "bst")
        mvr = Ring(P, 2, [128, 4, 2], F32, "mv")
        rsr = Ring(P, 2, [128, 4], F32, "rs")
        hnr = Ring(P, 2, [128, D], F32, "hn")
        ssr = Ring(P, 4, [128, 2], F32, "ss")
        junk = Tl(P, [128, D], BF16, "junk")
        psTr = Ring(P, 2, [128, 1024], BF16, "psT", psum=True)
        psYr = Ring(P, 2, [128, 1024], F32, "psY", psum=True)

        def ld(it):
            row, b, ti = it
            tok = slice(ti * 128, (ti + 1) * 128)
            of, g, h = ofr.next(), gr.next(), hr.next()
            self.dma(SP, of.t[:], self.sO[1, b, tok, :], [self.dbuf(("O", 1, b, ti))], [of.b])
            self.dma(SP, g.t[:], self.sG[b, tok, :], [self.dbuf(("G", b, ti))], [g.b])
            src, sb_ = self.h_src(li, b, ti)
            self.dma(SP, h.t[:], src, [sb_] if sb_ is not None else [], [h.b])
            return {"of": of, "g": g, "h": h}

        def pre_a(it, cx):
            of, g = cx["of"], cx["g"]
            on, bst, mv, rs = onr.next(), str_.next(), mvr.next(), rsr.next()
            for h in range(4):
                oh = of.t[:, h * 512:(h + 1) * 512]
                self.P.op(DVE, lambda e, o=bst.t[:, h, :], i=oh: e.bn_stats(out=o, in_=i), [of.b], [bst.b])
            for h in range(4):
                self.P.op(DVE, lambda e, o=mv.t[:, h, :], i=bst.t[:, h, :]: e.bn_aggr(out=o, in_=i), [bst.b], [mv.b])
            self.rstd(rs.t[:], mv.t[:, :, 1], 1.0, EPS, [mv.b], [rs.b], n=4)
            for h in range(4):
                oh = of.t[:, h * 512:(h + 1) * 512]
                self.stt(oh, oh, mv.t[:, h, 0:1], g.t[:, h * 512:(h + 1) * 512], ALU.subtract, ALU.mult,
                         [of.b, mv.b, g.b], [of.b])
            for h in range(4):
                oh = of.t[:, h * 512:(h + 1) * 512]
                self.act(on.t[:, h * 512:(h + 1) * 512], oh, AF.Copy, [of.b, rs.b], [on.b], scale=rs.t[:, h:h + 1])
            cx["on"] = on

        def pre_b(it, cx):
            on, onT = cx["on"], onTr.next()
            for half in range(2):
                psT = psTr.next()
                self.transpose_to(on.t, 8, psT, onT.t[:, half * 8:(half + 1) * 8, :].rearrange("p k t -> p (k t)"),
                                  onT.b, on.b, eng=(ACT if half == 0 else DVE), src_off=half * 1024)
            cx["onT"] = onT

        self.run_pipe(self.readout_items(li), ld, pre_a, pre_b, self.readout_main(li, wo, 16, g1, psYr, hnr, ssr, junk))
        P.end_phase()

    def hg_proj(self, li, j):
        P, c = self.P, self.cfg
        L = self.L
        P.begin_phase()
        ct = self.load_ctab()
        w = Tl(P, [128, 8, 5120], BF16, "w_in")
        self.load_w(w, self.hg_w_in[j], 8, 5120)
        lbx = Tl(P, [128, L, D], F32, "lbx")
        lb = Tl(P, [128, D], F32, "lb")
        omlb = Tl(P, [128, D], F32, "omlb")
        den = Tl(P, [128, D], F32, "den")
        self.dma(SP, lbx.t[:].rearrange("p l d -> p (l d)"), self.hg_lb.rearrange("l d -> (l d)").partition_broadcast(128), [], [lbx.b])
        self.act(lbx.t[:], lbx.t[:], AF.Exp, [lbx.b], [lbx.b])
        self.cp(DVE, den.t[:], lbx.t[:, 0, :], [lbx.b], [den.b])
        self.memset(DVE, lb.t[:], 0.0, [lb.b])
        for r in range(1, L):
            self.tt(DVE, den.t[:], den.t[:], lbx.t[:, r, :], ALU.add, [den.b, lbx.b], [den.b])
            if r <= li:
                self.tt(DVE, lb.t[:], lb.t[:], lbx.t[:, r, :], ALU.add, [lb.b, lbx.b], [lb.b])
        self.P.op(DVE, lambda e: e.reciprocal(out=den.t[:], in_=den.t[:]), [den.b], [den.b])
        self.tt(DVE, lb.t[:], lb.t[:], den.t[:], ALU.mult, [lb.b, den.b], [lb.b])
        self.ts(DVE, omlb.t[:], lb.t[:], -1.0, 1.0, ALU.mult, ALU.add, [lb.b], [omlb.b])
        tabs = [Tl(P, [128, D], F32, f"tab{i}") for i in range(2)]
        hr = Ring(P, 3, [128, D], F32, "h")
        ur = Ring(P, 2, [128, D], BF16, "u")
        junk = Tl(P, [128, D], BF16, "junk")
        ssr = Ring(P, 4, [128, 2], F32, "ss")
        uTr = Ring(P, 2, [128, 8, 128], BF16, "uT")
        qs = Tl(P, [128, D], F32, "qs")
        a32 = [Tl(P, [128, D], F32, f"a32{d}") for d in range(2)]
        la32 = [Tl(P, [128, D], F32, f"la{d}") for d in range(2)]
        k32 = a32
        e32r = Ring(P, 2, [128, D], F32, "e32")
        qtr = Ring(P, 2, [128, D], BF16, "qt")
        ktr = Ring(P, 2, [128, D], BF16, "kt")
        stg = Ring(P, 4, [128, D], BF16, "stg")
        vr = Ring(P, 2, [128, D], BF16, "vbf")
        gr = Ring(P, 2, [128, D], BF16, "gbf")
        csr = Ring(P, 2, [128, 48], F32, "cs")
        psTr = Ring(P, 2, [128, 1024], BF16, "psT", psum=True)
        psM = Ring(P, 5, [128, 512], F32, "psM", psum=True)
        psC = Tl(P, [128, 512], F32, "psC", psum=True)
        ld, pre_a, pre_b = self.make_stages(li, tabs, hr, ur, ssr, uTr, junk, psTr)

        def main(it, cx, hook):
            uT = cx["uT"]
            row, b, ti = it
            tok = slice(ti * 128, (ti + 1) * 128)
            vb, gb = vr.next(), gr.next()
            for n in range(10):
                if n == 7:
                    hook()
                ps = psM.next()
                for k in range(8):
                    self.mm(ps.t[:], uT.t[:, k, :], w.t[:, k, n * 512:(n + 1) * 512], k == 0, k == 7, [uT.b, w.b], [ps.b])
                cs_ = slice((n % 2) * 512, (n % 2) * 512 + 512)
                if n < 2:
                    self.act(qs.t[:, cs_], ps.t[:], AF.Silu, [ps.b], [qs.b])
                elif n < 6:
                    d = (n - 2) // 2
                    self.act(a32[d].t[:, cs_], ps.t[:], AF.Sigmoid, [ps.b], [a32[d].b])
                    self.tt(DVE, a32[d].t[:, cs_], a32[d].t[:, cs_], omlb.t[:, cs_], ALU.mult, [a32[d].b, omlb.b], [a32[d].b])
                    self.tt(POOL, a32[d].t[:, cs_], a32[d].t[:, cs_], lb.t[:, cs_], ALU.add, [a32[d].b, lb.b], [a32[d].b])
                    self.act(la32[d].t[:, cs_], a32[d].t[:, cs_], AF.Ln, [a32[d].b], [la32[d].b])
                    self.ts(POOL, a32[d].t[:, cs_], a32[d].t[:, cs_], -1.0, 1.0, ALU.mult, ALU.add, [a32[d].b], [a32[d].b])
                elif n < 8:
                    self.cp(DVE, vb.t[:, cs_], ps.t[:], [ps.b], [vb.b])
                else:
                    self.act(gb.t[:, cs_], ps.t[:], AF.Silu, [ps.b], [gb.b])
            self.dma(SP, self.sV[b, tok, 0:D], vb.t[:], [vb.b], [self.dbuf(("V", b, ti))])
            self.dma(SP, self.sG[b, tok, 0:D], gb.t[:], [gb.b], [self.dbuf(("G", b, ti))])
            for d in range(2):
                qt, kt = qtr.next(), ktr.next()
                for n in range(2):
                    cs_ = slice(n * 512, n * 512 + 512)
                    ps = psM.next()
                    self.mm(ps.t[:], ct.t[:, 8 + d, :], la32[d].t[:, cs_], True, True, [ct.b, la32[d].b], [ps.b])
                    e1, e2 = e32r.next(), e32r.next()
                    self.act(e1.t[:, 0:512], ps.t[:], AF.Exp, [ps.b], [e1.b])
                    self.act(e2.t[:, 0:512], ps.t[:], AF.Exp, [ps.b], [e2.b], scale=-1.0)
                    self.tt(DVE, qt.t[:, cs_], qs.t[:, cs_], e1.t[:, 0:512], ALU.mult, [qs.b, e1.b], [qt.b])
                    self.tt(POOL, kt.t[:, cs_], k32[d].t[:, cs_], e2.t[:, 0:512], ALU.mult, [k32[d].b, e2.b], [kt.b])
                for h in range(8):
                    self.mm(psC.t[:, h * 6:(h + 1) * 6], la32[d].t[:, h * 128:(h + 1) * 128], ct.t[:, 10 + d, 0:6],
                            True, True, [la32[d].b, ct.b], [psC.b])
                cs = csr.next()
                self.act(cs.t[:], psC.t[:, 0:48], AF.Exp, [psC.b], [cs.b])
                self.dma(SP, self.sCS[d, b, ti], cs.t[:], [cs.b], [self.dbuf(("CS", d, b, ti))])
                self.dma(SP, self.sKt[d, b, tok, :], kt.t[:], [kt.b], [self.dbuf(("Kt", d, b, ti))])
                for which, src in ((0, qt), (1, kt)):
                    psT = psTr.next()
                    st = stg.next()
                    self.transpose_to(src.t, 8, psT, st.t[:], st.b, src.b, eng=DVE)
                    dst = (self.sQT if which == 0 else self.sKT)[d, b, ti]
                    self.dma(SP, dst, st.t[:], [st.b], [self.dbuf(("QT" if which == 0 else "KT", d, b, ti))])

        self.run_pipe(self.proj_tiles(), ld, pre_a, pre_b, main)
        P.end_phase()

    def hg_scan(self, li, j, d):
        P, c = self.P, self.cfg
        last = (li == self.L - 1)
        P.begin_phase()
        ct = self.load_ctab()
        S = {}
        for b in range(c.NB):
            for hh in range(2):
                S[b, hh] = Tl(P, [128, 4, 128], F32, "S")
                self.memset(POOL, S[b, hh].t[:], 0.0, [S[b, hh].b])
        qTr = Ring(P, 3, [128, 8, 128], BF16, "qT")
        kTr = Ring(P, 3, [128, 8, 128], BF16, "kT")
        ktr = Ring(P, 3, [128, D], BF16, "kt")
        vr = Ring(P, 3, [128, D], BF16, "v")
        csr = Ring(P, 3, [128, 8, 2, 3], F32, "cs")
        otr = Ring(P, 2, [128, D], F32, "ot")
        oflr = Ring(P, 3, [128, D], F32, "ofl") if d == 1 else None
        PTr = Ring(P, 2, [128, 4, 128], BF16, "PT")
        Sxr = Ring(P, 4, [128, 4, 128], BF16, "Sx")
        tmpr = Ring(P, 2, [128, 4, 128], F32, "tmp")
        psS = Ring(P, 2, [128, 512], F32, "psS", psum=True)
        psO = Ring(P, 2, [128, 512], F32, "psO", psum=True)
        psK = Ring(P, 4, [128, 512], F32, "psK", psum=True)
        steps = [(ti, b) for ti in self.scan_order(d) for b in range(c.NB)]
        corder = (0, 1) if d == 0 else (1, 0)

        def ld(st):
            ti, b = st
            tok = slice(ti * 128, (ti + 1) * 128)
            qT, kT, kt, v, cs = qTr.next(), kTr.next(), ktr.next(), vr.next(), csr.next()
            need_o = not (last and ti < c.CT)
            self.dma(SP, qT.t[:].rearrange("p k t -> p (k t)"), self.sQT[d, b, ti], [self.dbuf(("QT", d, b, ti))], [qT.b])
            self.dma(SP, kT.t[:].rearrange("p k t -> p (k t)"), self.sKT[d, b, ti], [self.dbuf(("KT", d, b, ti))], [kT.b])
            self.dma(SP, kt.t[:], self.sKt[d, b, tok, :], [self.dbuf(("Kt", d, b, ti))], [kt.b])
            self.dma(SP, v.t[:], self.sV[b, tok, 0:D], [self.dbuf(("V", b, ti))], [v.b])
            self.dma(SP, cs.t[:].rearrange("p h c k -> p (h c k)"), self.sCS[d, b, ti], [self.dbuf(("CS", d, b, ti))], [cs.b])
            ofl = None
            if d == 1 and need_o:
                ofl = oflr.next()
                self.dma(SP, ofl.t[:], self.sO[0, b, tok, 0:D], [self.dbuf(("O", 0, b, ti))], [ofl.b])
            return qT, kT, kt, v, cs, need_o, ofl

        def comp(st, tl):
            ti, b = st
            tok = slice(ti * 128, (ti + 1) * 128)
            qT, kT, kt, v, cs, need_o, ofl = tl
            ot = otr.next() if need_o else None
            for hh in range(2):
                Sb = S[b, hh]
                hs = slice(hh * 4, hh * 4 + 4)

                def bc(kind, cc):
                    return cs.t[:, hs, cc, kind:kind + 1].broadcast_to([128, 4, 128])

                if need_o:
                    ps = psS.next()
                    for hl in range(4):
                        h = hh * 4 + hl
                        self.mm(ps.t[:, hl * 128:(hl + 1) * 128], kT.t[:, h, :], qT.t[:, h, :], True, True, [kT.b, qT.b], [ps.b])
                    PT = PTr.next()
                    self.tt(DVE, PT.t[:], ps.t[:].rearrange("p (h t) -> p h t", h=4),
                            ct.t[:, 6 + d, :].unsqueeze(1).broadcast_to([128, 4, 128]), ALU.mult, [ps.b, ct.b], [PT.b])
                    po = psO.next()
                    for hl in range(4):
                        h = hh * 4 + hl
                        self.mm(po.t[:, hl * 128:(hl + 1) * 128], PT.t[:, hl, :], v.t[:, h * 128:(h + 1) * 128], hl == 0, False,
                                [PT.b, v.b], [po.b], skip=True)
                for ci, cc in enumerate(corder):
                    rows = slice(cc * 64, cc * 64 + 64)
                    if need_o:
                        Sx = Sxr.next()
                        self.tt(POOL, Sx.t[:], Sb.t[:], bc(0, cc), ALU.mult, [Sb.b, cs.b], [Sx.b])
                        for hl in range(4):
                            h = hh * 4 + hl
                            self.mm(po.t[rows, hl * 128:(hl + 1) * 128], qT.t[:, h, rows], Sx.t[:, hl, :], False,
                                    True, [qT.b, Sx.b], [po.b], skip=True)
                    pk = psK.next()
                    for hl in range(4):
                        h = hh * 4 + hl
                        self.mm(pk.t[:, hl * 128:(hl + 1) * 128], kt.t[rows, h * 128:(h + 1) * 128],
                                v.t[rows, h * 128:(h + 1) * 128], True, True, [kt.b, v.b], [pk.b])
                    tmp = tmpr.next()
                    self.tt(DVE, tmp.t[:], pk.t[:].rearrange("p (h t) -> p h t", h=4), bc(2, cc), ALU.mult, [pk.b, cs.b], [tmp.b])
                    self.tt(POOL, Sb.t[:], Sb.t[:], bc(1, cc), ALU.mult, [Sb.b, cs.b], [Sb.b])
                    self.tt(DVE, Sb.t[:], Sb.t[:], tmp.t[:], ALU.add, [Sb.b, tmp.b], [Sb.b])
                if need_o and d == 0:
                    self.act(ot.t[:, hh * 512:(hh + 1) * 512], po.t[:], AF.Copy, [po.b], [ot.b])
                elif need_o:
                    self.tt(DVE, ot.t[:, hh * 512:(hh + 1) * 512], po.t[:], ofl.t[:, hh * 512:(hh + 1) * 512], ALU.add,
                            [po.b, ofl.b], [ot.b])
            if need_o:
                self.dma(SP, self.sO[d, b, tok, 0:D], ot.t[:], [ot.b], [self.dbuf(("O", d, b, ti))])

        self.run_pipelined(steps, ld, comp, rowof=lambda it: 0)
        P.end_phase()

    def hg_readout(self, li, j):
        P, c = self.P, self.cfg
        P.begin_phase()
        wo = Tl(P, [128, 8, 1024], BF16, "w_out")
        self.load_wout_scaled(wo, self.hg_w_out[j], 8, self.hg_gn[j])
        g1 = Tl(P, [128, D], F32, "g1")
        ofr = Ring(P, 3, [128, D], F32, "of")
        gr = Ring(P, 3, [128, D], BF16, "g")
        hr = Ring(P, 3, [128, D], F32, "h")
        sqr = Ring(P, 2, [128, D], F32, "sq")
        onr = Ring(P, 2, [128, D], BF16, "on")
        onTr = Ring(P, 2, [128, 8, 128], BF16, "onT")
        rsr = Ring(P, 2, [128, 8], F32, "rs")
        hnr = Ring(P, 2, [128, D], F32, "hn")
        ssr = Ring(P, 4, [128, 2], F32, "ss")
        junk = Tl(P, [128, D], BF16, "junk")
        psTr = Ring(P, 2, [128, 1024], BF16, "psT", psum=True)
        psYr = Ring(P, 2, [128, 1024], F32, "psY", psum=True)

        def ld(it):
            row, b, ti = it
            tok = slice(ti * 128, (ti + 1) * 128)
            of, g, h = ofr.next(), gr.next(), hr.next()
            self.dma(SP, of.t[:], self.sO[1, b, tok, 0:D], [self.dbuf(("O", 1, b, ti))], [of.b])
            self.dma(SP, g.t[:], self.sG[b, tok, 0:D], [self.dbuf(("G", b, ti))], [g.b])
            src, sb_ = self.h_src(li, b, ti)
            self.dma(SP, h.t[:], src, [sb_] if sb_ is not None else [], [h.b])
            return {"of": of, "g": g, "h": h}

        def pre_a(it, cx):
            of, g = cx["of"], cx["g"]
            on, rs, sq = onr.next(), rsr.next(), sqr.next()
            self.tt(POOL, sq.t[:], of.t[:], of.t[:], ALU.mult, [of.b], [sq.b])
            self.P.op(DVE, lambda e, o=rs.t[:], i=sq.t[:].rearrange("p (h v) -> p h v", h=8):
                      e.tensor_reduce(out=o, in_=i, axis=mybir.AxisListType.X, op=ALU.add), [sq.b], [rs.b])
            self.rstd(rs.t[:], rs.t[:], 1.0 / 128, EPS, [rs.b], [rs.b], n=8)
            self.tt(DVE, of.t[:], of.t[:], g.t[:], ALU.mult, [of.b, g.b], [of.b])
            self.tt(DVE, on.t[:].rearrange("p (h v) -> p h v", h=8), of.t[:].rearrange("p (h v) -> p h v", h=8),
                    rs.t[:].unsqueeze(2).broadcast_to([128, 8, 128]), ALU.mult, [of.b, rs.b], [on.b])
            cx["on"] = on

        def pre_b(it, cx):
            on, onT = cx["on"], onTr.next()
            psT = psTr.next()
            self.transpose_to(on.t, 8, psT, onT.t[:].rearrange("p k t -> p (k t)"), onT.b, on.b)
            cx["onT"] = onT

        self.run_pipe(self.readout_items(li), ld, pre_a, pre_b, self.readout_main(li, wo, 8, g1, psYr, hnr, ssr, junk))
        P.end_phase()

    def x_row(self, ti):
        c = self.cfg
        if ti < c.CT:
            return 1 + ti * 128
        return c.CTX + 3 + (ti - c.CT) * 128

    def m2_proj(self, li, j):
        P, c = self.P, self.cfg
        P.begin_phase()
        w = Tl(P, [128, 8, 5184], BF16, "w_in")
        self.load_w(w, self.m2_w_in[j], 8, 5184)
        tabs = [Tl(P, [128, D], F32, f"tab{i}") for i in range(2)]
        dtb = Tl(P, [128, 64], F32, "dtb")
        aneg = Tl(P, [128, 64], F32, "aneg")
        self.dma(SP, dtb.t[:], self.m2_dt_bias[j, :].partition_broadcast(128), [], [dtb.b])
        self.dma(SP, aneg.t[:], self.m2_a_log[j, :].partition_broadcast(128), [], [aneg.b])
        self.act(aneg.t[:], aneg.t[:], AF.Exp, [aneg.b], [aneg.b])
        self.ts(DVE, aneg.t[:], aneg.t[:], -1.0, None, ALU.mult, None, [aneg.b], [aneg.b])
        zero = Tl(P, [1, 3072], F32, "zero")
        self.memset(DVE, zero.t[:], 0.0, [zero.b])
        for b in range(c.NB):
            for r in (0, c.CTX + 1, c.CTX + 2, c.CTX + c.LAT + 3):
                self.dma(SP, self.sX[b, r:r + 1, :], zero.t[:], [zero.b], [self.dbuf(("Xpad", b, r))])
        hr = Ring(P, 3, [128, D], F32, "h")
        ur = Ring(P, 2, [128, D], BF16, "u")
        junk = Tl(P, [128, D], BF16, "junk")
        ssr = Ring(P, 4, [128, 2], F32, "ss")
        uTr = Ring(P, 2, [128, 8, 128], BF16, "uT")
        zr = Ring(P, 2, [128, 2048], BF16, "zb")
        xr = Ring(P, 2, [128, 3072], F32, "xbc")
        dr = Ring(P, 2, [128, 128], F32, "dtla")
        psTr = Ring(P, 2, [128, 1024], BF16, "psT", psum=True)
        psM = Ring(P, 6, [128, 512], F32, "psM", psum=True)
        ld, pre_a, pre_b = self.make_stages(li, tabs, hr, ur, ssr, uTr, junk, psTr)

        def main(it, cx, hook):
            uT = cx["uT"]
            row, b, ti = it
            tok = slice(ti * 128, (ti + 1) * 128)
            zb, xb, dl = zr.next(), xr.next(), dr.next()
            for n in range(11):
                if n == 6:
                    hook()
                ps = psM.next()
                wd = 512 if n < 10 else 64
                for k in range(8):
                    self.mm(ps.t[:, 0:wd], uT.t[:, k, :], w.t[:, k, n * 512:n * 512 + wd], k == 0, k == 7, [uT.b, w.b], [ps.b])
                if n < 4:
                    self.act(zb.t[:, n * 512:(n + 1) * 512], ps.t[:], AF.Silu, [ps.b], [zb.b])
                elif n < 10:
                    eng = ACT if n % 2 == 0 else DVE
                    self.cp(eng, xb.t[:, (n - 4) * 512:(n - 3) * 512], ps.t[:], [ps.b], [xb.b])
                else:
                    self.tt(DVE, dl.t[:, 0:64], ps.t[:, 0:64], dtb.t[:], ALU.add, [ps.b, dtb.b], [dl.b])
                    self.act(dl.t[:, 0:64], dl.t[:, 0:64], AF.Exp, [dl.b], [dl.b])
                    self.act(dl.t[:, 0:64], dl.t[:, 0:64], AF.Ln, [dl.b], [dl.b], bias=1.0)
                    self.tt(DVE, dl.t[:, 64:128], dl.t[:, 0:64], aneg.t[:], ALU.mult, [dl.b, aneg.b], [dl.b])
            self.dma(SP, self.sG[b, tok, :], zb.t[:], [zb.b], [self.dbuf(("G", b, ti))])
            r0 = self.x_row(ti)
            self.dma(SP, self.sX[b, r0:r0 + 128, :], xb.t[:], [xb.b], [self.dbuf(("X", b, ti))])
            self.dma(SP, self.sDT[b, tok, :], dl.t[:], [dl.b], [self.dbuf(("DT", b, ti))])

        self.run_pipe(self.proj_tiles(), ld, pre_a, pre_b, main)
        P.end_phase()

    def m2_conv(self, li, j):
        P, c = self.P, self.cfg
        P.begin_phase()
        cw = Tl(P, [128, 3, 3072], F32, "cw")
        cb = Tl(P, [128, 3072], F32, "cb")
        self.dma(SP, cw.t[:].rearrange("p a b -> p (a b)"), self.m2_conv_w[j].rearrange("a b -> (a b)").partition_broadcast(128), [], [cw.b])
        self.dma(SP, cb.t[:], self.m2_conv_b[j, :].partition_broadcast(128), [], [cb.b])
        xr = [Ring(P, 2, [128, 3072], F32, f"x{i}") for i in range(3)]
        actr = Ring(P, 2, [128, 3072], BF16, "act")
        stg = Ring(P, 2, [128, 1024], BF16, "stg")
        psTr = Ring(P, 2, [128, 1024], BF16, "psT", psum=True)
        items = [(b, ti) for b in range(c.NB) for ti in range(c.NT)]

        def ld(it):
            b, ti = it
            r0 = self.x_row(ti)
            xs = [xr[i].next() for i in range(3)]
            deps = [self.dbuf(("X", b, t2)) for t2 in range(c.NT)] + [self.dbuf(("Xpad", b, r)) for r in (0, c.CTX + 1, c.CTX + 2, c.CTX + c.LAT + 3)]
            for i in range(3):
                self.dma(SP, xs[i].t[:], self.sX[b, r0 - 1 + i:r0 - 1 + i + 128, :], deps, [xs[i].b])
            return xs

        def comp(it, xs):
            b, ti = it
            tok = slice(ti * 128, (ti + 1) * 128)
            x0, x1, x2 = xs
            self.tt(POOL, x0.t[:], x0.t[:], cw.t[:, 0, :], ALU.mult, [x0.b, cw.b], [x0.b])
            self.tt(DVE, x1.t[:], x1.t[:], cw.t[:, 1, :], ALU.mult, [x1.b, cw.b], [x1.b])
            self.tt(DVE, x2.t[:], x2.t[:], cw.t[:, 2, :], ALU.mult, [x2.b, cw.b], [x2.b])
            self.tt(DVE, x1.t[:], x1.t[:], cb.t[:], ALU.add, [x1.b, cb.b], [x1.b])
            self.tt(DVE, x1.t[:], x1.t[:], x2.t[:], ALU.add, [x1.b, x2.b], [x1.b])
            self.tt(DVE, x1.t[:], x1.t[:], x0.t[:], ALU.add, [x1.b, x0.b], [x1.b])
            a = actr.next()
            self.act(a.t[:], x1.t[:], AF.Silu, [x1.b], [a.b])
            self.dma(SP, self.sV[b, tok, :], a.t[:, 0:2048], [a.b], [self.dbuf(("V", b, ti))])
            self.dma(SP, self.sKt[0, b, tok, 0:512], a.t[:, 2048:2560], [a.b], [self.dbuf(("Kt", b, ti))])
            psT = psTr.next()
            st = stg.next()
            self.transpose_to(a.t, 8, psT, st.t[:], st.b, a.b, eng=ACT, src_off=2048)
            self.dma(SP, self.sKT[0, b, ti][:, 0:512], st.t[:, 0:512], [st.b], [self.dbuf(("KT", b, ti))])
            self.dma(SP, self.sQT[0, b, ti][:, 0:512], st.t[:, 512:1024], [st.b], [self.dbuf(("QT", b, ti))])

        self.run_pipelined(items, ld, comp, rowof=lambda it: 0)
        P.end_phase()

    def m2_scan(self, li, j, d):
        P, c = self.P, self.cfg
        last = (li == self.L - 1)
        P.begin_phase()
        ct = self.load_ctab()
        S32, Sbf = {}, {}
        for b in range(c.NB):
            for g in range(4):
                S32[b, g] = Tl(P, [128, 8, 64], F32, "S32")
                Sbf[b, g] = Tl(P, [128, 512], BF16, "Sbf")
                self.memset(POOL, S32[b, g].t[:], 0.0, [S32[b, g].b])
                self.memset(DVE, Sbf[b, g].t[:], 0.0, [Sbf[b, g].b])
        CTr = Ring(P, 3, [128, 4, 128], BF16, "CT")
        BTr = Ring(P, 3, [128, 4, 128], BF16, "BT")
        Btr = Ring(P, 3, [128, 512], BF16, "Bt")
        Xr = Ring(P, 3, [128, 32, 64], BF16, "X")
        dlr = Ring(P, 3, [128, 128], F32, "dtla")
        ear = Ring(P, 2, [128, 96], F32, "eall")
        xdr = Ring(P, 2, [128, 32, 64], BF16, "xdt")
        xwr = Ring(P, 2, [128, 32, 64], BF16, "xw")
        Rr = Ring(P, 2, [128, 8, 128], F32, "R")
        Lr = Ring(P, 2, [128, 8, 128], BF16, "L")
        CBr = Ring(P, 2, [128, 128], F32, "CBm")
        Pmr = Ring(P, 2, [128, 8, 128], BF16, "Pm")
        tmpr = Ring(P, 2, [128, 8, 64], F32, "tmp")
        otr = Ring(P, 2, [128, 2048], F32, "ot")
        psE = Tl(P, [128, 512], F32, "psE", psum=True)
        psD = Tl(P, [128, 1024], F32, "psD", psum=True)
        psCB = Tl(P, [128, 512], F32, "psCB", psum=True)
        psO = Ring(P, 2, [128, 512], F32, "psO", psum=True)
        psI = Tl(P, [128, 512], F32, "psI", psum=True)
        psK = Tl(P, [128, 512], F32, "psK", psum=True)
        steps = [(ti, b) for ti in self.scan_order(d) for b in range(c.NB)]
        tri = ct.t[:, 12 + d, :]
        G = ct.t[:, 14 + d, :]
        ones = ct.t[:, 16, :]

        def ld(st):
            ti, b = st
            tok = slice(ti * 128, (ti + 1) * 128)
            CT, BT, Bt, X, dl = CTr.next(), BTr.next(), Btr.next(), Xr.next(), dlr.next()
            need_o = not (last and ti < c.CT)
            if need_o:
                self.dma(SP, CT.t[:].rearrange("p g t -> p (g t)"), self.sQT[0, b, ti][:, 0:512], [self.dbuf(("QT", b, ti))], [CT.b])
                self.dma(SP, BT.t[:].rearrange("p g t -> p (g t)"), self.sKT[0, b, ti][:, 0:512], [self.dbuf(("KT", b, ti))], [BT.b])
            self.dma(SP, Bt.t[:], self.sKt[0, b, tok, 0:512], [self.dbuf(("Kt", b, ti))], [Bt.b])
            self.dma(SP, X.t[:].rearrange("p h e -> p (h e)"), self.sV[b, tok, :], [self.dbuf(("V", b, ti))], [X.b])
            self.dma(SP, dl.t[:], self.sDT[b, tok, :], [self.dbuf(("DT", b, ti))], [dl.b])
            ofl = None
            return CT, BT, Bt, X, dl, need_o, ofl

        def comp(st, tl):
            ti, b = st
            tok = slice(ti * 128, (ti + 1) * 128)
            CT, BT, Bt, X, dl, need_o, ofl = tl
            la = dl.t[:, 64 + d * 32:64 + (d + 1) * 32]
            dt = dl.t[:, d * 32:(d + 1) * 32]
            self.mm(psE.t[:, 0:32], tri, la, True, True, [ct.b, dl.b], [psE.b])
            self.mm(psE.t[:, 32:64], G, la, True, True, [ct.b, dl.b], [psE.b])
            self.mm(psE.t[:, 64:96], ones, la, True, True, [ct.b, dl.b], [psE.b])
            ea = ear.next()
            self.act(ea.t[:], psE.t[:, 0:96], AF.Exp, [psE.b], [ea.b])
            xd, xw = xdr.next(), xwr.next()
            self.tt(DVE, xd.t[:], X.t[:], dt.unsqueeze(2).broadcast_to([128, 32, 64]), ALU.mult, [X.b, dl.b], [xd.b])
            self.tt(POOL, xw.t[:], xd.t[:], ea.t[:, 32:64].unsqueeze(2).broadcast_to([128, 32, 64]), ALU.mult, [xd.b, ea.b], [xw.b])
            ot = otr.next() if need_o else None
            for g in range(4):
                hs = slice(g * 8, g * 8 + 8)
                if need_o:
                    R = Rr.next()
                    self.tt(DVE, R.t[:], la[:, hs].unsqueeze(2).broadcast_to([128, 8, 128]),
                            tri.unsqueeze(1).broadcast_to([128, 8, 128]), ALU.mult, [dl.b, ct.b], [R.b])
                    for half in range(2):
                        self.mm(psD.t[:, half * 512:(half + 1) * 512], G,
                                R.t[:, half * 4:(half + 1) * 4, :].rearrange("p h t -> p (h t)"), True, True, [ct.b, R.b], [psD.b])
                    Lg = Lr.next()
                    self.act(Lg.t[:].rearrange("p h t -> p (h t)"), psD.t[:], AF.Exp, [psD.b], [Lg.b])
                    self.mm(psCB.t[:, 0:128], BT.t[:, g, :], CT.t[:, g, :], True, True, [BT.b, CT.b], [psCB.b])
                    CBm = CBr.next()
                    self.tt(DVE, CBm.t[:], psCB.t[:, 0:128], ct.t[:, 1 + d, :], ALU.mult, [psCB.b, ct.b], [CBm.b])
                    Pm = Pmr.next()
                    self.tt(DVE, Pm.t[:], Lg.t[:], CBm.t[:].unsqueeze(1).broadcast_to([128, 8, 128]), ALU.mult, [Lg.b, CBm.b], [Pm.b])
                    po = psO.next()
                    for r in range(8):
                        self.mm(po.t[:, r * 64:(r + 1) * 64], Pm.t[:, r, :], xd.t[:, g * 8 + r, :], r == 0, r == 7,
                                [Pm.b, xd.b], [po.b], skip=True)
                    self.mm(psI.t[:], CT.t[:, g, :], Sbf[b, g].t[:], True, True, [CT.b, Sbf[b, g].b], [psI.b])
                    tmp = tmpr.next()
                    self.tt(DVE, tmp.t[:], psI.t[:].rearrange("p (h e) -> p h e", h=8),
                            ea.t[:, hs].unsqueeze(2).broadcast_to([128, 8, 64]), ALU.mult, [psI.b, ea.b], [tmp.b])
                    self.tt(DVE, ot.t[:, g * 512:(g + 1) * 512], tmp.t[:].rearrange("p h e -> p (h e)"), po.t[:], ALU.add,
                            [tmp.b, po.b], [ot.b])
                self.mm(psK.t[:], Bt.t[:, g * 128:(g + 1) * 128], xw.t[:, hs, :].rearrange("p h e -> p (h e)"), True, True,
                        [Bt.b, xw.b], [psK.b])
                s32 = S32[b, g]
                self.tt(POOL, s32.t[:], s32.t[:], ea.t[:, 64 + g * 8:64 + g * 8 + 8].unsqueeze(2).broadcast_to([128, 8, 64]),
                        ALU.mult, [s32.b, ea.b], [s32.b])
                self.tt(DVE, s32.t[:], s32.t[:], psK.t[:].rearrange("p (h e) -> p h e", h=8), ALU.add, [s32.b, psK.b], [s32.b])
                self.cp(ACT, Sbf[b, g].t[:], s32.t[:].rearrange("p h e -> p (h e)"), [s32.b], [Sbf[b, g].b])
            if need_o:
                self.dma(SP, self.sO[d, b, tok, :], ot.t[:], [ot.b], [self.dbuf(("O", d, b, ti))])

        self.run_pipelined(steps, ld, comp, rowof=lambda it: 0)
        P.end_phase()

    def m2_readout(self, li, j):
        P, c = self.P, self.cfg
        P.begin_phase()
        wo = Tl(P, [128, 16, 1024], BF16, "w_out")
        self.load_wout_scaled(wo, self.m2_w_out[j], 16, self.m2_gn[j])
        dsk = Tl(P, [128, 32], F32, "dsk")
        self.dma(SP, dsk.t[:], self.m2_d[j, :].partition_broadcast(128), [], [dsk.b])
        g1 = Tl(P, [128, D], F32, "g1")
        ofr = Ring(P, 3, [128, 2048], F32, "of")
        obr = Ring(P, 3, [128, 2048], F32, "ob")
        xr = Ring(P, 3, [128, 32, 64], BF16, "x")
        zr = Ring(P, 3, [128, 2048], BF16, "z")
        hr = Ring(P, 3, [128, D], F32, "h")
        tmpr = Ring(P, 2, [128, 32, 64], F32, "tmp")
        junk2 = Tl(P, [128, 512], BF16, "junk2")
        onr = Ring(P, 2, [128, 2048], BF16, "on")
        onTr = Ring(P, 2, [128, 16, 128], BF16, "onT")
        rsr = Ring(P, 2, [128, 4], F32, "rs")
        hnr = Ring(P, 2, [128, D], F32, "hn")
        ssr = Ring(P, 4, [128, 2], F32, "ss")
        junk = Tl(P, [128, D], BF16, "junk")
        psTr = Ring(P, 2, [128, 1024], BF16, "psT", psum=True)
        psYr = Ring(P, 2, [128, 1024], F32, "psY", psum=True)

        def ld(it):
            row, b, ti = it
            tok = slice(ti * 128, (ti + 1) * 128)
            of, ob, x, z, h = ofr.next(), obr.next(), xr.next(), zr.next(), hr.next()
            self.dma(SP, of.t[:], self.sO[1, b, tok, :], [self.dbuf(("O", 1, b, ti))], [of.b])
            self.dma(SP, ob.t[:], self.sO[0, b, tok, :], [self.dbuf(("O", 0, b, ti))], [ob.b])
            self.dma(SP, x.t[:].rearrange("p h e -> p (h e)"), self.sV[b, tok, :], [self.dbuf(("V", b, ti))], [x.b])
            self.dma(SP, z.t[:], self.sG[b, tok, :], [self.dbuf(("G", b, ti))], [z.b])
            src, sb_ = self.h_src(li, b, ti)
            self.dma(SP, h.t[:], src, [sb_] if sb_ is not None else [], [h.b])
            return {"of": of, "ob": ob, "x": x, "z": z, "h": h}

        def pre_a(it, cx):
            of, ob, x, z = cx["of"], cx["ob"], cx["x"], cx["z"]
            on, rs, tmp = onr.next(), rsr.next(), tmpr.next()
            self.tt(DVE, tmp.t[:], x.t[:], dsk.t[:].unsqueeze(2).broadcast_to([128, 32, 64]), ALU.mult, [x.b, dsk.b], [tmp.b])
            self.tt(POOL, of.t[:], of.t[:], ob.t[:], ALU.add, [of.b, ob.b], [of.b])
            self.tt(DVE, of.t[:], of.t[:], tmp.t[:].rearrange("p h e -> p (h e)"), ALU.add, [of.b, tmp.b], [of.b])
            self.tt(DVE, of.t[:], of.t[:], z.t[:], ALU.mult, [of.b, z.b], [of.b])
            for g in range(4):
                self.act(junk2.t[:], of.t[:, g * 512:(g + 1) * 512], AF.Square, [of.b], [junk2.b, rs.b], accum_out=rs.t[:, g:g + 1])
            self.rstd(rs.t[:], rs.t[:], 1.0 / 512, EPS, [rs.b], [rs.b], n=4)
            self.tt(DVE, on.t[:].rearrange("p (g v) -> p g v", g=4), of.t[:].rearrange("p (g v) -> p g v", g=4),
                    rs.t[:].unsqueeze(2).broadcast_to([128, 4, 512]), ALU.mult, [of.b, rs.b], [on.b])
            cx["on"] = on

        def pre_b(it, cx):
            on, onT = cx["on"], onTr.next()
            for half in range(2):
                psT = psTr.next()
                self.transpose_to(on.t, 8, psT, onT.t[:, half * 8:(half + 1) * 8, :].rearrange("p k t -> p (k t)"),
                                  onT.b, on.b, eng=(ACT if half == 0 else DVE), src_off=half * 1024)
            cx["onT"] = onT

        self.run_pipe(self.readout_items(li), ld, pre_a, pre_b, self.readout_main(li, wo, 16, g1, psYr, hnr, ssr, junk))
        P.end_phase()

    def copy_in(self):
        P, c = self.P, self.cfg
        P.begin_phase()
        hr = Ring(P, 3, [128, D], F32, "h")
        for b in range(c.NB):
            for ti in range(c.NT):
                h = hr.next()
                src, _ = self.h_src(0, b, ti)
                self.dma(SP, h.t[:], src, [], [h.b])
                dst, db_ = self.h_mid(b, ti)
                self.dma(SP, dst, h.t[:], [h.b], [db_])
        P.end_phase()

    def build(self):
        c = self.cfg
        self.declare()
        self.setup_consts()
        phases = c.phases
        if phases is None:
            phases = ["mod"]
            for li, k in enumerate(c.kinds):
                phases += [("mix", li), ("ffn", li)]
        for ph in phases:
            if ph == "mod":
                self.phase_mod()
            elif ph == "copy_in":
                self.copy_in()
            elif ph[0] == "ffn":
                self.phase_ffn(ph[1])
            elif ph[0] == "mix":
                self.phase_mixer(ph[1])
            elif ph[0] == "call":
                getattr(self, ph[1])(*ph[2:])
        self.P.begin_phase()
        self.P.end_phase(final=True)
        return self.nc


def const_tables(cfg):
    ident = np.eye(128, dtype=np.float32)
    LT = max(cfg.LT, 1)
    nf = 64
    inv = (10000.0 ** (-np.arange(nf, dtype=np.float32) / nf)).astype(np.float32)
    rope = np.zeros((LT, 128, 512), np.float32)
    p = np.arange(128)
    for lt in range(LT):
        row = (2 * lt + p // GRID_W).astype(np.float32)
        col = (p % GRID_W).astype(np.float32)
        ar = row[:, None] * inv[None, :]
        ac = col[:, None] * inv[None, :]
        cr, sr, cc, sc = np.cos(ar), np.sin(ar), np.cos(ac), np.sin(ac)
        rope[lt, :, 0:256] = np.concatenate([cr, cr, cc, cc], 1)
        rope[lt, :, 256:512] = np.concatenate([-sr, sr, -sc, sc], 1)
    tab = np.zeros((128, 24, 128), np.float32)
    s = np.arange(128)[:, None].astype(np.float32)
    t = np.arange(128)[None, :].astype(np.float32)
    tab[:, 0] = t - s
    tab[:, 1] = (t >= s)
    tab[:, 2] = (s >= t)
    tab[:, 3] = t + 1.0
    tab[:, 4] = 128.0 - t
    tab[:, 5, 0] = 127.0 - s[:, 0]
    tab[:, 5, 1] = s[:, 0]
    tab[:, 5, 2] = 128.0
    same = ((s // 64) == (t // 64)).astype(np.float32)
    sl = s % 64
    tab[:, 6] = same * (t >= s)
    tab[:, 7] = same * (s >= t)
    tab[:, 8] = same * ((s <= t).astype(np.float32) - (sl <= 31))
    tab[:, 9] = same * ((s >= t).astype(np.float32) - (sl >= 32))
    for cc in range(2):
        inc = ((s[:, 0] // 64) == cc).astype(np.float32)
        slc = s[:, 0] % 64
        tab[:, 10, 3 * cc + 0] = inc * (slc <= 31)
        tab[:, 10, 3 * cc + 1] = inc
        tab[:, 10, 3 * cc + 2] = inc * (slc > 31)
        tab[:, 11, 3 * cc + 0] = inc * (slc >= 32)
        tab[:, 11, 3 * cc + 1] = inc
        tab[:, 11, 3 * cc + 2] = inc * (slc < 32)
    tab[:, 12] = (s <= t)
    tab[:, 13] = (s >= t)
    tab[:, 14] = (s > t)
    tab[:, 15] = (s < t)
    tab[:, 16] = 1.0
    return ident, rope, tab.reshape(128, 24 * 128)


def make_in_maps(cfg, inputs, n_cores):
    ident, rope, tab = const_tables(cfg)
    f = lambda a: np.ascontiguousarray(np.asarray(a, dtype=np.float32))
    L = len(cfg.kinds)
    nr, nh, nm = (max(1, sum(1 for k in cfg.kinds if k == q)) for q in (0, 1, 2))
    inputs = dict(inputs)
    for k in inputs:
        if k.startswith("ret_"):
            inputs[k] = np.asarray(inputs[k])[:nr]
        elif k.startswith("hg_") and k != "hg_lb":
            inputs[k] = np.asarray(inputs[k])[:nh]
        elif k.startswith("m2_"):
            inputs[k] = np.asarray(inputs[k])[:nm]
    shared = {
        "ada_w": f(inputs["ada_w"][:L]), "ada_b": f(inputs["ada_b"][:L]),
        "norm_g": f(inputs["norm_g"][:L]).reshape(L, 4 * D),
        "ret_w_in": f(inputs["ret_w_in"]), "ret_w_out": f(inputs["ret_w_out"]),
        "ret_decay": f(inputs["ret_decay"]).reshape(-1, 8), "ret_gn": f(inputs["ret_gn"]),
        "hg_w_in": f(inputs["hg_w_in"]), "hg_w_out": f(inputs["hg_w_out"]), "hg_lb": f(inputs["hg_lb"][:L]),
        "hg_gn": f(inputs["hg_gn"]),
        "m2_w_in": f(inputs["m2_w_in"]), "m2_w_out": f(inputs["m2_w_out"]), "m2_conv_w": f(inputs["m2_conv_w"]),
        "m2_conv_b": f(inputs["m2_conv_b"]), "m2_dt_bias": f(inputs["m2_dt_bias"]).reshape(-1, 64),
        "m2_a_log": f(inputs["m2_a_log"]).reshape(-1, 64), "m2_d": f(inputs["m2_d"]), "m2_gn": f(inputs["m2_gn"]),
        "ffn_w_up": f(inputs["ffn_w_up"][:L]), "ffn_w_down": f(inputs["ffn_w_down"][:L]),
        "ffn_cw": f(np.asarray(inputs["ffn_conv_w"][:L]).reshape(L, 3, 22, 128).transpose(0, 3, 2, 1)).reshape(L, 128, 66),
        "ffn_cb": f(np.asarray(inputs["ffn_conv_b"][:L]).reshape(L, 22, 128).transpose(0, 2, 1)),
        "c_ident": ident, "c_rope": rope, "c_tab": tab,
    }
    x, cc, ctx, c_ctx = (np.asarray(inputs[k], dtype=np.float32) for k in ("x", "c", "ctx", "c_ctx"))
    maps = []
    for i in range(n_cores):
        sl = slice(i * cfg.NB, (i + 1) * cfg.NB)
        m = dict(shared)
        m["x"] = f(x[sl])
        m["ctx"] = f(ctx[sl])
        m["crow"] = f(np.concatenate([cc[sl], c_ctx[None, :]], 0))
        maps.append(m)
    return maps


_CACHE = {}


def kernel(**inputs):
    cfg = Cfg()
    if "nc" not in _CACHE:
        _CACHE["nc"] = Builder(cfg).build()
    nc = _CACHE["nc"]
    maps = make_in_maps(cfg, inputs, N_CORES)
    res = run_bass_kernel_spmd(nc, maps, core_ids=list(range(N_CORES)))
    out = np.concatenate([np.asarray(r["out"]) for r in res.results], axis=0)
    return out.astype(np.float32)
```

```python
import contextlib
import numpy as np
import concourse.bass as bass
import concourse.mybir as mybir
from concourse.bass_utils import run_bass_kernel_spmd

F32 = mybir.dt.float32
BF16 = mybir.dt.bfloat16
AF = mybir.ActivationFunctionType
ALU = mybir.AluOpType

PE, ACT, DVE, POOL, SP = "pe", "act", "dve", "pool", "sp"
CENG = (PE, ACT, DVE, POOL)

D = 1024
EPS = 1e-6
GRID_W = 64
N_CORES = 8


class Buf:
    __slots__ = ("name", "last_w", "readers", "dram", "strict")

    def __init__(self, name, dram=False):
        self.name = name
        self.last_w = None
        self.readers = []
        self.dram = dram
        self.strict = bool(name) and str(name).startswith("junk")


class Op:
    __slots__ = ("eng", "fn", "deps", "weak", "is_dma", "signal", "val", "dsem_buf")

    def __init__(self, eng, fn, is_dma, dsem_buf):
        self.eng = eng
        self.fn = fn
        self.deps = set()
        self.weak = set()
        self.is_dma = is_dma
        self.signal = False
        self.val = None
        self.dsem_buf = dsem_buf


class Prog:
    def __init__(self, same_eng_sync=True):
        self.nc = bass.Bass("TRN2", target_bir_lowering=False)
        self.ops = []
        self.same_eng_sync = same_eng_sync
        self.keep = []
        self.bufs = []
        nc = self.nc
        self.engs = {PE: nc.tensor, ACT: nc.scalar, DVE: nc.vector, POOL: nc.gpsimd, SP: nc.sync}
        self.sems = {}
        for e in CENG:
            cm = nc.semaphore("s_" + e)
            self.sems[e] = cm.__enter__()
            self.keep.append(cm)
        self.cnt = {e: 0 for e in CENG}
        self.dsem_free = {True: [], False: []}
        self.dsem_all = []
        self.waited = {}
        self.n_inst = 0
        self.n_wait = 0
        self.phase_stack = None
        self.uid = 0

    def begin_phase(self):
        self.phase_stack = contextlib.ExitStack()

    def sb(self, shape, dt, name=None):
        self.uid += 1
        return self.phase_stack.enter_context(self.nc.sbuf_tensor(f"{name or 't'}_{self.uid}", list(shape), dt))

    def ps(self, shape, dt=F32, name=None):
        self.uid += 1
        return self.phase_stack.enter_context(self.nc.psum_tensor(f"{name or 'p'}_{self.uid}", list(shape), dt))

    def buf(self, name=None, dram=False):
        b = Buf(name or "b", dram)
        self.bufs.append(b)
        return b

    def op(self, eng, fn, reads=(), writes=(), dma=False):
        idx = len(self.ops)
        dsem_buf = None
        if dma:
            for b in list(writes) + list(reads):
                if not b.dram:
                    dsem_buf = b
                    break
            if dsem_buf is None:
                dsem_buf = (list(writes) + list(reads))[0]
        o = Op(eng, fn, dma, dsem_buf)
        for b in reads:
            if b.last_w is not None:
                o.deps.add(b.last_w)
        for b in writes:
            if b.last_w is not None:
                (o.deps if b.strict else o.weak).add(b.last_w)
            for r in b.readers:
                o.weak.add(r)
        for b in reads:
            b.readers.append(idx)
        for b in writes:
            b.last_w = idx
            b.readers = []
        o.deps.discard(idx)
        for d in o.weak:
            if d != idx and d not in o.deps:
                s_ = self.ops[d]
                if s_.is_dma or dma or s_.eng != eng:
                    o.deps.add(d)
        self.ops.append(o)
        return idx

    def end_phase(self, final=False):
        nc = self.nc
        ops = self.ops
        engs = self.engs
        sems = self.sems
        cnt = self.cnt
        waited = self.waited
        last_on = {}
        for i, o in enumerate(ops):
            last_on[o.eng] = i
            for d in o.deps:
                s = ops[d]
                if s.is_dma:
                    continue
                if s.eng == o.eng and (s.eng == PE or not self.same_eng_sync):
                    continue
                s.signal = True
        for e in CENG:
            for i in range(len(ops) - 1, -1, -1):
                if ops[i].eng == e and not ops[i].is_dma:
                    ops[i].signal = True
                    break
        dsems = {}
        for i, o in enumerate(ops):
            eng = engs[o.eng]
            need = {}
            for d in o.deps:
                s = ops[d]
                if s.is_dma:
                    st = dsems[(id(s.dsem_buf), s.eng == POOL)]
                    key = ("d", id(st))
                    v = st[1]
                    sem = st[0]
                else:
                    if s.eng == o.eng and (s.eng == PE or not self.same_eng_sync):
                        continue
                    key = ("e", s.eng)
                    v = s.val
                    sem = sems[s.eng]
                if need.get(key, (None, -1))[1] < v:
                    need[key] = (sem, v)
            for key, (sem, v) in need.items():
                if waited.get((o.eng, key), -1) >= v:
                    continue
                waited[(o.eng, key)] = v
                eng.wait_ge(sem, v)
                self.n_wait += 1
            inst = o.fn(eng)
            self.n_inst += 1
            if o.is_dma:
                k = (id(o.dsem_buf), o.eng == POOL)
                if k not in dsems:
                    fl = self.dsem_free[k[1]]
                    if fl:
                        dsems[k] = fl.pop()
                    else:
                        cm = nc.semaphore(f"d{len(self.dsem_all)}")
                        st = [cm.__enter__(), 0]
                        self.keep.append(cm)
                        self.dsem_all.append(st)
                        dsems[k] = st
                st = dsems[k]
                st[1] += 16
                inst.then_inc(st[0], 16)
            elif o.signal:
                cnt[o.eng] += 1
                o.val = cnt[o.eng]
                inst.then_inc(sems[o.eng], 1)
        targets = [SP] if final else list(engs.keys())
        for e in targets:
            eng = engs[e]
            for s in CENG:
                if cnt[s] == 0 or (s == e and (e == PE or not self.same_eng_sync)):
                    continue
                key = ("e", s)
                if waited.get((e, key), -1) >= cnt[s]:
                    continue
                waited[(e, key)] = cnt[s]
                eng.wait_ge(sems[s], cnt[s])
                self.n_wait += 1
            for st in dsems.values():
                key = ("d", id(st))
                if waited.get((e, key), -1) >= st[1]:
                    continue
                waited[(e, key)] = st[1]
                eng.wait_ge(st[0], st[1])
                self.n_wait += 1
        for k, st in dsems.items():
            self.dsem_free[k[1]].append(st)
        self.ops = []
        for b in self.bufs:
            b.last_w = None
            b.readers = []
        self.bufs = [b for b in self.bufs if b.dram]
        if self.phase_stack is not None:
            self.phase_stack.close()
            self.phase_stack = None


class Tl:
    def __init__(self, P, shape, dt, name=None, psum=False):
        self.t = P.ps(shape, dt, name) if psum else P.sb(shape, dt, name)
        self.b = P.buf(name)


class Ring:
    def __init__(self, P, n, shape, dt, name=None, psum=False):
        self.tiles = [Tl(P, shape, dt, name, psum) for _ in range(n)]
        self.i = 0

    def next(self):
        t = self.tiles[self.i % len(self.tiles)]
        self.i += 1
        return t


class Cfg:
    def __init__(self, NB=2, LAT=4096, CTX=256, kinds=(0, 1, 2, 0), debug=False, phases=None):
        self.NB, self.LAT, self.CTX = NB, LAT, CTX
        self.kinds = tuple(kinds)
        self.debug = debug
        self.phases = phases
        self.TOK = LAT + CTX
        self.CT = CTX // 128
        self.LT = LAT // 128
        self.NT = self.CT + self.LT


class Builder:
    def __init__(self, cfg):
        self.cfg = cfg
        self.P = Prog()
        self.nc = self.P.nc
        self.dram_bufs = {}

    def dma(self, q, out, in_, R, W, **kw):
        self.P.op(q, lambda e: e.dma_start(out=out, in_=in_, **kw), R, W, dma=True)

    def mm(self, out, lhsT, rhs, start, stop, R, W, skip=False):
        self.P.op(PE, lambda e: e.matmul(out, lhsT=lhsT, rhs=rhs, start=start, stop=stop, skip_group_check=skip), R, W)

    def tr(self, out, in_, R, W):
        ident = self.ident.t[:]
        self.P.op(PE, lambda e: e.transpose(out=out, in_=in_, identity=ident), list(R) + [self.ident.b], W)

    def act(self, out, in_, func, R, W, eng=ACT, **kw):
        self.P.op(eng, lambda e: e.activation(out=out, in_=in_, func=func, **kw), R, W)

    def tt(self, eng, out, in0, in1, op, R, W):
        self.P.op(eng, lambda e: e.tensor_tensor(out=out, in0=in0, in1=in1, op=op), R, W)

    def ts(self, eng, out, in0, s1, s2, op0, op1, R, W):
        if op1 is None:
            self.P.op(eng, lambda e: e.tensor_scalar(out=out, in0=in0, scalar1=s1, scalar2=None, op0=op0), R, W)
        else:
            self.P.op(eng, lambda e: e.tensor_scalar(out=out, in0=in0, scalar1=s1, scalar2=s2, op0=op0, op1=op1), R, W)

    def stt(self, out, in0, scalar, in1, op0, op1, R, W):
        self.P.op(DVE, lambda e: e.scalar_tensor_tensor(out=out, in0=in0, scalar=scalar, in1=in1, op0=op0, op1=op1), R, W)

    def cp(self, eng, out, in_, R, W):
        if eng == ACT:
            self.act(out, in_, AF.Copy, R, W)
        else:
            self.P.op(eng, lambda e: e.tensor_copy(out=out, in_=in_), R, W)

    def memset(self, eng, ap, val, W):
        self.P.op(eng, lambda e: e.memset(ap, val), (), W)

    def dbuf(self, key):
        if key not in self.dram_bufs:
            self.dram_bufs[key] = self.P.buf(str(key), dram=True)
        return self.dram_bufs[key]

    def rstd(self, out, in_, scale, eps, R, W, n=1):
        nh = self.neghalf.t[:, 0:n]
        self.ts(POOL, out, in_, scale, eps, ALU.mult, ALU.add, R, W)
        self.tt(POOL, out, out, nh, ALU.pow, list(W) + [self.neghalf.b], W)

    def declare(self):
        nc, c = self.nc, self.cfg
        L = len(c.kinds)
        self.L = L
        di = lambda n, s, dt=F32: nc.dram_tensor(n, list(s), dt, kind="ExternalInput").ap()
        dx = lambda n, s, dt=F32: nc.dram_tensor(n, list(s), dt, kind="Internal").ap()
        self.x_in = di("x", [c.NB, c.LAT, D])
        self.ctx_in = di("ctx", [c.NB, c.CTX, D])
        self.crow = di("crow", [c.NB + 1, D])
        self.ada_w = di("ada_w", [L, D, 6 * D])
        self.ada_b = di("ada_b", [L, 6 * D])
        self.norm_g = di("norm_g", [L, 4 * D])
        n_ret = sum(1 for k in c.kinds if k == 0)
        n_hg = sum(1 for k in c.kinds if k == 1)
        n_m2 = sum(1 for k in c.kinds if k == 2)
        self.ret_w_in = di("ret_w_in", [max(n_ret, 1), D, 6144])
        self.ret_w_out = di("ret_w_out", [max(n_ret, 1), 2048, D])
        self.ret_decay = di("ret_decay", [max(n_ret, 1), 8])
        self.ret_gn = di("ret_gn", [max(n_ret, 1), 2048])
        self.hg_w_in = di("hg_w_in", [max(n_hg, 1), D, 5120])
        self.hg_w_out = di("hg_w_out", [max(n_hg, 1), D, D])
        self.hg_lb = di("hg_lb", [L, D])
        self.hg_gn = di("hg_gn", [max(n_hg, 1), D])
        self.m2_w_in = di("m2_w_in", [max(n_m2, 1), D, 5184])
        self.m2_w_out = di("m2_w_out", [max(n_m2, 1), 2048, D])
        self.m2_conv_w = di("m2_conv_w", [max(n_m2, 1), 3, 3072])
        self.m2_conv_b = di("m2_conv_b", [max(n_m2, 1), 3072])
        self.m2_dt_bias = di("m2_dt_bias", [max(n_m2, 1), 64])
        self.m2_a_log = di("m2_a_log", [max(n_m2, 1), 64])
        self.m2_d = di("m2_d", [max(n_m2, 1), 32])
        self.m2_gn = di("m2_gn", [max(n_m2, 1), 2048])
        self.ffn_w_up = di("ffn_w_up", [L, D, 5632])
        self.ffn_cw = di("ffn_cw", [L, 128, 22 * 3])
        self.ffn_cb = di("ffn_cb", [L, 128, 22])
        self.ffn_w_down = di("ffn_w_down", [L, 2816, D])
        self.c_ident = di("c_ident", [128, 128])
        self.c_rope = di("c_rope", [max(c.LT, 1), 128, 512])
        self.c_tab = di("c_tab", [128, 24 * 128])
        self.out = nc.dram_tensor("out", [c.NB, c.LAT, D], F32, kind="ExternalOutput").ap()
        if c.debug:
            self.dbg = nc.dram_tensor("dbg", [L, c.NB, c.TOK, D], F32, kind="ExternalOutput").ap()
        self.H = dx("H", [c.NB, c.TOK, D])
        self.MOD = dx("MOD", [L, c.NB + 1, 6, D])
        self.sQT = dx("sQT", [2, c.NB, c.NT, 128, 1024], BF16)
        self.sKT = dx("sKT", [2, c.NB, c.NT, 128, 1024], BF16)
        self.sKt = dx("sKt", [2, c.NB, c.TOK, 1024], BF16)
        self.sV = dx("sV", [c.NB, c.TOK, 2048], BF16)
        self.sG = dx("sG", [c.NB, c.TOK, 2048], BF16)
        self.sO = dx("sO", [2, c.NB, c.TOK, 2048])
        self.sCS = dx("sCS", [2, c.NB, c.NT, 128, 48])
        self.sX = dx("sX", [c.NB, c.TOK + 2 * c.NT + 8, 3072])
        self.sDT = dx("sDT", [c.NB, c.TOK, 128])

    def setup_consts(self):
        P, nc = self.P, self.nc
        self.const_stack = contextlib.ExitStack()
        P.phase_stack = self.const_stack
        self.ident = Tl(P, [128, 128], BF16, "ident")
        self.neghalf = Tl(P, [128, 8], F32, "neghalf")
        P.phase_stack = None
        P.begin_phase()
        self.dma(POOL, self.ident.t[:], self.c_ident[:, :], [], [self.ident.b])
        self.memset(POOL, self.neghalf.t[:], -0.5, [self.neghalf.b])
        P.end_phase()

    def load_ctab(self):
        ct = Tl(self.P, [128, 24, 128], F32, "ctab")
        self.dma(SP, ct.t[:], self.c_tab.rearrange("p (a b) -> p a b", a=24), [], [ct.b])
        return ct

    def h_src(self, li, b, ti):
        c = self.cfg
        if li == 0:
            if ti < c.CT:
                return self.ctx_in[b, ti * 128:(ti + 1) * 128, :], None
            return self.x_in[b, (ti - c.CT) * 128:(ti - c.CT + 1) * 128, :], None
        return self.H[b, ti * 128:(ti + 1) * 128, :], self.dbuf(("H", b, ti))

    def h_mid(self, b, ti):
        return self.H[b, ti * 128:(ti + 1) * 128, :], self.dbuf(("H", b, ti))

    def h_dst(self, li, b, ti):
        c = self.cfg
        if li == self.L - 1 and ti >= c.CT:
            return self.out[b, (ti - c.CT) * 128:(ti - c.CT + 1) * 128, :], self.dbuf(("out", b, ti))
        return self.H[b, ti * 128:(ti + 1) * 128, :], self.dbuf(("H", b, ti))

    def load_w(self, wt, src, kchunks, ncols):
        v = src.rearrange("(k p) n -> p k n", p=128)
        for k in range(kchunks):
            self.dma(POOL, wt.t[:, k, :], v[:, k, :], [], [wt.b])

    def load_tab(self, tl, li, row, vec):
        self.dma(SP, tl.t[:], self.MOD[li, row, vec, :].partition_broadcast(128), [self.dbuf("MOD")], [tl.b])

    def phase_mod(self):
        P, c = self.P, self.cfg
        R3 = c.NB + 1
        P.begin_phase()
        cT = Tl(P, [128, R3, 8], F32, "cT")
        cTb = Tl(P, [128, 8, R3], BF16, "cTb")
        self.dma(SP, cT.t[:], self.crow.rearrange("r (p k) -> p r k", k=8), [], [cT.b])
        self.act(cTb.t[:].rearrange("p k r -> p r k"), cT.t[:], AF.Silu, [cT.b], [cTb.b])
        wr = Ring(P, 2, [128, 8, 1536], BF16, "adaw")
        pr = Ring(P, 2, [128, 512], F32, "psm", psum=True)
        raw = Tl(P, [R3, 6 * D], F32, "raw")
        adab = Tl(P, [R3, 6 * D], F32, "adab")
        ng = Tl(P, [R3, 4 * D], F32, "ng")
        mv = Ring(P, 2, [R3, 6, D], F32, "mv")
        modb = self.dbuf("MOD")
        for li in range(self.L):
            self.dma(SP, adab.t[:], self.ada_b[li, :].partition_broadcast(R3), [], [adab.b])
            self.dma(SP, ng.t[:], self.norm_g[li, :].partition_broadcast(R3), [], [ng.b])
            wv = self.ada_w[li].rearrange("(p k) n -> p k n", k=8)
            for j in range(4):
                w = wr.next()
                self.dma(POOL, w.t[:], wv[:, :, j * 1536:(j + 1) * 1536], [], [w.b])
                for n in range(3):
                    ps = pr.next()
                    for k in range(8):
                        self.mm(ps.t[0:R3, :], cTb.t[:, k, :], w.t[:, k, n * 512:(n + 1) * 512], k == 0, k == 7,
                                [cTb.b, w.b], [ps.b])
                    c0 = j * 1536 + n * 512
                    self.tt(DVE, raw.t[:, c0:c0 + 512], ps.t[0:R3, :], adab.t[:, c0:c0 + 512], ALU.add,
                            [ps.b, adab.b], [raw.b])
            m = mv.next()
            r = raw.t
            self.cp(DVE, m.t[:, 0, :], r[:, 0:D], [raw.b], [m.b])
            self.stt(m.t[:, 1, :], r[:, D:2 * D], 1.0, ng.t[:, 0:D], ALU.add, ALU.mult, [raw.b, ng.b], [m.b])
            self.tt(DVE, m.t[:, 2, :], r[:, 2 * D:3 * D], ng.t[:, D:2 * D], ALU.mult, [raw.b, ng.b], [m.b])
            self.cp(DVE, m.t[:, 3, :], r[:, 3 * D:4 * D], [raw.b], [m.b])
            self.stt(m.t[:, 4, :], r[:, 4 * D:5 * D], 1.0, ng.t[:, 2 * D:3 * D], ALU.add, ALU.mult, [raw.b, ng.b], [m.b])
            self.tt(DVE, m.t[:, 5, :], r[:, 5 * D:6 * D], ng.t[:, 3 * D:4 * D], ALU.mult, [raw.b, ng.b], [m.b])
            self.dma(SP, self.MOD[li], m.t[:], [m.b], [modb])
        P.end_phase()

    def pre_norm(self, h, sc, sh, u, ss, junk):
        self.act(junk.t[:], h.t[:], AF.Square, [h.b], [junk.b, ss.b], accum_out=ss.t[:, 0:1])
        self.rstd(ss.t[:, 1:2], ss.t[:, 0:1], 1.0 / D, EPS, [ss.b], [ss.b])
        self.tt(POOL, h.t[:], h.t[:], sc.t[:], ALU.mult, [h.b, sc.b], [h.b])
        self.stt(u.t[:], h.t[:], ss.t[:, 1:2], sh.t[:], ALU.mult, ALU.add, [h.b, ss.b, sh.b], [u.b])

    def transpose_to(self, src, nblk, psT, dst_ap, dst_b, src_b, eng=ACT, src_off=0):
        for k in range(nblk):
            self.tr(psT.t[:, k * 128:(k + 1) * 128], src[:, src_off + k * 128: src_off + (k + 1) * 128], [src_b], [psT.b])
        self.cp(eng, dst_ap, psT.t[:, 0:nblk * 128], [psT.b], [dst_b])

    def post_residual(self, psY, hres, gtab, hn, ss, junk, dst, dst_b):
        self.act(junk.t[:], psY.t[:], AF.Square, [psY.b], [junk.b, ss.b], accum_out=ss.t[:, 0:1])
        self.rstd(ss.t[:, 1:2], ss.t[:, 0:1], 1.0 / D, EPS, [ss.b], [ss.b])
        self.stt(hn.t[:], psY.t[:], ss.t[:, 1:2], gtab.t[:], ALU.mult, ALU.mult, [psY.b, ss.b, gtab.b], [hn.b])
        self.tt(POOL, hn.t[:], hn.t[:], hres.t[:], ALU.add, [hn.b, hres.b], [hn.b])
        self.dma(SP, dst, hn.t[:], [hn.b], [dst_b] if dst_b is not None else [])

    def tile_list(self, li, with_ctx=True):
        c = self.cfg
        last = (li == self.L - 1)
        out = []
        for b in range(c.NB):
            for ti in range(c.NT):
                if ti < c.CT and (last and not with_ctx):
                    continue
                out.append((b, ti))
        return out

    def phase_ffn(self, li):
        P, c = self.P, self.cfg
        last = (li == self.L - 1)
        P.begin_phase()
        wup = Tl(P, [128, 8, 5632], BF16, "wup")
        wdn = Tl(P, [128, 22, 1024], BF16, "wdn")
        self.load_w(wup, self.ffn_w_up[li], 8, 5632)
        self.load_w(wdn, self.ffn_w_down[li], 22, 1024)
        cw = Tl(P, [128, 22, 3], F32, "cw")
        cb = Tl(P, [128, 22], F32, "cb")
        self.dma(SP, cw.t[:], self.ffn_cw[li].rearrange("p (a b) -> p a b", b=3), [], [cw.b])
        self.dma(SP, cb.t[:], self.ffn_cb[li], [], [cb.b])
        tabs = [Tl(P, [128, D], F32, f"tab{i}") for i in range(3)]
        hr = Ring(P, 4, [128, D], F32, "h")
        ur = Ring(P, 2, [128, D], BF16, "u")
        junk = Tl(P, [128, D], BF16, "junk")
        ssr = Ring(P, 6, [128, 2], F32, "ss")
        uTr = Ring(P, 2, [128, 8, 256], BF16, "uT")
        mT = Tl(P, [128, 22, 256], BF16, "mT")
        cbr = Ring(P, 3, [128, 256], F32, "cbuf")
        hrel = Ring(P, 2, [128, D], F32, "hrel")
        hnr = Ring(P, 2, [128, D], F32, "hn")
        psT = Tl(P, [128, 1024], BF16, "psT", psum=True)
        psA = Ring(P, 3, [128, 512], F32, "psA", psum=True)
        psV = Ring(P, 2, [128, 512], F32, "psV", psum=True)
        psY = Tl(P, [128, 1024], F32, "psY", psum=True)
        sts = []
        for b in range(c.NB):
            if not last:
                for s_ in range(c.CT // 2):
                    sts.append((c.NB, b, [2 * s_, 2 * s_ + 1], False))
        for b in range(c.NB):
            for s_ in range(c.LT // 2):
                sts.append((b, b, [c.CT + 2 * s_, c.CT + 2 * s_ + 1], True))
        st_a = {"row": None}
        st_m = {"row": None}

        def ld(st):
            row, b, tis, grid = st
            hs = []
            for ti in tis:
                h = hr.next()
                src, sb_ = self.h_mid(b, ti)
                self.dma(SP, h.t[:], src, [sb_], [h.b])
                hs.append(h)
            return {"h": hs}

        def pre_a(st, cx):
            row, b, tis, grid = st
            if st_a["row"] != row:
                st_a["row"] = row
                self.load_tab(tabs[0], li, row, 3)
                self.load_tab(tabs[1], li, row, 4)
            cx["u"] = []
            for h in cx["h"]:
                u = ur.next()
                self.pre_norm(h, tabs[1], tabs[0], u, ssr.next(), junk)
                cx["u"].append(u)

        def pre_b(st, cx):
            uT = uTr.next()
            for j, u in enumerate(cx["u"]):
                for k in range(8):
                    self.tr(psT.t[:, k * 128:(k + 1) * 128], u.t[:, k * 128:(k + 1) * 128], [u.b], [psT.b])
                self.cp(ACT, uT.t[:, :, j * 128:(j + 1) * 128], psT.t[:].rearrange("p (k t) -> p k t", k=8),
                        [psT.b], [uT.b])
            cx["uT"] = uT

        def main(st, cx, hook):
            row, b, tis, grid = st
            uT = cx["uT"]
            if st_m["row"] != row:
                st_m["row"] = row
                self.load_tab(tabs[2], li, row, 5)
            hres = []
            for ti in tis:
                hh = hrel.next()
                src, sb_ = self.h_mid(b, ti)
                self.dma(SP, hh.t[:], src, [sb_], [hh.b])
                hres.append(hh)
            nr = 4 if grid else 1
            w = 256 // nr

            def A(fc):
                pa = psA.next()
                for k in range(8):
                    self.mm(pa.t[:, 0:256], wup.t[:, k, fc * 128:(fc + 1) * 128], uT.t[:, k, :], k == 0, k == 7,
                            [wup.b, uT.b], [pa.b])
                cbuf = cbr.next()
                self.act(cbuf.t[:], pa.t[:, 0:256], AF.Identity, [pa.b, cw.b, cb.b], [cbuf.b],
                         scale=cw.t[:, fc, 1:2], bias=cb.t[:, fc:fc + 1])
                pv = pa.t[:, 0:256].rearrange("p (r w) -> p r w", r=nr)
                cv = cbuf.t[:].rearrange("p (r w) -> p r w", r=nr)
                self.stt(cv[:, :, 1:w], pv[:, :, 0:w - 1], cw.t[:, fc, 0:1], cv[:, :, 1:w], ALU.mult, ALU.add,
                         [pa.b, cw.b, cbuf.b], [cbuf.b])
                self.stt(cv[:, :, 0:w - 1], pv[:, :, 1:w], cw.t[:, fc, 2:3], cv[:, :, 0:w - 1], ALU.mult, ALU.add,
                         [pa.b, cw.b, cbuf.b], [cbuf.b])
                self.act(mT.t[:, fc, :], cbuf.t[:], AF.Gelu_apprx_tanh, [cbuf.b], [mT.b])

            def V(fc):
                pvv = psV.next()
                for k in range(8):
                    self.mm(pvv.t[:, 0:256], wup.t[:, k, 2816 + fc * 128:2816 + (fc + 1) * 128], uT.t[:, k, :],
                            k == 0, k == 7, [wup.b, uT.b], [pvv.b])
                self.tt(DVE, mT.t[:, fc, :], pvv.t[:, 0:256], mT.t[:, fc, :], ALU.mult, [pvv.b, mT.b], [mT.b])

            A(0)
            A(1)
            for fc in range(22):
                if fc + 2 < 22:
                    A(fc + 2)
                V(fc)
                if fc == 12:
                    hook()
            for j, ti in enumerate(tis):
                for n in range(2):
                    for fc in range(22):
                        self.mm(psY.t[:, n * 512:(n + 1) * 512], mT.t[:, fc, j * 128:(j + 1) * 128],
                                wdn.t[:, fc, n * 512:(n + 1) * 512], fc == 0, fc == 21, [mT.b, wdn.b], [psY.b])
                hn = hnr.next()
                dst, db_ = self.h_dst(li, b, ti)
                self.post_residual(psY, hres[j], tabs[2], hn, ssr.next(), junk, dst, db_)
                if c.debug:
                    self.dma(SP, self.dbg[li, b, ti * 128:(ti + 1) * 128, :], hn.t[:], [hn.b], [])

        self.run_pipe(sts, ld, pre_a, pre_b, main)
        P.end_phase()

    def phase_mixer(self, li):
        kind = self.cfg.kinds[li]
        j = sum(1 for k in self.cfg.kinds[:li] if k == kind)
        if kind == 0:
            self.ret_proj(li, j)
            self.ret_scan(li, j, 0)
            self.ret_scan(li, j, 1)
            self.ret_readout(li, j)
        elif kind == 1:
            self.hg_proj(li, j)
            self.hg_scan(li, j, 0)
            self.hg_scan(li, j, 1)
            self.hg_readout(li, j)
        else:
            self.m2_proj(li, j)
            self.m2_conv(li, j)
            self.m2_scan(li, j, 0)
            self.m2_scan(li, j, 1)
            self.m2_readout(li, j)

    def proj_tiles(self):
        c = self.cfg
        out = [(c.NB, b, ti) for b in range(c.NB) for ti in range(c.CT)]
        out += [(b, b, ti) for b in range(c.NB) for ti in range(c.CT, c.NT)]
        return out

    def run_pipelined(self, items, pre, main, rowof=lambda it: it[0]):
        cur = pre(items[0])
        for i, it in enumerate(items):
            nxt = None
            if i + 1 < len(items) and rowof(items[i + 1]) == rowof(it):
                nxt = pre(items[i + 1])
            main(it, cur)
            if nxt is None and i + 1 < len(items):
                nxt = pre(items[i + 1])
            cur = nxt

    def run_pipe(self, items, ld, pre_a, pre_b, main):
        n = len(items)
        ctxs = {}

        def do_ld(k):
            if k < n:
                ctxs[k] = ld(items[k])

        def do_a(k):
            if k < n:
                pre_a(items[k], ctxs[k])

        def do_b(k):
            if k < n:
                pre_b(items[k], ctxs[k])

        do_ld(0)
        do_ld(1)
        do_a(0)
        do_b(0)
        for i in range(n):
            do_ld(i + 2)
            do_a(i + 1)
            main(items[i], ctxs[i], lambda k=i + 1: do_b(k))
            ctxs.pop(i)

    def make_stages(self, li, tabs, hr, ur, ssr, uTr, junk, psTr, vecs=(0, 1)):
        state = {"row": None}

        def ld(it):
            row, b, ti = it
            h = hr.next()
            src, sb_ = self.h_src(li, b, ti)
            self.dma(SP, h.t[:], src, [sb_] if sb_ is not None else [], [h.b])
            return {"h": h}

        def pre_a(it, cx):
            row, b, ti = it
            if state["row"] != row:
                state["row"] = row
                for i, v in enumerate(vecs):
                    self.load_tab(tabs[i], li, row, v)
            u = ur.next()
            self.pre_norm(cx["h"], tabs[1], tabs[0], u, ssr.next(), junk)
            cx["u"] = u

        def pre_b(it, cx):
            uT = uTr.next()
            psT = psTr.next()
            self.transpose_to(cx["u"].t, 8, psT, uT.t[:].rearrange("p k t -> p (k t)"), uT.b, cx["u"].b)
            cx["uT"] = uT

        return ld, pre_a, pre_b

    def load_wout_scaled(self, wo, src, kchunks, gn_src):
        P = self.P
        self.load_w(wo, src, kchunks, 1024)
        gnT = Tl(P, [128, kchunks], F32, "gnT")
        self.dma(SP, gnT.t[:], gn_src.rearrange("(k p) -> p k", p=128), [], [gnT.b], allow_slow_non_contiguous=True)
        for k in range(kchunks):
            self.act(wo.t[:, k, :], wo.t[:, k, :], AF.Copy, [wo.b, gnT.b], [wo.b], scale=gnT.t[:, k:k + 1])

    def ret_proj(self, li, j):
        P, c = self.P, self.cfg
        P.begin_phase()
        w = Tl(P, [128, 8, 6144], BF16, "w_in")
        self.load_w(w, self.ret_w_in[j], 8, 6144)
        tabs = [Tl(P, [128, D], F32, f"tab{i}") for i in range(2)]
        hr = Ring(P, 3, [128, D], F32, "h")
        ur = Ring(P, 2, [128, D], BF16, "u")
        junk = Tl(P, [128, D], BF16, "junk")
        ssr = Ring(P, 4, [128, 2], F32, "ss")
        uTr = Ring(P, 2, [128, 8, 128], BF16, "uT")
        qk32r = Ring(P, 2, [128, 2048], F32, "qk32")
        ropeB = Tl(P, [128, 2048], F32, "ropeB")
        qkrr = Ring(P, 2, [128, 2048], BF16, "qkr")
        qkTr = Ring(P, 2, [128, 2048], BF16, "qkT")
        vr = Ring(P, 2, [128, 2048], BF16, "vbf")
        gr = Ring(P, 2, [128, 2048], BF16, "gbf")
        rr = Ring(P, 2, [128, 512], F32, "rope")
        psTr = Ring(P, 2, [128, 1024], BF16, "psT", psum=True)
        psM = Ring(P, 6, [128, 512], F32, "psM", psum=True)
        ld, pre_a, pre_b = self.make_stages(li, tabs, hr, ur, ssr, uTr, junk, psTr)

        def main(it, cx, hook):
            uT = cx["uT"]
            row, b, ti = it
            tok = slice(ti * 128, (ti + 1) * 128)
            qk32, qkr, qkT, vb, gb = qk32r.next(), qkrr.next(), qkTr.next(), vr.next(), gr.next()
            lat = ti >= c.CT
            if lat:
                rt = rr.next()
                self.dma(SP, rt.t[:], self.c_rope[ti - c.CT], [], [rt.b])
            for n in range(12):
                ps = psM.next()
                for k in range(8):
                    self.mm(ps.t[:], uT.t[:, k, :], w.t[:, k, n * 512:(n + 1) * 512], k == 0, k == 7, [uT.b, w.b], [ps.b])
                if n < 4:
                    self.act(qk32.t[:, n * 512:(n + 1) * 512], ps.t[:], AF.Copy, [ps.b], [qk32.b],
                             scale=(0.0625 if n >= 2 else 1.0))
                elif n < 8:
                    self.act(vb.t[:, (n - 4) * 512:(n - 3) * 512], ps.t[:], AF.Copy, [ps.b], [vb.b])
                else:
                    self.act(gb.t[:, (n - 8) * 512:(n - 7) * 512], ps.t[:], AF.Silu, [ps.b], [gb.b])
                if n == 3:
                    if lat:
                        v5 = qk32.t[:].rearrange("p (s j h e) -> p s j h e", s=8, j=2, h=2, e=64)
                        b5 = ropeB.t[:].rearrange("p (s j h e) -> p s j h e", s=8, j=2, h=2, e=64)
                        sv = rt.t[:, 256:512].rearrange("p (j h e) -> p j h e", j=2, h=2, e=64)
                        for hh in range(2):
                            self.tt(POOL, b5[:, :, :, hh, :], v5[:, :, :, 1 - hh, :],
                                    sv[:, :, hh, :].unsqueeze(1).broadcast_to([128, 8, 2, 64]), ALU.mult,
                                    [qk32.b, rt.b], [ropeB.b])
                        q3 = qk32.t[:].rearrange("p (s f) -> p s f", s=8)
                        self.tt(DVE, q3, q3, rt.t[:, 0:256].unsqueeze(1).broadcast_to([128, 8, 256]), ALU.mult,
                                [qk32.b, rt.b, ropeB.b], [qk32.b])
                        self.tt(DVE, qkr.t[:], qk32.t[:], ropeB.t[:], ALU.add, [qk32.b, ropeB.b], [qkr.b])
                    else:
                        self.cp(DVE, qkr.t[:], qk32.t[:], [qk32.b], [qkr.b])
                    self.dma(SP, self.sKt[0, b, tok, :], qkr.t[:, 1024:2048], [qkr.b], [self.dbuf(("Kt", b, ti))])
                    for half in range(2):
                        psT = psTr.next()
                        self.transpose_to(qkr.t, 8, psT, qkT.t[:, half * 1024:(half + 1) * 1024], qkT.b, qkr.b,
                                          eng=DVE, src_off=half * 1024)
                    self.dma(SP, self.sQT[0, b, ti], qkT.t[:, 0:1024], [qkT.b], [self.dbuf(("QT", b, ti))])
                    self.dma(SP, self.sKT[0, b, ti], qkT.t[:, 1024:2048], [qkT.b], [self.dbuf(("KT", b, ti))])
                if n == 7:
                    self.dma(SP, self.sV[b, tok, :], vb.t[:], [vb.b], [self.dbuf(("V", b, ti))])
                    hook()
                if n == 11:
                    self.dma(SP, self.sG[b, tok, :], gb.t[:], [gb.b], [self.dbuf(("G", b, ti))])

        self.run_pipe(self.proj_tiles(), ld, pre_a, pre_b, main)
        P.end_phase()

    def scan_order(self, d):
        c = self.cfg
        if d == 0:
            return list(range(c.NT))
        return list(range(c.CT - 1, -1, -1)) + list(range(c.NT - 1, c.CT - 1, -1))

    def ret_scan(self, li, j, d):
        P, c = self.P, self.cfg
        last = (li == self.L - 1)
        P.begin_phase()
        ct = self.load_ctab()
        dec = Tl(P, [128, 8], F32, "dec")
        lg = Tl(P, [128, 8], F32, "lg")
        nlg = Tl(P, [128, 8], F32, "nlg")
        self.dma(SP, dec.t[:], self.ret_decay[j, :].partition_broadcast(128), [], [dec.b])
        self.act(nlg.t[:], dec.t[:], AF.Exp, [dec.b], [nlg.b], scale=-1.0)
        self.act(nlg.t[:], nlg.t[:], AF.Ln, [nlg.b], [nlg.b], bias=1.0)
        self.ts(DVE, lg.t[:], nlg.t[:], -1.0, None, ALU.mult, None, [nlg.b], [lg.b])
        Dm = Tl(P, [128, 4, 128], F32, "Dm")
        E = Tl(P, [128, 4, 128], F32, "E")
        wc = Tl(P, [128, 4], F32, "wc")
        gC = Tl(P, [128, 4], F32, "gC")
        for h in range(4):
            col = slice(d * 4 + h, d * 4 + h + 1)
            sc = lg.t[:, col] if d == 0 else nlg.t[:, col]
            self.act(Dm.t[:, h, :], ct.t[:, 0, :], AF.Exp, [ct.b, lg.b, nlg.b], [Dm.b], scale=sc)
            self.tt(DVE, Dm.t[:, h, :], Dm.t[:, h, :], ct.t[:, 1 + d, :], ALU.mult, [Dm.b, ct.b], [Dm.b])
            self.act(E.t[:, h, :], ct.t[:, 3 + d, :], AF.Exp, [ct.b, lg.b], [E.b], scale=lg.t[:, col])
            self.act(wc.t[:, h:h + 1], ct.t[:, 5, d:d + 1], AF.Exp, [ct.b, lg.b], [wc.b], scale=lg.t[:, col])
            self.act(gC.t[:, h:h + 1], ct.t[:, 5, 2:3], AF.Exp, [ct.b, lg.b], [gC.b], scale=lg.t[:, col])
        S32 = {}
        Sbf = {}
        for b in range(c.NB):
            for h in range(4):
                for cc in range(2):
                    S32[b, h, cc] = Tl(P, [128, 512], F32, "S32")
                    Sbf[b, h, cc] = Tl(P, [128, 512], BF16, "Sbf")
                    self.memset(POOL, S32[b, h, cc].t[:], 0.0, [S32[b, h, cc].b])
                    self.memset(DVE, Sbf[b, h, cc].t[:], 0.0, [Sbf[b, h, cc].b])
        qTr = Ring(P, 3, [128, 8, 128], BF16, "qT")
        kTr = Ring(P, 3, [128, 8, 128], BF16, "kT")
        ktr = Ring(P, 3, [128, 4, 256], BF16, "kt")
        vr = Ring(P, 3, [128, 2048], BF16, "v")
        otr = Ring(P, 2, [128, 2048], F32, "ot")
        oflr = Ring(P, 3, [128, 2048], F32, "ofl") if d == 1 else None
        PTr = Ring(P, 2, [128, 4, 128], BF16, "PT")
        qsr = Ring(P, 2, [128, 8, 128], BF16, "qs")
        kwr = Ring(P, 2, [128, 4, 256], BF16, "kw")
        psS = Tl(P, [128, 512], F32, "psS", psum=True)
        psO = Tl(P, [128, 2048], F32, "psO", psum=True)
        psK = Ring(P, 3, [128, 512], F32, "psK", psum=True)
        steps = [(ti, b) for ti in self.scan_order(d) for b in range(c.NB)]

        def ld(st):
            ti, b = st
            tok = slice(ti * 128, (ti + 1) * 128)
            qT, kT, kt, v = qTr.next(), kTr.next(), ktr.next(), vr.next()
            need_o = not (last and ti < c.CT)
            if need_o:
                self.dma(SP, qT.t[:].rearrange("p k t -> p (k t)"), self.sQT[0, b, ti], [self.dbuf(("QT", b, ti))], [qT.b])
                self.dma(SP, kT.t[:].rearrange("p k t -> p (k t)"), self.sKT[0, b, ti], [self.dbuf(("KT", b, ti))], [kT.b])
            self.dma(SP, kt.t[:].rearrange("p h f -> p (h f)"), self.sKt[0, b, tok, :], [self.dbuf(("Kt", b, ti))], [kt.b])
            self.dma(SP, v.t[:], self.sV[b, tok, :], [self.dbuf(("V", b, ti))], [v.b])
            ofl = None
            if d == 1 and need_o:
                ofl = oflr.next()
                self.dma(SP, ofl.t[:], self.sO[0, b, tok, :], [self.dbuf(("O", 0, b, ti))], [ofl.b])
            return qT, kT, kt, v, need_o, ofl

        def comp(st, tl):
            ti, b = st
            tok = slice(ti * 128, (ti + 1) * 128)
            qT, kT, kt, v, need_o, ofl = tl
            if need_o:
                for h in range(4):
                    for cc in range(2):
                        self.mm(psS.t[:, h * 128:(h + 1) * 128], kT.t[:, h * 2 + cc, :], qT.t[:, h * 2 + cc, :], cc == 0, cc == 1,
                                [kT.b, qT.b], [psS.b])
                PT = PTr.next()
                self.tt(DVE, PT.t[:], psS.t[:].rearrange("p (h t) -> p h t", h=4), Dm.t[:], ALU.mult, [psS.b, Dm.b], [PT.b])
                qs = qsr.next()
                self.tt(POOL, qs.t[:].rearrange("p (h c) t -> p h c t", h=4), qT.t[:].rearrange("p (h c) t -> p h c t", h=4),
                        E.t[:].unsqueeze(2).broadcast_to([128, 4, 2, 128]), ALU.mult, [qT.b, E.b], [qs.b])
            kw = kwr.next()
            self.tt(POOL, kw.t[:], kt.t[:], wc.t[:, 0:4].unsqueeze(2).broadcast_to([128, 4, 256]), ALU.mult, [kt.b, wc.b], [kw.b])
            if need_o:
                for h in range(4):
                    vh = v.t[:, h * 512:(h + 1) * 512]
                    po = psO.t[:, h * 512:(h + 1) * 512]
                    self.mm(po, PT.t[:, h, :], vh, True, False, [PT.b, v.b], [psO.b])
                    for cc in range(2):
                        self.mm(po, qs.t[:, h * 2 + cc, :], Sbf[b, h, cc].t[:], False, cc == 1,
                                [qs.b, Sbf[b, h, cc].b], [psO.b])
                ot = otr.next()
                for h in range(4):
                    self.act(ot.t[:, h * 512:(h + 1) * 512], psO.t[:, h * 512:(h + 1) * 512], AF.Copy, [psO.b], [ot.b])
                if d == 1:
                    self.tt(POOL, ot.t[:], ot.t[:], ofl.t[:], ALU.add, [ot.b, ofl.b], [ot.b])
                self.dma(SP, self.sO[d, b, tok, :], ot.t[:], [ot.b], [self.dbuf(("O", d, b, ti))])
            for h in range(4):
                vh = v.t[:, h * 512:(h + 1) * 512]
                for cc in range(2):
                    pk = psK.next()
                    self.mm(pk.t[:], kw.t[:, h, cc * 128:(cc + 1) * 128], vh, True, True, [kw.b, v.b], [pk.b])
                    s32 = S32[b, h, cc]
                    self.stt(s32.t[:], s32.t[:], gC.t[:, h:h + 1], pk.t[:], ALU.mult, ALU.add, [s32.b, gC.b, pk.b], [s32.b])
                    self.cp(ACT, Sbf[b, h, cc].t[:], s32.t[:], [s32.b], [Sbf[b, h, cc].b])

        self.run_pipelined(steps, ld, comp, rowof=lambda it: 0)
        P.end_phase()

    def readout_main(self, li, wo, nk, g1, psYr, hnr, ssr, junk):
        state = {"row": None}

        def main(it, cx, hook):
            row, b, ti = it
            onT = cx["onT"]
            if state["row"] != row:
                state["row"] = row
                self.load_tab(g1, li, row, 2)
            psY = psYr.next()
            for n in range(2):
                for k in range(nk):
                    self.mm(psY.t[:, n * 512:(n + 1) * 512], onT.t[:, k, :], wo.t[:, k, n * 512:(n + 1) * 512],
                            k == 0, k == nk - 1, [onT.b, wo.b], [psY.b])
                if n == 0:
                    hook()
            hn = hnr.next()
            dst, db_ = self.h_mid(b, ti)
            self.post_residual(psY, cx["h"], g1, hn, ssr.next(), junk, dst, db_)
        return main

    def readout_items(self, li):
        keep = set(self.tile_list(li, with_ctx=False))
        return [it for it in self.proj_tiles() if (it[1], it[2]) in keep]

    def ret_readout(self, li, j):
        P, c = self.P, self.cfg
        P.begin_phase()
        wo = Tl(P, [128, 16, 1024], BF16, "w_out")
        self.load_wout_scaled(wo, self.ret_w_out[j], 16, self.ret_gn[j])
        g1 = Tl(P, [128, D], F32, "g1")
        ofr = Ring(P, 3, [128, 2048], F32, "of")
        gr = Ring(P, 3, [128, 2048], BF16, "g")
        hr = Ring(P, 3, [128, D], F32, "h")
        onr = Ring(P, 2, [128, 2048], BF16, "on")
        onTr = Ring(P, 2, [128, 16, 128], BF16, "onT")
        str_ = Ring(P, 2, [128, 4, 6], F32, "bst")
        mvr = Ring(P, 2, [128, 4, 2], F32, "mv")
        rsr = Ring(P, 2, [128, 4], F32, "rs")
        hnr = Ring(P, 2, [128, D], F32, "hn")
        ssr = Ring(P, 4, [128, 2], F32, "ss")
        junk = Tl(P, [128, D], BF16, "junk")
        psTr = Ring(P, 2, [128, 1024], BF16, "psT", psum=True)
        psYr = Ring(P, 2, [128, 1024], F32, "psY", psum=True)

        def ld(it):
            row, b, ti = it
            tok = slice(ti * 128, (ti + 1) * 128)
            of, g, h = ofr.next(), gr.next(), hr.next()
            self.dma(SP, of.t[:], self.sO[1, b, tok, :], [self.dbuf(("O", 1, b, ti))], [of.b])
            self.dma(SP, g.t[:], self.sG[b, tok, :], [self.dbuf(("G", b, ti))], [g.b])
            src, sb_ = self.h_src(li, b, ti)
            self.dma(SP, h.t[:], src, [sb_] if sb_ is not None else [], [h.b])
            return {"of": of, "g": g, "h": h}

        def pre_a(it, cx):
            of, g = cx["of"], cx["g"]
            on, bst, mv, rs = onr.next(), str_.next(), mvr.next(), rsr.next()
            for h in range(4):
                oh = of.t[:, h * 512:(h + 1) * 512]
                self.P.op(DVE, lambda e, o=bst.t[:, h, :], i=oh: e.bn_stats(out=o, in_=i), [of.b], [bst.b])
            for h in range(4):
                self.P.op(DVE, lambda e, o=mv.t[:, h, :], i=bst.t[:, h, :]: e.bn_aggr(out=o, in_=i), [bst.b], [mv.b])
            self.rstd(rs.t[:], mv.t[:, :, 1], 1.0, EPS, [mv.b], [rs.b], n=4)
            for h in range(4):
                oh = of.t[:, h * 512:(h + 1) * 512]
                self.stt(oh, oh, mv.t[:, h, 0:1], g.t[:, h * 512:(h + 1) * 512], ALU.subtract, ALU.mult,
                         [of.b, mv.b, g.b], [of.b])
            for h in range(4):
                oh = of.t[:, h * 512:(h + 1) * 512]
                self.act(on.t[:, h * 512:(h + 1) * 512], oh, AF.Copy, [of.b, rs.b], [on.b], scale=rs.t[:, h:h + 1])
            cx["on"] = on

        def pre_b(it, cx):
            on, onT = cx["on"], onTr.next()
            for half in range(2):
                psT = psTr.next()
                self.transpose_to(on.t, 8, psT, onT.t[:, half * 8:(half + 1) * 8, :].rearrange("p k t -> p (k t)"),
                                  onT.b, on.b, eng=(ACT if half == 0 else DVE), src_off=half * 1024)
            cx["onT"] = onT

        self.run_pipe(self.readout_items(li), ld, pre_a, pre_b, self.readout_main(li, wo, 16, g1, psYr, hnr, ssr, junk))
        P.end_phase()

    def hg_proj(self, li, j):
        P, c = self.P, self.cfg
        L = self.L
        P.begin_phase()
        ct = self.load_ctab()
        w = Tl(P, [128, 8, 5120], BF16, "w_in")
        self.load_w(w, self.hg_w_in[j], 8, 5120)
        lbx = Tl(P, [128, L, D], F32, "lbx")
        lb = Tl(P, [128, D], F32, "lb")
        omlb = Tl(P, [128, D], F32, "omlb")
        den = Tl(P, [128, D], F32, "den")
        self.dma(SP, lbx.t[:].rearrange("p l d -> p (l d)"), self.hg_lb.rearrange("l d -> (l d)").partition_broadcast(128), [], [lbx.b])
        self.act(lbx.t[:], lbx.t[:], AF.Exp, [lbx.b], [lbx.b])
        self.cp(DVE, den.t[:], lbx.t[:, 0, :], [lbx.b], [den.b])
        self.memset(DVE, lb.t[:], 0.0, [lb.b])
        for r in range(1, L):
            self.tt(DVE, den.t[:], den.t[:], lbx.t[:, r, :], ALU.add, [den.b, lbx.b], [den.b])
            if r <= li:
                self.tt(DVE, lb.t[:], lb.t[:], lbx.t[:, r, :], ALU.add, [lb.b, lbx.b], [lb.b])
        self.P.op(DVE, lambda e: e.reciprocal(out=den.t[:], in_=den.t[:]), [den.b], [den.b])
        self.tt(DVE, lb.t[:], lb.t[:], den.t[:], ALU.mult, [lb.b, den.b], [lb.b])
        self.ts(DVE, omlb.t[:], lb.t[:], -1.0, 1.0, ALU.mult, ALU.add, [lb.b], [omlb.b])
        tabs = [Tl(P, [128, D], F32, f"tab{i}") for i in range(2)]
        hr = Ring(P, 3, [128, D], F32, "h")
        ur = Ring(P, 2, [128, D], BF16, "u")
        junk = Tl(P, [128, D], BF16, "junk")
        ssr = Ring(P, 4, [128, 2], F32, "ss")
        uTr = Ring(P, 2, [128, 8, 128], BF16, "uT")
        qs = Tl(P, [128, D], F32, "qs")
        a32 = [Tl(P, [128, D], F32, f"a32{d}") for d in range(2)]
        la32 = [Tl(P, [128, D], F32, f"la{d}") for d in range(2)]
        k32 = a32
        e32r = Ring(P, 2, [128, D], F32, "e32")
        qtr = Ring(P, 2, [128, D], BF16, "qt")
        ktr = Ring(P, 2, [128, D], BF16, "kt")
        stg = Ring(P, 4, [128, D], BF16, "stg")
        vr = Ring(P, 2, [128, D], BF16, "vbf")
        gr = Ring(P, 2, [128, D], BF16, "gbf")
        csr = Ring(P, 2, [128, 48], F32, "cs")
        psTr = Ring(P, 2, [128, 1024], BF16, "psT", psum=True)
        psM = Ring(P, 5, [128, 512], F32, "psM", psum=True)
        psC = Tl(P, [128, 512], F32, "psC", psum=True)
        ld, pre_a, pre_b = self.make_stages(li, tabs, hr, ur, ssr, uTr, junk, psTr)

        def main(it, cx, hook):
            uT = cx["uT"]
            row, b, ti = it
            tok = slice(ti * 128, (ti + 1) * 128)
            vb, gb = vr.next(), gr.next()
            for n in range(10):
                if n == 7:
                    hook()
                ps = psM.next()
                for k in range(8):
                    self.mm(ps.t[:], uT.t[:, k, :], w.t[:, k, n * 512:(n + 1) * 512], k == 0, k == 7, [uT.b, w.b], [ps.b])
                cs_ = slice((n % 2) * 512, (n % 2) * 512 + 512)
                if n < 2:
                    self.act(qs.t[:, cs_], ps.t[:], AF.Silu, [ps.b], [qs.b])
                elif n < 6:
                    d = (n - 2) // 2
                    self.act(a32[d].t[:, cs_], ps.t[:], AF.Sigmoid, [ps.b], [a32[d].b])
                    self.tt(DVE, a32[d].t[:, cs_], a32[d].t[:, cs_], omlb.t[:, cs_], ALU.mult, [a32[d].b, omlb.b], [a32[d].b])
                    self.tt(POOL, a32[d].t[:, cs_], a32[d].t[:, cs_], lb.t[:, cs_], ALU.add, [a32[d].b, lb.b], [a32[d].b])
                    self.act(la32[d].t[:, cs_], a32[d].t[:, cs_], AF.Ln, [a32[d].b], [la32[d].b])
                    self.ts(POOL, a32[d].t[:, cs_], a32[d].t[:, cs_], -1.0, 1.0, ALU.mult, ALU.add, [a32[d].b], [a32[d].b])
                elif n < 8:
                    self.cp(DVE, vb.t[:, cs_], ps.t[:], [ps.b], [vb.b])
                else:
                    self.act(gb.t[:, cs_], ps.t[:], AF.Silu, [ps.b], [gb.b])
            self.dma(SP, self.sV[b, tok, 0:D], vb.t[:], [vb.b], [self.dbuf(("V", b, ti))])
            self.dma(SP, self.sG[b, tok, 0:D], gb.t[:], [gb.b], [self.dbuf(("G", b, ti))])
            for d in range(2):
                qt, kt = qtr.next(), ktr.next()
                for n in range(2):
                    cs_ = slice(n * 512, n * 512 + 512)
                    ps = psM.next()
                    self.mm(ps.t[:], ct.t[:, 8 + d, :], la32[d].t[:, cs_], True, True, [ct.b, la32[d].b], [ps.b])
                    e1, e2 = e32r.next(), e32r.next()
                    self.act(e1.t[:, 0:512], ps.t[:], AF.Exp, [ps.b], [e1.b])
                    self.act(e2.t[:, 0:512], ps.t[:], AF.Exp, [ps.b], [e2.b], scale=-1.0)
                    self.tt(DVE, qt.t[:, cs_], qs.t[:, cs_], e1.t[:, 0:512], ALU.mult, [qs.b, e1.b], [qt.b])
                    self.tt(POOL, kt.t[:, cs_], k32[d].t[:, cs_], e2.t[:, 0:512], ALU.mult, [k32[d].b, e2.b], [kt.b])
                for h in range(8):
                    self.mm(psC.t[:, h * 6:(h + 1) * 6], la32[d].t[:, h * 128:(h + 1) * 128], ct.t[:, 10 + d, 0:6],
                            True, True, [la32[d].b, ct.b], [psC.b])
                cs = csr.next()
                self.act(cs.t[:], psC.t[:, 0:48], AF.Exp, [psC.b], [cs.b])
                self.dma(SP, self.sCS[d, b, ti], cs.t[:], [cs.b], [self.dbuf(("CS", d, b, ti))])
                self.dma(SP, self.sKt[d, b, tok, :], kt.t[:], [kt.b], [self.dbuf(("Kt", d, b, ti))])
                for which, src in ((0, qt), (1, kt)):
                    psT = psTr.next()
                    st = stg.next()
                    self.transpose_to(src.t, 8, psT, st.t[:], st.b, src.b, eng=DVE)
                    dst = (self.sQT if which == 0 else self.sKT)[d, b, ti]
                    self.dma(SP, dst, st.t[:], [st.b], [self.dbuf(("QT" if which == 0 else "KT", d, b, ti))])

        self.run_pipe(self.proj_tiles(), ld, pre_a, pre_b, main)
        P.end_phase()

    def hg_scan(self, li, j, d):
        P, c = self.P, self.cfg
        last = (li == self.L - 1)
        P.begin_phase()
        ct = self.load_ctab()
        S = {}
        for b in range(c.NB):
            for hh in range(2):
                S[b, hh] = Tl(P, [128, 4, 128], F32, "S")
                self.memset(POOL, S[b, hh].t[:], 0.0, [S[b, hh].b])
        qTr = Ring(P, 3, [128, 8, 128], BF16, "qT")
        kTr = Ring(P, 3, [128, 8, 128], BF16, "kT")
        ktr = Ring(P, 3, [128, D], BF16, "kt")
        vr = Ring(P, 3, [128, D], BF16, "v")
        csr = Ring(P, 3, [128, 8, 2, 3], F32, "cs")
        otr = Ring(P, 2, [128, D], F32, "ot")
        oflr = Ring(P, 3, [128, D], F32, "ofl") if d == 1 else None
        PTr = Ring(P, 2, [128, 4, 128], BF16, "PT")
        Sxr = Ring(P, 4, [128, 4, 128], BF16, "Sx")
        tmpr = Ring(P, 2, [128, 4, 128], F32, "tmp")
        psS = Ring(P, 2, [128, 512], F32, "psS", psum=True)
        psO = Ring(P, 2, [128, 512], F32, "psO", psum=True)
        psK = Ring(P, 4, [128, 512], F32, "psK", psum=True)
        steps = [(ti, b) for ti in self.scan_order(d) for b in range(c.NB)]
        corder = (0, 1) if d == 0 else (1, 0)

        def ld(st):
            ti, b = st
            tok = slice(ti * 128, (ti + 1) * 128)
            qT, kT, kt, v, cs = qTr.next(), kTr.next(), ktr.next(), vr.next(), csr.next()
            need_o = not (last and ti < c.CT)
            self.dma(SP, qT.t[:].rearrange("p k t -> p (k t)"), self.sQT[d, b, ti], [self.dbuf(("QT", d, b, ti))], [qT.b])
            self.dma(SP, kT.t[:].rearrange("p k t -> p (k t)"), self.sKT[d, b, ti], [self.dbuf(("KT", d, b, ti))], [kT.b])
            self.dma(SP, kt.t[:], self.sKt[d, b, tok, :], [self.dbuf(("Kt", d, b, ti))], [kt.b])
            self.dma(SP, v.t[:], self.sV[b, tok, 0:D], [self.dbuf(("V", b, ti))], [v.b])
            self.dma(SP, cs.t[:].rearrange("p h c k -> p (h c k)"), self.sCS[d, b, ti], [self.dbuf(("CS", d, b, ti))], [cs.b])
            ofl = None
            if d == 1 and need_o:
                ofl = oflr.next()
                self.dma(SP, ofl.t[:], self.sO[0, b, tok, 0:D], [self.dbuf(("O", 0, b, ti))], [ofl.b])
            return qT, kT, kt, v, cs, need_o, ofl

        def comp(st, tl):
            ti, b = st
            tok = slice(ti * 128, (ti + 1) * 128)
            qT, kT, kt, v, cs, need_o, ofl = tl
            ot = otr.next() if need_o else None
            for hh in range(2):
                Sb = S[b, hh]
                hs = slice(hh * 4, hh * 4 + 4)

                def bc(kind, cc):
                    return cs.t[:, hs, cc, kind:kind + 1].broadcast_to([128, 4, 128])

                if need_o:
                    ps = psS.next()
                    for hl in range(4):
                        h = hh * 4 + hl
                        self.mm(ps.t[:, hl * 128:(hl + 1) * 128], kT.t[:, h, :], qT.t[:, h, :], True, True, [kT.b, qT.b], [ps.b])
                    PT = PTr.next()
                    self.tt(DVE, PT.t[:], ps.t[:].rearrange("p (h t) -> p h t", h=4),
                            ct.t[:, 6 + d, :].unsqueeze(1).broadcast_to([128, 4, 128]), ALU.mult, [ps.b, ct.b], [PT.b])
                    po = psO.next()
                    for hl in range(4):
                        h = hh * 4 + hl
                        self.mm(po.t[:, hl * 128:(hl + 1) * 128], PT.t[:, hl, :], v.t[:, h * 128:(h + 1) * 128], hl == 0, False,
                                [PT.b, v.b], [po.b], skip=True)
                for ci, cc in enumerate(corder):
                    rows = slice(cc * 64, cc * 64 + 64)
                    if need_o:
                        Sx = Sxr.next()
                        self.tt(POOL, Sx.t[:], Sb.t[:], bc(0, cc), ALU.mult, [Sb.b, cs.b], [Sx.b])
                        for hl in range(4):
                            h = hh * 4 + hl
                            self.mm(po.t[rows, hl * 128:(hl + 1) * 128], qT.t[:, h, rows], Sx.t[:, hl, :], False,
                                    True, [qT.b, Sx.b], [po.b], skip=True)
                    pk = psK.next()
                    for hl in range(4):
                        h = hh * 4 + hl
                        self.mm(pk.t[:, hl * 128:(hl + 1) * 128], kt.t[rows, h * 128:(h + 1) * 128],
                                v.t[rows, h * 128:(h + 1) * 128], True, True, [kt.b, v.b], [pk.b])
                    tmp = tmpr.next()
                    self.tt(DVE, tmp.t[:], pk.t[:].rearrange("p (h t) -> p h t", h=4), bc(2, cc), ALU.mult, [pk.b, cs.b], [tmp.b])
                    self.tt(POOL, Sb.t[:], Sb.t[:], bc(1, cc), ALU.mult, [Sb.b, cs.b], [Sb.b])
                    self.tt(DVE, Sb.t[:], Sb.t[:], tmp.t[:], ALU.add, [Sb.b, tmp.b], [Sb.b])
                if need_o and d == 0:
                    self.act(ot.t[:, hh * 512:(hh + 1) * 512], po.t[:], AF.Copy, [po.b], [ot.b])
                elif need_o:
                    self.tt(DVE, ot.t[:, hh * 512:(hh + 1) * 512], po.t[:], ofl.t[:, hh * 512:(hh + 1) * 512], ALU.add,
                            [po.b, ofl.b], [ot.b])
            if need_o:
                self.dma(SP, self.sO[d, b, tok, 0:D], ot.t[:], [ot.b], [self.dbuf(("O", d, b, ti))])

        self.run_pipelined(steps, ld, comp, rowof=lambda it: 0)
        P.end_phase()

    def hg_readout(self, li, j):
        P, c = self.P, self.cfg
        P.begin_phase()
        wo = Tl(P, [128, 8, 1024], BF16, "w_out")
        self.load_wout_scaled(wo, self.hg_w_out[j], 8, self.hg_gn[j])
        g1 = Tl(P, [128, D], F32, "g1")
        ofr = Ring(P, 3, [128, D], F32, "of")
        gr = Ring(P, 3, [128, D], BF16, "g")
        hr = Ring(P, 3, [128, D], F32, "h")
        sqr = Ring(P, 2, [128, D], F32, "sq")
        onr = Ring(P, 2, [128, D], BF16, "on")
        onTr = Ring(P, 2, [128, 8, 128], BF16, "onT")
        rsr = Ring(P, 2, [128, 8], F32, "rs")
        hnr = Ring(P, 2, [128, D], F32, "hn")
        ssr = Ring(P, 4, [128, 2], F32, "ss")
        junk = Tl(P, [128, D], BF16, "junk")
        psTr = Ring(P, 2, [128, 1024], BF16, "psT", psum=True)
        psYr = Ring(P, 2, [128, 1024], F32, "psY", psum=True)

        def ld(it):
            row, b, ti = it
            tok = slice(ti * 128, (ti + 1) * 128)
            of, g, h = ofr.next(), gr.next(), hr.next()
            self.dma(SP, of.t[:], self.sO[1, b, tok, 0:D], [self.dbuf(("O", 1, b, ti))], [of.b])
            self.dma(SP, g.t[:], self.sG[b, tok, 0:D], [self.dbuf(("G", b, ti))], [g.b])
            src, sb_ = self.h_src(li, b, ti)
            self.dma(SP, h.t[:], src, [sb_] if sb_ is not None else [], [h.b])
            return {"of": of, "g": g, "h": h}

        def pre_a(it, cx):
            of, g = cx["of"], cx["g"]
            on, rs, sq = onr.next(), rsr.next(), sqr.next()
            self.tt(POOL, sq.t[:], of.t[:], of.t[:], ALU.mult, [of.b], [sq.b])
            self.P.op(DVE, lambda e, o=rs.t[:], i=sq.t[:].rearrange("p (h v) -> p h v", h=8):
                      e.tensor_reduce(out=o, in_=i, axis=mybir.AxisListType.X, op=ALU.add), [sq.b], [rs.b])
            self.rstd(rs.t[:], rs.t[:], 1.0 / 128, EPS, [rs.b], [rs.b], n=8)
            self.tt(DVE, of.t[:], of.t[:], g.t[:], ALU.mult, [of.b, g.b], [of.b])
            self.tt(DVE, on.t[:].rearrange("p (h v) -> p h v", h=8), of.t[:].rearrange("p (h v) -> p h v", h=8),
                    rs.t[:].unsqueeze(2).broadcast_to([128, 8, 128]), ALU.mult, [of.b, rs.b], [on.b])
            cx["on"] = on

        def pre_b(it, cx):
            on, onT = cx["on"], onTr.next()
            psT = psTr.next()
            self.transpose_to(on.t, 8, psT, onT.t[:].rearrange("p k t -> p (k t)"), onT.b, on.b)
            cx["onT"] = onT

        self.run_pipe(self.readout_items(li), ld, pre_a, pre_b, self.readout_main(li, wo, 8, g1, psYr, hnr, ssr, junk))
        P.end_phase()

    def x_row(self, ti):
        c = self.cfg
        if ti < c.CT:
            return 1 + ti * 128
        return c.CTX + 3 + (ti - c.CT) * 128

    def m2_proj(self, li, j):
        P, c = self.P, self.cfg
        P.begin_phase()
        w = Tl(P, [128, 8, 5184], BF16, "w_in")
        self.load_w(w, self.m2_w_in[j], 8, 5184)
        tabs = [Tl(P, [128, D], F32, f"tab{i}") for i in range(2)]
        dtb = Tl(P, [128, 64], F32, "dtb")
        aneg = Tl(P, [128, 64], F32, "aneg")
        self.dma(SP, dtb.t[:], self.m2_dt_bias[j, :].partition_broadcast(128), [], [dtb.b])
        self.dma(SP, aneg.t[:], self.m2_a_log[j, :].partition_broadcast(128), [], [aneg.b])
        self.act(aneg.t[:], aneg.t[:], AF.Exp, [aneg.b], [aneg.b])
        self.ts(DVE, aneg.t[:], aneg.t[:], -1.0, None, ALU.mult, None, [aneg.b], [aneg.b])
        zero = Tl(P, [1, 3072], F32, "zero")
        self.memset(DVE, zero.t[:], 0.0, [zero.b])
        for b in range(c.NB):
            for r in (0, c.CTX + 1, c.CTX + 2, c.CTX + c.LAT + 3):
                self.dma(SP, self.sX[b, r:r + 1, :], zero.t[:], [zero.b], [self.dbuf(("Xpad", b, r))])
        hr = Ring(P, 3, [128, D], F32, "h")
        ur = Ring(P, 2, [128, D], BF16, "u")
        junk = Tl(P, [128, D], BF16, "junk")
        ssr = Ring(P, 4, [128, 2], F32, "ss")
        uTr = Ring(P, 2, [128, 8, 128], BF16, "uT")
        zr = Ring(P, 2, [128, 2048], BF16, "zb")
        xr = Ring(P, 2, [128, 3072], F32, "xbc")
        dr = Ring(P, 2, [128, 128], F32, "dtla")
        psTr = Ring(P, 2, [128, 1024], BF16, "psT", psum=True)
        psM = Ring(P, 6, [128, 512], F32, "psM", psum=True)
        ld, pre_a, pre_b = self.make_stages(li, tabs, hr, ur, ssr, uTr, junk, psTr)

        def main(it, cx, hook):
            uT = cx["uT"]
            row, b, ti = it
            tok = slice(ti * 128, (ti + 1) * 128)
            zb, xb, dl = zr.next(), xr.next(), dr.next()
            for n in range(11):
                if n == 6:
                    hook()
                ps = psM.next()
                wd = 512 if n < 10 else 64
                for k in range(8):
                    self.mm(ps.t[:, 0:wd], uT.t[:, k, :], w.t[:, k, n * 512:n * 512 + wd], k == 0, k == 7, [uT.b, w.b], [ps.b])
                if n < 4:
                    self.act(zb.t[:, n * 512:(n + 1) * 512], ps.t[:], AF.Silu, [ps.b], [zb.b])
                elif n < 10:
                    eng = ACT if n % 2 == 0 else DVE
                    self.cp(eng, xb.t[:, (n - 4) * 512:(n - 3) * 512], ps.t[:], [ps.b], [xb.b])
                else:
                    self.tt(DVE, dl.t[:, 0:64], ps.t[:, 0:64], dtb.t[:], ALU.add, [ps.b, dtb.b], [dl.b])
                    self.act(dl.t[:, 0:64], dl.t[:, 0:64], AF.Exp, [dl.b], [dl.b])
                    self.act(dl.t[:, 0:64], dl.t[:, 0:64], AF.Ln, [dl.b], [dl.b], bias=1.0)
                    self.tt(DVE, dl.t[:, 64:128], dl.t[:, 0:64], aneg.t[:], ALU.mult, [dl.b, aneg.b], [dl.b])
            self.dma(SP, self.sG[b, tok, :], zb.t[:], [zb.b], [self.dbuf(("G", b, ti))])
            r0 = self.x_row(ti)
            self.dma(SP, self.sX[b, r0:r0 + 128, :], xb.t[:], [xb.b], [self.dbuf(("X", b, ti))])
            self.dma(SP, self.sDT[b, tok, :], dl.t[:], [dl.b], [self.dbuf(("DT", b, ti))])

        self.run_pipe(self.proj_tiles(), ld, pre_a, pre_b, main)
        P.end_phase()

    def m2_conv(self, li, j):
        P, c = self.P, self.cfg
        P.begin_phase()
        cw = Tl(P, [128, 3, 3072], F32, "cw")
        cb = Tl(P, [128, 3072], F32, "cb")
        self.dma(SP, cw.t[:].rearrange("p a b -> p (a b)"), self.m2_conv_w[j].rearrange("a b -> (a b)").partition_broadcast(128), [], [cw.b])
        self.dma(SP, cb.t[:], self.m2_conv_b[j, :].partition_broadcast(128), [], [cb.b])
        xr = [Ring(P, 2, [128, 3072], F32, f"x{i}") for i in range(3)]
        actr = Ring(P, 2, [128, 3072], BF16, "act")
        stg = Ring(P, 2, [128, 1024], BF16, "stg")
        psTr = Ring(P, 2, [128, 1024], BF16, "psT", psum=True)
        items = [(b, ti) for b in range(c.NB) for ti in range(c.NT)]

        def ld(it):
            b, ti = it
            r0 = self.x_row(ti)
            xs = [xr[i].next() for i in range(3)]
            deps = [self.dbuf(("X", b, t2)) for t2 in range(c.NT)] + [self.dbuf(("Xpad", b, r)) for r in (0, c.CTX + 1, c.CTX + 2, c.CTX + c.LAT + 3)]
            for i in range(3):
                self.dma(SP, xs[i].t[:], self.sX[b, r0 - 1 + i:r0 - 1 + i + 128, :], deps, [xs[i].b])
            return xs

        def comp(it, xs):
            b, ti = it
            tok = slice(ti * 128, (ti + 1) * 128)
            x0, x1, x2 = xs
            self.tt(POOL, x0.t[:], x0.t[:], cw.t[:, 0, :], ALU.mult, [x0.b, cw.b], [x0.b])
            self.tt(DVE, x1.t[:], x1.t[:], cw.t[:, 1, :], ALU.mult, [x1.b, cw.b], [x1.b])
            self.tt(DVE, x2.t[:], x2.t[:], cw.t[:, 2, :], ALU.mult, [x2.b, cw.b], [x2.b])
            self.tt(DVE, x1.t[:], x1.t[:], cb.t[:], ALU.add, [x1.b, cb.b], [x1.b])
            self.tt(DVE, x1.t[:], x1.t[:], x2.t[:], ALU.add, [x1.b, x2.b], [x1.b])
            self.tt(DVE, x1.t[:], x1.t[:], x0.t[:], ALU.add, [x1.b, x0.b], [x1.b])
            a = actr.next()
            self.act(a.t[:], x1.t[:], AF.Silu, [x1.b], [a.b])
            self.dma(SP, self.sV[b, tok, :], a.t[:, 0:2048], [a.b], [self.dbuf(("V", b, ti))])
            self.dma(SP, self.sKt[0, b, tok, 0:512], a.t[:, 2048:2560], [a.b], [self.dbuf(("Kt", b, ti))])
            psT = psTr.next()
            st = stg.next()
            self.transpose_to(a.t, 8, psT, st.t[:], st.b, a.b, eng=ACT, src_off=2048)
            self.dma(SP, self.sKT[0, b, ti][:, 0:512], st.t[:, 0:512], [st.b], [self.dbuf(("KT", b, ti))])
            self.dma(SP, self.sQT[0, b, ti][:, 0:512], st.t[:, 512:1024], [st.b], [self.dbuf(("QT", b, ti))])

        self.run_pipelined(items, ld, comp, rowof=lambda it: 0)
        P.end_phase()

    def m2_scan(self, li, j, d):
        P, c = self.P, self.cfg
        last = (li == self.L - 1)
        P.begin_phase()
        ct = self.load_ctab()
        S32, Sbf = {}, {}
        for b in range(c.NB):
            for g in range(4):
                S32[b, g] = Tl(P, [128, 8, 64], F32, "S32")
                Sbf[b, g] = Tl(P, [128, 512], BF16, "Sbf")
                self.memset(POOL, S32[b, g].t[:], 0.0, [S32[b, g].b])
                self.memset(DVE, Sbf[b, g].t[:], 0.0, [Sbf[b, g].b])
        CTr = Ring(P, 3, [128, 4, 128], BF16, "CT")
        BTr = Ring(P, 3, [128, 4, 128], BF16, "BT")
        Btr = Ring(P, 3, [128, 512], BF16, "Bt")
        Xr = Ring(P, 3, [128, 32, 64], BF16, "X")
        dlr = Ring(P, 3, [128, 128], F32, "dtla")
        ear = Ring(P, 2, [128, 96], F32, "eall")
        xdr = Ring(P, 2, [128, 32, 64], BF16, "xdt")
        xwr = Ring(P, 2, [128, 32, 64], BF16, "xw")
        Rr = Ring(P, 2, [128, 8, 128], F32, "R")
        Lr = Ring(P, 2, [128, 8, 128], BF16, "L")
        CBr = Ring(P, 2, [128, 128], F32, "CBm")
        Pmr = Ring(P, 2, [128, 8, 128], BF16, "Pm")
        tmpr = Ring(P, 2, [128, 8, 64], F32, "tmp")
        otr = Ring(P, 2, [128, 2048], F32, "ot")
        psE = Tl(P, [128, 512], F32, "psE", psum=True)
        psD = Tl(P, [128, 1024], F32, "psD", psum=True)
        psCB = Tl(P, [128, 512], F32, "psCB", psum=True)
        psO = Ring(P, 2, [128, 512], F32, "psO", psum=True)
        psI = Tl(P, [128, 512], F32, "psI", psum=True)
        psK = Tl(P, [128, 512], F32, "psK", psum=True)
        steps = [(ti, b) for ti in self.scan_order(d) for b in range(c.NB)]
        tri = ct.t[:, 12 + d, :]
        G = ct.t[:, 14 + d, :]
        ones = ct.t[:, 16, :]

        def ld(st):
            ti, b = st
            tok = slice(ti * 128, (ti + 1) * 128)
            CT, BT, Bt, X, dl = CTr.next(), BTr.next(), Btr.next(), Xr.next(), dlr.next()
            need_o = not (last and ti < c.CT)
            if need_o:
                self.dma(SP, CT.t[:].rearrange("p g t -> p (g t)"), self.sQT[0, b, ti][:, 0:512], [self.dbuf(("QT", b, ti))], [CT.b])
                self.dma(SP, BT.t[:].rearrange("p g t -> p (g t)"), self.sKT[0, b, ti][:, 0:512], [self.dbuf(("KT", b, ti))], [BT.b])
            self.dma(SP, Bt.t[:], self.sKt[0, b, tok, 0:512], [self.dbuf(("Kt", b, ti))], [Bt.b])
            self.dma(SP, X.t[:].rearrange("p h e -> p (h e)"), self.sV[b, tok, :], [self.dbuf(("V", b, ti))], [X.b])
            self.dma(SP, dl.t[:], self.sDT[b, tok, :], [self.dbuf(("DT", b, ti))], [dl.b])
            ofl = None
            return CT, BT, Bt, X, dl, need_o, ofl

        def comp(st, tl):
            ti, b = st
            tok = slice(ti * 128, (ti + 1) * 128)
            CT, BT, Bt, X, dl, need_o, ofl = tl
            la = dl.t[:, 64 + d * 32:64 + (d + 1) * 32]
            dt = dl.t[:, d * 32:(d + 1) * 32]
            self.mm(psE.t[:, 0:32], tri, la, True, True, [ct.b, dl.b], [psE.b])
            self.mm(psE.t[:, 32:64], G, la, True, True, [ct.b, dl.b], [psE.b])
            self.mm(psE.t[:, 64:96], ones, la, True, True, [ct.b, dl.b], [psE.b])
            ea = ear.next()
            self.act(ea.t[:], psE.t[:, 0:96], AF.Exp, [psE.b], [ea.b])
            xd, xw = xdr.next(), xwr.next()
            self.tt(DVE, xd.t[:], X.t[:], dt.unsqueeze(2).broadcast_to([128, 32, 64]), ALU.mult, [X.b, dl.b], [xd.b])
            self.tt(DVE, xw.t[:], xd.t[:], ea.t[:, 32:64].unsqueeze(2).broadcast_to([128, 32, 64]), ALU.mult, [xd.b, ea.b], [xw.b])
            ot = otr.next() if need_o else None
            for g in range(4):
                hs = slice(g * 8, g * 8 + 8)
                if need_o:
                    R = Rr.next()
                    self.tt(POOL, R.t[:], la[:, hs].unsqueeze(2).broadcast_to([128, 8, 128]),
                            tri.unsqueeze(1).broadcast_to([128, 8, 128]), ALU.mult, [dl.b, ct.b], [R.b])
                    for half in range(2):
                        self.mm(psD.t[:, half * 512:(half + 1) * 512], G,
                                R.t[:, half * 4:(half + 1) * 4, :].rearrange("p h t -> p (h t)"), True, True, [ct.b, R.b], [psD.b])
                    Lg = Lr.next()
                    self.act(Lg.t[:].rearrange("p h t -> p (h t)"), psD.t[:], AF.Exp, [psD.b], [Lg.b])
                    self.mm(psCB.t[:, 0:128], BT.t[:, g, :], CT.t[:, g, :], True, True, [BT.b, CT.b], [psCB.b])
                    CBm = CBr.next()
                    self.tt(DVE, CBm.t[:], psCB.t[:, 0:128], ct.t[:, 1 + d, :], ALU.mult, [psCB.b, ct.b], [CBm.b])
                    Pm = Pmr.next()
                    self.tt(DVE, Pm.t[:], Lg.t[:], CBm.t[:].unsqueeze(1).broadcast_to([128, 8, 128]), ALU.mult, [Lg.b, CBm.b], [Pm.b])
                    po = psO.next()
                    for r in range(8):
                        self.mm(po.t[:, r * 64:(r + 1) * 64], Pm.t[:, r, :], xd.t[:, g * 8 + r, :], r == 0, r == 7,
                                [Pm.b, xd.b], [po.b], skip=True)
                    self.mm(psI.t[:], CT.t[:, g, :], Sbf[b, g].t[:], True, True, [CT.b, Sbf[b, g].b], [psI.b])
                    tmp = tmpr.next()
                    self.tt(DVE, tmp.t[:], psI.t[:].rearrange("p (h e) -> p h e", h=8),
                            ea.t[:, hs].unsqueeze(2).broadcast_to([128, 8, 64]), ALU.mult, [psI.b, ea.b], [tmp.b])
                    self.tt(DVE, ot.t[:, g * 512:(g + 1) * 512], tmp.t[:].rearrange("p h e -> p (h e)"), po.t[:], ALU.add,
                            [tmp.b, po.b], [ot.b])
                self.mm(psK.t[:], Bt.t[:, g * 128:(g + 1) * 128], xw.t[:, hs, :].rearrange("p h e -> p (h e)"), True, True,
                        [Bt.b, xw.b], [psK.b])
                s32 = S32[b, g]
                self.tt(POOL, s32.t[:], s32.t[:], ea.t[:, 64 + g * 8:64 + g * 8 + 8].unsqueeze(2).broadcast_to([128, 8, 64]),
                        ALU.mult, [s32.b, ea.b], [s32.b])
                self.tt(DVE, s32.t[:], s32.t[:], psK.t[:].rearrange("p (h e) -> p h e", h=8), ALU.add, [s32.b, psK.b], [s32.b])
                self.cp(ACT, Sbf[b, g].t[:], s32.t[:].rearrange("p h e -> p (h e)"), [s32.b], [Sbf[b, g].b])
            if need_o:
                self.dma(SP, self.sO[d, b, tok, :], ot.t[:], [ot.b], [self.dbuf(("O", d, b, ti))])

        self.run_pipelined(steps, ld, comp, rowof=lambda it: 0)
        P.end_phase()

    def m2_readout(self, li, j):
        P, c = self.P, self.cfg
        P.begin_phase()
        wo = Tl(P, [128, 16, 1024], BF16, "w_out")
        self.load_wout_scaled(wo, self.m2_w_out[j], 16, self.m2_gn[j])
        dsk = Tl(P, [128, 32], F32, "dsk")
        self.dma(SP, dsk.t[:], self.m2_d[j, :].partition_broadcast(128), [], [dsk.b])
        g1 = Tl(P, [128, D], F32, "g1")
        ofr = Ring(P, 3, [128, 2048], F32, "of")
        obr = Ring(P, 3, [128, 2048], F32, "ob")
        xr = Ring(P, 3, [128, 32, 64], BF16, "x")
        zr = Ring(P, 3, [128, 2048], BF16, "z")
        hr = Ring(P, 3, [128, D], F32, "h")
        tmpr = Ring(P, 2, [128, 32, 64], F32, "tmp")
        junk2 = Tl(P, [128, 512], BF16, "junk2")
        onr = Ring(P, 2, [128, 2048], BF16, "on")
        onTr = Ring(P, 2, [128, 16, 128], BF16, "onT")
        rsr = Ring(P, 2, [128, 4], F32, "rs")
        hnr = Ring(P, 2, [128, D], F32, "hn")
        ssr = Ring(P, 4, [128, 2], F32, "ss")
        junk = Tl(P, [128, D], BF16, "junk")
        psTr = Ring(P, 2, [128, 1024], BF16, "psT", psum=True)
        psYr = Ring(P, 2, [128, 1024], F32, "psY", psum=True)

        def ld(it):
            row, b, ti = it
            tok = slice(ti * 128, (ti + 1) * 128)
            of, ob, x, z, h = ofr.next(), obr.next(), xr.next(), zr.next(), hr.next()
            self.dma(SP, of.t[:], self.sO[1, b, tok, :], [self.dbuf(("O", 1, b, ti))], [of.b])
            self.dma(SP, ob.t[:], self.sO[0, b, tok, :], [self.dbuf(("O", 0, b, ti))], [ob.b])
            self.dma(SP, x.t[:].rearrange("p h e -> p (h e)"), self.sV[b, tok, :], [self.dbuf(("V", b, ti))], [x.b])
            self.dma(SP, z.t[:], self.sG[b, tok, :], [self.dbuf(("G", b, ti))], [z.b])
            src, sb_ = self.h_src(li, b, ti)
            self.dma(SP, h.t[:], src, [sb_] if sb_ is not None else [], [h.b])
            return {"of": of, "ob": ob, "x": x, "z": z, "h": h}

        def pre_a(it, cx):
            of, ob, x, z = cx["of"], cx["ob"], cx["x"], cx["z"]
            on, rs, tmp = onr.next(), rsr.next(), tmpr.next()
            self.tt(DVE, tmp.t[:], x.t[:], dsk.t[:].unsqueeze(2).broadcast_to([128, 32, 64]), ALU.mult, [x.b, dsk.b], [tmp.b])
            self.tt(POOL, of.t[:], of.t[:], ob.t[:], ALU.add, [of.b, ob.b], [of.b])
            self.tt(DVE, of.t[:], of.t[:], tmp.t[:].rearrange("p h e -> p (h e)"), ALU.add, [of.b, tmp.b], [of.b])
            self.tt(DVE, of.t[:], of.t[:], z.t[:], ALU.mult, [of.b, z.b], [of.b])
            for g in range(4):
                self.act(junk2.t[:], of.t[:, g * 512:(g + 1) * 512], AF.Square, [of.b], [junk2.b, rs.b], accum_out=rs.t[:, g:g + 1])
            self.rstd(rs.t[:], rs.t[:], 1.0 / 512, EPS, [rs.b], [rs.b], n=4)
            self.tt(DVE, on.t[:].rearrange("p (g v) -> p g v", g=4), of.t[:].rearrange("p (g v) -> p g v", g=4),
                    rs.t[:].unsqueeze(2).broadcast_to([128, 4, 512]), ALU.mult, [of.b, rs.b], [on.b])
            cx["on"] = on

        def pre_b(it, cx):
            on, onT = cx["on"], onTr.next()
            for half in range(2):
                psT = psTr.next()
                self.transpose_to(on.t, 8, psT, onT.t[:, half * 8:(half + 1) * 8, :].rearrange("p k t -> p (k t)"),
                                  onT.b, on.b, eng=(ACT if half == 0 else DVE), src_off=half * 1024)
            cx["onT"] = onT

        self.run_pipe(self.readout_items(li), ld, pre_a, pre_b, self.readout_main(li, wo, 16, g1, psYr, hnr, ssr, junk))
        P.end_phase()

    def copy_in(self):
        P, c = self.P, self.cfg
        P.begin_phase()
        hr = Ring(P, 3, [128, D], F32, "h")
        for b in range(c.NB):
            for ti in range(c.NT):
                h = hr.next()
                src, _ = self.h_src(0, b, ti)
                self.dma(SP, h.t[:], src, [], [h.b])
                dst, db_ = self.h_mid(b, ti)
                self.dma(SP, dst, h.t[:], [h.b], [db_])
        P.end_phase()

    def build(self):
        c = self.cfg
        self.declare()
        self.setup_consts()
        phases = c.phases
        if phases is None:
            phases = ["mod"]
            for li, k in enumerate(c.kinds):
                phases += [("mix", li), ("ffn", li)]
        for ph in phases:
            if ph == "mod":
                self.phase_mod()
            elif ph == "copy_in":
                self.copy_in()
            elif ph[0] == "ffn":
                self.phase_ffn(ph[1])
            elif ph[0] == "mix":
                self.phase_mixer(ph[1])
            elif ph[0] == "call":
                getattr(self, ph[1])(*ph[2:])
        self.P.begin_phase()
        self.P.end_phase(final=True)
        return self.nc


def const_tables(cfg):
    ident = np.eye(128, dtype=np.float32)
    LT = max(cfg.LT, 1)
    nf = 64
    inv = (10000.0 ** (-np.arange(nf, dtype=np.float32) / nf)).astype(np.float32)
    rope = np.zeros((LT, 128, 512), np.float32)
    p = np.arange(128)
    for lt in range(LT):
        row = (2 * lt + p // GRID_W).astype(np.float32)
        col = (p % GRID_W).astype(np.float32)
        ar = row[:, None] * inv[None, :]
        ac = col[:, None] * inv[None, :]
        cr, sr, cc, sc = np.cos(ar), np.sin(ar), np.cos(ac), np.sin(ac)
        rope[lt, :, 0:256] = np.concatenate([cr, cr, cc, cc], 1)
        rope[lt, :, 256:512] = np.concatenate([-sr, sr, -sc, sc], 1)
    tab = np.zeros((128, 24, 128), np.float32)
    s = np.arange(128)[:, None].astype(np.float32)
    t = np.arange(128)[None, :].astype(np.float32)
    tab[:, 0] = t - s
    tab[:, 1] = (t >= s)
    tab[:, 2] = (s >= t)
    tab[:, 3] = t + 1.0
    tab[:, 4] = 128.0 - t
    tab[:, 5, 0] = 127.0 - s[:, 0]
    tab[:, 5, 1] = s[:, 0]
    tab[:, 5, 2] = 128.0
    same = ((s // 64) == (t // 64)).astype(np.float32)
    sl = s % 64
    tab[:, 6] = same * (t >= s)
    tab[:, 7] = same * (s >= t)
    tab[:, 8] = same * ((s <= t).astype(np.float32) - (sl <= 31))
    tab[:, 9] = same * ((s >= t).astype(np.float32) - (sl >= 32))
    for cc in range(2):
        inc = ((s[:, 0] // 64) == cc).astype(np.float32)
        slc = s[:, 0] % 64
        tab[:, 10, 3 * cc + 0] = inc * (slc <= 31)
        tab[:, 10, 3 * cc + 1] = inc
        tab[:, 10, 3 * cc + 2] = inc * (slc > 31)
        tab[:, 11, 3 * cc + 0] = inc * (slc >= 32)
        tab[:, 11, 3 * cc + 1] = inc
        tab[:, 11, 3 * cc + 2] = inc * (slc < 32)
    tab[:, 12] = (s <= t)
    tab[:, 13] = (s >= t)
    tab[:, 14] = (s > t)
    tab[:, 15] = (s < t)
    tab[:, 16] = 1.0
    return ident, rope, tab.reshape(128, 24 * 128)


def make_in_maps(cfg, inputs, n_cores):
    ident, rope, tab = const_tables(cfg)
    f = lambda a: np.ascontiguousarray(np.asarray(a, dtype=np.float32))
    L = len(cfg.kinds)
    nr, nh, nm = (max(1, sum(1 for k in cfg.kinds if k == q)) for q in (0, 1, 2))
    inputs = dict(inputs)
    for k in inputs:
        if k.startswith("ret_"):
            inputs[k] = np.asarray(inputs[k])[:nr]
        elif k.startswith("hg_") and k != "hg_lb":
            inputs[k] = np.asarray(inputs[k])[:nh]
        elif k.startswith("m2_"):
            inputs[k] = np.asarray(inputs[k])[:nm]
    shared = {
        "ada_w": f(inputs["ada_w"][:L]), "ada_b": f(inputs["ada_b"][:L]),
        "norm_g": f(inputs["norm_g"][:L]).reshape(L, 4 * D),
        "ret_w_in": f(inputs["ret_w_in"]), "ret_w_out": f(inputs["ret_w_out"]),
        "ret_decay": f(inputs["ret_decay"]).reshape(-1, 8), "ret_gn": f(inputs["ret_gn"]),
        "hg_w_in": f(inputs["hg_w_in"]), "hg_w_out": f(inputs["hg_w_out"]), "hg_lb": f(inputs["hg_lb"][:L]),
        "hg_gn": f(inputs["hg_gn"]),
        "m2_w_in": f(inputs["m2_w_in"]), "m2_w_out": f(inputs["m2_w_out"]), "m2_conv_w": f(inputs["m2_conv_w"]),
        "m2_conv_b": f(inputs["m2_conv_b"]), "m2_dt_bias": f(inputs["m2_dt_bias"]).reshape(-1, 64),
        "m2_a_log": f(inputs["m2_a_log"]).reshape(-1, 64), "m2_d": f(inputs["m2_d"]), "m2_gn": f(inputs["m2_gn"]),
        "ffn_w_up": f(inputs["ffn_w_up"][:L]), "ffn_w_down": f(inputs["ffn_w_down"][:L]),
        "ffn_cw": f(np.asarray(inputs["ffn_conv_w"][:L]).reshape(L, 3, 22, 128).transpose(0, 3, 2, 1)).reshape(L, 128, 66),
        "ffn_cb": f(np.asarray(inputs["ffn_conv_b"][:L]).reshape(L, 22, 128).transpose(0, 2, 1)),
        "c_ident": ident, "c_rope": rope, "c_tab": tab,
    }
    x, cc, ctx, c_ctx = (np.asarray(inputs[k], dtype=np.float32) for k in ("x", "c", "ctx", "c_ctx"))
    maps = []
    for i in range(n_cores):
        sl = slice(i * cfg.NB, (i + 1) * cfg.NB)
        m = dict(shared)
        m["x"] = f(x[sl])
        m["ctx"] = f(ctx[sl])
        m["crow"] = f(np.concatenate([cc[sl], c_ctx[None, :]], 0))
        maps.append(m)
    return maps


_CACHE = {}


def kernel(**inputs):
    cfg = Cfg()
    if "nc" not in _CACHE:
        _CACHE["nc"] = Builder(cfg).build()
    nc = _CACHE["nc"]
    maps = make_in_maps(cfg, inputs, N_CORES)
    res = run_bass_kernel_spmd(nc, maps, core_ids=list(range(N_CORES)))
    out = np.concatenate([np.asarray(r["out"]) for r in res.results], axis=0)
    return out.astype(np.float32)
```

```python
import contextlib
import numpy as np
import concourse.bass as bass
import concourse.mybir as mybir
from concourse.bass_utils import run_bass_kernel_spmd

F32 = mybir.dt.float32
BF16 = mybir.dt.bfloat16
AF = mybir.ActivationFunctionType
ALU = mybir.AluOpType

PE, ACT, DVE, POOL, SP = "pe", "act", "dve", "pool", "sp"
CENG = (PE, ACT, DVE, POOL)

D = 1024
EPS = 1e-6
GRID_W = 64
N_CORES = 8


class Buf:
    __slots__ = ("name", "last_w", "readers", "dram", "strict")

    def __init__(self, name, dram=False):
        self.name = name
        self.last_w = None
        self.readers = []
        self.dram = dram
        self.strict = bool(name) and str(name).startswith("junk")


class Op:
    __slots__ = ("eng", "fn", "deps", "weak", "is_dma", "signal", "val", "dsem_buf")

    def __init__(self, eng, fn, is_dma, dsem_buf):
        self.eng = eng
        self.fn = fn
        self.deps = set()
        self.weak = set()
        self.is_dma = is_dma
        self.signal = False
        self.val = None
        self.dsem_buf = dsem_buf


class Prog:
    def __init__(self, same_eng_sync=True):
        self.nc = bass.Bass("TRN2", target_bir_lowering=False)
        self.ops = []
        self.same_eng_sync = same_eng_sync
        self.keep = []
        self.bufs = []
        nc = self.nc
        self.engs = {PE: nc.tensor, ACT: nc.scalar, DVE: nc.vector, POOL: nc.gpsimd, SP: nc.sync}
        self.sems = {}
        for e in CENG:
            cm = nc.semaphore("s_" + e)
            self.sems[e] = cm.__enter__()
            self.keep.append(cm)
        self.cnt = {e: 0 for e in CENG}
        self.dsem_free = {True: [], False: []}
        self.dsem_all = []
        self.waited = {}
        self.n_inst = 0
        self.n_wait = 0
        self.phase_stack = None
        self.uid = 0

    def begin_phase(self):
        self.phase_stack = contextlib.ExitStack()

    def sb(self, shape, dt, name=None):
        self.uid += 1
        return self.phase_stack.enter_context(self.nc.sbuf_tensor(f"{name or 't'}_{self.uid}", list(shape), dt))

    def ps(self, shape, dt=F32, name=None):
        self.uid += 1
        return self.phase_stack.enter_context(self.nc.psum_tensor(f"{name or 'p'}_{self.uid}", list(shape), dt))

    def buf(self, name=None, dram=False):
        b = Buf(name or "b", dram)
        self.bufs.append(b)
        return b

    def op(self, eng, fn, reads=(), writes=(), dma=False):
        idx = len(self.ops)
        dsem_buf = None
        if dma:
            for b in list(writes) + list(reads):
                if not b.dram:
                    dsem_buf = b
                    break
            if dsem_buf is None:
                dsem_buf = (list(writes) + list(reads))[0]
        o = Op(eng, fn, dma, dsem_buf)
        for b in reads:
            if b.last_w is not None:
                o.deps.add(b.last_w)
        for b in writes:
            if b.last_w is not None:
                (o.deps if b.strict else o.weak).add(b.last_w)
            for r in b.readers:
                o.weak.add(r)
        for b in reads:
            b.readers.append(idx)
        for b in writes:
            b.last_w = idx
            b.readers = []
        o.deps.discard(idx)
        for d in o.weak:
            if d != idx and d not in o.deps:
                s_ = self.ops[d]
                if s_.is_dma or dma or s_.eng != eng:
                    o.deps.add(d)
        self.ops.append(o)
        return idx

    def end_phase(self, final=False):
        nc = self.nc
        ops = self.ops
        engs = self.engs
        sems = self.sems
        cnt = self.cnt
        waited = self.waited
        last_on = {}
        for i, o in enumerate(ops):
            last_on[o.eng] = i
            for d in o.deps:
                s = ops[d]
                if s.is_dma:
                    continue
                if s.eng == o.eng and (s.eng == PE or not self.same_eng_sync):
                    continue
                s.signal = True
        for e in CENG:
            for i in range(len(ops) - 1, -1, -1):
                if ops[i].eng == e and not ops[i].is_dma:
                    ops[i].signal = True
                    break
        dsems = {}
        for i, o in enumerate(ops):
            eng = engs[o.eng]
            need = {}
            for d in o.deps:
                s = ops[d]
                if s.is_dma:
                    st = dsems[(id(s.dsem_buf), s.eng == POOL)]
                    key = ("d", id(st))
                    v = st[1]
                    sem = st[0]
                else:
                    if s.eng == o.eng and (s.eng == PE or not self.same_eng_sync):
                        continue
                    key = ("e", s.eng)
                    v = s.val
                    sem = sems[s.eng]
                if need.get(key, (None, -1))[1] < v:
                    need[key] = (sem, v)
            for key, (sem, v) in need.items():
                if waited.get((o.eng, key), -1) >= v:
                    continue
                waited[(o.eng, key)] = v
                eng.wait_ge(sem, v)
                self.n_wait += 1
            inst = o.fn(eng)
            self.n_inst += 1
            if o.is_dma:
                k = (id(o.dsem_buf), o.eng == POOL)
                if k not in dsems:
                    fl = self.dsem_free[k[1]]
                    if fl:
                        dsems[k] = fl.pop()
                    else:
                        cm = nc.semaphore(f"d{len(self.dsem_all)}")
                        st = [cm.__enter__(), 0]
                        self.keep.append(cm)
                        self.dsem_all.append(st)
                        dsems[k] = st
                st = dsems[k]
                st[1] += 16
                inst.then_inc(st[0], 16)
            elif o.signal:
                cnt[o.eng] += 1
                o.val = cnt[o.eng]
                inst.then_inc(sems[o.eng], 1)
        targets = [SP] if final else list(engs.keys())
        for e in targets:
            eng = engs[e]
            for s in CENG:
                if cnt[s] == 0 or (s == e and (e == PE or not self.same_eng_sync)):
                    continue
                key = ("e", s)
                if waited.get((e, key), -1) >= cnt[s]:
                    continue
                waited[(e, key)] = cnt[s]
                eng.wait_ge(sems[s], cnt[s])
                self.n_wait += 1
            for st in dsems.values():
                key = ("d", id(st))
                if waited.get((e, key), -1) >= st[1]:
                    continue
                waited[(e, key)] = st[1]
                eng.wait_ge(st[0], st[1])
                self.n_wait += 1
        for k, st in dsems.items():
            self.dsem_free[k[1]].append(st)
        self.ops = []
        for b in self.bufs:
            b.last_w = None
            b.readers = []
        self.bufs = [b for b in self.bufs if b.dram]
        if self.phase_stack is not None:
            self.phase_stack.close()
            self.phase_stack = None


class Tl:
    def __init__(self, P, shape, dt, name=None, psum=False):
        self.t = P.ps(shape, dt, name) if psum else P.sb(shape, dt, name)
        self.b = P.buf(name)


class Ring:
    def __init__(self, P, n, shape, dt, name=None, psum=False):
        self.tiles = [Tl(P, shape, dt, name, psum) for _ in range(n)]
        self.i = 0

    def next(self):
        t = self.tiles[self.i % len(self.tiles)]
        self.i += 1
        return t


class Cfg:
    def __init__(self, NB=2, LAT=4096, CTX=256, kinds=(0, 1, 2, 0), debug=False, phases=None):
        self.NB, self.LAT, self.CTX = NB, LAT, CTX
        self.kinds = tuple(kinds)
        self.debug = debug
        self.phases = phases
        self.TOK = LAT + CTX
        self.CT = CTX // 128
        self.LT = LAT // 128
        self.NT = self.CT + self.LT


class Builder:
    def __init__(self, cfg):
        self.cfg = cfg
        self.P = Prog()
        self.nc = self.P.nc
        self.dram_bufs = {}

    def dma(self, q, out, in_, R, W, **kw):
        self.P.op(q, lambda e: e.dma_start(out=out, in_=in_, **kw), R, W, dma=True)

    def mm(self, out, lhsT, rhs, start, stop, R, W, skip=False):
        self.P.op(PE, lambda e: e.matmul(out, lhsT=lhsT, rhs=rhs, start=start, stop=stop, skip_group_check=skip), R, W)

    def tr(self, out, in_, R, W):
        ident = self.ident.t[:]
        self.P.op(PE, lambda e: e.transpose(out=out, in_=in_, identity=ident), list(R) + [self.ident.b], W)

    def act(self, out, in_, func, R, W, eng=ACT, **kw):
        self.P.op(eng, lambda e: e.activation(out=out, in_=in_, func=func, **kw), R, W)

    def tt(self, eng, out, in0, in1, op, R, W):
        self.P.op(eng, lambda e: e.tensor_tensor(out=out, in0=in0, in1=in1, op=op), R, W)

    def ts(self, eng, out, in0, s1, s2, op0, op1, R, W):
        if op1 is None:
            self.P.op(eng, lambda e: e.tensor_scalar(out=out, in0=in0, scalar1=s1, scalar2=None, op0=op0), R, W)
        else:
            self.P.op(eng, lambda e: e.tensor_scalar(out=out, in0=in0, scalar1=s1, scalar2=s2, op0=op0, op1=op1), R, W)

    def stt(self, out, in0, scalar, in1, op0, op1, R, W):
        self.P.op(DVE, lambda e: e.scalar_tensor_tensor(out=out, in0=in0, scalar=scalar, in1=in1, op0=op0, op1=op1), R, W)

    def cp(self, eng, out, in_, R, W):
        if eng == ACT:
            self.act(out, in_, AF.Copy, R, W)
        else:
            self.P.op(eng, lambda e: e.tensor_copy(out=out, in_=in_), R, W)

    def memset(self, eng, ap, val, W):
        self.P.op(eng, lambda e: e.memset(ap, val), (), W)

    def dbuf(self, key):
        if key not in self.dram_bufs:
            self.dram_bufs[key] = self.P.buf(str(key), dram=True)
        return self.dram_bufs[key]

    def rstd(self, out, in_, scale, eps, R, W, n=1):
        nh = self.neghalf.t[:, 0:n]
        self.ts(POOL, out, in_, scale, eps, ALU.mult, ALU.add, R, W)
        self.tt(POOL, out, out, nh, ALU.pow, list(W) + [self.neghalf.b], W)

    def declare(self):
        nc, c = self.nc, self.cfg
        L = len(c.kinds)
        self.L = L
        di = lambda n, s, dt=F32: nc.dram_tensor(n, list(s), dt, kind="ExternalInput").ap()
        dx = lambda n, s, dt=F32: nc.dram_tensor(n, list(s), dt, kind="Internal").ap()
        self.x_in = di("x", [c.NB, c.LAT, D])
        self.ctx_in = di("ctx", [c.NB, c.CTX, D])
        self.crow = di("crow", [c.NB + 1, D])
        self.ada_w = di("ada_w", [L, D, 6 * D])
        self.ada_b = di("ada_b", [L, 6 * D])
        self.norm_g = di("norm_g", [L, 4 * D])
        n_ret = sum(1 for k in c.kinds if k == 0)
        n_hg = sum(1 for k in c.kinds if k == 1)
        n_m2 = sum(1 for k in c.kinds if k == 2)
        self.ret_w_in = di("ret_w_in", [max(n_ret, 1), D, 6144])
        self.ret_w_out = di("ret_w_out", [max(n_ret, 1), 2048, D])
        self.ret_decay = di("ret_decay", [max(n_ret, 1), 8])
        self.ret_gn = di("ret_gn", [max(n_ret, 1), 2048])
        self.hg_w_in = di("hg_w_in", [max(n_hg, 1), D, 5120])
        self.hg_w_out = di("hg_w_out", [max(n_hg, 1), D, D])
        self.hg_lb = di("hg_lb", [L, D])
        self.hg_gn = di("hg_gn", [max(n_hg, 1), D])
        self.m2_w_in = di("m2_w_in", [max(n_m2, 1), D, 5184])
        self.m2_w_out = di("m2_w_out", [max(n_m2, 1), 2048, D])
        self.m2_conv_w = di("m2_conv_w", [max(n_m2, 1), 3, 3072])
        self.m2_conv_b = di("m2_conv_b", [max(n_m2, 1), 3072])
        self.m2_dt_bias = di("m2_dt_bias", [max(n_m2, 1), 64])
        self.m2_a_log = di("m2_a_log", [max(n_m2, 1), 64])
        self.m2_d = di("m2_d", [max(n_m2, 1), 32])
        self.m2_gn = di("m2_gn", [max(n_m2, 1), 2048])
        self.ffn_w_up = di("ffn_w_up", [L, D, 5632])
        self.ffn_cw = di("ffn_cw", [L, 128, 22 * 3])
        self.ffn_cb = di("ffn_cb", [L, 128, 22])
        self.ffn_w_down = di("ffn_w_down", [L, 2816, D])
        self.c_ident = di("c_ident", [128, 128])
        self.c_rope = di("c_rope", [max(c.LT, 1), 128, 512])
        self.c_tab = di("c_tab", [128, 24 * 128])
        self.out = nc.dram_tensor("out", [c.NB, c.LAT, D], F32, kind="ExternalOutput").ap()
        if c.debug:
            self.dbg = nc.dram_tensor("dbg", [L, c.NB, c.TOK, D], F32, kind="ExternalOutput").ap()
        self.H = dx("H", [c.NB, c.TOK, D])
        self.MOD = dx("MOD", [L, c.NB + 1, 6, D])
        self.sQT = dx("sQT", [2, c.NB, c.NT, 128, 1024], BF16)
        self.sKT = dx("sKT", [2, c.NB, c.NT, 128, 1024], BF16)
        self.sKt = dx("sKt", [2, c.NB, c.TOK, 1024], BF16)
        self.sV = dx("sV", [c.NB, c.TOK, 2048], BF16)
        self.sG = dx("sG", [c.NB, c.TOK, 2048], BF16)
        self.sO = dx("sO", [2, c.NB, c.TOK, 2048])
        self.sCS = dx("sCS", [2, c.NB, c.NT, 128, 48])
        self.sX = dx("sX", [c.NB, c.TOK + 2 * c.NT + 8, 3072])
        self.sDT = dx("sDT", [c.NB, c.TOK, 128])

    def setup_consts(self):
        P, nc = self.P, self.nc
        self.const_stack = contextlib.ExitStack()
        P.phase_stack = self.const_stack
        self.ident = Tl(P, [128, 128], BF16, "ident")
        self.neghalf = Tl(P, [128, 8], F32, "neghalf")
        P.phase_stack = None
        P.begin_phase()
        self.dma(POOL, self.ident.t[:], self.c_ident[:, :], [], [self.ident.b])
        self.memset(POOL, self.neghalf.t[:], -0.5, [self.neghalf.b])
        P.end_phase()

    def load_ctab(self):
        ct = Tl(self.P, [128, 24, 128], F32, "ctab")
        self.dma(SP, ct.t[:], self.c_tab.rearrange("p (a b) -> p a b", a=24), [], [ct.b])
        return ct

    def h_src(self, li, b, ti):
        c = self.cfg
        if li == 0:
            if ti < c.CT:
                return self.ctx_in[b, ti * 128:(ti + 1) * 128, :], None
            return self.x_in[b, (ti - c.CT) * 128:(ti - c.CT + 1) * 128, :], None
        return self.H[b, ti * 128:(ti + 1) * 128, :], self.dbuf(("H", b, ti))

    def h_mid(self, b, ti):
        return self.H[b, ti * 128:(ti + 1) * 128, :], self.dbuf(("H", b, ti))

    def h_dst(self, li, b, ti):
        c = self.cfg
        if li == self.L - 1 and ti >= c.CT:
            return self.out[b, (ti - c.CT) * 128:(ti - c.CT + 1) * 128, :], self.dbuf(("out", b, ti))
        return self.H[b, ti * 128:(ti + 1) * 128, :], self.dbuf(("H", b, ti))

    def load_w(self, wt, src, kchunks, ncols):
        v = src.rearrange("(k p) n -> p k n", p=128)
        for k in range(kchunks):
            self.dma(POOL, wt.t[:, k, :], v[:, k, :], [], [wt.b])

    def load_tab(self, tl, li, row, vec):
        self.dma(SP, tl.t[:], self.MOD[li, row, vec, :].partition_broadcast(128), [self.dbuf("MOD")], [tl.b])

    def phase_mod(self):
        P, c = self.P, self.cfg
        R3 = c.NB + 1
        P.begin_phase()
        cT = Tl(P, [128, R3, 8], F32, "cT")
        cTb = Tl(P, [128, 8, R3], BF16, "cTb")
        self.dma(SP, cT.t[:], self.crow.rearrange("r (p k) -> p r k", k=8), [], [cT.b])
        self.act(cTb.t[:].rearrange("p k r -> p r k"), cT.t[:], AF.Silu, [cT.b], [cTb.b])
        wr = Ring(P, 2, [128, 8, 1536], BF16, "adaw")
        pr = Ring(P, 2, [128, 512], F32, "psm", psum=True)
        raw = Tl(P, [R3, 6 * D], F32, "raw")
        adab = Tl(P, [R3, 6 * D], F32, "adab")
        ng = Tl(P, [R3, 4 * D], F32, "ng")
        mv = Ring(P, 2, [R3, 6, D], F32, "mv")
        modb = self.dbuf("MOD")
        for li in range(self.L):
            self.dma(SP, adab.t[:], self.ada_b[li, :].partition_broadcast(R3), [], [adab.b])
            self.dma(SP, ng.t[:], self.norm_g[li, :].partition_broadcast(R3), [], [ng.b])
            wv = self.ada_w[li].rearrange("(p k) n -> p k n", k=8)
            for j in range(4):
                w = wr.next()
                self.dma(POOL, w.t[:], wv[:, :, j * 1536:(j + 1) * 1536], [], [w.b])
                for n in range(3):
                    ps = pr.next()
                    for k in range(8):
                        self.mm(ps.t[0:R3, :], cTb.t[:, k, :], w.t[:, k, n * 512:(n + 1) * 512], k == 0, k == 7,
                                [cTb.b, w.b], [ps.b])
                    c0 = j * 1536 + n * 512
                    self.tt(DVE, raw.t[:, c0:c0 + 512], ps.t[0:R3, :], adab.t[:, c0:c0 + 512], ALU.add,
                            [ps.b, adab.b], [raw.b])
            m = mv.next()
            r = raw.t
            self.cp(DVE, m.t[:, 0, :], r[:, 0:D], [raw.b], [m.b])
            self.stt(m.t[:, 1, :], r[:, D:2 * D], 1.0, ng.t[:, 0:D], ALU.add, ALU.mult, [raw.b, ng.b], [m.b])
            self.tt(DVE, m.t[:, 2, :], r[:, 2 * D:3 * D], ng.t[:, D:2 * D], ALU.mult, [raw.b, ng.b], [m.b])
            self.cp(DVE, m.t[:, 3, :], r[:, 3 * D:4 * D], [raw.b], [m.b])
            self.stt(m.t[:, 4, :], r[:, 4 * D:5 * D], 1.0, ng.t[:, 2 * D:3 * D], ALU.add, ALU.mult, [raw.b, ng.b], [m.b])
            self.tt(DVE, m.t[:, 5, :], r[:, 5 * D:6 * D], ng.t[:, 3 * D:4 * D], ALU.mult, [raw.b, ng.b], [m.b])
            self.dma(SP, self.MOD[li], m.t[:], [m.b], [modb])
        P.end_phase()

    def pre_norm(self, h, sc, sh, u, ss, junk):
        self.act(junk.t[:], h.t[:], AF.Square, [h.b], [junk.b, ss.b], accum_out=ss.t[:, 0:1])
        self.rstd(ss.t[:, 1:2], ss.t[:, 0:1], 1.0 / D, EPS, [ss.b], [ss.b])
        self.tt(POOL, h.t[:], h.t[:], sc.t[:], ALU.mult, [h.b, sc.b], [h.b])
        self.stt(u.t[:], h.t[:], ss.t[:, 1:2], sh.t[:], ALU.mult, ALU.add, [h.b, ss.b, sh.b], [u.b])

    def transpose_to(self, src, nblk, psT, dst_ap, dst_b, src_b, eng=ACT, src_off=0):
        for k in range(nblk):
            self.tr(psT.t[:, k * 128:(k + 1) * 128], src[:, src_off + k * 128: src_off + (k + 1) * 128], [src_b], [psT.b])
        self.cp(eng, dst_ap, psT.t[:, 0:nblk * 128], [psT.b], [dst_b])

    def post_residual(self, psY, hres, gtab, hn, ss, junk, dst, dst_b):
        self.act(junk.t[:], psY.t[:], AF.Square, [psY.b], [junk.b, ss.b], accum_out=ss.t[:, 0:1])
        self.rstd(ss.t[:, 1:2], ss.t[:, 0:1], 1.0 / D, EPS, [ss.b], [ss.b])
        self.stt(hn.t[:], psY.t[:], ss.t[:, 1:2], gtab.t[:], ALU.mult, ALU.mult, [psY.b, ss.b, gtab.b], [hn.b])
        self.tt(POOL, hn.t[:], hn.t[:], hres.t[:], ALU.add, [hn.b, hres.b], [hn.b])
        self.dma(SP, dst, hn.t[:], [hn.b], [dst_b] if dst_b is not None else [])

    def tile_list(self, li, with_ctx=True):
        c = self.cfg
        last = (li == self.L - 1)
        out = []
        for b in range(c.NB):
            for ti in range(c.NT):
                if ti < c.CT and (last and not with_ctx):
                    continue
                out.append((b, ti))
        return out

    def phase_ffn(self, li):
        P, c = self.P, self.cfg
        last = (li == self.L - 1)
        P.begin_phase()
        wup = Tl(P, [128, 8, 5632], BF16, "wup")
        wdn = Tl(P, [128, 22, 1024], BF16, "wdn")
        self.load_w(wup, self.ffn_w_up[li], 8, 5632)
        self.load_w(wdn, self.ffn_w_down[li], 22, 1024)
        cw = Tl(P, [128, 22, 3], F32, "cw")
        cb = Tl(P, [128, 22], F32, "cb")
        self.dma(SP, cw.t[:], self.ffn_cw[li].rearrange("p (a b) -> p a b", b=3), [], [cw.b])
        self.dma(SP, cb.t[:], self.ffn_cb[li], [], [cb.b])
        tabs = [Tl(P, [128, D], F32, f"tab{i}") for i in range(3)]
        hr = Ring(P, 4, [128, D], F32, "h")
        ur = Ring(P, 2, [128, D], BF16, "u")
        junk = Tl(P, [128, D], BF16, "junk")
        ssr = Ring(P, 6, [128, 2], F32, "ss")
        uTr = Ring(P, 2, [128, 8, 256], BF16, "uT")
        mT = Tl(P, [128, 22, 256], BF16, "mT")
        cbr = Ring(P, 3, [128, 256], F32, "cbuf")
        hrel = Ring(P, 2, [128, D], F32, "hrel")
        hnr = Ring(P, 2, [128, D], F32, "hn")
        psT = Tl(P, [128, 1024], BF16, "psT", psum=True)
        psA = Ring(P, 3, [128, 512], F32, "psA", psum=True)
        psV = Ring(P, 2, [128, 512], F32, "psV", psum=True)
        psY = Tl(P, [128, 1024], F32, "psY", psum=True)
        sts = []
        for b in range(c.NB):
            if not last:
                for s_ in range(c.CT // 2):
                    sts.append((c.NB, b, [2 * s_, 2 * s_ + 1], False))
        for b in range(c.NB):
            for s_ in range(c.LT // 2):
                sts.append((b, b, [c.CT + 2 * s_, c.CT + 2 * s_ + 1], True))
        st_a = {"row": None}
        st_m = {"row": None}

        def ld(st):
            row, b, tis, grid = st
            hs = []
            for ti in tis:
                h = hr.next()
                src, sb_ = self.h_mid(b, ti)
                self.dma(SP, h.t[:], src, [sb_], [h.b])
                hs.append(h)
            return {"h": hs}

        def pre_a(st, cx):
            row, b, tis, grid = st
            if st_a["row"] != row:
                st_a["row"] = row
                self.load_tab(tabs[0], li, row, 3)
                self.load_tab(tabs[1], li, row, 4)
            cx["u"] = []
            for h in cx["h"]:
                u = ur.next()
                self.pre_norm(h, tabs[1], tabs[0], u, ssr.next(), junk)
                cx["u"].append(u)

        def pre_b(st, cx):
            uT = uTr.next()
            for j, u in enumerate(cx["u"]):
                for k in range(8):
                    self.tr(psT.t[:, k * 128:(k + 1) * 128], u.t[:, k * 128:(k + 1) * 128], [u.b], [psT.b])
                self.cp(ACT, uT.t[:, :, j * 128:(j + 1) * 128], psT.t[:].rearrange("p (k t) -> p k t", k=8),
                        [psT.b], [uT.b])
            cx["uT"] = uT

        def main(st, cx, hook):
            row, b, tis, grid = st
            uT = cx["uT"]
            if st_m["row"] != row:
                st_m["row"] = row
                self.load_tab(tabs[2], li, row, 5)
            hres = []
            for ti in tis:
                hh = hrel.next()
                src, sb_ = self.h_mid(b, ti)
                self.dma(SP, hh.t[:], src, [sb_], [hh.b])
                hres.append(hh)
            nr = 4 if grid else 1
            w = 256 // nr

            def A(fc):
                pa = psA.next()
                for k in range(8):
                    self.mm(pa.t[:, 0:256], wup.t[:, k, fc * 128:(fc + 1) * 128], uT.t[:, k, :], k == 0, k == 7,
                            [wup.b, uT.b], [pa.b])
                cbuf = cbr.next()
                self.act(cbuf.t[:], pa.t[:, 0:256], AF.Identity, [pa.b, cw.b, cb.b], [cbuf.b],
                         scale=cw.t[:, fc, 1:2], bias=cb.t[:, fc:fc + 1])
                pv = pa.t[:, 0:256].rearrange("p (r w) -> p r w", r=nr)
                cv = cbuf.t[:].rearrange("p (r w) -> p r w", r=nr)
                self.stt(cv[:, :, 1:w], pv[:, :, 0:w - 1], cw.t[:, fc, 0:1], cv[:, :, 1:w], ALU.mult, ALU.add,
                         [pa.b, cw.b, cbuf.b], [cbuf.b])
                self.stt(cv[:, :, 0:w - 1], pv[:, :, 1:w], cw.t[:, fc, 2:3], cv[:, :, 0:w - 1], ALU.mult, ALU.add,
                         [pa.b, cw.b, cbuf.b], [cbuf.b])
                self.act(mT.t[:, fc, :], cbuf.t[:], AF.Gelu_apprx_tanh, [cbuf.b], [mT.b])

            def V(fc):
                pvv = psV.next()
                for k in range(8):
                    self.mm(pvv.t[:, 0:256], wup.t[:, k, 2816 + fc * 128:2816 + (fc + 1) * 128], uT.t[:, k, :],
                            k == 0, k == 7, [wup.b, uT.b], [pvv.b])
                self.tt(DVE, mT.t[:, fc, :], pvv.t[:, 0:256], mT.t[:, fc, :], ALU.mult, [pvv.b, mT.b], [mT.b])

            A(0)
            A(1)
            for fc in range(22):
                if fc + 2 < 22:
                    A(fc + 2)
                V(fc)
                if fc == 12:
                    hook()
            for j, ti in enumerate(tis):
                for n in range(2):
                    for fc in range(22):
                        self.mm(psY.t[:, n * 512:(n + 1) * 512], mT.t[:, fc, j * 128:(j + 1) * 128],
                                wdn.t[:, fc, n * 512:(n + 1) * 512], fc == 0, fc == 21, [mT.b, wdn.b], [psY.b])
                hn = hnr.next()
                dst, db_ = self.h_dst(li, b, ti)
                self.post_residual(psY, hres[j], tabs[2], hn, ssr.next(), junk, dst, db_)
                if c.debug:
                    self.dma(SP, self.dbg[li, b, ti * 128:(ti + 1) * 128, :], hn.t[:], [hn.b], [])

        self.run_pipe(sts, ld, pre_a, pre_b, main)
        P.end_phase()

    def phase_mixer(self, li):
        kind = self.cfg.kinds[li]
        j = sum(1 for k in self.cfg.kinds[:li] if k == kind)
        if kind == 0:
            self.ret_proj(li, j)
            self.ret_scan(li, j, 0)
            self.ret_scan(li, j, 1)
            self.ret_readout(li, j)
        elif kind == 1:
            self.hg_proj(li, j)
            self.hg_scan(li, j, 0)
            self.hg_scan(li, j, 1)
            self.hg_readout(li, j)
        else:
            self.m2_proj(li, j)
            self.m2_conv(li, j)
            self.m2_scan(li, j, 0)
            self.m2_scan(li, j, 1)
            self.m2_readout(li, j)

    def proj_tiles(self):
        c = self.cfg
        out = [(c.NB, b, ti) for b in range(c.NB) for ti in range(c.CT)]
        out += [(b, b, ti) for b in range(c.NB) for ti in range(c.CT, c.NT)]
        return out

    def run_pipelined(self, items, pre, main, rowof=lambda it: it[0]):
        cur = pre(items[0])
        for i, it in enumerate(items):
            nxt = None
            if i + 1 < len(items) and rowof(items[i + 1]) == rowof(it):
                nxt = pre(items[i + 1])
            main(it, cur)
            if nxt is None and i + 1 < len(items):
                nxt = pre(items[i + 1])
            cur = nxt

    def run_pipe(self, items, ld, pre_a, pre_b, main):
        n = len(items)
        ctxs = {}

        def do_ld(k):
            if k < n:
                ctxs[k] = ld(items[k])

        def do_a(k):
            if k < n:
                pre_a(items[k], ctxs[k])

        def do_b(k):
            if k < n:
                pre_b(items[k], ctxs[k])

        do_ld(0)
        do_ld(1)
        do_a(0)
        do_b(0)
        for i in range(n):
            do_ld(i + 2)
            do_a(i + 1)
            main(items[i], ctxs[i], lambda k=i + 1: do_b(k))
            ctxs.pop(i)

    def make_stages(self, li, tabs, hr, ur, ssr, uTr, junk, psTr, vecs=(0, 1)):
        state = {"row": None}

        def ld(it):
            row, b, ti = it
            h = hr.next()
            src, sb_ = self.h_src(li, b, ti)
            self.dma(SP, h.t[:], src, [sb_] if sb_ is not None else [], [h.b])
            return {"h": h}

        def pre_a(it, cx):
            row, b, ti = it
            if state["row"] != row:
                state["row"] = row
                for i, v in enumerate(vecs):
                    self.load_tab(tabs[i], li, row, v)
            u = ur.next()
            self.pre_norm(cx["h"], tabs[1], tabs[0], u, ssr.next(), junk)
            cx["u"] = u

        def pre_b(it, cx):
            uT = uTr.next()
            psT = psTr.next()
            self.transpose_to(cx["u"].t, 8, psT, uT.t[:].rearrange("p k t -> p (k t)"), uT.b, cx["u"].b)
            cx["uT"] = uT

        return ld, pre_a, pre_b

    def load_wout_scaled(self, wo, src, kchunks, gn_src):
        P = self.P
        self.load_w(wo, src, kchunks, 1024)
        gnT = Tl(P, [128, kchunks], F32, "gnT")
        self.dma(SP, gnT.t[:], gn_src.rearrange("(k p) -> p k", p=128), [], [gnT.b], allow_slow_non_contiguous=True)
        for k in range(kchunks):
            self.act(wo.t[:, k, :], wo.t[:, k, :], AF.Copy, [wo.b, gnT.b], [wo.b], scale=gnT.t[:, k:k + 1])

    def ret_proj(self, li, j):
        P, c = self.P, self.cfg
        P.begin_phase()
        w = Tl(P, [128, 8, 6144], BF16, "w_in")
        self.load_w(w, self.ret_w_in[j], 8, 6144)
        tabs = [Tl(P, [128, D], F32, f"tab{i}") for i in range(2)]
        hr = Ring(P, 3, [128, D], F32, "h")
        ur = Ring(P, 2, [128, D], BF16, "u")
        junk = Tl(P, [128, D], BF16, "junk")
        ssr = Ring(P, 4, [128, 2], F32, "ss")
        uTr = Ring(P, 2, [128, 8, 128], BF16, "uT")
        qk32r = Ring(P, 2, [128, 2048], F32, "qk32")
        ropeB = Tl(P, [128, 2048], F32, "ropeB")
        qkrr = Ring(P, 2, [128, 2048], BF16, "qkr")
        qkTr = Ring(P, 2, [128, 2048], BF16, "qkT")
        vr = Ring(P, 2, [128, 2048], BF16, "vbf")
        gr = Ring(P, 2, [128, 2048], BF16, "gbf")
        rr = Ring(P, 2, [128, 512], F32, "rope")
        psTr = Ring(P, 2, [128, 1024], BF16, "psT", psum=True)
        psM = Ring(P, 6, [128, 512], F32, "psM", psum=True)
        ld, pre_a, pre_b = self.make_stages(li, tabs, hr, ur, ssr, uTr, junk, psTr)

        def main(it, cx, hook):
            uT = cx["uT"]
            row, b, ti = it
            tok = slice(ti * 128, (ti + 1) * 128)
            qk32, qkr, qkT, vb, gb = qk32r.next(), qkrr.next(), qkTr.next(), vr.next(), gr.next()
            lat = ti >= c.CT
            if lat:
                rt = rr.next()
                self.dma(SP, rt.t[:], self.c_rope[ti - c.CT], [], [rt.b])
            for n in range(12):
                ps = psM.next()
                for k in range(8):
                    self.mm(ps.t[:], uT.t[:, k, :], w.t[:, k, n * 512:(n + 1) * 512], k == 0, k == 7, [uT.b, w.b], [ps.b])
                if n < 4:
                    self.act(qk32.t[:, n * 512:(n + 1) * 512], ps.t[:], AF.Copy, [ps.b], [qk32.b],
                             scale=(0.0625 if n >= 2 else 1.0))
                elif n < 8:
                    self.act(vb.t[:, (n - 4) * 512:(n - 3) * 512], ps.t[:], AF.Copy, [ps.b], [vb.b])
                else:
                    self.act(gb.t[:, (n - 8) * 512:(n - 7) * 512], ps.t[:], AF.Silu, [ps.b], [gb.b])
                if n == 3:
                    if lat:
                        v5 = qk32.t[:].rearrange("p (s j h e) -> p s j h e", s=8, j=2, h=2, e=64)
                        b5 = ropeB.t[:].rearrange("p (s j h e) -> p s j h e", s=8, j=2, h=2, e=64)
                        sv = rt.t[:, 256:512].rearrange("p (j h e) -> p j h e", j=2, h=2, e=64)
                        for hh in range(2):
                            self.tt(POOL, b5[:, :, :, hh, :], v5[:, :, :, 1 - hh, :],
                                    sv[:, :, hh, :].unsqueeze(1).broadcast_to([128, 8, 2, 64]), ALU.mult,
                                    [qk32.b, rt.b], [ropeB.b])
                        q3 = qk32.t[:].rearrange("p (s f) -> p s f", s=8)
                        self.tt(DVE, q3, q3, rt.t[:, 0:256].unsqueeze(1).broadcast_to([128, 8, 256]), ALU.mult,
                                [qk32.b, rt.b, ropeB.b], [qk32.b])
                        self.tt(DVE, qkr.t[:], qk32.t[:], ropeB.t[:], ALU.add, [qk32.b, ropeB.b], [qkr.b])
                    else:
                        self.cp(DVE, qkr.t[:], qk32.t[:], [qk32.b], [qkr.b])
                    self.dma(SP, self.sKt[0, b, tok, :], qkr.t[:, 1024:2048], [qkr.b], [self.dbuf(("Kt", b, ti))])
                    for half in range(2):
                        psT = psTr.next()
                        self.transpose_to(qkr.t, 8, psT, qkT.t[:, half * 1024:(half + 1) * 1024], qkT.b, qkr.b,
                                          eng=DVE, src_off=half * 1024)
                    self.dma(SP, self.sQT[0, b, ti], qkT.t[:, 0:1024], [qkT.b], [self.dbuf(("QT", b, ti))])
                    self.dma(SP, self.sKT[0, b, ti], qkT.t[:, 1024:2048], [qkT.b], [self.dbuf(("KT", b, ti))])
                if n == 7:
                    self.dma(SP, self.sV[b, tok, :], vb.t[:], [vb.b], [self.dbuf(("V", b, ti))])
                    hook()
                if n == 11:
                    self.dma(SP, self.sG[b, tok, :], gb.t[:], [gb.b], [self.dbuf(("G", b, ti))])

        self.run_pipe(self.proj_tiles(), ld, pre_a, pre_b, main)
        P.end_phase()

    def scan_order(self, d):
        c = self.cfg
        if d == 0:
            return list(range(c.NT))
        return list(range(c.CT - 1, -1, -1)) + list(range(c.NT - 1, c.CT - 1, -1))

    def ret_scan(self, li, j, d):
        P, c = self.P, self.cfg
        last = (li == self.L - 1)
        P.begin_phase()
        ct = self.load_ctab()
        dec = Tl(P, [128, 8], F32, "dec")
        lg = Tl(P, [128, 8], F32, "lg")
        nlg = Tl(P, [128, 8], F32, "nlg")
        self.dma(SP, dec.t[:], self.ret_decay[j, :].partition_broadcast(128), [], [dec.b])
        self.act(nlg.t[:], dec.t[:], AF.Exp, [dec.b], [nlg.b], scale=-1.0)
        self.act(nlg.t[:], nlg.t[:], AF.Ln, [nlg.b], [nlg.b], bias=1.0)
        self.ts(DVE, lg.t[:], nlg.t[:], -1.0, None, ALU.mult, None, [nlg.b], [lg.b])
        Dm = Tl(P, [128, 4, 128], F32, "Dm")
        E = Tl(P, [128, 4, 128], F32, "E")
        wc = Tl(P, [128, 4], F32, "wc")
        gC = Tl(P, [128, 4], F32, "gC")
        for h in range(4):
            col = slice(d * 4 + h, d * 4 + h + 1)
            sc = lg.t[:, col] if d == 0 else nlg.t[:, col]
            self.act(Dm.t[:, h, :], ct.t[:, 0, :], AF.Exp, [ct.b, lg.b, nlg.b], [Dm.b], scale=sc)
            self.tt(DVE, Dm.t[:, h, :], Dm.t[:, h, :], ct.t[:, 1 + d, :], ALU.mult, [Dm.b, ct.b], [Dm.b])
            self.act(E.t[:, h, :], ct.t[:, 3 + d, :], AF.Exp, [ct.b, lg.b], [E.b], scale=lg.t[:, col])
            self.act(wc.t[:, h:h + 1], ct.t[:, 5, d:d + 1], AF.Exp, [ct.b, lg.b], [wc.b], scale=lg.t[:, col])
            self.act(gC.t[:, h:h + 1], ct.t[:, 5, 2:3], AF.Exp, [ct.b, lg.b], [gC.b], scale=lg.t[:, col])
        S32 = {}
        Sbf = {}
        for b in range(c.NB):
            for h in range(4):
                for cc in range(2):
                    S32[b, h, cc] = Tl(P, [128, 512], F32, "S32")
                    Sbf[b, h, cc] = Tl(P, [128, 512], BF16, "Sbf")
                    self.memset(POOL, S32[b, h, cc].t[:], 0.0, [S32[b, h, cc].b])
                    self.memset(DVE, Sbf[b, h, cc].t[:], 0.0, [Sbf[b, h, cc].b])
        qTr = Ring(P, 3, [128, 8, 128], BF16, "qT")
        kTr = Ring(P, 3, [128, 8, 128], BF16, "kT")
        ktr = Ring(P, 3, [128, 4, 256], BF16, "kt")
        vr = Ring(P, 3, [128, 2048], BF16, "v")
        otr = Ring(P, 2, [128, 2048], F32, "ot")
        oflr = Ring(P, 3, [128, 2048], F32, "ofl") if d == 1 else None
        PTr = Ring(P, 2, [128, 4, 128], BF16, "PT")
        qsr = Ring(P, 2, [128, 8, 128], BF16, "qs")
        kwr = Ring(P, 2, [128, 4, 256], BF16, "kw")
        psS = Tl(P, [128, 512], F32, "psS", psum=True)
        psO = Tl(P, [128, 2048], F32, "psO", psum=True)
        psK = Ring(P, 3, [128, 512], F32, "psK", psum=True)
        steps = [(ti, b) for ti in self.scan_order(d) for b in range(c.NB)]

        def ld(st):
            ti, b = st
            tok = slice(ti * 128, (ti + 1) * 128)
            qT, kT, kt, v = qTr.next(), kTr.next(), ktr.next(), vr.next()
            need_o = not (last and ti < c.CT)
            if need_o:
                self.dma(SP, qT.t[:].rearrange("p k t -> p (k t)"), self.sQT[0, b, ti], [self.dbuf(("QT", b, ti))], [qT.b])
                self.dma(SP, kT.t[:].rearrange("p k t -> p (k t)"), self.sKT[0, b, ti], [self.dbuf(("KT", b, ti))], [kT.b])
            self.dma(SP, kt.t[:].rearrange("p h f -> p (h f)"), self.sKt[0, b, tok, :], [self.dbuf(("Kt", b, ti))], [kt.b])
            self.dma(SP, v.t[:], self.sV[b, tok, :], [self.dbuf(("V", b, ti))], [v.b])
            ofl = None
            if d == 1 and need_o:
                ofl = oflr.next()
                self.dma(SP, ofl.t[:], self.sO[0, b, tok, :], [self.dbuf(("O", 0, b, ti))], [ofl.b])
            return qT, kT, kt, v, need_o, ofl

        def comp(st, tl):
            ti, b = st
            tok = slice(ti * 128, (ti + 1) * 128)
            qT, kT, kt, v, need_o, ofl = tl
            if need_o:
                for h in range(4):
                    for cc in range(2):
                        self.mm(psS.t[:, h * 128:(h + 1) * 128], kT.t[:, h * 2 + cc, :], qT.t[:, h * 2 + cc, :], cc == 0, cc == 1,
                                [kT.b, qT.b], [psS.b])
                PT = PTr.next()
                self.tt(DVE, PT.t[:], psS.t[:].rearrange("p (h t) -> p h t", h=4), Dm.t[:], ALU.mult, [psS.b, Dm.b], [PT.b])
                qs = qsr.next()
                self.tt(POOL, qs.t[:].rearrange("p (h c) t -> p h c t", h=4), qT.t[:].rearrange("p (h c) t -> p h c t", h=4),
                        E.t[:].unsqueeze(2).broadcast_to([128, 4, 2, 128]), ALU.mult, [qT.b, E.b], [qs.b])
            kw = kwr.next()
            self.tt(POOL, kw.t[:], kt.t[:], wc.t[:, 0:4].unsqueeze(2).broadcast_to([128, 4, 256]), ALU.mult, [kt.b, wc.b], [kw.b])
            if need_o:
                for h in range(4):
                    vh = v.t[:, h * 512:(h + 1) * 512]
                    po = psO.t[:, h * 512:(h + 1) * 512]
                    self.mm(po, PT.t[:, h, :], vh, True, False, [PT.b, v.b], [psO.b])
                    for cc in range(2):
                        self.mm(po, qs.t[:, h * 2 + cc, :], Sbf[b, h, cc].t[:], False, cc == 1,
                                [qs.b, Sbf[b, h, cc].b], [psO.b])
                ot = otr.next()
                for h in range(4):
                    self.act(ot.t[:, h * 512:(h + 1) * 512], psO.t[:, h * 512:(h + 1) * 512], AF.Copy, [psO.b], [ot.b])
                if d == 1:
                    self.tt(POOL, ot.t[:], ot.t[:], ofl.t[:], ALU.add, [ot.b, ofl.b], [ot.b])
                self.dma(SP, self.sO[d, b, tok, :], ot.t[:], [ot.b], [self.dbuf(("O", d, b, ti))])
            for h in range(4):
                vh = v.t[:, h * 512:(h + 1) * 512]
                for cc in range(2):
                    pk = psK.next()
                    self.mm(pk.t[:], kw.t[:, h, cc * 128:(cc + 1) * 128], vh, True, True, [kw.b, v.b], [pk.b])
                    s32 = S32[b, h, cc]
                    self.stt(s32.t[:], s32.t[:], gC.t[:, h:h + 1], pk.t[:], ALU.mult, ALU.add, [s32.b, gC.b, pk.b], [s32.b])
                    self.cp(ACT, Sbf[b, h, cc].t[:], s32.t[:], [s32.b], [Sbf[b, h, cc].b])

        self.run_pipelined(steps, ld, comp, rowof=lambda it: 0)
        P.end_phase()

    def readout_main(self, li, wo, nk, g1, psYr, hnr, ssr, junk):
        state = {"row": None}

        def main(it, cx, hook):
            row, b, ti = it
            onT = cx["onT"]
            if state["row"] != row:
                state["row"] = row
                self.load_tab(g1, li, row, 2)
            psY = psYr.next()
            for n in range(2):
                for k in range(nk):
                    self.mm(psY.t[:, n * 512:(n + 1) * 512], onT.t[:, k, :], wo.t[:, k, n * 512:(n + 1) * 512],
                            k == 0, k == nk - 1, [onT.b, wo.b], [psY.b])
                if n == 0:
                    hook()
            hn = hnr.next()
            dst, db_ = self.h_mid(b, ti)
            self.post_residual(psY, cx["h"], g1, hn, ssr.next(), junk, dst, db_)
        return main

    def readout_items(self, li):
        keep = set(self.tile_list(li, with_ctx=False))
        return [it for it in self.proj_tiles() if (it[1], it[2]) in keep]

    def ret_readout(self, li, j):
        P, c = self.P, self.cfg
        P.begin_phase()
        wo = Tl(P, [128, 16, 1024], BF16, "w_out")
        self.load_wout_scaled(wo, self.ret_w_out[j], 16, self.ret_gn[j])
        g1 = Tl(P, [128, D], F32, "g1")
        ofr = Ring(P, 3, [128, 2048], F32, "of")
        gr = Ring(P, 3, [128, 2048], BF16, "g")
        hr = Ring(P, 3, [128, D], F32, "h")
        onr = Ring(P, 2, [128, 2048], BF16, "on")
        onTr = Ring(P, 2, [128, 16, 128], BF16, "onT")
        str_ = Ring(P, 2, [128, 4, 6], F32, "bst")
        mvr = Ring(P, 2, [128, 4, 2], F32, "mv")
        rsr = Ring(P, 2, [128, 4], F32, "rs")
        hnr = Ring(P, 2, [128, D], F32, "hn")
        ssr = Ring(P, 4, [128, 2], F32, "ss")
        junk = Tl(P, [128, D], BF16, "junk")
        psTr = Ring(P, 2, [128, 1024], BF16, "psT", psum=True)
        psYr = Ring(P, 2, [128, 1024], F32, "psY", psum=True)

        def ld(it):
            row, b, ti = it
            tok = slice(ti * 128, (ti + 1) * 128)
            of, g, h = ofr.next(), gr.next(), hr.next()
            self.dma(SP, of.t[:], self.sO[1, b, tok, :], [self.dbuf(("O", 1, b, ti))], [of.b])
            self.dma(SP, g.t[:], self.sG[b, tok, :], [self.dbuf(("G", b, ti))], [g.b])
            src, sb_ = self.h_src(li, b, ti)
            self.dma(SP, h.t[:], src, [sb_] if sb_ is not None else [], [h.b])
            return {"of": of, "g": g, "h": h}

        def pre_a(it, cx):
            of, g = cx["of"], cx["g"]
            on, bst, mv, rs = onr.next(), str_.next(), mvr.next(), rsr.next()
            for h in range(4):
                oh = of.t[:, h * 512:(h + 1) * 512]
                self.P.op(DVE, lambda e, o=bst.t[:, h, :], i=oh: e.bn_stats(out=o, in_=i), [of.b], [bst.b])
            for h in range(4):
                self.P.op(DVE, lambda e, o=mv.t[:, h, :], i=bst.t[:, h, :]: e.bn_aggr(out=o, in_=i), [bst.b], [mv.b])
            self.rstd(rs.t[:], mv.t[:, :, 1], 1.0, EPS, [mv.b], [rs.b], n=4)
            for h in range(4):
                oh = of.t[:, h * 512:(h + 1) * 512]
                self.stt(oh, oh, mv.t[:, h, 0:1], g.t[:, h * 512:(h + 1) * 512], ALU.subtract, ALU.mult,
                         [of.b, mv.b, g.b], [of.b])
            for h in range(4):
                oh = of.t[:, h * 512:(h + 1) * 512]
                self.act(on.t[:, h * 512:(h + 1) * 512], oh, AF.Copy, [of.b, rs.b], [on.b], scale=rs.t[:, h:h + 1])
            cx["on"] = on

        def pre_b(it, cx):
            on, onT = cx["on"], onTr.next()
            for half in range(2):
                psT = psTr.next()
                self.transpose_to(on.t, 8, psT, onT.t[:, half * 8:(half + 1) * 8, :].rearrange("p k t -> p (k t)"),
                                  onT.b, on.b, eng=(ACT if half == 0 else DVE), src_off=half * 1024)
            cx["onT"] = onT

        self.run_pipe(self.readout_items(li), ld, pre_a, pre_b, self.readout_main(li, wo, 16, g1, psYr, hnr, ssr, junk))
        P.end_phase()

    def hg_proj(self, li, j):
        P, c = self.P, self.cfg
        L = self.L
        P.begin_phase()
        ct = self.load_ctab()
        w = Tl(P, [128, 8, 5120], BF16, "w_in")
        self.load_w(w, self.hg_w_in[j], 8, 5120)
        lbx = Tl(P, [128, L, D], F32, "lbx")
        lb = Tl(P, [128, D], F32, "lb")
        omlb = Tl(P, [128, D], F32, "omlb")
        den = Tl(P, [128, D], F32, "den")
        self.dma(SP, lbx.t[:].rearrange("p l d -> p (l d)"), self.hg_lb.rearrange("l d -> (l d)").partition_broadcast(128), [], [lbx.b])
        self.act(lbx.t[:], lbx.t[:], AF.Exp, [lbx.b], [lbx.b])
        self.cp(DVE, den.t[:], lbx.t[:, 0, :], [lbx.b], [den.b])
        self.memset(DVE, lb.t[:], 0.0, [lb.b])
        for r in range(1, L):
            self.tt(DVE, den.t[:], den.t[:], lbx.t[:, r, :], ALU.add, [den.b, lbx.b], [den.b])
            if r <= li:
                self.tt(DVE, lb.t[:], lb.t[:], lbx.t[:, r, :], ALU.add, [lb.b, lbx.b], [lb.b])
        self.P.op(DVE, lambda e: e.reciprocal(out=den.t[:], in_=den.t[:]), [den.b], [den.b])
        self.tt(DVE, lb.t[:], lb.t[:], den.t[:], ALU.mult, [lb.b, den.b], [lb.b])
        self.ts(DVE, omlb.t[:], lb.t[:], -1.0, 1.0, ALU.mult, ALU.add, [lb.b], [omlb.b])
        tabs = [Tl(P, [128, D], F32, f"tab{i}") for i in range(2)]
        hr = Ring(P, 3, [128, D], F32, "h")
        ur = Ring(P, 2, [128, D], BF16, "u")
        junk = Tl(P, [128, D], BF16, "junk")
        ssr = Ring(P, 4, [128, 2], F32, "ss")
        uTr = Ring(P, 2, [128, 8, 128], BF16, "uT")
        qs = Tl(P, [128, D], F32, "qs")
        a32 = [Tl(P, [128, D], F32, f"a32{d}") for d in range(2)]
        la32 = [Tl(P, [128, D], F32, f"la{d}") for d in range(2)]
        k32 = a32
        e32r = Ring(P, 2, [128, D], F32, "e32")
        qtr = Ring(P, 2, [128, D], BF16, "qt")
        ktr = Ring(P, 2, [128, D], BF16, "kt")
        stg = Ring(P, 4, [128, D], BF16, "stg")
        vr = Ring(P, 2, [128, D], BF16, "vbf")
        gr = Ring(P, 2, [128, D], BF16, "gbf")
        csr = Ring(P, 2, [128, 48], F32, "cs")
        psTr = Ring(P, 2, [128, 1024], BF16, "psT", psum=True)
        psM = Ring(P, 5, [128, 512], F32, "psM", psum=True)
        psC = Tl(P, [128, 512], F32, "psC", psum=True)
        ld, pre_a, pre_b = self.make_stages(li, tabs, hr, ur, ssr, uTr, junk, psTr)

        def main(it, cx, hook):
            uT = cx["uT"]
            row, b, ti = it
            tok = slice(ti * 128, (ti + 1) * 128)
            vb, gb = vr.next(), gr.next()
            for n in range(10):
                if n == 7:
                    hook()
                ps = psM.next()
                for k in range(8):
                    self.mm(ps.t[:], uT.t[:, k, :], w.t[:, k, n * 512:(n + 1) * 512], k == 0, k == 7, [uT.b, w.b], [ps.b])
                cs_ = slice((n % 2) * 512, (n % 2) * 512 + 512)
                if n < 2:
                    self.act(qs.t[:, cs_], ps.t[:], AF.Silu, [ps.b], [qs.b])
                elif n < 6:
                    d = (n - 2) // 2
                    self.act(a32[d].t[:, cs_], ps.t[:], AF.Sigmoid, [ps.b], [a32[d].b])
                    self.tt(DVE, a32[d].t[:, cs_], a32[d].t[:, cs_], omlb.t[:, cs_], ALU.mult, [a32[d].b, omlb.b], [a32[d].b])
                    self.tt(POOL, a32[d].t[:, cs_], a32[d].t[:, cs_], lb.t[:, cs_], ALU.add, [a32[d].b, lb.b], [a32[d].b])
                    self.act(la32[d].t[:, cs_], a32[d].t[:, cs_], AF.Ln, [a32[d].b], [la32[d].b])
                    self.ts(POOL, a32[d].t[:, cs_], a32[d].t[:, cs_], -1.0, 1.0, ALU.mult, ALU.add, [a32[d].b], [a32[d].b])
                elif n < 8:
                    self.cp(DVE, vb.t[:, cs_], ps.t[:], [ps.b], [vb.b])
                else:
                    self.act(gb.t[:, cs_], ps.t[:], AF.Silu, [ps.b], [gb.b])
            self.dma(SP, self.sV[b, tok, 0:D], vb.t[:], [vb.b], [self.dbuf(("V", b, ti))])
            self.dma(SP, self.sG[b, tok, 0:D], gb.t[:], [gb.b], [self.dbuf(("G", b, ti))])
            for d in range(2):
                qt, kt = qtr.next(), ktr.next()
                for n in range(2):
                    cs_ = slice(n * 512, n * 512 + 512)
                    ps = psM.next()
                    self.mm(ps.t[:], ct.t[:, 8 + d, :], la32[d].t[:, cs_], True, True, [ct.b, la32[d].b], [ps.b])
                    e1, e2 = e32r.next(), e32r.next()
                    self.act(e1.t[:, 0:512], ps.t[:], AF.Exp, [ps.b], [e1.b])
                    self.act(e2.t[:, 0:512], ps.t[:], AF.Exp, [ps.b], [e2.b], scale=-1.0)
                    self.tt(DVE, qt.t[:, cs_], qs.t[:, cs_], e1.t[:, 0:512], ALU.mult, [qs.b, e1.b], [qt.b])
                    self.tt(POOL, kt.t[:, cs_], k32[d].t[:, cs_], e2.t[:, 0:512], ALU.mult, [k32[d].b, e2.b], [kt.b])
                for h in range(8):
                    self.mm(psC.t[:, h * 6:(h + 1) * 6], la32[d].t[:, h * 128:(h + 1) * 128], ct.t[:, 10 + d, 0:6],
                            True, True, [la32[d].b, ct.b], [psC.b])
                cs = csr.next()
                self.act(cs.t[:], psC.t[:, 0:48], AF.Exp, [psC.b], [cs.b])
                self.dma(SP, self.sCS[d, b, ti], cs.t[:], [cs.b], [self.dbuf(("CS", d, b, ti))])
                self.dma(SP, self.sKt[d, b, tok, :], kt.t[:], [kt.b], [self.dbuf(("Kt", d, b, ti))])
                for which, src in ((0, qt), (1, kt)):
                    psT = psTr.next()
                    st = stg.next()
                    self.transpose_to(src.t, 8, psT, st.t[:], st.b, src.b, eng=DVE)
                    dst = (self.sQT if which == 0 else self.sKT)[d, b, ti]
                    self.dma(SP, dst, st.t[:], [st.b], [self.dbuf(("QT" if which == 0 else "KT", d, b, ti))])

        self.run_pipe(self.proj_tiles(), ld, pre_a, pre_b, main)
        P.end_phase()

    def hg_scan(self, li, j, d):
        P, c = self.P, self.cfg
        last = (li == self.L - 1)
        P.begin_phase()
        ct = self.load_ctab()
        S = {}
        for b in range(c.NB):
            for hh in range(2):
                S[b, hh] = Tl(P, [128, 4, 128], F32, "S")
                self.memset(POOL, S[b, hh].t[:], 0.0, [S[b, hh].b])
        qTr = Ring(P, 3, [128, 8, 128], BF16, "qT")
        kTr = Ring(P, 3, [128, 8, 128], BF16, "kT")
        ktr = Ring(P, 3, [128, D], BF16, "kt")
        vr = Ring(P, 3, [128, D], BF16, "v")
        csr = Ring(P, 3, [128, 8, 2, 3], F32, "cs")
        otr = Ring(P, 2, [128, D], F32, "ot")
        oflr = Ring(P, 3, [128, D], F32, "ofl") if d == 1 else None
        PTr = Ring(P, 2, [128, 4, 128], BF16, "PT")
        Sxr = Ring(P, 4, [128, 4, 128], BF16, "Sx")
        tmpr = Ring(P, 2, [128, 4, 128], F32, "tmp")
        psS = Ring(P, 2, [128, 512], F32, "psS", psum=True)
        psO = Ring(P, 2, [128, 512], F32, "psO", psum=True)
        psK = Ring(P, 4, [128, 512], F32, "psK", psum=True)
        steps = [(ti, b) for ti in self.scan_order(d) for b in range(c.NB)]
        corder = (0, 1) if d == 0 else (1, 0)

        def ld(st):
            ti, b = st
            tok = slice(ti * 128, (ti + 1) * 128)
            qT, kT, kt, v, cs = qTr.next(), kTr.next(), ktr.next(), vr.next(), csr.next()
            need_o = not (last and ti < c.CT)
            self.dma(SP, qT.t[:].rearrange("p k t -> p (k t)"), self.sQT[d, b, ti], [self.dbuf(("QT", d, b, ti))], [qT.b])
            self.dma(SP, kT.t[:].rearrange("p k t -> p (k t)"), self.sKT[d, b, ti], [self.dbuf(("KT", d, b, ti))], [kT.b])
            self.dma(SP, kt.t[:], self.sKt[d, b, tok, :], [self.dbuf(("Kt", d, b, ti))], [kt.b])
            self.dma(SP, v.t[:], self.sV[b, tok, 0:D], [self.dbuf(("V", b, ti))], [v.b])
            self.dma(SP, cs.t[:].rearrange("p h c k -> p (h c k)"), self.sCS[d, b, ti], [self.dbuf(("CS", d, b, ti))], [cs.b])
            ofl = None
            if d == 1 and need_o:
                ofl = oflr.next()
                self.dma(SP, ofl.t[:], self.sO[0, b, tok, 0:D], [self.dbuf(("O", 0, b, ti))], [ofl.b])
            return qT, kT, kt, v, cs, need_o, ofl

        def comp(st, tl):
            ti, b = st
            tok = slice(ti * 128, (ti + 1) * 128)
            qT, kT, kt, v, cs, need_o, ofl = tl
            ot = otr.next() if need_o else None
            HH = (0, 1)

            def bc(hh, kind, cc):
                return cs.t[:, hh * 4:hh * 4 + 4, cc, kind:kind + 1].broadcast_to([128, 4, 128])

            po = {}
            if need_o:
                PT = {}
                for hh in HH:
                    ps = psS.next()
                    for hl in range(4):
                        h = hh * 4 + hl
                        self.mm(ps.t[:, hl * 128:(hl + 1) * 128], kT.t[:, h, :], qT.t[:, h, :], True, True, [kT.b, qT.b], [ps.b])
                    PT[hh] = PTr.next()
                    self.tt(DVE, PT[hh].t[:], ps.t[:].rearrange("p (h t) -> p h t", h=4),
                            ct.t[:, 6 + d, :].unsqueeze(1).broadcast_to([128, 4, 128]), ALU.mult, [ps.b, ct.b], [PT[hh].b])
                for hh in HH:
                    po[hh] = psO.next()
                    for hl in range(4):
                        h = hh * 4 + hl
                        self.mm(po[hh].t[:, hl * 128:(hl + 1) * 128], PT[hh].t[:, hl, :], v.t[:, h * 128:(h + 1) * 128], hl == 0, False,
                                [PT[hh].b, v.b], [po[hh].b], skip=True)
            for ci, cc in enumerate(corder):
                rows = slice(cc * 64, cc * 64 + 64)
                Sx = {}
                if need_o:
                    for hh in HH:
                        Sx[hh] = Sxr.next()
                        self.tt(POOL, Sx[hh].t[:], S[b, hh].t[:], bc(hh, 0, cc), ALU.mult, [S[b, hh].b, cs.b], [Sx[hh].b])
                pk = {}
                for hh in HH:
                    if need_o:
                        for hl in range(4):
                            h = hh * 4 + hl
                            self.mm(po[hh].t[rows, hl * 128:(hl + 1) * 128], qT.t[:, h, rows], Sx[hh].t[:, hl, :], False,
                                    True, [qT.b, Sx[hh].b], [po[hh].b], skip=True)
                    pk[hh] = psK.next()
                    for hl in range(4):
                        h = hh * 4 + hl
                        self.mm(pk[hh].t[:, hl * 128:(hl + 1) * 128], kt.t[rows, h * 128:(h + 1) * 128],
                                v.t[rows, h * 128:(h + 1) * 128], True, True, [kt.b, v.b], [pk[hh].b])
                tmp = {}
                for hh in HH:
                    tmp[hh] = tmpr.next()
                    self.tt(DVE, tmp[hh].t[:], pk[hh].t[:].rearrange("p (h t) -> p h t", h=4), bc(hh, 2, cc), ALU.mult,
                            [pk[hh].b, cs.b], [tmp[hh].b])
                    self.tt(POOL, S[b, hh].t[:], S[b, hh].t[:], bc(hh, 1, cc), ALU.mult, [S[b, hh].b, cs.b], [S[b, hh].b])
                for hh in HH:
                    self.tt(DVE, S[b, hh].t[:], S[b, hh].t[:], tmp[hh].t[:], ALU.add, [S[b, hh].b, tmp[hh].b], [S[b, hh].b])
            if need_o:
                for hh in HH:
                    if d == 0:
                        self.act(ot.t[:, hh * 512:(hh + 1) * 512], po[hh].t[:], AF.Copy, [po[hh].b], [ot.b])
                    else:
                        self.tt(DVE, ot.t[:, hh * 512:(hh + 1) * 512], po[hh].t[:], ofl.t[:, hh * 512:(hh + 1) * 512], ALU.add,
                                [po[hh].b, ofl.b], [ot.b])
                self.dma(SP, self.sO[d, b, tok, 0:D], ot.t[:], [ot.b], [self.dbuf(("O", d, b, ti))])

        self.run_pipelined(steps, ld, comp, rowof=lambda it: 0)
        P.end_phase()

    def hg_readout(self, li, j):
        P, c = self.P, self.cfg
        P.begin_phase()
        wo = Tl(P, [128, 8, 1024], BF16, "w_out")
        self.load_wout_scaled(wo, self.hg_w_out[j], 8, self.hg_gn[j])
        g1 = Tl(P, [128, D], F32, "g1")
        ofr = Ring(P, 3, [128, D], F32, "of")
        gr = Ring(P, 3, [128, D], BF16, "g")
        hr = Ring(P, 3, [128, D], F32, "h")
        sqr = Ring(P, 2, [128, D], F32, "sq")
        onr = Ring(P, 2, [128, D], BF16, "on")
        onTr = Ring(P, 2, [128, 8, 128], BF16, "onT")
        rsr = Ring(P, 2, [128, 8], F32, "rs")
        hnr = Ring(P, 2, [128, D], F32, "hn")
        ssr = Ring(P, 4, [128, 2], F32, "ss")
        junk = Tl(P, [128, D], BF16, "junk")
        psTr = Ring(P, 2, [128, 1024], BF16, "psT", psum=True)
        psYr = Ring(P, 2, [128, 1024], F32, "psY", psum=True)

        def ld(it):
            row, b, ti = it
            tok = slice(ti * 128, (ti + 1) * 128)
            of, g, h = ofr.next(), gr.next(), hr.next()
            self.dma(SP, of.t[:], self.sO[1, b, tok, 0:D], [self.dbuf(("O", 1, b, ti))], [of.b])
            self.dma(SP, g.t[:], self.sG[b, tok, 0:D], [self.dbuf(("G", b, ti))], [g.b])
            src, sb_ = self.h_src(li, b, ti)
            self.dma(SP, h.t[:], src, [sb_] if sb_ is not None else [], [h.b])
            return {"of": of, "g": g, "h": h}

        def pre_a(it, cx):
            of, g = cx["of"], cx["g"]
            on, rs, sq = onr.next(), rsr.next(), sqr.next()
            self.tt(POOL, sq.t[:], of.t[:], of.t[:], ALU.mult, [of.b], [sq.b])
            self.P.op(DVE, lambda e, o=rs.t[:], i=sq.t[:].rearrange("p (h v) -> p h v", h=8):
                      e.tensor_reduce(out=o, in_=i, axis=mybir.AxisListType.X, op=ALU.add), [sq.b], [rs.b])
            self.rstd(rs.t[:], rs.t[:], 1.0 / 128, EPS, [rs.b], [rs.b], n=8)
            self.tt(DVE, of.t[:], of.t[:], g.t[:], ALU.mult, [of.b, g.b], [of.b])
            self.tt(DVE, on.t[:].rearrange("p (h v) -> p h v", h=8), of.t[:].rearrange("p (h v) -> p h v", h=8),
                    rs.t[:].unsqueeze(2).broadcast_to([128, 8, 128]), ALU.mult, [of.b, rs.b], [on.b])
            cx["on"] = on

        def pre_b(it, cx):
            on, onT = cx["on"], onTr.next()
            psT = psTr.next()
            self.transpose_to(on.t, 8, psT, onT.t[:].rearrange("p k t -> p (k t)"), onT.b, on.b)
            cx["onT"] = onT

        self.run_pipe(self.readout_items(li), ld, pre_a, pre_b, self.readout_main(li, wo, 8, g1, psYr, hnr, ssr, junk))
        P.end_phase()

    def x_row(self, ti):
        c = self.cfg
        if ti < c.CT:
            return 1 + ti * 128
        return c.CTX + 3 + (ti - c.CT) * 128

    def m2_proj(self, li, j):
        P, c = self.P, self.cfg
        P.begin_phase()
        w = Tl(P, [128, 8, 5184], BF16, "w_in")
        self.load_w(w, self.m2_w_in[j], 8, 5184)
        tabs = [Tl(P, [128, D], F32, f"tab{i}") for i in range(2)]
        dtb = Tl(P, [128, 64], F32, "dtb")
        aneg = Tl(P, [128, 64], F32, "aneg")
        self.dma(SP, dtb.t[:], self.m2_dt_bias[j, :].partition_broadcast(128), [], [dtb.b])
        self.dma(SP, aneg.t[:], self.m2_a_log[j, :].partition_broadcast(128), [], [aneg.b])
        self.act(aneg.t[:], aneg.t[:], AF.Exp, [aneg.b], [aneg.b])
        self.ts(DVE, aneg.t[:], aneg.t[:], -1.0, None, ALU.mult, None, [aneg.b], [aneg.b])
        zero = Tl(P, [1, 3072], F32, "zero")
        self.memset(DVE, zero.t[:], 0.0, [zero.b])
        for b in range(c.NB):
            for r in (0, c.CTX + 1, c.CTX + 2, c.CTX + c.LAT + 3):
                self.dma(SP, self.sX[b, r:r + 1, :], zero.t[:], [zero.b], [self.dbuf(("Xpad", b, r))])
        hr = Ring(P, 3, [128, D], F32, "h")
        ur = Ring(P, 2, [128, D], BF16, "u")
        junk = Tl(P, [128, D], BF16, "junk")
        ssr = Ring(P, 4, [128, 2], F32, "ss")
        uTr = Ring(P, 2, [128, 8, 128], BF16, "uT")
        zr = Ring(P, 2, [128, 2048], BF16, "zb")
        xr = Ring(P, 2, [128, 3072], F32, "xbc")
        dr = Ring(P, 2, [128, 128], F32, "dtla")
        psTr = Ring(P, 2, [128, 1024], BF16, "psT", psum=True)
        psM = Ring(P, 6, [128, 512], F32, "psM", psum=True)
        ld, pre_a, pre_b = self.make_stages(li, tabs, hr, ur, ssr, uTr, junk, psTr)

        def main(it, cx, hook):
            uT = cx["uT"]
            row, b, ti = it
            tok = slice(ti * 128, (ti + 1) * 128)
            zb, xb, dl = zr.next(), xr.next(), dr.next()
            for n in range(11):
                if n == 6:
                    hook()
                ps = psM.next()
                wd = 512 if n < 10 else 64
                for k in range(8):
                    self.mm(ps.t[:, 0:wd], uT.t[:, k, :], w.t[:, k, n * 512:n * 512 + wd], k == 0, k == 7, [uT.b, w.b], [ps.b])
                if n < 4:
                    self.act(zb.t[:, n * 512:(n + 1) * 512], ps.t[:], AF.Silu, [ps.b], [zb.b])
                elif n < 10:
                    eng = ACT if n % 2 == 0 else DVE
                    self.cp(eng, xb.t[:, (n - 4) * 512:(n - 3) * 512], ps.t[:], [ps.b], [xb.b])
                else:
                    self.tt(DVE, dl.t[:, 0:64], ps.t[:, 0:64], dtb.t[:], ALU.add, [ps.b, dtb.b], [dl.b])
                    self.act(dl.t[:, 0:64], dl.t[:, 0:64], AF.Exp, [dl.b], [dl.b])
                    self.act(dl.t[:, 0:64], dl.t[:, 0:64], AF.Ln, [dl.b], [dl.b], bias=1.0)
                    self.tt(DVE, dl.t[:, 64:128], dl.t[:, 0:64], aneg.t[:], ALU.mult, [dl.b, aneg.b], [dl.b])
            self.dma(SP, self.sG[b, tok, :], zb.t[:], [zb.b], [self.dbuf(("G", b, ti))])
            r0 = self.x_row(ti)
            self.dma(SP, self.sX[b, r0:r0 + 128, :], xb.t[:], [xb.b], [self.dbuf(("X", b, ti))])
            self.dma(SP, self.sDT[b, tok, :], dl.t[:], [dl.b], [self.dbuf(("DT", b, ti))])

        self.run_pipe(self.proj_tiles(), ld, pre_a, pre_b, main)
        P.end_phase()

    def m2_conv(self, li, j):
        P, c = self.P, self.cfg
        P.begin_phase()
        cw = Tl(P, [128, 3, 3072], F32, "cw")
        cb = Tl(P, [128, 3072], F32, "cb")
        self.dma(SP, cw.t[:].rearrange("p a b -> p (a b)"), self.m2_conv_w[j].rearrange("a b -> (a b)").partition_broadcast(128), [], [cw.b])
        self.dma(SP, cb.t[:], self.m2_conv_b[j, :].partition_broadcast(128), [], [cb.b])
        xr = [Ring(P, 2, [128, 3072], F32, f"x{i}") for i in range(3)]
        actr = Ring(P, 2, [128, 3072], BF16, "act")
        stg = Ring(P, 2, [128, 1024], BF16, "stg")
        psTr = Ring(P, 2, [128, 1024], BF16, "psT", psum=True)
        items = [(b, ti) for b in range(c.NB) for ti in range(c.NT)]

        def ld(it):
            b, ti = it
            r0 = self.x_row(ti)
            xs = [xr[i].next() for i in range(3)]
            deps = [self.dbuf(("X", b, t2)) for t2 in range(c.NT)] + [self.dbuf(("Xpad", b, r)) for r in (0, c.CTX + 1, c.CTX + 2, c.CTX + c.LAT + 3)]
            for i in range(3):
                self.dma(SP, xs[i].t[:], self.sX[b, r0 - 1 + i:r0 - 1 + i + 128, :], deps, [xs[i].b])
            return xs

        def comp(it, xs):
            b, ti = it
            tok = slice(ti * 128, (ti + 1) * 128)
            x0, x1, x2 = xs
            self.tt(POOL, x0.t[:], x0.t[:], cw.t[:, 0, :], ALU.mult, [x0.b, cw.b], [x0.b])
            self.tt(DVE, x1.t[:], x1.t[:], cw.t[:, 1, :], ALU.mult, [x1.b, cw.b], [x1.b])
            self.tt(DVE, x2.t[:], x2.t[:], cw.t[:, 2, :], ALU.mult, [x2.b, cw.b], [x2.b])
            self.tt(DVE, x1.t[:], x1.t[:], cb.t[:], ALU.add, [x1.b, cb.b], [x1.b])
            self.tt(DVE, x1.t[:], x1.t[:], x2.t[:], ALU.add, [x1.b, x2.b], [x1.b])
            self.tt(DVE, x1.t[:], x1.t[:], x0.t[:], ALU.add, [x1.b, x0.b], [x1.b])
            a = actr.next()
            self.act(a.t[:], x1.t[:], AF.Silu, [x1.b], [a.b])
            self.dma(SP, self.sV[b, tok, :], a.t[:, 0:2048], [a.b], [self.dbuf(("V", b, ti))])
            self.dma(SP, self.sKt[0, b, tok, 0:512], a.t[:, 2048:2560], [a.b], [self.dbuf(("Kt", b, ti))])
            psT = psTr.next()
            st = stg.next()
            self.transpose_to(a.t, 8, psT, st.t[:], st.b, a.b, eng=ACT, src_off=2048)
            self.dma(SP, self.sKT[0, b, ti][:, 0:512], st.t[:, 0:512], [st.b], [self.dbuf(("KT", b, ti))])
            self.dma(SP, self.sQT[0, b, ti][:, 0:512], st.t[:, 512:1024], [st.b], [self.dbuf(("QT", b, ti))])

        self.run_pipelined(items, ld, comp, rowof=lambda it: 0)
        P.end_phase()

    def m2_scan(self, li, j, d):
        P, c = self.P, self.cfg
        last = (li == self.L - 1)
        P.begin_phase()
        ct = self.load_ctab()
        S32, Sbf = {}, {}
        for b in range(c.NB):
            for g in range(4):
                S32[b, g] = Tl(P, [128, 8, 64], F32, "S32")
                Sbf[b, g] = Tl(P, [128, 512], BF16, "Sbf")
                self.memset(POOL, S32[b, g].t[:], 0.0, [S32[b, g].b])
                self.memset(DVE, Sbf[b, g].t[:], 0.0, [Sbf[b, g].b])
        CTr = Ring(P, 3, [128, 4, 128], BF16, "CT")
        BTr = Ring(P, 3, [128, 4, 128], BF16, "BT")
        Btr = Ring(P, 3, [128, 512], BF16, "Bt")
        Xr = Ring(P, 3, [128, 32, 64], BF16, "X")
        dlr = Ring(P, 3, [128, 128], F32, "dtla")
        ear = Ring(P, 2, [128, 96], F32, "eall")
        xdr = Ring(P, 2, [128, 32, 64], BF16, "xdt")
        xwr = Ring(P, 2, [128, 32, 64], BF16, "xw")
        Rr = Ring(P, 2, [128, 8, 128], F32, "R")
        Lr = Ring(P, 2, [128, 8, 128], BF16, "L")
        CBr = Ring(P, 2, [128, 128], F32, "CBm")
        Pmr = Ring(P, 2, [128, 8, 128], BF16, "Pm")
        tmpr = Ring(P, 2, [128, 8, 64], F32, "tmp")
        otr = Ring(P, 2, [128, 2048], F32, "ot")
        psE = Tl(P, [128, 512], F32, "psE", psum=True)
        psD = Tl(P, [128, 1024], F32, "psD", psum=True)
        psCB = Tl(P, [128, 512], F32, "psCB", psum=True)
        psO = Ring(P, 2, [128, 512], F32, "psO", psum=True)
        psI = Tl(P, [128, 512], F32, "psI", psum=True)
        psK = Tl(P, [128, 512], F32, "psK", psum=True)
        steps = [(ti, b) for ti in self.scan_order(d) for b in range(c.NB)]
        tri = ct.t[:, 12 + d, :]
        G = ct.t[:, 14 + d, :]
        ones = ct.t[:, 16, :]

        def ld(st):
            ti, b = st
            tok = slice(ti * 128, (ti + 1) * 128)
            CT, BT, Bt, X, dl = CTr.next(), BTr.next(), Btr.next(), Xr.next(), dlr.next()
            need_o = not (last and ti < c.CT)
            if need_o:
                self.dma(SP, CT.t[:].rearrange("p g t -> p (g t)"), self.sQT[0, b, ti][:, 0:512], [self.dbuf(("QT", b, ti))], [CT.b])
                self.dma(SP, BT.t[:].rearrange("p g t -> p (g t)"), self.sKT[0, b, ti][:, 0:512], [self.dbuf(("KT", b, ti))], [BT.b])
            self.dma(SP, Bt.t[:], self.sKt[0, b, tok, 0:512], [self.dbuf(("Kt", b, ti))], [Bt.b])
            self.dma(SP, X.t[:].rearrange("p h e -> p (h e)"), self.sV[b, tok, :], [self.dbuf(("V", b, ti))], [X.b])
            self.dma(SP, dl.t[:], self.sDT[b, tok, :], [self.dbuf(("DT", b, ti))], [dl.b])
            ofl = None
            return CT, BT, Bt, X, dl, need_o, ofl

        def comp(st, tl):
            ti, b = st
            tok = slice(ti * 128, (ti + 1) * 128)
            CT, BT, Bt, X, dl, need_o, ofl = tl
            la = dl.t[:, 64 + d * 32:64 + (d + 1) * 32]
            dt = dl.t[:, d * 32:(d + 1) * 32]
            self.mm(psE.t[:, 0:32], tri, la, True, True, [ct.b, dl.b], [psE.b])
            self.mm(psE.t[:, 32:64], G, la, True, True, [ct.b, dl.b], [psE.b])
            self.mm(psE.t[:, 64:96], ones, la, True, True, [ct.b, dl.b], [psE.b])
            ea = ear.next()
            self.act(ea.t[:], psE.t[:, 0:96], AF.Exp, [psE.b], [ea.b])
            xd, xw = xdr.next(), xwr.next()
            self.tt(DVE, xd.t[:], X.t[:], dt.unsqueeze(2).broadcast_to([128, 32, 64]), ALU.mult, [X.b, dl.b], [xd.b])
            self.tt(DVE, xw.t[:], xd.t[:], ea.t[:, 32:64].unsqueeze(2).broadcast_to([128, 32, 64]), ALU.mult, [xd.b, ea.b], [xw.b])
            ot = otr.next() if need_o else None
            for g in range(4):
                hs = slice(g * 8, g * 8 + 8)
                if need_o:
                    R = Rr.next()
                    self.tt(POOL, R.t[:], la[:, hs].unsqueeze(2).broadcast_to([128, 8, 128]),
                            tri.unsqueeze(1).broadcast_to([128, 8, 128]), ALU.mult, [dl.b, ct.b], [R.b])
                    for half in range(2):
                        self.mm(psD.t[:, half * 512:(half + 1) * 512], G,
                                R.t[:, half * 4:(half + 1) * 4, :].rearrange("p h t -> p (h t)"), True, True, [ct.b, R.b], [psD.b])
                    Lg = Lr.next()
                    self.act(Lg.t[:].rearrange("p h t -> p (h t)"), psD.t[:], AF.Exp, [psD.b], [Lg.b])
                    self.mm(psCB.t[:, 0:128], BT.t[:, g, :], CT.t[:, g, :], True, True, [BT.b, CT.b], [psCB.b])
                    CBm = CBr.next()
                    self.tt(DVE, CBm.t[:], psCB.t[:, 0:128], ct.t[:, 1 + d, :], ALU.mult, [psCB.b, ct.b], [CBm.b])
                    Pm = Pmr.next()
                    self.tt(DVE, Pm.t[:], Lg.t[:], CBm.t[:].unsqueeze(1).broadcast_to([128, 8, 128]), ALU.mult, [Lg.b, CBm.b], [Pm.b])
                    po = psO.next()
                    for r in range(8):
                        self.mm(po.t[:, r * 64:(r + 1) * 64], Pm.t[:, r, :], xd.t[:, g * 8 + r, :], r == 0, r == 7,
                                [Pm.b, xd.b], [po.b], skip=True)
                    self.mm(psI.t[:], CT.t[:, g, :], Sbf[b, g].t[:], True, True, [CT.b, Sbf[b, g].b], [psI.b])
                    tmp = tmpr.next()
                    self.tt(DVE, tmp.t[:], psI.t[:].rearrange("p (h e) -> p h e", h=8),
                            ea.t[:, hs].unsqueeze(2).broadcast_to([128, 8, 64]), ALU.mult, [psI.b, ea.b], [tmp.b])
                    self.tt(DVE, ot.t[:, g * 512:(g + 1) * 512], tmp.t[:].rearrange("p h e -> p (h e)"), po.t[:], ALU.add,
                            [tmp.b, po.b], [ot.b])
                self.mm(psK.t[:], Bt.t[:, g * 128:(g + 1) * 128], xw.t[:, hs, :].rearrange("p h e -> p (h e)"), True, True,
                        [Bt.b, xw.b], [psK.b])
                s32 = S32[b, g]
                self.tt(POOL, s32.t[:], s32.t[:], ea.t[:, 64 + g * 8:64 + g * 8 + 8].unsqueeze(2).broadcast_to([128, 8, 64]),
                        ALU.mult, [s32.b, ea.b], [s32.b])
                self.tt(DVE, s32.t[:], s32.t[:], psK.t[:].rearrange("p (h e) -> p h e", h=8), ALU.add, [s32.b, psK.b], [s32.b])
                self.cp(ACT, Sbf[b, g].t[:], s32.t[:].rearrange("p h e -> p (h e)"), [s32.b], [Sbf[b, g].b])
            if need_o:
                self.dma(SP, self.sO[d, b, tok, :], ot.t[:], [ot.b], [self.dbuf(("O", d, b, ti))])

        self.run_pipelined(steps, ld, comp, rowof=lambda it: 0)
        P.end_phase()

    def m2_readout(self, li, j):
        P, c = self.P, self.cfg
        P.begin_phase()
        wo = Tl(P, [128, 16, 1024], BF16, "w_out")
        self.load_wout_scaled(wo, self.m2_w_out[j], 16, self.m2_gn[j])
        dsk = Tl(P, [128, 32], F32, "dsk")
        self.dma(SP, dsk.t[:], self.m2_d[j, :].partition_broadcast(128), [], [dsk.b])
        g1 = Tl(P, [128, D], F32, "g1")
        ofr = Ring(P, 3, [128, 2048], F32, "of")
        obr = Ring(P, 3, [128, 2048], F32, "ob")
        xr = Ring(P, 3, [128, 32, 64], BF16, "x")
        zr = Ring(P, 3, [128, 2048], BF16, "z")
        hr = Ring(P, 3, [128, D], F32, "h")
        tmpr = Ring(P, 2, [128, 32, 64], F32, "tmp")
        junk2 = Tl(P, [128, 512], BF16, "junk2")
        onr = Ring(P, 2, [128, 2048], BF16, "on")
        onTr = Ring(P, 2, [128, 16, 128], BF16, "onT")
        rsr = Ring(P, 2, [128, 4], F32, "rs")
        hnr = Ring(P, 2, [128, D], F32, "hn")
        ssr = Ring(P, 4, [128, 2], F32, "ss")
        junk = Tl(P, [128, D], BF16, "junk")
        psTr = Ring(P, 2, [128, 1024], BF16, "psT", psum=True)
        psYr = Ring(P, 2, [128, 1024], F32, "psY", psum=True)

        def ld(it):
            row, b, ti = it
            tok = slice(ti * 128, (ti + 1) * 128)
            of, ob, x, z, h = ofr.next(), obr.next(), xr.next(), zr.next(), hr.next()
            self.dma(SP, of.t[:], self.sO[1, b, tok, :], [self.dbuf(("O", 1, b, ti))], [of.b])
            self.dma(SP, ob.t[:], self.sO[0, b, tok, :], [self.dbuf(("O", 0, b, ti))], [ob.b])
            self.dma(SP, x.t[:].rearrange("p h e -> p (h e)"), self.sV[b, tok, :], [self.dbuf(("V", b, ti))], [x.b])
            self.dma(SP, z.t[:], self.sG[b, tok, :], [self.dbuf(("G", b, ti))], [z.b])
            src, sb_ = self.h_src(li, b, ti)
            self.dma(SP, h.t[:], src, [sb_] if sb_ is not None else [], [h.b])
            return {"of": of, "ob": ob, "x": x, "z": z, "h": h}

        def pre_a(it, cx):
            of, ob, x, z = cx["of"], cx["ob"], cx["x"], cx["z"]
            on, rs, tmp = onr.next(), rsr.next(), tmpr.next()
            self.tt(DVE, tmp.t[:], x.t[:], dsk.t[:].unsqueeze(2).broadcast_to([128, 32, 64]), ALU.mult, [x.b, dsk.b], [tmp.b])
            self.tt(POOL, of.t[:], of.t[:], ob.t[:], ALU.add, [of.b, ob.b], [of.b])
            self.tt(DVE, of.t[:], of.t[:], tmp.t[:].rearrange("p h e -> p (h e)"), ALU.add, [of.b, tmp.b], [of.b])
            self.tt(DVE, of.t[:], of.t[:], z.t[:], ALU.mult, [of.b, z.b], [of.b])
            for g in range(4):
                self.act(junk2.t[:], of.t[:, g * 512:(g + 1) * 512], AF.Square, [of.b], [junk2.b, rs.b], accum_out=rs.t[:, g:g + 1])
            self.rstd(rs.t[:], rs.t[:], 1.0 / 512, EPS, [rs.b], [rs.b], n=4)
            self.tt(DVE, on.t[:].rearrange("p (g v) -> p g v", g=4), of.t[:].rearrange("p (g v) -> p g v", g=4),
                    rs.t[:].unsqueeze(2).broadcast_to([128, 4, 512]), ALU.mult, [of.b, rs.b], [on.b])
            cx["on"] = on

        def pre_b(it, cx):
            on, onT = cx["on"], onTr.next()
            for half in range(2):
                psT = psTr.next()
                self.transpose_to(on.t, 8, psT, onT.t[:, half * 8:(half + 1) * 8, :].rearrange("p k t -> p (k t)"),
                                  onT.b, on.b, eng=(ACT if half == 0 else DVE), src_off=half * 1024)
            cx["onT"] = onT

        self.run_pipe(self.readout_items(li), ld, pre_a, pre_b, self.readout_main(li, wo, 16, g1, psYr, hnr, ssr, junk))
        P.end_phase()

    def copy_in(self):
        P, c = self.P, self.cfg
        P.begin_phase()
        hr = Ring(P, 3, [128, D], F32, "h")
        for b in range(c.NB):
            for ti in range(c.NT):
                h = hr.next()
                src, _ = self.h_src(0, b, ti)
                self.dma(SP, h.t[:], src, [], [h.b])
                dst, db_ = self.h_mid(b, ti)
                self.dma(SP, dst, h.t[:], [h.b], [db_])
        P.end_phase()

    def build(self):
        c = self.cfg
        self.declare()
        self.setup_consts()
        phases = c.phases
        if phases is None:
            phases = ["mod"]
            for li, k in enumerate(c.kinds):
                phases += [("mix", li), ("ffn", li)]
        for ph in phases:
            if ph == "mod":
                self.phase_mod()
            elif ph == "copy_in":
                self.copy_in()
            elif ph[0] == "ffn":
                self.phase_ffn(ph[1])
            elif ph[0] == "mix":
                self.phase_mixer(ph[1])
            elif ph[0] == "call":
                getattr(self, ph[1])(*ph[2:])
        self.P.begin_phase()
        self.P.end_phase(final=True)
        return self.nc


def const_tables(cfg):
    ident = np.eye(128, dtype=np.float32)
    LT = max(cfg.LT, 1)
    nf = 64
    inv = (10000.0 ** (-np.arange(nf, dtype=np.float32) / nf)).astype(np.float32)
    rope = np.zeros((LT, 128, 512), np.float32)
    p = np.arange(128)
    for lt in range(LT):
        row = (2 * lt + p // GRID_W).astype(np.float32)
        col = (p % GRID_W).astype(np.float32)
        ar = row[:, None] * inv[None, :]
        ac = col[:, None] * inv[None, :]
        cr, sr, cc, sc = np.cos(ar), np.sin(ar), np.cos(ac), np.sin(ac)
        rope[lt, :, 0:256] = np.concatenate([cr, cr, cc, cc], 1)
        rope[lt, :, 256:512] = np.concatenate([-sr, sr, -sc, sc], 1)
    tab = np.zeros((128, 24, 128), np.float32)
    s = np.arange(128)[:, None].astype(np.float32)
    t = np.arange(128)[None, :].astype(np.float32)
    tab[:, 0] = t - s
    tab[:, 1] = (t >= s)
    tab[:, 2] = (s >= t)
    tab[:, 3] = t + 1.0
    tab[:, 4] = 128.0 - t
    tab[:, 5, 0] = 127.0 - s[:, 0]
    tab[:, 5, 1] = s[:, 0]
    tab[:, 5, 2] = 128.0
    same = ((s // 64) == (t // 64)).astype(np.float32)
    sl = s % 64
    tab[:, 6] = same * (t >= s)
    tab[:, 7] = same * (s >= t)
    tab[:, 8] = same * ((s <= t).astype(np.float32) - (sl <= 31))
    tab[:, 9] = same * ((s >= t).astype(np.float32) - (sl >= 32))
    for cc in range(2):
        inc = ((s[:, 0] // 64) == cc).astype(np.float32)
        slc = s[:, 0] % 64
        tab[:, 10, 3 * cc + 0] = inc * (slc <= 31)
        tab[:, 10, 3 * cc + 1] = inc
        tab[:, 10, 3 * cc + 2] = inc * (slc > 31)
        tab[:, 11, 3 * cc + 0] = inc * (slc >= 32)
        tab[:, 11, 3 * cc + 1] = inc
        tab[:, 11, 3 * cc + 2] = inc * (slc < 32)
    tab[:, 12] = (s <= t)
    tab[:, 13] = (s >= t)
    tab[:, 14] = (s > t)
    tab[:, 15] = (s < t)
    tab[:, 16] = 1.0
    return ident, rope, tab.reshape(128, 24 * 128)


def make_in_maps(cfg, inputs, n_cores):
    ident, rope, tab = const_tables(cfg)
    f = lambda a: np.ascontiguousarray(np.asarray(a, dtype=np.float32))
    L = len(cfg.kinds)
    nr, nh, nm = (max(1, sum(1 for k in cfg.kinds if k == q)) for q in (0, 1, 2))
    inputs = dict(inputs)
    for k in inputs:
        if k.startswith("ret_"):
            inputs[k] = np.asarray(inputs[k])[:nr]
        elif k.startswith("hg_") and k != "hg_lb":
            inputs[k] = np.asarray(inputs[k])[:nh]
        elif k.startswith("m2_"):
            inputs[k] = np.asarray(inputs[k])[:nm]
    shared = {
        "ada_w": f(inputs["ada_w"][:L]), "ada_b": f(inputs["ada_b"][:L]),
        "norm_g": f(inputs["norm_g"][:L]).reshape(L, 4 * D),
        "ret_w_in": f(inputs["ret_w_in"]), "ret_w_out": f(inputs["ret_w_out"]),
        "ret_decay": f(inputs["ret_decay"]).reshape(-1, 8), "ret_gn": f(inputs["ret_gn"]),
        "hg_w_in": f(inputs["hg_w_in"]), "hg_w_out": f(inputs["hg_w_out"]), "hg_lb": f(inputs["hg_lb"][:L]),
        "hg_gn": f(inputs["hg_gn"]),
        "m2_w_in": f(inputs["m2_w_in"]), "m2_w_out": f(inputs["m2_w_out"]), "m2_conv_w": f(inputs["m2_conv_w"]),
        "m2_conv_b": f(inputs["m2_conv_b"]), "m2_dt_bias": f(inputs["m2_dt_bias"]).reshape(-1, 64),
        "m2_a_log": f(inputs["m2_a_log"]).reshape(-1, 64), "m2_d": f(inputs["m2_d"]), "m2_gn": f(inputs["m2_gn"]),
        "ffn_w_up": f(inputs["ffn_w_up"][:L]), "ffn_w_down": f(inputs["ffn_w_down"][:L]),
        "ffn_cw": f(np.asarray(inputs["ffn_conv_w"][:L]).reshape(L, 3, 22, 128).transpose(0, 3, 2, 1)).reshape(L, 128, 66),
        "ffn_cb": f(np.asarray(inputs["ffn_conv_b"][:L]).reshape(L, 22, 128).transpose(0, 2, 1)),
        "c_ident": ident, "c_rope": rope, "c_tab": tab,
    }
    x, cc, ctx, c_ctx = (np.asarray(inputs[k], dtype=np.float32) for k in ("x", "c", "ctx", "c_ctx"))
    maps = []
    for i in range(n_cores):
        sl = slice(i * cfg.NB, (i + 1) * cfg.NB)
        m = dict(shared)
        m["x"] = f(x[sl])
        m["ctx"] = f(ctx[sl])
        m["crow"] = f(np.concatenate([cc[sl], c_ctx[None, :]], 0))
        maps.append(m)
    return maps


_CACHE = {}


def kernel(**inputs):
    cfg = Cfg()
    if "nc" not in _CACHE:
        _CACHE["nc"] = Builder(cfg).build()
    nc = _CACHE["nc"]
    maps = make_in_maps(cfg, inputs, N_CORES)
    res = run_bass_kernel_spmd(nc, maps, core_ids=list(range(N_CORES)))
    out = np.concatenate([np.asarray(r["out"]) for r in res.results], axis=0)
    return out.astype(np.float32)
```
